# Optimizing a Trainium2 kernel written in Bass

```python
import jax, jax.numpy as jnp
from jax import lax
import numpy as np

D_MODEL = 1024
BATCH = 8
SEQ = 2048
DEPTH = 4
DEC_BATCH = 32
DEC_SEQ = 8
PAST_LEN = 16384
PAGE_SIZE = 128

N_A_LAYERS = DEPTH // 2
N_B_LAYERS = DEPTH - N_A_LAYERS
MIX_W = D_MODEL
MEM_HEADS = 4
MEM_HEAD_DIM = 64
MEM_W = MEM_HEADS * MEM_HEAD_DIM
N_MEM = 256
POOL_W = MIX_W - MEM_W
POOL_WINDOWS = (2, 4, 8, 16)
POOL_GROUPS = len(POOL_WINDOWS)
POOL_GROUP_W = POOL_W // POOL_GROUPS
POOL_STATE = max(POOL_WINDOWS) - 1
MLA_HEADS = 12
QK_NOPE = 64
QK_ROPE = 32
QK_DIM = QK_NOPE + QK_ROPE
V_HEAD = 64
KV_LORA = 256
Q_LORA = 384
KV_ROW = KV_LORA + QK_ROPE
ROPE_BASE = 10000.0
D_FF = -(-(8 * D_MODEL) // (3 * 256)) * 256
Q_BLOCK = 128
EPS = 1e-6
NEG = -1e30

kernel_name = 'yoco_pool_mla_memory_decoder_step'


def rmsnorm(x, g):
    xf = x.astype(jnp.float32)
    y = xf * lax.rsqrt(jnp.mean(xf * xf, axis=-1, keepdims=True) + EPS)
    return (y * g.astype(jnp.float32)).astype(x.dtype)


def rope(x, pos):
    half = x.shape[-1] // 2
    inv_freq = ROPE_BASE ** (-jnp.arange(half, dtype=jnp.float32) / half)
    ang = pos.astype(jnp.float32)[:, None] * inv_freq[None, :]
    shape = (1, pos.shape[0]) + (1,) * (x.ndim - 3) + (half,)
    cos = jnp.cos(ang).reshape(shape)
    sin = jnp.sin(ang).reshape(shape)
    xf = x.astype(jnp.float32)
    x1, x2 = xf[..., :half], xf[..., half:]
    return jnp.concatenate([x1 * cos - x2 * sin, x1 * sin + x2 * cos], axis=-1).astype(x.dtype)


def pool_mixer(u, prev, pos, w_grp, scale):
    b, s, _ = u.shape
    p_len = prev.shape[1]
    ext = jnp.concatenate([prev.astype(u.dtype), u], axis=1)
    csum = jnp.pad(jnp.cumsum(ext.astype(jnp.float32), axis=1), ((0, 0), (1, 0), (0, 0)))
    outs = []
    for g, w in enumerate(POOL_WINDOWS):
        sl = slice(g * POOL_GROUP_W, (g + 1) * POOL_GROUP_W)
        win_sum = csum[:, p_len + 1:p_len + s + 1, sl] - csum[:, p_len + 1 - w:p_len + s + 1 - w, sl]
        cnt = jnp.minimum(w, pos + 1).astype(jnp.float32)
        outs.append(win_sum / cnt[None, :, None])
    pooled = jnp.stack(outs, axis=2)
    diff = (pooled - u.astype(jnp.float32).reshape(b, s, POOL_GROUPS, POOL_GROUP_W)).astype(u.dtype)
    mixed = jnp.einsum('bsgc,gce->bsge', diff, w_grp).reshape(b, s, POOL_W)
    return mixed * scale, ext[:, -p_len:]


def memory_kv(mem, norm_mem, w_mem_kv, g_mem_k):
    b, m, _ = mem.shape
    mf = mem.astype(jnp.float32)
    hm = mf * lax.rsqrt(jnp.mean(mf * mf, axis=-1, keepdims=True) + EPS)
    hm = (hm[None] * norm_mem.astype(jnp.float32)[:, None, None, :]).astype(mem.dtype)
    kv = jnp.einsum('lbmd,lde->lbme', hm, w_mem_kv)
    k = kv[..., :MEM_W].reshape(DEPTH, b, m, MEM_HEADS, MEM_HEAD_DIM)
    k = rmsnorm(k, g_mem_k[:, None, None, None, :])
    v = kv[..., MEM_W:].reshape(DEPTH, b, m, MEM_HEADS, MEM_HEAD_DIM)
    return k, v


def mem_attend(qm, mk, mv, g_q):
    b, s, _ = qm.shape
    q = rmsnorm(qm.reshape(b, s, MEM_HEADS, MEM_HEAD_DIM), g_q)
    sc = jnp.einsum('bshd,bmhd->bhsm', q, mk).astype(jnp.float32) * MEM_HEAD_DIM ** -0.5
    p = jax.nn.softmax(sc, axis=-1).astype(mv.dtype)
    return jnp.einsum('bhsm,bmhd->bshd', p, mv).reshape(b, s, MEM_W)


def shared_kv_rows(x, pos, norm_kv, w_dkv, g_kv_lora):
    h = rmsnorm(x, norm_kv)
    ck = h @ w_dkv
    c = rmsnorm(ck[..., :KV_LORA], g_kv_lora)
    kr = rope(ck[..., KV_LORA:], pos)
    return jnp.concatenate([c, kr], axis=-1)


def key_inv_rms(kv, w_uk):
    def one(rows):
        kn = (rows[:, :KV_LORA] @ w_uk).astype(jnp.float32).reshape(rows.shape[0], MLA_HEADS, QK_NOPE)
        kr = rows[:, KV_LORA:].astype(jnp.float32)
        ss = jnp.sum(kn * kn, axis=-1) + jnp.sum(kr * kr, axis=-1, keepdims=True)
        return lax.rsqrt(ss / QK_DIM + EPS)
    return lax.map(one, kv)


def mla_queries(cq_raw, pos, g_q_lora, w_uq, g_q, g_k_nope, g_k_rope, w_uk):
    b, s, _ = cq_raw.shape
    cq = rmsnorm(cq_raw, g_q_lora)
    q = rmsnorm((cq @ w_uq).reshape(b, s, MLA_HEADS, QK_DIM), g_q)
    q_nope = q[..., :QK_NOPE] * g_k_nope
    q_rope = rope(q[..., QK_NOPE:], pos) * jnp.concatenate([g_k_rope, g_k_rope])
    q_lat = jnp.einsum('bshd,chd->bshc', q_nope, w_uk.reshape(KV_LORA, MLA_HEADS, QK_NOPE))
    return q_lat, q_rope


def mla_attend(q_lat, q_rope, q_pos, kv, k_inv, k_pos):
    c = kv[..., :KV_LORA]
    kr = kv[..., KV_LORA:]
    sc = jnp.einsum('bshc,bkc->bhsk', q_lat, c) + jnp.einsum('bshr,bkr->bhsk', q_rope, kr)
    sc = sc.astype(jnp.float32) * jnp.swapaxes(k_inv, 1, 2)[:, :, None, :] * QK_DIM ** -0.5
    sc = jnp.where((k_pos[None, :] <= q_pos[:, None])[None, None], sc, NEG)
    p = jax.nn.softmax(sc, axis=-1).astype(c.dtype)
    return jnp.einsum('bhsk,bkc->bshc', p, c)


def mla_blocked(q_lat, q_rope, q_pos, kv, k_inv, k_pos):
    b, s = q_lat.shape[0], q_lat.shape[1]
    if s > Q_BLOCK and s % Q_BLOCK == 0:
        nb = s // Q_BLOCK
        def split(a):
            return jnp.moveaxis(a.reshape((b, nb, Q_BLOCK) + a.shape[2:]), 1, 0)
        out = lax.map(lambda blk: mla_attend(blk[0], blk[1], blk[2], kv, k_inv, k_pos),
                      (split(q_lat), split(q_rope), q_pos.reshape(nb, Q_BLOCK)))
        return jnp.moveaxis(out, 0, 1).reshape((b, s) + out.shape[3:])
    return mla_attend(q_lat, q_rope, q_pos, kv, k_inv, k_pos)


def trunk(x, pos, pool_prev, mem_k, mem_v, kv_past, p):
    b, s, _ = x.shape
    pool_new = []
    kv_new = None
    for l in range(DEPTH):
        h = rmsnorm(x, p['norm_mix'][l])
        if l < N_A_LAYERS:
            proj = h @ p['w_in_a'][l]
            tok, st = pool_mixer(proj[..., :POOL_W], pool_prev[l], pos, p['w_pool_grp'][l], p['pool_scale'][l])
            pool_new.append(st)
            qm = proj[..., POOL_W:]
        else:
            j = l - N_A_LAYERS
            if j == 0:
                kv_new = shared_kv_rows(x, pos, p['norm_kv'], p['w_dkv'], p['g_kv_lora'])
                kv_all = kv_new if kv_past is None else jnp.concatenate([kv_past, kv_new.astype(kv_past.dtype)], axis=1)
                k_inv = key_inv_rms(kv_all, p['w_uk'])
                k_pos = jnp.arange(kv_all.shape[1])
            proj = h @ p['w_in_b'][j]
            q_lat, q_rope = mla_queries(proj[..., :Q_LORA], pos, p['g_q_lora'][j], p['w_uq'][j], p['g_q'][j],
                                        p['g_k_nope'], p['g_k_rope'], p['w_uk'])
            o_lat = mla_blocked(q_lat, q_rope, pos, kv_all, k_inv, k_pos)
            tok = jnp.einsum('bshc,chd->bshd', o_lat,
                             p['w_uv'].reshape(KV_LORA, MLA_HEADS, V_HEAD)).reshape(b, s, MLA_HEADS * V_HEAD)
            qm = proj[..., Q_LORA:]
        mo = mem_attend(qm, mem_k[l], mem_v[l], p['g_mem_q'][l])
        x = x + jnp.concatenate([tok, mo], axis=-1) @ p['w_out'][l]
        gu = rmsnorm(x, p['norm_ffn'][l]) @ p['w_ffn_in'][l]
        x = x + (jax.nn.silu(gu[..., :D_FF]) * gu[..., D_FF:]) @ p['w_ffn_out'][l]
    return x, kv_new, jnp.stack(pool_new)


def setup_inputs(seed: int = 0) -> dict:
    key = jax.random.key(seed)
    keys = jax.random.split(key, 40)
    ctr = [0]
    def nk():
        ctr[0] += 1
        return keys[ctr[0] - 1]
    def nrm(shape, scale=1.0):
        return jax.random.normal(nk(), shape, jnp.float32) * scale
    def gain(shape):
        return 1.0 + 0.1 * jax.random.normal(nk(), shape, jnp.float32)
    n_pages = PAST_LEN // PAGE_SIZE
    n_used = DEC_BATCH * n_pages
    n_pool = n_used + (n_used + 3) // 4
    d = {}
    d['x_prompt'] = nrm((BATCH, SEQ, D_MODEL))
    d['x_sample'] = nrm((DEC_BATCH, DEC_SEQ, D_MODEL))
    d['cache_kv'] = nrm((n_pool, PAGE_SIZE, KV_ROW))
    d['state_pool'] = nrm((N_A_LAYERS, DEC_BATCH, POOL_STATE, POOL_W))
    d['cache_mem_k'] = nrm((DEPTH, DEC_BATCH, N_MEM, MEM_HEADS, MEM_HEAD_DIM))
    d['cache_mem_v'] = nrm((DEPTH, DEC_BATCH, N_MEM, MEM_HEADS, MEM_HEAD_DIM))
    perm = jax.random.permutation(nk(), n_pool).astype(jnp.int32)
    d['page_table'] = perm[:n_used].reshape(DEC_BATCH, n_pages)
    d['mem_prompt'] = nrm((BATCH, N_MEM, D_MODEL))
    d['norm_mix'] = gain((DEPTH, D_MODEL))
    d['norm_ffn'] = gain((DEPTH, D_MODEL))
    d['w_out'] = nrm((DEPTH, MIX_W, D_MODEL), MIX_W ** -0.5)
    d['w_ffn_in'] = nrm((DEPTH, D_MODEL, 2 * D_FF), D_MODEL ** -0.5)
    d['w_ffn_out'] = nrm((DEPTH, D_FF, D_MODEL), D_FF ** -0.5)
    d['norm_mem'] = gain((DEPTH, D_MODEL))
    d['w_mem_kv'] = nrm((DEPTH, D_MODEL, 2 * MEM_W), D_MODEL ** -0.5)
    d['g_mem_q'] = gain((DEPTH, MEM_HEAD_DIM))
    d['g_mem_k'] = gain((DEPTH, MEM_HEAD_DIM))
    d['w_in_a'] = nrm((N_A_LAYERS, D_MODEL, POOL_W + MEM_W), D_MODEL ** -0.5)
    d['w_pool_grp'] = nrm((N_A_LAYERS, POOL_GROUPS, POOL_GROUP_W, POOL_GROUP_W), POOL_GROUP_W ** -0.5)
    d['pool_scale'] = gain((N_A_LAYERS, POOL_W))
    d['w_in_b'] = nrm((N_B_LAYERS, D_MODEL, Q_LORA + MEM_W), D_MODEL ** -0.5)
    d['g_q_lora'] = gain((N_B_LAYERS, Q_LORA))
    d['w_uq'] = nrm((N_B_LAYERS, Q_LORA, MLA_HEADS * QK_DIM), Q_LORA ** -0.5)
    d['g_q'] = gain((N_B_LAYERS, QK_DIM))
    d['norm_kv'] = gain((D_MODEL,))
    d['w_dkv'] = nrm((D_MODEL, KV_ROW), D_MODEL ** -0.5)
    d['g_kv_lora'] = gain((KV_LORA,))
    d['w_uk'] = nrm((KV_LORA, MLA_HEADS * QK_NOPE), KV_LORA ** -0.5)
    d['w_uv'] = nrm((KV_LORA, MLA_HEADS * V_HEAD), KV_LORA ** -0.5)
    d['g_k_nope'] = gain((QK_NOPE,))
    d['g_k_rope'] = gain((QK_ROPE // 2,))
    return d


def reference(x_prompt, x_sample, cache_kv, state_pool, cache_mem_k, cache_mem_v, page_table, mem_prompt,
              norm_mix, norm_ffn, w_out, w_ffn_in, w_ffn_out, norm_mem, w_mem_kv, g_mem_q, g_mem_k,
              w_in_a, w_pool_grp, pool_scale, w_in_b, g_q_lora, w_uq, g_q,
              norm_kv, w_dkv, g_kv_lora, w_uk, w_uv, g_k_nope, g_k_rope):
    p = dict(norm_mix=norm_mix, norm_ffn=norm_ffn, w_out=w_out, w_ffn_in=w_ffn_in, w_ffn_out=w_ffn_out,
             g_mem_q=g_mem_q, w_in_a=w_in_a, w_pool_grp=w_pool_grp, pool_scale=pool_scale,
             w_in_b=w_in_b, g_q_lora=g_q_lora, w_uq=w_uq, g_q=g_q, norm_kv=norm_kv, w_dkv=w_dkv,
             g_kv_lora=g_kv_lora, w_uk=w_uk, w_uv=w_uv, g_k_nope=g_k_nope, g_k_rope=g_k_rope)
    b_p, s_p, _ = x_prompt.shape
    pos_p = jnp.arange(s_p)
    mem_k_prompt, mem_v_prompt = memory_kv(mem_prompt, norm_mem, w_mem_kv, g_mem_k)
    pool0 = jnp.zeros((N_A_LAYERS, b_p, POOL_STATE, POOL_W), x_prompt.dtype)
    y_prompt, kv_prompt, pool_prompt = trunk(x_prompt, pos_p, pool0, mem_k_prompt, mem_v_prompt, None, p)
    n_seq, n_pages = page_table.shape
    past_len = n_pages * cache_kv.shape[1]
    kv_past = cache_kv[page_table].reshape(n_seq, past_len, KV_ROW)
    pos_s = past_len + jnp.arange(x_sample.shape[1])
    y_sample, kv_sample, pool_sample = trunk(x_sample, pos_s, state_pool, cache_mem_k, cache_mem_v, kv_past, p)
    return (y_prompt, y_sample, kv_prompt, kv_sample, pool_prompt, pool_sample, mem_k_prompt, mem_v_prompt)
```

```python
import numpy as np
import concourse.bass as bass
import concourse.mybir as mybir
from concourse.bass_utils import run_bass_kernel_spmd
from contextlib import ExitStack

F32 = mybir.dt.float32
BF16 = mybir.dt.bfloat16
I32 = mybir.dt.int32
AF = mybir.ActivationFunctionType
ALU = mybir.AluOpType
AX = mybir.AxisListType

D = 1024
DFF = 2816
EPS = 1e-6
TB = 512
NCORES = 8


class Prog:
    ENGS = ('pe', 'act', 'dve', 'pool', 'sp')
    NDMA = 8

    def __init__(self, nc, es, pfx=''):
        self.nc, self.es = nc, es
        self.pfx = pfx
        self.ops = []
        self.lastw = {}
        self.readers = {}
        self.out_dmas = []

    def sb(self, name, shape, dt):
        return self.es.enter_context(self.nc.sbuf_tensor(self.pfx + name, shape, dt))

    def ps(self, name, shape, dt):
        return self.es.enter_context(self.nc.psum_tensor(self.pfx + name, shape, dt))

    def op(self, eng, fn, r=(), w=(), dma=False, out=False):
        i = len(self.ops)
        deps = set()
        pr = [k for k in r if isinstance(k, tuple) and k[0] == 'ps']
        if pr:
            w = list(w) + [k for k in pr if k not in w]
        for k in r:
            lw = self.lastw.get(k)
            if lw is not None:
                deps.add(lw)
        for k in w:
            lw = self.lastw.get(k)
            if lw is not None:
                deps.add(lw)
            for rd in self.readers.get(k, ()):
                deps.add(rd)
        keep = []
        for d in deps:
            de, ddma = self.ops[d][0], self.ops[d][3]
            if de == eng and not dma and not ddma and eng == 'pe':
                continue
            keep.append(d)
        for k in r:
            self.readers.setdefault(k, []).append(i)
        for k in w:
            self.lastw[k] = i
            self.readers[k] = []
        self.ops.append([eng, fn, keep, dma])
        if out:
            self.out_dmas.append(i)
        return i

    def finish(self):
        import os
        stop = int(os.environ.get("K_STOP", "0"))
        if stop:
            self.ops = self.ops[:stop]
            self.out_dmas = [i for i in range(len(self.ops)) if self.ops[i][3]]
        self.ops.append(['sp', None, list(self.out_dmas), False])

    def emit(self, tag=''):
        nc, es = self.nc, self.es
        ops = self.ops
        n = len(ops)
        needed = [False] * n
        for o in ops:
            for d in o[2]:
                needed[d] = True
        csem = {e: es.enter_context(nc.semaphore(tag + 'c_' + e)) for e in self.ENGS}
        ccnt = {e: 0 for e in self.ENGS}
        dsem = {e: [es.enter_context(nc.semaphore('%sd_%s%d' % (tag, e, j))) for j in range(self.NDMA)]
                for e in ('sp', 'pool', 'act')}
        dcnt = {e: [0] * self.NDMA for e in dsem}
        dlast = {e: [None] * self.NDMA for e in dsem}
        drr = {e: 0 for e in dsem}
        sig = [None] * n
        extra = [None] * n
        streams = {e: [] for e in self.ENGS}
        for i, o in enumerate(ops):
            eng, fn, deps, dma = o
            streams[eng].append(i)
            if dma:
                j = drr[eng]
                drr[eng] = (j + 1) % self.NDMA
                dcnt[eng][j] += 1
                sig[i] = (dsem[eng][j], 16 * dcnt[eng][j])
                extra[i] = dlast[eng][j]
                dlast[eng][j] = i
            elif needed[i]:
                ccnt[eng] += 1
                sig[i] = (csem[eng], ccnt[eng])
        self.nwaits = 0

        def body(e, eng):
            known = {}
            for i in streams[eng]:
                _, fn, deps, dma = ops[i]
                req = {}
                dl = list(deps)
                if extra[i] is not None:
                    dl.append(extra[i])
                for d in dl:
                    s, v = sig[d]
                    if req.get(id(s), (None, 0))[1] < v:
                        req[id(s)] = (s, v)
                for s, v in req.values():
                    if known.get(id(s), 0) < v:
                        e.wait_ge(s, v)
                        known[id(s)] = v
                        self.nwaits += 1
                if fn is None:
                    continue
                ins = fn(e)
                if sig[i] is not None:
                    ins.then_inc(sig[i][0], 16 if dma else 1)

        with nc.Block() as block:
            @block.tensor
            def _(e):
                body(e, 'pe')

            @block.scalar
            def _(e):
                body(e, 'act')

            @block.vector
            def _(e):
                body(e, 'dve')

            @block.gpsimd
            def _(e):
                body(e, 'pool')

            @block.sync
            def _(e):
                body(e, 'sp')


class Rot:
    def __init__(self, items):
        self.items, self.i = items, 0

    def next(self):
        x = self.items[self.i % len(self.items)]
        self.i += 1
        return x


class Ops:
    def __init__(self, P):
        self.P = P

    def MM(self, o, l, rh, st, sp, r, w):
        self.P.op('pe', lambda e: e.matmul(o, lhsT=l, rhs=rh, start=st, stop=sp), r=r, w=w)

    def TR(self, o, i, idn, r, w):
        self.P.op('pe', lambda e: e.transpose(out=o, in_=i, identity=idn), r=r, w=w)

    def ACT(self, o, i, f, r, w, **kw):
        self.P.op('act', lambda e: e.activation(out=o, in_=i, func=f, **kw), r=r, w=w)

    def TT(self, eng, o, a, b, op, r, w):
        self.P.op(eng, lambda e: e.tensor_tensor(out=o, in0=a, in1=b, op=op), r=r, w=w)

    def STT(self, eng, o, a, s, b, op0, op1, r, w):
        self.P.op(eng, lambda e: e.scalar_tensor_tensor(out=o, in0=a, scalar=s, in1=b, op0=op0, op1=op1), r=r, w=w)

    def TS(self, eng, o, a, s1, s2, op0, op1, r, w):
        self.P.op(eng, lambda e: e.tensor_scalar(out=o, in0=a, scalar1=s1, scalar2=s2, op0=op0, op1=op1), r=r, w=w)

    def CP(self, eng, o, i, r, w):
        if eng == 'act':
            self.P.op('act', lambda e: e.activation(out=o, in_=i, func=AF.Copy), r=r, w=w)
        else:
            self.P.op(eng, lambda e: e.tensor_copy(out=o, in_=i), r=r, w=w)

    def TSA(self, eng, o, a, s1, r, w):
        self.P.op(eng, lambda e: e.tensor_scalar_add(out=o, in0=a, scalar1=s1), r=r, w=w)

    def TSM(self, eng, o, a, s1, r, w):
        self.P.op(eng, lambda e: e.tensor_scalar_mul(out=o, in0=a, scalar1=s1), r=r, w=w)

    def RCP(self, o, i, r, w):
        self.P.op('dve', lambda e: e.reciprocal(out=o, in_=i), r=r, w=w)

    def MSET(self, eng, o, v, w):
        self.P.op(eng, lambda e: e.memset(o, v), w=w)

    def RED(self, o, i, r, w):
        self.P.op('dve', lambda e: e.tensor_reduce(out=o, in_=i, axis=AX.X, op=ALU.add), r=r, w=w)

    def DMA(self, eng, o, i, r=(), w=(), out=False, slow=False):
        if slow:
            self.P.op(eng, lambda e: e.dma_start(out=o, in_=i, allow_slow_non_contiguous=True), r=r, w=w, dma=True, out=out)
        else:
            self.P.op(eng, lambda e: e.dma_start(out=o, in_=i), r=r, w=w, dma=True, out=out)


CKV_ROWS = [5120 * 128]
PT_COLS = [128]


def in_specs(NT):
    return [
        ("xp", [NT, D], F32), ("xs", [32, D], F32), ("ckv", [CKV_ROWS[0], 288], F32), ("pt", [4, PT_COLS[0]], I32),
        ("spool", [2, 4, 15, 768], F32), ("cmk", [4, 4, 256, 256], F32), ("cmv", [4, 4, 256, 256], F32),
        ("memp", [256, D], F32),
        ("norm_mix", [4, D], F32), ("norm_ffn", [4, D], F32), ("w_out", [4, D, D], F32),
        ("w_ffn_in", [4, D, 2 * DFF], F32), ("w_ffn_out", [4, DFF, D], F32), ("norm_mem", [4, D], F32),
        ("w_mem_kv", [4, D, 512], F32), ("g_mem_q", [4, 64], F32), ("g_mem_k", [4, 64], F32),
        ("w_in_a", [2, D, D], F32), ("w_pool_grp", [2, 4, 192, 192], F32), ("pool_scale", [2, 768], F32),
        ("w_in_b", [2, D, 640], F32), ("g_q_lora", [2, 384], F32), ("w_uq", [2, 384, 1152], F32), ("g_q", [2, 96], F32),
        ("norm_kv", [D], F32), ("w_dkv", [D, 288], F32), ("g_kv_lora", [256], F32), ("w_uk", [256, 768], F32),
        ("w_uv", [256, 768], F32), ("g_k_nope", [64], F32), ("g_k_rope", [16], F32),
        ("c_ident", [128, 128], F32), ("c_perm", [96, 96], F32), ("c_cos", [96, 2048 + 32], F32),
        ("c_sin", [96, 2048 + 32], F32), ("c_tri", [128, 128], F32), ("c_rcnt", [96, 8, 16], F32),
        ("c_smask", [32, 4, 96], F32),
    ]


def out_specs(NT):
    return [
        ("y_p", [NT, D]), ("y_s", [32, D]), ("kv_p", [NT, 288]), ("kv_s", [32, 288]),
        ("pool_p", [2, 15, 768]), ("pool_s", [2, 4, 15, 768]), ("mk_p", [4, 256, 256]), ("mv_p", [4, 256, 256]),
    ]


def host_consts():
    c = {}
    c["c_ident"] = np.eye(128, dtype=np.float32)
    perm = np.zeros((96, 96), np.float32)
    for i in range(16):
        perm[80 + i, 64 + i] = -1.0
        perm[64 + i, 80 + i] = 1.0
    c["c_perm"] = perm
    half = 16
    inv_freq = (np.float32(10000.0) ** (-np.arange(half, dtype=np.float32) / np.float32(half))).astype(np.float32)
    pos = np.concatenate([np.arange(2048), np.tile(16384 + np.arange(8), 4)]).astype(np.float32)
    ang = (pos[None, :] * inv_freq[:, None]).astype(np.float32)
    cos = np.ones((96, 2080), np.float32)
    sin = np.zeros((96, 2080), np.float32)
    cos[64:80] = np.cos(ang)
    cos[80:96] = np.cos(ang)
    sin[64:80] = np.sin(ang)
    sin[80:96] = np.sin(ang)
    c["c_cos"], c["c_sin"] = cos, sin
    kk = np.arange(128)
    c["c_tri"] = (kk[None, :] >= kk[:, None]).astype(np.float32)
    rc = np.zeros((96, 8, 16), np.float32)
    for ch in range(8):
        w = 2 ** (ch // 2 + 1)
        rc[:, ch, :] = (1.0 / np.minimum(w, np.arange(16) + 1)).astype(np.float32)[None, :]
    c["c_rcnt"] = rc
    sm = np.zeros((32, 4, 12, 8), np.float32)
    for k in range(32):
        s = k // 8
        for q in range(8):
            if k % 8 <= q:
                sm[k, s, :, q] = 1.0
    c["c_smask"] = sm.reshape(32, 4, 96)
    return c


def build(NT=2048, SEG=512, do_prompt=True, do_sample=True, nlayers=4, NPG=128):
    nc = bass.Bass("TRN2", target_bir_lowering=False)
    I = {n: nc.dram_tensor(n, s, dt, kind="ExternalInput").ap() for n, s, dt in in_specs(NT)}
    O = {n: nc.dram_tensor(n, s, F32, kind="ExternalOutput").ap() for n, s in out_specs(NT)}
    SC = None
    if do_prompt and do_sample:
        SC = {n: nc.dram_tensor("sc_" + n, shp, BF16).ap() for n, shp in (
            ("w_ffn_in", [4, D, 2 * DFF]), ("w_ffn_out", [4, DFF, D]), ("w_in_a", [2, D, D]), ("w_in_b", [2, D, 640]),
            ("w_out", [4, D, D]), ("w_pool_grp", [2, 4, 192, 192]), ("w_uq", [2, 384, 1152]))}
    if do_sample:
        with ExitStack() as es:
            sample_program(nc, es, I, O, nlayers, NPG, SC)
    if do_prompt:
        with ExitStack() as es:
            prompt_program(nc, es, I, O, NT, SEG, nlayers, SC)
    return nc


def load_small(P, o, I, C):
    g1 = P.sb("gst1", [128, 128], F32)
    g2 = P.sb("gst2", [32, 128], F32)
    gT1 = P.sb("gT1", [128, 128], F32)
    gT2 = P.sb("gT2", [128, 32], F32)
    o.MSET('dve', g1[:], 0.0, w=['gst1'])
    o.MSET('dve', g2[:], 0.0, w=['gst2'])
    o.DMA('sp', g1[0:32, :], I["norm_mix"].rearrange("l (k p) -> (l k) p", p=128), w=['gst1'])
    o.DMA('sp', g1[32:64, :], I["norm_ffn"].rearrange("l (k p) -> (l k) p", p=128), w=['gst1'])
    o.DMA('sp', g1[64:96, :], I["norm_mem"].rearrange("l (k p) -> (l k) p", p=128), w=['gst1'])
    o.DMA('sp', g1[96:104, :], I["norm_kv"].rearrange("(k p) -> k p", p=128), w=['gst1'])
    o.DMA('sp', g1[104:106, :], I["g_kv_lora"].rearrange("(k p) -> k p", p=128), w=['gst1'])
    o.DMA('sp', g1[106:112, :], I["g_q_lora"].rearrange("l (k p) -> (l k) p", p=128), w=['gst1'])
    o.DMA('sp', g2[0:16, 0:96], I["pool_scale"].rearrange("l (k p) -> (l k) p", p=96), w=['gst2'])
    o.DMA('sp', g2[16:20, 0:64], I["g_mem_q"][:, :], w=['gst2'])
    o.DMA('sp', g2[16:20, 64:128], I["g_mem_q"][:, :], w=['gst2'])
    o.DMA('sp', g2[20:24, 0:64], I["g_mem_k"][:, :], w=['gst2'])
    o.DMA('sp', g2[20:24, 64:128], I["g_mem_k"][:, :], w=['gst2'])
    o.DMA('sp', g2[24:26, 0:96], I["g_q"][:, :], w=['gst2'])
    o.DMA('sp', g2[26:27, 0:64], I["g_k_nope"].rearrange("(o p) -> o p", o=1), w=['gst2'])
    o.DMA('sp', g2[26:27, 64:80], I["g_k_rope"].rearrange("(o p) -> o p", o=1), w=['gst2'])
    o.DMA('sp', g2[26:27, 80:96], I["g_k_rope"].rearrange("(o p) -> o p", o=1), w=['gst2'])
    ps = C['ps']
    o.TR(ps[0][:, 0:128], g1[:, :], C['ident'][:, :], r=['gst1', 'ident'], w=[('ps', 0)])
    o.CP('dve', gT1[:, :], ps[0][:, 0:128], r=[('ps', 0)], w=['gains'])
    o.TR(ps[1][:, 0:32], g2[:, :], C['ident'][0:32, 0:32], r=['gst2', 'ident'], w=[('ps', 1)])
    o.CP('dve', gT2[:, :], ps[1][:, 0:32], r=[('ps', 1)], w=['gains'])
    G = {'gmix': gT1[:, 0:32], 'gffn': gT1[:, 32:64], 'gmem': gT1[:, 64:96], 'gkv': gT1[:, 96:104], 'gkvl': gT1[:, 104:106],
         'gql': gT1[:, 106:112], 'pscale': gT2[0:96, 0:16], 'gmq': gT2[:, 16:20], 'gmk': gT2[:, 20:24], 'gq': gT2[0:96, 24:26],
         'gk': gT2[0:96, 26:27]}
    return G


def alloc_common(P, o, I):
    C = {}
    C['ident'] = P.sb("ident", [128, 128], F32)
    C['ones'] = P.sb("ones", [128, 128], BF16)
    C['blk2'] = P.sb("blk2", [128, 128], BF16)
    C['perm'] = P.sb("perm", [96, 96], BF16)
    C['tri'] = P.sb("tri", [128, 128], BF16)
    C['rcnt'] = P.sb("rcnt", [96, 8, 16], F32)
    o.DMA('sp', C['ident'][:], I["c_ident"][:, :], w=['ident'])
    o.DMA('pool', C['perm'][:], I["c_perm"][:, :], w=['perm'])
    o.DMA('pool', C['tri'][:], I["c_tri"][:, :], w=['tri'])
    o.DMA('sp', C['rcnt'][:], I["c_rcnt"][:, :, :], w=['rcnt'])
    o.MSET('dve', C['ones'][:], 1.0, w=['ones'])
    o.MSET('dve', C['blk2'][:], 0.0, w=['blk2'])
    o.MSET('dve', C['blk2'][0:64, 0:64], 1.0, w=['blk2'])
    o.MSET('dve', C['blk2'][64:128, 64:128], 1.0, w=['blk2'])
    C['wuk'] = P.sb("wuk", [128, 2, 768], BF16)
    C['wuv'] = P.sb("wuv", [128, 2, 768], BF16)
    C['wdkv'] = P.sb("wdkv", [128, 8, 352], BF16)
    o.DMA('pool', C['wuk'][:], I["w_uk"].rearrange("(k p) n -> p k n", p=128), w=['wuk'])
    o.DMA('pool', C['wuv'][:], I["w_uv"].rearrange("(k p) n -> p k n", p=128), w=['wuv'])
    o.MSET('pool', C['wdkv'][:, :, 256:320], 0.0, w=['wdkv'])
    o.DMA('pool', C['wdkv'][:, :, 0:256], I["w_dkv"].rearrange("(k p) n -> p k n", p=128)[:, :, 0:256], w=['wdkv'])
    o.DMA('pool', C['wdkv'][:, :, 320:352], I["w_dkv"].rearrange("(k p) n -> p k n", p=128)[:, :, 256:288], w=['wdkv'])
    ps = [P.ps("ps%d" % i, [128, 512], F32) for i in range(8)]
    C['ps'] = ps
    C['G'] = load_small(P, o, I, C)
    C['rA'] = Rot([0, 1])
    C['rS'] = Rot([2, 3])
    C['rO'] = Rot([4, 5])
    C['rG'] = Rot([6, 7])
    C['tmp'] = P.sb("tmpn", [128, 512], F32)
    C['rstd'] = P.sb("rstd", [128, 512], F32)
    C['rtmp'] = Rot([0])
    return C


def norm_fm(o, C, xaps, gcols, haps, np_, Dn, tb, rkeys, hkey, ones, eps=EPS, sq=None):
    ps = C['ps']
    b = C['rA'].next()
    pk = ('ps', b)
    nch = len(xaps)
    sqaps, sqkey = (haps, hkey) if sq is None else sq
    for c in range(nch):
        o.ACT(sqaps[c], xaps[c], AF.Square, r=rkeys, w=[sqkey])
    for c in range(nch):
        o.MM(ps[b][0:np_, 0:tb], ones, sqaps[c], c == 0, c == nch - 1, r=[sqkey, 'ones', 'blk2'], w=[pk])
    o.ACT(C['tmp'][0:np_, 0:tb], ps[b][0:np_, 0:tb], AF.Ln, r=[pk], w=['tmpn'], scale=1.0 / Dn, bias=eps)
    o.ACT(C['rstd'][0:np_, 0:tb], C['tmp'][0:np_, 0:tb], AF.Exp, r=['tmpn'], w=['rstd'], scale=-0.5)
    for c in range(nch):
        o.STT('dve', haps[c], xaps[c], gcols[c], C['rstd'][0:np_, 0:tb], ALU.mult, ALU.mult,
              r=list(rkeys) + ['rstd', 'gains', sqkey], w=[hkey])


def prompt_program(nc, es, I, O, NT, SEG, nlayers, SC=None):
    P = Prog(nc, es)
    o = Ops(P)
    C = alloc_common(P, o, I)
    ps = C['ps']
    G = C['G']
    ident, ones, blk2, perm, tri = C['ident'], C['ones'], C['blk2'], C['perm'], C['tri']
    NB = SEG // TB
    NSEG = NT // SEG
    NKC = NT // 128
    tb = TB

    xT = P.sb("xT", [128, 8, SEG], F32)
    hT = P.sb("hT", [128, 8, SEG], BF16)
    win = P.sb("win", [128, 8, 1024], BF16)
    wout = P.sb("wout", [128, 10, 1024], BF16)
    wmisc = P.sb("wmisc", [128, 3456], BF16)
    NRG, NRO = 2, 3
    ringG = [P.sb("ringG%d" % i, [128, 4096], BF16) for i in range(NRG)]
    ringO = [P.sb("ringO%d" % i, [128, 2048], BF16) for i in range(NRO)]
    cT = P.sb("cT", [128, 2, NT], BF16)
    KT = [P.sb("KT%d" % i, [96, NT], BF16) for i in range(2)]
    Vh = [P.sb("Vh%d" % i, [128, NKC, 128], BF16) for i in range(2)]
    kinv = P.sb("kinv", [128, NKC, 12], F32)
    cat = P.sb("cat", [128, 10, tb], BF16)
    qm = P.sb("qm", [128, 2, tb], F32)
    qn = P.sb("qn", [128, 2, tb], BF16)
    PT = [P.sb("PT%d" % i, [128, tb], BF16) for i in range(4)]
    rPT = Rot([0, 1, 2, 3])
    rc = P.sb("rc", [128, tb], F32)
    cosb = P.sb("cosb", [96, tb], F32)
    sinb = P.sb("sinb", [96, tb], F32)
    memKT = P.sb("memKT", [128, 2, 256], BF16)
    Vaug = P.sb("Vaug", [128, 2, 4, 128], BF16)
    halo = P.sb("halo", [96, 2, 8, 16], F32)
    AFN, ABN = 7616, 4096
    arf = P.sb("arena_f", [128, AFN], F32)
    arb = P.sb("arena_b", [128, ABN], BF16)

    def cv(ar, off, parts, n, pat=None, **kw):
        v = ar[0:parts, off:off + n]
        return v.rearrange(pat, **kw) if pat else v
    xin = cv(arf, 0, 128, 1024)
    memT = cv(arf, 1024, 128, 2048, "p (c t) -> p c t", t=256)
    kf = cv(arf, 3072, 128, 512, "p (c t) -> p c t", t=256)
    kout = cv(arf, 3584, 128, 512, "p (c t) -> p c t", t=256)
    vout = cv(arf, 4096, 128, 512, "p (c t) -> p c t", t=256)
    hmT = cv(arb, 0, 128, 2048, "p (c t) -> p c t", t=256)
    E_ = 16 + tb
    U = cv(arf, 0, 96, 8 * E_, "p (c t) -> p c t", t=E_)
    tA = cv(arf, 8 * E_, 96, 2 * E_, "p (c t) -> p c t", t=E_)
    tB_ = cv(arf, 10 * E_, 96, 2 * E_, "p (c t) -> p c t", t=E_)
    t1a = cv(arf, 12 * E_, 96, 512)
    pso = cv(arf, 12 * E_ + 512, 16, 768)
    dif = cv(arb, 0, 96, 4096, "p (c t) -> p c t", t=tb)
    cq = cv(arf, 0, 128, 1536, "p (c t) -> p c t", t=tb)
    ckf = cv(arf, 0, 128, 1024, "p (c t) -> p c t", t=tb)
    krf = cv(arf, 1024, 96, 512)
    t1 = cv(arf, 1536, 96, 512)
    t2 = cv(arf, 2048, 96, 512)
    kvo = cv(arf, 2560, 128, 288)
    ssn = cv(arf, 2848, 128, 16)
    cqn = cv(arb, 0, 128, 1536, "p (c t) -> p c t", t=tb)
    QT = [cv(arb, 1536, 96, 512), cv(arb, 2048, 96, 512)]
    qg = cv(arb, 2560, 96, 512)
    krb = cv(arb, 1536, 96, 512)
    sqn = cv(arb, 2048, 128, 768)
    sqr = cv(arb, 2816, 128, 32)
    sg = [cv(arf, 0, 128, 512), cv(arf, 512, 128, 512)]
    stg = [cv(arf, 1024, 128, 512), cv(arf, 1536, 128, 512)]
    aT = [cv(arb, 0, 128, 1024, "p (c t) -> p c t", t=tb), cv(arb, 1024, 128, 1024, "p (c t) -> p c t", t=tb)]
    wmem = ringG[0]
    ARK = ['xin', 'memT', 'kf', 'kout', 'vout', 'hmT', 'U', 'tA', 'tB', 't1', 't2', 'pso', 'dif', 'cq', 'ckf', 'krf', 'kvo',
           'ssn', 'ssn2', 'cqn', ('QT', 0), ('QT', 1), 'qg', 'krb', 'sqn', 'sqr', ('sg', 0), ('sg', 1), ('aT', 0), ('aT', 1), ('stg', 0), ('stg', 1)]
    dummy = P.sb("dummyb", [128, 8], F32)

    def barrier():
        P.op('pool', lambda e: e.memset(dummy[:], 0.0), r=ARK, w=ARK)

    o.MSET('pool', Vaug[:], 1.0, w=['Vaug'])
    for i in range(2):
        o.MSET('pool', Vh[i][:], 1.0, w=[('Vh', i)])

    W = SC if SC is not None else I
    WQ = 'sp' if SC is not None else 'pool'
    wv_ffn_in = [W["w_ffn_in"][l].rearrange("(k p) n -> p k n", p=128) for l in range(4)]
    wv_ffn_out = [W["w_ffn_out"][l].rearrange("(k p) n -> p k n", p=128) for l in range(4)]
    NFG = DFF // 256

    def XK(j):
        return [('xT', j, c) for c in range(8)]

    ring_state = {'g': 0, 'o': 0}

    def load_G(l, fg):
        s = ring_state['g'] % NRG
        ring_state['g'] += 1
        rg = ringG[s]
        k = ('rg', s)
        gv = rg[:, 0:2048].rearrange("p (k n) -> p k n", n=256)
        uv = rg[:, 2048:4096].rearrange("p (k n) -> p k n", n=256)
        o.DMA(WQ, gv, wv_ffn_in[l][:, :, fg * 256:(fg + 1) * 256], w=[k])
        o.DMA(WQ, uv, wv_ffn_in[l][:, :, DFF + fg * 256:DFF + (fg + 1) * 256], w=[k])
        return gv, uv, k

    def load_O(l, fg):
        s = ring_state['o'] % NRO
        ring_state['o'] += 1
        k = ('ro', s)
        ov = ringO[s][:, 0:2048].rearrange("p (k n) -> p k n", n=1024)
        o.DMA(WQ, ov, wv_ffn_out[l][:, fg * 2:fg * 2 + 2, :], w=[k])
        return ov, k

    def load_mix_weights(l):
        if l < 2:
            o.DMA(WQ, win[:], W["w_in_a"][l].rearrange("(k p) n -> p k n", p=128), w=['win'])
            o.DMA(WQ, wout[0:96, 0:8, :], W["w_out"][l][0:768, :].rearrange("(k p) n -> p k n", p=96), w=['wout'])
            o.DMA(WQ, wout[:, 8:10, :], W["w_out"][l][768:1024, :].rearrange("(k p) n -> p k n", p=128), w=['wout'])
            gv = wmisc[0:96, 0:1536].rearrange("p (k n) -> p k n", n=192)
            o.DMA(WQ, gv, W["w_pool_grp"][l].rearrange("g (c p) e -> p (g c) e", p=96), w=['wmisc'])
        else:
            j = l - 2
            o.DMA(WQ, win[:, :, 0:640], W["w_in_b"][j].rearrange("(k p) n -> p k n", p=128), w=['win'])
            o.DMA(WQ, wout[:, 0:8, :], W["w_out"][l].rearrange("(k p) n -> p k n", p=128), w=['wout'])
            uq = wmisc[:, 0:3456].rearrange("p (k n) -> p k n", n=1152)
            o.DMA(WQ, uq, W["w_uq"][j].rearrange("(k p) n -> p k n", p=128), w=['wmisc'])

    def mem_kv(l, write_out):
        barrier()
        o.DMA('pool', wmem[:, 0:4096].rearrange("p (k n) -> p k n", n=512),
              I["w_mem_kv"][l].rearrange("(k p) n -> p k n", p=128), w=[('rg', 0)])
        wm = wmem[:, 0:4096].rearrange("p (k n) -> p k n", n=512)
        for mt in range(2):
            o.DMA('sp', xin[:], I["memp"][mt * 128:(mt + 1) * 128, :], w=['xin'])
            for c in range(8):
                b = C['rA'].next()
                o.TR(ps[b][:, 0:128], xin[:, c * 128:(c + 1) * 128], ident[:], r=['xin', 'ident'], w=[('ps', b)])
                o.CP('dve' if c % 2 else 'act', memT[:, c, mt * 128:(mt + 1) * 128], ps[b][:, 0:128], r=[('ps', b)], w=['memT'])
        norm_fm(o, C, [memT[:, c, :] for c in range(8)], [G['gmem'][:, l * 8 + c:l * 8 + c + 1] for c in range(8)],
                [hmT[:, c, :] for c in range(8)], 128, D, 256, ['memT'], 'hmT', ones[:])
        for hp in range(2):
            b = C['rA'].next()
            for k in range(8):
                o.MM(ps[b][:, 0:256], wm[:, k, hp * 128:(hp + 1) * 128], hmT[:, k, :], k == 0, k == 7,
                     r=['hmT', ('rg', 0)], w=[('ps', b)])
            norm_fm(o, C, [ps[b][:, 0:256]], [G['gmk'][:, l:l + 1]], [kf[:, hp, :]], 128, 64, 256, [('ps', b)], 'kf', blk2[:],
                    sq=([qn[:, 0, 0:256]], 'qn'))
        o.CP('dve', memKT[:], kf[:], r=['kf'], w=['memKT'])
        for mc in range(2):
            b = C['rA'].next()
            for k in range(8):
                o.MM(ps[b][:, 0:256], hmT[:, k, mc * 128:(mc + 1) * 128], wm[:, k, 256:512], k == 0, k == 7,
                     r=['hmT', ('rg', 0)], w=[('ps', b)])
            if write_out:
                o.CP('act', vout[:, mc, :], ps[b][:, 0:256], r=[('ps', b)], w=['vout'])
            for h in range(4):
                off = 0 if h % 2 == 0 else 64
                o.CP('dve', Vaug[:, mc, h, off:off + 64], ps[b][:, h * 64:(h + 1) * 64], r=[('ps', b)], w=['Vaug'])
        if write_out:
            for hp in range(2):
                for mc in range(2):
                    b = C['rA'].next()
                    o.TR(ps[b][:, 0:128], kf[:, hp, mc * 128:(mc + 1) * 128], ident[:], r=['kf', 'ident'], w=[('ps', b)])
                    o.CP('act', kout[:, mc, hp * 128:(hp + 1) * 128], ps[b][:, 0:128], r=[('ps', b)], w=['kout'])
            o.DMA('sp', O["mk_p"][l].rearrange("(m p) f -> p m f", p=128), kout[:], r=['kout'], out=True)
            o.DMA('sp', O["mv_p"][l].rearrange("(m p) f -> p m f", p=128), vout[:], r=['vout'], out=True)

    def mem_attend(l, j, catbase):
        for hp in range(2):
            norm_fm(o, C, [qm[:, hp, :]], [G['gmq'][:, l:l + 1]], [qn[:, hp, :]], 128, 64, tb, ['qm'], 'qn', blk2[:])
        st = {}

        def SS(h):
            hp, base = h // 2, (h % 2) * 64
            pts = []
            for mc in range(2):
                bsc = C['rS'].next()
                o.MM(ps[bsc][:, 0:tb], memKT[base:base + 64, hp, mc * 128:(mc + 1) * 128], qn[base:base + 64, hp, :], True, True,
                     r=['memKT', 'qn'], w=[('ps', bsc)])
                pi = rPT.next()
                o.ACT(PT[pi][:, :], ps[bsc][:, 0:tb], AF.Exp, r=[('ps', bsc)], w=[('PT', pi)], scale=0.125)
                pts.append(pi)
            st[h] = pts

        def PVV(h):
            hp, base = h // 2, (h % 2) * 64
            pts = st[h]
            bo = C['rO'].next()
            for mc in range(2):
                o.MM(ps[bo][:, 0:tb], Vaug[:, mc, h, :], PT[pts[mc]][:, :], mc == 0, mc == 1,
                     r=['Vaug', ('PT', pts[mc])], w=[('ps', bo)])
            vr = slice(base, base + 64)
            dr = slice(64 - base, 128 - base)
            o.ACT(rc[dr, :], ps[bo][dr, 0:tb], AF.Ln, r=[('ps', bo)], w=['rc'])
            o.ACT(rc[dr, :], rc[dr, :], AF.Exp, r=['rc'], w=['rc'], scale=-1.0)
            o.TT('dve', cat[vr, catbase + hp, :], ps[bo][vr, 0:tb], rc[dr, :], ALU.mult, r=[('ps', bo), 'rc'], w=['cat'])

        SS(0)
        for h in range(4):
            if h + 1 < 4:
                SS(h + 1)
            PVV(h)

    def out_proj(l, j):
        blk = slice(j * tb, (j + 1) * tb)
        if l < 2:
            chunks = [(96, c) for c in range(8)] + [(128, 8), (128, 9)]
        else:
            chunks = [(128, c) for c in range(8)]
        for oc in range(8):
            b = C['rA'].next()
            for i, (kk, c) in enumerate(chunks):
                o.MM(ps[b][:, 0:tb], wout[0:kk, c, oc * 128:(oc + 1) * 128], cat[0:kk, c, :], i == 0, i == len(chunks) - 1,
                     r=['wout', 'cat'], w=[('ps', b)])
            o.TT('dve', xT[:, oc, blk], xT[:, oc, blk], ps[b][:, 0:tb], ALU.add, r=[('ps', b), ('xT', j, oc)], w=[('xT', j, oc)])

    def mix_a(l, sg_, j):
        barrier()
        blk = slice(j * tb, (j + 1) * tb)
        jg = sg_ * NB + j
        norm_fm(o, C, [xT[:, c, blk] for c in range(8)], [G['gmix'][:, l * 8 + c:l * 8 + c + 1] for c in range(8)],
                [hT[:, c, blk] for c in range(8)], 128, D, tb, XK(j), ('hT', j), ones[:])
        if jg == 0:
            o.MSET('pool', U[:, :, 0:16], 0.0, w=['U'])
        elif j == 0:
            o.CP('pool', U[:, :, 0:16], halo[:, l, :, :], r=[('halo', l)], w=['U'])
        for c in range(8):
            b = C['rA'].next()
            for k in range(8):
                o.MM(ps[b][0:96, 0:tb], win[:, k, c * 96:(c + 1) * 96], hT[:, k, blk], k == 0, k == 7,
                     r=['win', ('hT', j)], w=[('ps', b)])
            o.CP('act', U[:, c, 16:16 + tb], ps[b][0:96, 0:tb], r=[('ps', b)], w=['U'])
        for hp in range(2):
            b = C['rA'].next()
            for k in range(8):
                o.MM(ps[b][:, 0:tb], win[:, k, 768 + hp * 128:768 + (hp + 1) * 128], hT[:, k, blk], k == 0, k == 7,
                     r=['win', ('hT', j)], w=[('ps', b)])
            o.CP('act', qm[:, hp, :], ps[b][:, 0:tb], r=[('ps', b)], w=['qm'])
        gw = wmisc[0:96, 0:1536].rearrange("p (k n) -> p k n", n=192)
        E = 16 + tb
        for g in range(4):
            w_ = 2 ** (g + 1)
            src = U[:, 2 * g:2 * g + 2, :]
            bufs = [tA, tB_]
            d = 1
            i = 0
            srck = 'U'
            while d < w_:
                dst = bufs[i % 2]
                dk = 'tA' if i % 2 == 0 else 'tB'
                lo = 2 * d
                o.TT('pool', dst[:, :, lo:E], src[:, :, lo:E], src[:, :, lo - d:E - d], ALU.add, r=[srck], w=[dk])
                src, srck = dst, dk
                d *= 2
                i += 1
            o.STT('dve', dif[:, 2 * g:2 * g + 2, :], src[:, :, 16:E], 1.0 / w_, U[:, 2 * g:2 * g + 2, 16:E], ALU.mult, ALU.subtract,
                  r=[srck, 'U'], w=['dif'])
            if jg == 0:
                o.TT('dve', t1a[:, 0:32].rearrange("p (c t) -> p c t", t=16), src[:, :, 16:32], C['rcnt'][:, 2 * g:2 * g + 2, :],
                     ALU.mult, r=[srck, 'rcnt'], w=['t1'])
                o.TT('dve', dif[:, 2 * g:2 * g + 2, 0:16], t1a[:, 0:32].rearrange("p (c t) -> p c t", t=16),
                     U[:, 2 * g:2 * g + 2, 16:32], ALU.subtract, r=['t1', 'U', 'dif'], w=['dif'])
            for ec in range(2):
                b = C['rA'].next()
                for cc in range(2):
                    o.MM(ps[b][0:96, 0:tb], gw[:, 2 * g + cc, ec * 96:(ec + 1) * 96], dif[:, 2 * g + cc, :], cc == 0, cc == 1,
                         r=['wmisc', 'dif'], w=[('ps', b)])
                ch = 2 * g + ec
                o.TSM('dve', cat[0:96, ch, :], ps[b][0:96, 0:tb], G['pscale'][:, l * 8 + ch:l * 8 + ch + 1],
                      r=[('ps', b), 'gains'], w=['cat'])
        if jg == NSEG * NB - 1:
            for c in range(8):
                b = C['rA'].next()
                o.TR(ps[b][0:15, 0:96], U[:, c, E - 15:E], ident[0:96, 0:96], r=['U', 'ident'], w=[('ps', b)])
                o.CP('act', pso[0:15, c * 96:(c + 1) * 96], ps[b][0:15, 0:96], r=[('ps', b)], w=['pso'])
            o.DMA('sp', O["pool_p"][l], pso[0:15, :], r=['pso'], out=True)
        elif j == NB - 1:
            o.CP('pool', halo[:, l, :, :], U[:, :, tb:tb + 16], r=['U'], w=[('halo', l)])
        else:
            o.CP('pool', tA[:, 0, 0:128].rearrange("p (c t) -> p c t", t=16), U[:, :, tb:tb + 16], r=['U'], w=['tA'])
            o.CP('pool', U[:, :, 0:16], tA[:, 0, 0:128].rearrange("p (c t) -> p c t", t=16), r=['tA'], w=['U'])
        mem_attend(l, j, 8)
        out_proj(l, j)

    def kv_stage(sg_, j):
        barrier()
        blk = slice(j * tb, (j + 1) * tb)
        jg = sg_ * NB + j
        gb = slice(jg * tb, (jg + 1) * tb)
        wd = C['wdkv']
        norm_fm(o, C, [xT[:, c, blk] for c in range(8)], [G['gkv'][:, c:c + 1] for c in range(8)],
                [hT[:, c, blk] for c in range(8)], 128, D, tb, XK(j), ('hT', j), ones[:])
        o.DMA('sp', cosb[:], I["c_cos"][:, gb], w=['cosb'])
        o.DMA('sp', sinb[:], I["c_sin"][:, gb], w=['sinb'])
        for cc in range(2):
            b = C['rA'].next()
            for k in range(8):
                o.MM(ps[b][:, 0:tb], wd[:, k, cc * 128:(cc + 1) * 128], hT[:, k, blk], k == 0, k == 7,
                     r=['wdkv', ('hT', j)], w=[('ps', b)])
            o.CP('act', ckf[:, cc, :], ps[b][:, 0:tb], r=[('ps', b)], w=['ckf'])
        br = C['rG'].next()
        for k in range(8):
            o.MM(ps[br][0:96, 0:tb], wd[:, k, 256:352], hT[:, k, blk], k == 0, k == 7, r=['wdkv', ('hT', j)], w=[('ps', br)])
        o.MSET('pool', krb[0:64, :], 0.0, w=['krb'])
        o.CP('act', krb[64:96, :], ps[br][64:96, 0:tb], r=[('ps', br)], w=['krb'])
        b2 = C['rG'].next()
        o.MM(ps[b2][0:96, 0:tb], perm[0:96, 0:96], krb[0:96, :], True, True, r=['perm', 'krb'], w=[('ps', b2)])
        o.TT('dve', t1[64:96, :], ps[br][64:96, 0:tb], cosb[64:96, :], ALU.mult, r=[('ps', br), 'cosb'], w=['t1'])
        o.TT('dve', t2[64:96, :], ps[b2][64:96, 0:tb], sinb[64:96, :], ALU.mult, r=[('ps', b2), 'sinb'], w=['t2'])
        o.TT('dve', krf[64:96, :], t1[64:96, :], t2[64:96, :], ALU.add, r=['t1', 't2'], w=['krf'])
        for i in range(2):
            o.CP('pool', KT[i][64:96, gb], krf[64:96, :], r=['krf'], w=[('KT', i)])
        norm_fm(o, C, [ckf[:, cc, :] for cc in range(2)], [G['gkvl'][:, cc:cc + 1] for cc in range(2)],
                [cqn[:, cc, :] for cc in range(2)], 128, 256, tb, ['ckf'], 'cqn', ones[:])
        o.CP('pool', cT[:, :, gb], cqn[:, 0:2, :], r=['cqn'], w=['cT'])
        for cc in range(2):
            o.STT('dve', ckf[:, cc, :], ckf[:, cc, :], G['gkvl'][:, cc:cc + 1], C['rstd'][:, 0:tb], ALU.mult, ALU.mult,
                  r=['ckf', 'rstd', 'cqn'], w=['ckf'])
        for tt in range(tb // 128):
            ts_ = slice(tt * 128, (tt + 1) * 128)
            kc = jg * 4 + tt
            b = C['rA'].next()
            o.TR(ps[b][:, 0:128], ckf[:, 0, ts_], ident[:], r=['ckf', 'ident'], w=[('ps', b)])
            o.TR(ps[b][:, 128:256], ckf[:, 1, ts_], ident[:], r=['ckf', 'ident'], w=[('ps', b)])
            o.TR(ps[b][:, 256:288], krf[64:96, ts_], ident[64:96, 64:96], r=['krf', 'ident'], w=[('ps', b)])
            o.CP('act', kvo[:, :], ps[b][:, 0:288], r=[('ps', b)], w=['kvo'])
            o.DMA('sp', O["kv_p"][jg * tb + tt * 128: jg * tb + (tt + 1) * 128, :], kvo[:, :], r=['kvo'], out=True)
            b3, b4 = C['rG'].next(), C['rG'].next()
            for cc in range(2):
                o.MM(ps[b3][:, 0:512], cT[:, cc, jg * tb + tt * 128: jg * tb + (tt + 1) * 128], C['wuk'][:, cc, 0:512], cc == 0, cc == 1,
                     r=['cT', 'wuk'], w=[('ps', b3)])
            for cc in range(2):
                o.MM(ps[b4][:, 0:256], cT[:, cc, jg * tb + tt * 128: jg * tb + (tt + 1) * 128], C['wuk'][:, cc, 512:768], cc == 0, cc == 1,
                     r=['cT', 'wuk'], w=[('ps', b4)])
            o.ACT(sqn[:, 0:512], ps[b3][:, 0:512], AF.Square, r=[('ps', b3)], w=['sqn'])
            o.ACT(sqn[:, 512:768], ps[b4][:, 0:256], AF.Square, r=[('ps', b4)], w=['sqn'])
            o.RED(ssn[:, 0:12], sqn[:, :].rearrange("p (h d) -> p h d", d=64), r=['sqn'], w=['ssn'])
            o.ACT(sqr[:, 0:32], kvo[:, 256:288], AF.Square, r=['kvo'], w=['sqr'])
            o.RED(ssn[:, 12:13], sqr[:, 0:32].rearrange("p (h d) -> p h d", d=32), r=['sqr'], w=['ssn2'])
            o.TSA('dve', ssn[:, 0:12], ssn[:, 0:12], ssn[:, 12:13], r=['ssn', 'ssn2'], w=['ssn'])
            o.ACT(ssn[:, 0:12], ssn[:, 0:12], AF.Ln, r=['ssn'], w=['ssn'], scale=1.0, bias=96.0 * EPS)
            o.ACT(kinv[:, kc, :], ssn[:, 0:12], AF.Exp, r=['ssn'], w=['kinv'], scale=-0.5)

    def mix_b(l, sg_, j):
        barrier()
        jl = l - 2
        blk = slice(j * tb, (j + 1) * tb)
        jg = sg_ * NB + j
        gb = slice(jg * tb, (jg + 1) * tb)
        norm_fm(o, C, [xT[:, c, blk] for c in range(8)], [G['gmix'][:, l * 8 + c:l * 8 + c + 1] for c in range(8)],
                [hT[:, c, blk] for c in range(8)], 128, D, tb, XK(j), ('hT', j), ones[:])
        o.DMA('sp', cosb[:], I["c_cos"][:, gb], w=['cosb'])
        o.DMA('sp', sinb[:], I["c_sin"][:, gb], w=['sinb'])
        for c in range(3):
            b = C['rA'].next()
            for k in range(8):
                o.MM(ps[b][:, 0:tb], win[:, k, c * 128:(c + 1) * 128], hT[:, k, blk], k == 0, k == 7,
                     r=['win', ('hT', j)], w=[('ps', b)])
            o.CP('act', cq[:, c, :], ps[b][:, 0:tb], r=[('ps', b)], w=['cq'])
        for hp in range(2):
            b = C['rA'].next()
            for k in range(8):
                o.MM(ps[b][:, 0:tb], win[:, k, 384 + hp * 128:384 + (hp + 1) * 128], hT[:, k, blk], k == 0, k == 7,
                     r=['win', ('hT', j)], w=[('ps', b)])
            o.CP('act', qm[:, hp, :], ps[b][:, 0:tb], r=[('ps', b)], w=['qm'])
        norm_fm(o, C, [cq[:, c, :] for c in range(3)], [G['gql'][:, jl * 3 + c:jl * 3 + c + 1] for c in range(3)],
                [cqn[:, c, :] for c in range(3)], 128, 384, tb, ['cq'], 'cqn', ones[:])
        uq = wmisc[:, 0:3456].rearrange("p (k n) -> p k n", n=1152)
        nk = (jg + 1) * 4
        BQ, BN = 6, 7

        def prepA(h):
            ki = h % 2
            for k in range(3):
                o.MM(ps[BQ][0:96, 0:tb], uq[:, k, h * 96:(h + 1) * 96], cqn[:, k, :], k == 0, k == 2,
                     r=['wmisc', 'cqn'], w=[('ps', BQ)])
            for kb in range(jg + 1):
                bk = C['rA'].next()
                for cc in range(2):
                    o.MM(ps[bk][0:64, 0:tb], C['wuk'][:, cc, h * 64:(h + 1) * 64], cT[:, cc, kb * tb:(kb + 1) * tb], cc == 0, cc == 1,
                         r=['wuk', 'cT'], w=[('ps', bk)])
                o.CP('dve', KT[ki][0:64, kb * tb:(kb + 1) * tb], ps[bk][0:64, 0:tb], r=[('ps', bk)], w=[('KT', ki)])
            off = 0 if h % 2 == 0 else 64
            for kg in range(nk // 8 + (1 if nk % 8 else 0)):
                bk = C['rA'].next()
                n8 = min(8, nk - kg * 8)
                for q8 in range(n8):
                    kc = kg * 8 + q8
                    for cc in range(2):
                        o.MM(ps[bk][:, q8 * 64:(q8 + 1) * 64], cT[:, cc, kc * 128:(kc + 1) * 128], C['wuv'][:, cc, h * 64:(h + 1) * 64],
                             cc == 0, cc == 1, r=['wuv', 'cT'], w=[('ps', bk)])
                o.CP('dve', Vh[ki][:, kg * 8:kg * 8 + n8, off:off + 64],
                     ps[bk][:, 0:n8 * 64].rearrange("p (k d) -> p k d", d=64), r=[('ps', bk)], w=[('Vh', ki)])

        def prepN1(h):
            o.ACT(qg[:, :], ps[BQ][0:96, 0:tb], AF.Square, r=[('ps', BQ)], w=['qg'])
            o.MM(ps[BN][0:96, 0:tb], ones[0:96, 0:96], qg[:, :], True, True, r=['qg', 'ones'], w=[('ps', BN)])

        def prepN2(h):
            qi = h % 2
            o.ACT(C['tmp'][0:96, 0:tb], ps[BN][0:96, 0:tb], AF.Ln, r=[('ps', BN)], w=['tmpn'], scale=1.0 / 96, bias=EPS)
            o.ACT(C['rstd'][0:96, 0:tb], C['tmp'][0:96, 0:tb], AF.Exp, r=['tmpn'], w=['rstd'], scale=-0.5)
            o.STT('dve', qg[:, :], ps[BQ][0:96, 0:tb], G['gq'][:, jl:jl + 1], C['rstd'][0:96, 0:tb], ALU.mult, ALU.mult,
                  r=[('ps', BQ), 'rstd', 'gains', 'qg'], w=['qg'])
            o.MM(ps[BN][0:96, 0:tb], perm[:, :], qg[:, :], True, True, r=['perm', 'qg'], w=[('ps', BN)])
            o.STT('dve', t1[:, :], qg[:, :], G['gk'][:, 0:1], cosb[:, :], ALU.mult, ALU.mult, r=['qg', 'cosb', 'gains'], w=['t1'])
            o.STT('dve', t2[:, :], ps[BN][0:96, 0:tb], G['gk'][:, 0:1], sinb[:, :], ALU.mult, ALU.mult,
                  r=[('ps', BN), 'sinb', 'gains'], w=['t2'])
            o.TT('pool', QT[qi][:, :], t1[:, :], t2[:, :], ALU.add, r=['t1', 't2'], w=[('QT', qi)])

        def attend(h):
            qi = ki = h % 2
            off = 0 if h % 2 == 0 else 64
            bo = C['rO'].next()
            pend = {}

            def S(kc):
                r_ = kc - 4 * jg
                q0 = 128 * max(r_, 0)
                bs = C['rS'].next()
                o.MM(ps[bs][:, q0:tb], KT[ki][0:96, kc * 128:(kc + 1) * 128], QT[qi][:, q0:tb], True, True,
                     r=[('KT', ki), ('QT', qi)], w=[('ps', bs)])
                pi = rPT.next()
                o.ACT(PT[pi][:, q0:tb], ps[bs][:, q0:tb], AF.Exp, r=[('ps', bs), 'kinv'], w=[('PT', pi)],
                      scale=kinv[:, kc, h:h + 1])
                if r_ >= 0:
                    o.TT('pool', PT[pi][:, q0:q0 + 128], PT[pi][:, q0:q0 + 128], tri[:, :], ALU.mult, r=[('PT', pi), 'tri'], w=[('PT', pi)])
                pend[kc] = (pi, q0)

            def PV(kc):
                pi, q0 = pend.pop(kc)
                o.MM(ps[bo][:, q0:tb], Vh[ki][:, kc, :], PT[pi][:, q0:tb], kc == 0, kc == nk - 1,
                     r=[('Vh', ki), ('PT', pi)], w=[('ps', bo)])

            hook2 = min(2, nk - 1)
            S(0)
            if h + 1 < 12:
                prepA(h + 1)
            for kc in range(nk):
                if kc + 1 < nk:
                    S(kc + 1)
                PV(kc)
                if h + 1 < 12:
                    if kc == 0:
                        prepN1(h + 1)
                    if kc == hook2:
                        prepN2(h + 1)
            vr = slice(off, off + 64)
            dr = slice(64 - off, 128 - off)
            o.ACT(rc[dr, :], ps[bo][dr, 0:tb], AF.Ln, r=[('ps', bo)], w=['rc'])
            o.ACT(rc[dr, :], rc[dr, :], AF.Exp, r=['rc'], w=['rc'], scale=-1.0)
            o.TT('dve', cat[vr, h // 2, :], ps[bo][vr, 0:tb], rc[dr, :], ALU.mult, r=[('ps', bo), 'rc'], w=['cat'])

        prepA(0)
        prepN1(0)
        prepN2(0)
        for h in range(12):
            attend(h)
        mem_attend(l, j, 6)
        out_proj(l, j)

    def ffn(l, pre, hook=None):
        barrier()
        for j in range(NB):
            blk = slice(j * tb, (j + 1) * tb)
            norm_fm(o, C, [xT[:, c, blk] for c in range(8)], [G['gffn'][:, l * 8 + c:l * 8 + c + 1] for c in range(8)],
                    [hT[:, c, blk] for c in range(8)], 128, D, tb, XK(j), ('hT', j), ones[:])
        rOut = Rot([0, 1, 6, 7])
        rStg = Rot([0, 1])
        pendG, pendO = list(pre[0]), list(pre[1])
        nx = {'g': len(pendG), 'o': len(pendO)}
        units = [(fg, j) for fg in range(NFG) for j in range(NB)]
        grpG, grpO = {}, {}

        def GU(u):
            fg, j = units[u]
            if fg not in grpG:
                grpG[fg] = pendG.pop(0)
            gv, uv, rk = grpG[fg]
            blk = slice(j * tb, (j + 1) * tb)
            ai = u % 2
            for cc in range(2):
                bg, bu = 2 + cc, 4 + cc
                for k in range(8):
                    o.MM(ps[bg][:, 0:tb], gv[:, k, cc * 128:(cc + 1) * 128], hT[:, k, blk], k == 0, k == 7,
                         r=[rk, ('hT', j)], w=[('ps', bg)])
                for k in range(8):
                    o.MM(ps[bu][:, 0:tb], uv[:, k, cc * 128:(cc + 1) * 128], hT[:, k, blk], k == 0, k == 7,
                         r=[rk, ('hT', j)], w=[('ps', bu)])
                o.ACT(sg[cc][:, :], ps[bg][:, 0:tb], AF.Silu, r=[('ps', bg)], w=[('sg', cc)])
                o.TT('dve', aT[ai][:, cc, :], sg[cc][:, :], ps[bu][:, 0:tb], ALU.mult, r=[('sg', cc), ('ps', bu)], w=[('aT', ai)])
            if j == NB - 1 and nx['g'] < NFG:
                pendG.append(load_G(l, nx['g']))
                nx['g'] += 1

        def OUT(u):
            fg, j = units[u]
            if fg not in grpO:
                grpO[fg] = pendO.pop(0)
            ov, rk = grpO[fg]
            blk = slice(j * tb, (j + 1) * tb)
            ai = u % 2
            for oc in range(8):
                b = rOut.next()
                for cc in range(2):
                    o.MM(ps[b][:, 0:tb], ov[:, cc, oc * 128:(oc + 1) * 128], aT[ai][:, cc, :], cc == 0, cc == 1,
                         r=[rk, ('aT', ai)], w=[('ps', b)])
                if oc % 2 == 0:
                    o.TT('dve', xT[:, oc, blk], xT[:, oc, blk], ps[b][:, 0:tb], ALU.add, r=[('ps', b), ('xT', j, oc)], w=[('xT', j, oc)])
                else:
                    si = rStg.next()
                    o.CP('act', stg[si][:, :], ps[b][:, 0:tb], r=[('ps', b)], w=[('stg', si)])
                    o.TT('pool', xT[:, oc, blk], xT[:, oc, blk], stg[si][:, :], ALU.add, r=[('stg', si), ('xT', j, oc)], w=[('xT', j, oc)])
            if j == NB - 1:
                if nx['o'] < NFG:
                    pendO.append(load_O(l, nx['o']))
                    nx['o'] += 1
                if fg == 1 and hook is not None:
                    hook()

        GU(0)
        for u in range(len(units)):
            if u + 1 < len(units):
                GU(u + 1)
            OUT(u)

    for sg_ in range(NSEG):
        barrier()
        for tt in range(SEG // 128):
            o.DMA('sp', xin[:], I["xp"][sg_ * SEG + tt * 128: sg_ * SEG + (tt + 1) * 128, :], w=['xin'])
            for c in range(8):
                b = C['rA'].next()
                o.TR(ps[b][:, 0:128], xin[:, c * 128:(c + 1) * 128], ident[:], r=['xin', 'ident'], w=[('ps', b)])
                o.CP('dve' if c % 2 else 'act', xT[:, c, tt * 128:(tt + 1) * 128], ps[b][:, 0:128], r=[('ps', b)], w=[('xT', tt // 4, c)])
        for l in range(nlayers):
            mem_kv(l, write_out=(sg_ == 0))
            if l == 0 and sg_ == 0:
                load_mix_weights(0)
            pre = []
            for j in range(NB):
                if l < 2:
                    mix_a(l, sg_, j)
                else:
                    mix_b(l, sg_, j)
                if j == 0:
                    pre = ([load_G(l, fg) for fg in range(NRG)], [load_O(l, fg) for fg in range(NRO)])
            ffn(l, pre, hook=(lambda l=l: load_mix_weights((l + 1) % nlayers)) if not (l == nlayers - 1 and sg_ == NSEG - 1) else None)
            if l == 1:
                for j in range(NB):
                    kv_stage(sg_, j)
        barrier()
        for tt in range(SEG // 128):
            for c in range(8):
                b = C['rA'].next()
                o.TR(ps[b][:, 0:128], xT[:, c, tt * 128:(tt + 1) * 128], ident[:], r=[('xT', tt // 4, c), 'ident'], w=[('ps', b)])
                o.CP('dve' if c % 2 else 'act', xin[:, c * 128:(c + 1) * 128], ps[b][:, 0:128], r=[('ps', b)], w=['xin'])
            o.DMA('sp', O["y_p"][sg_ * SEG + tt * 128: sg_ * SEG + (tt + 1) * 128, :], xin[:], r=['xin'], out=True)
    P.finish()
    P.emit('p')
    print("prompt program: ops", len(P.ops), "waits", P.nwaits)


def sample_program(nc, es, I, O, nlayers, NPG=128, SC=None):
    P = Prog(nc, es, 'S_')
    o = Ops(P)
    C = alloc_common(P, o, I)
    ps = C['ps']
    G = C['G']
    ident, ones, blk2, perm = C['ident'], C['ones'], C['blk2'], C['perm']
    tb = 32
    NS = 4
    NPT = NS * NPG

    xT = P.sb("s_xT", [128, 8, tb], F32)
    hT = P.sb("s_hT", [128, 8, tb], BF16)
    win = P.sb("s_win", [128, 8, 1024], BF16)
    wout = P.sb("s_wout", [128, 10, 1024], BF16)
    wmisc = P.sb("s_wmisc", [128, 3456], BF16)
    NRING = 3
    ring = [P.sb("s_ring%d" % i, [128, 6144], BF16) for i in range(NRING)]
    cat = P.sb("s_cat", [128, 10, tb], BF16)
    qm = P.sb("s_qm", [128, 2, tb], F32)
    qn = P.sb("s_qn", [128, 2, tb], BF16)
    xin = P.sb("s_xin", [32, 1024], F32)
    memKT = P.sb("s_memKT", [128, NS, 2, 256], BF16)
    Vaug = P.sb("s_Vaug", [128, NS, 2, 4, 128], BF16)
    mst = P.sb("s_mst", [128, 2, 256], F32)
    Us = P.sb("s_U", [96, 32, 24], F32)
    tA = P.sb("s_tA", [96, 8, 24], F32)
    tB_ = P.sb("s_tB", [96, 8, 24], F32)
    dif = P.sb("s_dif", [96, 8, tb], BF16)
    spt = P.sb("s_spt", [16, 768], F32)
    pso = P.sb("s_pso", [16, 768], F32)
    PTm = P.sb("s_PTm", [128, 256], BF16)
    rc = P.sb("s_rc", [128, 128], F32)
    sg = [P.sb("s_sg%d" % i, [128, tb], F32) for i in range(2)]
    aT = [P.sb("s_aT%d" % i, [128, 2, tb], BF16) for i in range(2)]
    cosb = P.sb("s_cos", [96, tb], F32)
    sinb = P.sb("s_sin", [96, tb], F32)
    ckf = P.sb("s_ckf", [128, 2, tb], F32)
    cTs = P.sb("s_cT", [128, 2, tb], BF16)
    krf = P.sb("s_krf", [96, tb], F32)
    krb = P.sb("s_krb", [96, tb], BF16)
    t1 = P.sb("s_t1", [96, tb], F32)
    t2 = P.sb("s_t2", [96, tb], F32)
    kvo = P.sb("s_kvo", [32, 288], F32)
    KVbn = P.sb("s_KVbn", [32, 289], BF16)
    sqn = P.sb("s_sqn", [128, 768], BF16)
    sqr = P.sb("s_sqr", [128, 32], BF16)
    ssn = P.sb("s_ssn", [128, 4, 16], F32)
    kinvn = P.sb("s_kinvn", [32, 12], F32)
    cq = P.sb("s_cq", [128, 3, tb], F32)
    cqn = P.sb("s_cqn", [128, 3, tb], BF16)
    qg = P.sb("s_qg", [96, tb], BF16)
    QTs = P.sb("s_QT", [96, 12, tb], BF16)
    QL = P.sb("s_QL", [128, 2, 12, tb], BF16)
    wukf = P.sb("s_wukf", [128, 2, 768], F32)
    wukT = P.sb("s_wukT", [64, 12, 256], BF16)
    identb = P.sb("s_identb", [128, 128], BF16)
    smask = P.sb("s_smask", [32, 4, 96], BF16)
    ptb = P.sb("s_ptb", [128, NPT], I32)
    idxall = P.sb("s_idx", [128, NPT], I32)
    iot = P.sb("s_iota", [128, 1], I32)
    kinv = P.sb("s_kinv", [128, NPT, 12], F32)
    NKB = 16
    sqn2 = [P.sb("s_sqn2_%d" % i, [128, 800], BF16) for i in range(2)]
    KVb = [P.sb("s_KVb%d" % i, [128, 289], BF16) for i in range(NKB)]
    KVTc = [P.sb("s_KVTc%d" % i, [128, 4, 2, 128], BF16) for i in range(2)]
    KVTr = [P.sb("s_KVTr%d" % i, [96, 4, 128], BF16) for i in range(2)]
    tms = P.sb("s_tms", [128, 4, 96], F32)
    PTs = [P.sb("s_PTs%d" % i, [128, 4, 96], BF16) for i in range(2)]
    PTn = P.sb("s_PTn", [32, 96], BF16)
    olat = P.sb("s_olat", [96, 256], F32)
    rcs = P.sb("s_rcs", [96, 1], F32)
    olT = P.sb("s_olT", [128, 2, 96], BF16)

    o.MSET('pool', Vaug[:], 1.0, w=['Vaug'])
    for i in range(NKB):
        o.MSET('pool', KVb[i][:, 288:289], 1.0, w=[('KVb', i)])
    o.MSET('pool', KVbn[:, 288:289], 1.0, w=['KVbn'])
    o.CP('pool', identb[:], ident[:], r=['ident'], w=['identb'])
    o.DMA('pool', smask[:], I["c_smask"][:, :, :], w=['smask'])
    o.DMA('sp', wukf[:], I["w_uk"].rearrange("(k p) n -> p k n", p=128), w=['wukf'])
    for h in range(12):
        b = C['rA'].next()
        for cc in range(2):
            o.TR(ps[b][0:64, cc * 128:(cc + 1) * 128], wukf[:, cc, h * 64:(h + 1) * 64], ident[:, :], r=['wukf', 'ident'], w=[('ps', b)])
        o.CP('dve', wukT[:, h, :], ps[b][0:64, 0:256], r=[('ps', b)], w=['wukT'])
    o.DMA('sp', ptb[:], I["pt"].rearrange("s p -> (s p)").partition_broadcast(128), w=['ptb'])
    P.op('pool', lambda e: e.iota(iot[:], pattern=[[0, 1]], base=0, channel_multiplier=1), w=['iota'])
    ptf = P.sb("s_ptf", [128, NPT], F32)
    iotf = P.sb("s_iotf", [128, 1], F32)
    o.CP('dve', ptf[:], ptb[:], r=['ptb'], w=['ptf'])
    o.CP('dve', iotf[:], iot[:], r=['iota'], w=['iotf'])
    o.TS('dve', ptf[:], ptf[:], 128.0, iotf[:, 0:1], ALU.mult, ALU.add, r=['ptf', 'iotf'], w=['ptf'])
    o.CP('dve', idxall[:], ptf[:], r=['ptf'], w=['idx'])

    wv_ffn_in = [I["w_ffn_in"][l].rearrange("(k p) n -> p k n", p=128) for l in range(4)]
    wv_ffn_out = [I["w_ffn_out"][l].rearrange("(k p) n -> p k n", p=128) for l in range(4)]
    NFG = DFF // 256
    XK = [('xT', c) for c in range(8)]
    ring_state = {'n': 0}

    def load_ffn_group(l, fg):
        s = ring_state['n'] % NRING
        ring_state['n'] += 1
        rg = ring[s]
        k = ('ring', s)
        gv = rg[:, 0:2048].rearrange("p (k n) -> p k n", n=256)
        uv = rg[:, 2048:4096].rearrange("p (k n) -> p k n", n=256)
        ov = rg[:, 4096:6144].rearrange("p (k n) -> p k n", n=1024)
        o.DMA('pool', gv, wv_ffn_in[l][:, :, fg * 256:(fg + 1) * 256], w=[k])
        o.DMA('pool', uv, wv_ffn_in[l][:, :, DFF + fg * 256:DFF + (fg + 1) * 256], w=[k])
        o.DMA('pool', ov, wv_ffn_out[l][:, fg * 2:fg * 2 + 2, :], w=[k])
        if SC is not None:
            si = SC["w_ffn_in"][l].rearrange("(k p) n -> p k n", p=128)
            so = SC["w_ffn_out"][l].rearrange("(k p) n -> p k n", p=128)
            o.DMA('sp', si[:, :, fg * 256:(fg + 1) * 256], gv, r=[k], out=True)
            o.DMA('sp', si[:, :, DFF + fg * 256:DFF + (fg + 1) * 256], uv, r=[k], out=True)
            o.DMA('sp', so[:, fg * 2:fg * 2 + 2, :], ov, r=[k], out=True)
        return gv, uv, ov, k

    def load_mix_weights(l):
        if l < 2:
            o.DMA('pool', win[:], I["w_in_a"][l].rearrange("(k p) n -> p k n", p=128), w=['win'])
            o.DMA('pool', wout[0:96, 0:8, :], I["w_out"][l][0:768, :].rearrange("(k p) n -> p k n", p=96), w=['wout'])
            o.DMA('pool', wout[:, 8:10, :], I["w_out"][l][768:1024, :].rearrange("(k p) n -> p k n", p=128), w=['wout'])
            gv = wmisc[0:96, 0:1536].rearrange("p (k n) -> p k n", n=192)
            o.DMA('pool', gv, I["w_pool_grp"][l].rearrange("g (c p) e -> p (g c) e", p=96), w=['wmisc'])
            if SC is not None:
                o.DMA('sp', SC["w_in_a"][l].rearrange("(k p) n -> p k n", p=128), win[:], r=['win'], out=True)
                o.DMA('sp', SC["w_out"][l][0:768, :].rearrange("(k p) n -> p k n", p=96), wout[0:96, 0:8, :], r=['wout'], out=True)
                o.DMA('sp', SC["w_out"][l][768:1024, :].rearrange("(k p) n -> p k n", p=128), wout[:, 8:10, :], r=['wout'], out=True)
                o.DMA('sp', SC["w_pool_grp"][l].rearrange("g (c p) e -> p (g c) e", p=96), gv, r=['wmisc'], out=True)
        else:
            j = l - 2
            o.DMA('pool', win[:, :, 0:640], I["w_in_b"][j].rearrange("(k p) n -> p k n", p=128), w=['win'])
            o.DMA('pool', wout[:, 0:8, :], I["w_out"][l].rearrange("(k p) n -> p k n", p=128), w=['wout'])
            uq = wmisc[:, 0:3456].rearrange("p (k n) -> p k n", n=1152)
            o.DMA('pool', uq, I["w_uq"][j].rearrange("(k p) n -> p k n", p=128), w=['wmisc'])
            if SC is not None:
                o.DMA('sp', SC["w_in_b"][j].rearrange("(k p) n -> p k n", p=128), win[:, :, 0:640], r=['win'], out=True)
                o.DMA('sp', SC["w_out"][l].rearrange("(k p) n -> p k n", p=128), wout[:, 0:8, :], r=['wout'], out=True)
                o.DMA('sp', SC["w_uq"][j].rearrange("(k p) n -> p k n", p=128), uq, r=['wmisc'], out=True)

    def xnorm(gkey, l):
        norm_fm(o, C, [xT[:, c, :] for c in range(8)], [G[gkey][:, l * 8 + c:l * 8 + c + 1] for c in range(8)],
                [hT[:, c, :] for c in range(8)], 128, D, tb, XK, 'hT', ones[:])

    def mem_load(l):
        for s_ in range(NS):
            o.DMA('sp', mst[:], I["cmk"][l, s_].rearrange("(m p) f -> p m f", p=128), w=['mst'])
            for mc in range(2):
                b = C['rA'].next()
                for hp in range(2):
                    o.TR(ps[b][:, hp * 128:(hp + 1) * 128], mst[:, mc, hp * 128:(hp + 1) * 128], ident[:], r=['mst', 'ident'], w=[('ps', b)])
                o.CP('dve', memKT[:, s_, :, mc * 128:(mc + 1) * 128], ps[b][:, 0:256].rearrange("p (h m) -> p h m", m=128),
                     r=[('ps', b)], w=['memKT'])
            o.DMA('sp', mst[:], I["cmv"][l, s_].rearrange("(m p) f -> p m f", p=128), r=['mst'], w=['mst'])
            for mc in range(2):
                for h in range(4):
                    off = 0 if h % 2 == 0 else 64
                    o.CP('pool', Vaug[:, s_, mc, h, off:off + 64], mst[:, mc, h * 64:(h + 1) * 64], r=['mst'], w=['Vaug'])

    def mem_attend(l, catbase):
        for hp in range(2):
            norm_fm(o, C, [qm[:, hp, :]], [G['gmq'][:, l:l + 1]], [qn[:, hp, :]], 128, 64, tb, ['qm'], 'qn', blk2[:])
        bsb = [2, 3]
        for mc in range(2):
            for s_ in range(NS):
                for h in range(4):
                    hp, par = h // 2, h % 2
                    base = par * 64
                    col = ((mc * NS + s_) * 2 + hp) * 8
                    o.MM(ps[bsb[par]][:, col:col + 8], memKT[base:base + 64, s_, hp, mc * 128:(mc + 1) * 128],
                         qn[base:base + 64, hp, s_ * 8:(s_ + 1) * 8], True, True, r=['memKT', 'qn'], w=[('ps', bsb[par])])
        for par in range(2):
            o.ACT(PTm[:, par * 128:(par + 1) * 128], ps[bsb[par]][:, 0:128], AF.Exp, r=[('ps', bsb[par])], w=['PTm'], scale=0.125)
        bo = C['rO'].next()
        for s_ in range(NS):
            for h in range(4):
                hp, par = h // 2, h % 2
                colo = (s_ * 4 + h) * 8
                for mc in range(2):
                    col = par * 128 + ((mc * NS + s_) * 2 + hp) * 8
                    o.MM(ps[bo][:, colo:colo + 8], Vaug[:, s_, mc, h, :], PTm[:, col:col + 8], mc == 0, mc == 1,
                         r=['Vaug', 'PTm'], w=[('ps', bo)])
        o.RCP(rc[:, 0:128], ps[bo][:, 0:128], r=[('ps', bo)], w=['rc'])
        for h in range(4):
            hp, base = h // 2, (h % 2) * 64
            vr = slice(base, base + 64)
            dr = slice(64 - base, 128 - base)
            pv = ps[bo][vr, 0:128].rearrange("p (s h i) -> p s h i", h=4, i=8)[:, :, h, :]
            rv = rc[dr, 0:128].rearrange("p (s h i) -> p s h i", h=4, i=8)[:, :, h, :]
            o.TT('dve', cat[vr, catbase + hp, :].rearrange("p (s i) -> p s i", i=8), pv, rv, ALU.mult, r=[('ps', bo), 'rc'], w=['cat'])

    def out_proj(l):
        if l < 2:
            chunks = [(96, c) for c in range(8)] + [(128, 8), (128, 9)]
        else:
            chunks = [(128, c) for c in range(8)]
        for oc in range(8):
            b = C['rA'].next()
            for i, (kk, c) in enumerate(chunks):
                o.MM(ps[b][:, 0:tb], wout[0:kk, c, oc * 128:(oc + 1) * 128], cat[0:kk, c, :], i == 0, i == len(chunks) - 1,
                     r=['wout', 'cat'], w=[('ps', b)])
            o.TT('dve', xT[:, oc, :], xT[:, oc, :], ps[b][:, 0:tb], ALU.add, r=[('ps', b), ('xT', oc)], w=[('xT', oc)])

    def mix_a(l):
        xnorm('gmix', l)
        for s_ in range(NS):
            o.DMA('sp', spt[0:15, :], I["spool"][l, s_], w=['spt'])
            b = C['rA'].next()
            for c in range(8):
                o.TR(ps[b][0:96, c * 16:c * 16 + 15], spt[0:15, c * 96:(c + 1) * 96], ident[0:15, 0:15], r=['spt', 'ident'], w=[('ps', b)])
            o.CP('dve', Us[:, :, 1:16].rearrange("p (c s) t -> p c s t", s=NS)[:, :, s_, :],
                 ps[b][0:96, 0:128].rearrange("p (c t) -> p c t", t=16)[:, :, 0:15], r=[('ps', b)], w=['U'])
        for c in range(8):
            b = C['rA'].next()
            for k in range(8):
                o.MM(ps[b][0:96, 0:tb], win[:, k, c * 96:(c + 1) * 96], hT[:, k, :], k == 0, k == 7, r=['win', 'hT'], w=[('ps', b)])
            o.CP('act', Us[:, c * NS:(c + 1) * NS, 16:24], ps[b][0:96, 0:tb].rearrange("p (s i) -> p s i", i=8), r=[('ps', b)], w=['U'])
        for hp in range(2):
            b = C['rA'].next()
            for k in range(8):
                o.MM(ps[b][:, 0:tb], win[:, k, 768 + hp * 128:768 + (hp + 1) * 128], hT[:, k, :], k == 0, k == 7,
                     r=['win', 'hT'], w=[('ps', b)])
            o.CP('act', qm[:, hp, :], ps[b][:, 0:tb], r=[('ps', b)], w=['qm'])
        gw = wmisc[0:96, 0:1536].rearrange("p (k n) -> p k n", n=192)
        E = 24
        for g in range(4):
            w_ = 2 ** (g + 1)
            src = Us[:, 8 * g:8 * g + 8, :]
            bufs = [tA, tB_]
            d, i, srck = 1, 0, 'U'
            while d < w_:
                dst = bufs[i % 2]
                dk = 'tA' if i % 2 == 0 else 'tB'
                lo = 2 * d
                o.TT('pool', dst[:, :, lo:E], src[:, :, lo:E], src[:, :, lo - d:E - d], ALU.add, r=[srck], w=[dk])
                src, srck = dst, dk
                d *= 2
                i += 1
            o.STT('dve', dif[:, 2 * g:2 * g + 2, :].rearrange("p c (s i) -> p (c s) i", i=8), src[:, :, 16:E], 1.0 / w_,
                  Us[:, 8 * g:8 * g + 8, 16:E], ALU.mult, ALU.subtract, r=[srck, 'U'], w=['dif'])
            for ec in range(2):
                b = C['rA'].next()
                for cc in range(2):
                    o.MM(ps[b][0:96, 0:tb], gw[:, 2 * g + cc, ec * 96:(ec + 1) * 96], dif[:, 2 * g + cc, :], cc == 0, cc == 1,
                         r=['wmisc', 'dif'], w=[('ps', b)])
                ch = 2 * g + ec
                o.TSM('dve', cat[0:96, ch, :], ps[b][0:96, 0:tb], G['pscale'][:, l * 8 + ch:l * 8 + ch + 1], r=[('ps', b), 'gains'], w=['cat'])
        for s_ in range(NS):
            b1, b2 = C['rA'].next(), C['rA'].next()
            for c in range(8):
                bb = b1 if c < 4 else b2
                o.TR(ps[bb][0:15, (c % 4) * 96:(c % 4 + 1) * 96], Us[:, c * NS + s_, 9:24], ident[0:96, 0:96], r=['U', 'ident'], w=[('ps', bb)])
            o.CP('act', pso[0:15, 0:384], ps[b1][0:15, 0:384], r=[('ps', b1)], w=['pso'])
            o.CP('act', pso[0:15, 384:768], ps[b2][0:15, 0:384], r=[('ps', b2)], w=['pso'])
            o.DMA('sp', O["pool_s"][l, s_], pso[0:15, :], r=['pso'], out=True)
        mem_attend(l, 8)
        out_proj(l)

    def ffn(l, pre, hook=None):
        xnorm('gffn', l)
        rOut = Rot([0, 1, 6, 7])
        pend = list(pre)
        nxt = len(pre)
        for fg in range(NFG):
            gv, uv, ov, rk = pend.pop(0)
            ai = fg % 2
            for cc in range(2):
                bg, bu = 2 + cc, 4 + cc
                for k in range(8):
                    o.MM(ps[bg][:, 0:tb], gv[:, k, cc * 128:(cc + 1) * 128], hT[:, k, :], k == 0, k == 7, r=[rk, 'hT'], w=[('ps', bg)])
                for k in range(8):
                    o.MM(ps[bu][:, 0:tb], uv[:, k, cc * 128:(cc + 1) * 128], hT[:, k, :], k == 0, k == 7, r=[rk, 'hT'], w=[('ps', bu)])
                o.ACT(sg[cc][:, :], ps[bg][:, 0:tb], AF.Silu, r=[('ps', bg)], w=[('sg', cc)])
                o.TT('dve', aT[ai][:, cc, :], sg[cc][:, :], ps[bu][:, 0:tb], ALU.mult, r=[('sg', cc), ('ps', bu)], w=[('aT', ai)])
            for oc in range(8):
                b = rOut.next()
                for cc in range(2):
                    o.MM(ps[b][:, 0:tb], ov[:, cc, oc * 128:(oc + 1) * 128], aT[ai][:, cc, :], cc == 0, cc == 1,
                         r=[rk, ('aT', ai)], w=[('ps', b)])
                o.TT('dve', xT[:, oc, :], xT[:, oc, :], ps[b][:, 0:tb], ALU.add, r=[('ps', b), ('xT', oc)], w=[('xT', oc)])
            if nxt < NFG:
                pend.append(load_ffn_group(l, nxt))
                nxt += 1
            if fg == 1 and hook is not None:
                hook()

    def kv_stage():
        wd = C['wdkv']
        norm_fm(o, C, [xT[:, c, :] for c in range(8)], [G['gkv'][:, c:c + 1] for c in range(8)],
                [hT[:, c, :] for c in range(8)], 128, D, tb, XK, 'hT', ones[:])
        o.DMA('sp', cosb[:], I["c_cos"][:, 2048:2080], w=['cosb'])
        o.DMA('sp', sinb[:], I["c_sin"][:, 2048:2080], w=['sinb'])
        for cc in range(2):
            b = C['rA'].next()
            for k in range(8):
                o.MM(ps[b][:, 0:tb], wd[:, k, cc * 128:(cc + 1) * 128], hT[:, k, :], k == 0, k == 7, r=['wdkv', 'hT'], w=[('ps', b)])
            o.CP('act', ckf[:, cc, :], ps[b][:, 0:tb], r=[('ps', b)], w=['ckf'])
        br = C['rG'].next()
        for k in range(8):
            o.MM(ps[br][0:96, 0:tb], wd[:, k, 256:352], hT[:, k, :], k == 0, k == 7, r=['wdkv', 'hT'], w=[('ps', br)])
        o.CP('act', krb[64:96, :], ps[br][64:96, 0:tb], r=[('ps', br)], w=['krb'])
        b2 = C['rG'].next()
        o.MM(ps[b2][0:96, 0:tb], perm[64:96, 0:96], krb[64:96, :], True, True, r=['perm', 'krb'], w=[('ps', b2)])
        o.TT('dve', t1[64:96, :], ps[br][64:96, 0:tb], cosb[64:96, :], ALU.mult, r=[('ps', br), 'cosb'], w=['t1'])
        o.TT('dve', t2[64:96, :], ps[b2][64:96, 0:tb], sinb[64:96, :], ALU.mult, r=[('ps', b2), 'sinb'], w=['t2'])
        o.TT('dve', krf[64:96, :], t1[64:96, :], t2[64:96, :], ALU.add, r=['t1', 't2'], w=['krf'])
        o.CP('pool', krb[64:96, :], krf[64:96, :], r=['krf'], w=['krb'])
        norm_fm(o, C, [ckf[:, cc, :] for cc in range(2)], [G['gkvl'][:, cc:cc + 1] for cc in range(2)],
                [cTs[:, cc, :] for cc in range(2)], 128, 256, tb, ['ckf'], 'cTs', ones[:])
        for cc in range(2):
            o.STT('dve', ckf[:, cc, :], ckf[:, cc, :], G['gkvl'][:, cc:cc + 1], C['rstd'][:, 0:tb], ALU.mult, ALU.mult,
                  r=['ckf', 'rstd', 'cTs'], w=['ckf'])
        b = C['rA'].next()
        o.TR(ps[b][0:32, 0:128], ckf[:, 0, :], ident[:], r=['ckf', 'ident'], w=[('ps', b)])
        o.TR(ps[b][0:32, 128:256], ckf[:, 1, :], ident[:], r=['ckf', 'ident'], w=[('ps', b)])
        o.TR(ps[b][0:32, 256:288], krf[64:96, :], ident[64:96, 64:96], r=['krf', 'ident'], w=[('ps', b)])
        o.CP('act', kvo[:, :], ps[b][0:32, 0:288], r=[('ps', b)], w=['kvo'])
        o.DMA('sp', O["kv_s"][:, :], kvo[:, :], r=['kvo'], out=True)
        o.CP('dve', KVbn[:, 0:288], kvo[:, :], r=['kvo'], w=['KVbn'])
        b3, b4 = C['rG'].next(), C['rG'].next()
        for cc in range(2):
            o.MM(ps[b3][0:32, 0:512], cTs[:, cc, :], C['wuk'][:, cc, 0:512], cc == 0, cc == 1, r=['cTs', 'wuk'], w=[('ps', b3)])
        for cc in range(2):
            o.MM(ps[b4][0:32, 0:256], cTs[:, cc, :], C['wuk'][:, cc, 512:768], cc == 0, cc == 1, r=['cTs', 'wuk'], w=[('ps', b4)])
        o.ACT(sqn[0:32, 0:512], ps[b3][0:32, 0:512], AF.Square, r=[('ps', b3)], w=['sqn'])
        o.ACT(sqn[0:32, 512:768], ps[b4][0:32, 0:256], AF.Square, r=[('ps', b4)], w=['sqn'])
        o.RED(ssn[0:32, 0, 0:12], sqn[0:32, :].rearrange("p (h d) -> p h d", d=64), r=['sqn'], w=['ssn'])
        o.ACT(sqr[0:32, 0:32], kvo[:, 256:288], AF.Square, r=['kvo'], w=['sqr'])
        o.RED(ssn[0:32, 0, 12:13], sqr[0:32, 0:32].rearrange("p (h d) -> p h d", d=32), r=['sqr'], w=['ssn2'])
        o.TSA('dve', ssn[0:32, 0, 0:12], ssn[0:32, 0, 0:12], ssn[0:32, 0, 12:13], r=['ssn', 'ssn2'], w=['ssn'])
        o.ACT(ssn[0:32, 0, 0:12], ssn[0:32, 0, 0:12], AF.Ln, r=['ssn'], w=['ssn'], scale=1.0, bias=96.0 * EPS)
        o.ACT(kinvn[:, :], ssn[0:32, 0, 0:12], AF.Exp, r=['ssn'], w=['kinvn'], scale=-0.5)

    def mix_b(l):
        jl = l - 2
        xnorm('gmix', l)
        for c in range(3):
            b = C['rA'].next()
            for k in range(8):
                o.MM(ps[b][:, 0:tb], win[:, k, c * 128:(c + 1) * 128], hT[:, k, :], k == 0, k == 7, r=['win', 'hT'], w=[('ps', b)])
            o.CP('act', cq[:, c, :], ps[b][:, 0:tb], r=[('ps', b)], w=['cq'])
        for hp in range(2):
            b = C['rA'].next()
            for k in range(8):
                o.MM(ps[b][:, 0:tb], win[:, k, 384 + hp * 128:384 + (hp + 1) * 128], hT[:, k, :], k == 0, k == 7,
                     r=['win', 'hT'], w=[('ps', b)])
            o.CP('act', qm[:, hp, :], ps[b][:, 0:tb], r=[('ps', b)], w=['qm'])
        norm_fm(o, C, [cq[:, c, :] for c in range(3)], [G['gql'][:, jl * 3 + c:jl * 3 + c + 1] for c in range(3)],
                [cqn[:, c, :] for c in range(3)], 128, 384, tb, ['cq'], 'cqn', ones[:])
        uq = wmisc[:, 0:3456].rearrange("p (k n) -> p k n", n=1152)
        for h in range(12):
            bq = C['rG'].next()
            for k in range(3):
                o.MM(ps[bq][0:96, 0:tb], uq[:, k, h * 96:(h + 1) * 96], cqn[:, k, :], k == 0, k == 2, r=['wmisc', 'cqn'], w=[('ps', bq)])
            norm_fm(o, C, [ps[bq][0:96, 0:tb]], [G['gq'][:, jl:jl + 1]], [qg[:, :]], 96, 96, tb, [('ps', bq)], 'qg', ones[0:96, 0:96])
            b2 = C['rG'].next()
            o.MM(ps[b2][0:96, 0:tb], perm[:, :], qg[:, :], True, True, r=['perm', 'qg'], w=[('ps', b2)])
            o.STT('dve', t1[:, :], qg[:, :], G['gk'][:, 0:1], cosb[:, :], ALU.mult, ALU.mult, r=['qg', 'cosb', 'gains'], w=['t1'])
            o.STT('dve', t2[:, :], ps[b2][0:96, 0:tb], G['gk'][:, 0:1], sinb[:, :], ALU.mult, ALU.mult,
                  r=[('ps', b2), 'sinb', 'gains'], w=['t2'])
            o.TT('pool', QTs[:, h, :], t1[:, :], t2[:, :], ALU.add, r=['t1', 't2'], w=['QTs'])
        for cc in range(2):
            b = C['rA'].next()
            for h in range(12):
                o.MM(ps[b][:, h * tb:(h + 1) * tb], wukT[0:64, h, cc * 128:(cc + 1) * 128], QTs[0:64, h, :], True, True,
                     r=['wukT', 'QTs'], w=[('ps', b)])
            o.CP('dve', QL[:, cc, :, :], ps[b][:, 0:12 * tb].rearrange("p (h t) -> p h t", t=tb), r=[('ps', b)], w=['QL'])
        BO, PSB, BSC = 4, 5, 3
        ngr = NPG // 4
        NG = NS * ngr
        psc = ps[PSB][:, :].bitcast(BF16)
        psr = ps[PSB + 1][:, :].bitcast(BF16)

        def slot(g, q):
            return (g % 4) * 4 + q

        def GATHER(g):
            for q in range(4):
                pg = g * 4 + q
                sl = slot(g, q)
                P.op('pool', lambda e, sl=sl, pg=pg: e.indirect_dma_start(
                    out=KVb[sl][:, 0:288], out_offset=None, in_=I["ckv"][:, :],
                    in_offset=bass.IndirectOffsetOnAxis(ap=idxall[:, pg:pg + 1], axis=0)),
                    r=['idx'], w=[('KVb', sl)], dma=True)

        def T(g):
            kt = g % 2
            for q in range(4):
                sl = slot(g, q)
                for cc in range(2):
                    o.TR(psc[:, (q * 2 + cc) * 128:(q * 2 + cc + 1) * 128], KVb[sl][:, cc * 128:(cc + 1) * 128], identb[:, :],
                         r=[('KVb', sl), 'identb'], w=[('ps', PSB)])
                o.TR(psr[0:96, q * 128:(q + 1) * 128], KVb[sl][:, 192:288], identb[:, :], r=[('KVb', sl), 'identb'], w=[('ps', PSB + 1)])
            o.CP('dve', KVTc[kt][:, :, :, :], psc[:, 0:1024].rearrange("p (q c k) -> p q c k", c=2, k=128), r=[('ps', PSB)], w=[('KVTc', kt)])
            o.CP('act', KVTr[kt][64:96, :, :], psr[64:96, 0:512].rearrange("p (q k) -> p q k", k=128), r=[('ps', PSB + 1)], w=[('KVTr', kt)])

        def KN(g):
            kt = g % 2
            for q in range(4):
                sl = slot(g, q)
                b3, b4 = (7, 0) if q % 2 == 0 else (1, 2)
                sq_ = sqn2[q % 2]
                sk = ('sqn', q % 2)
                for cc in range(2):
                    o.MM(ps[b3][:, 0:512], KVTc[kt][:, q, cc, :], C['wuk'][:, cc, 0:512], cc == 0, cc == 1,
                         r=[('KVTc', kt), 'wuk'], w=[('ps', b3)])
                for cc in range(2):
                    o.MM(ps[b4][:, 0:256], KVTc[kt][:, q, cc, :], C['wuk'][:, cc, 512:768], cc == 0, cc == 1,
                         r=[('KVTc', kt), 'wuk'], w=[('ps', b4)])
                o.ACT(sq_[:, 0:512], ps[b3][:, 0:512], AF.Square, r=[('ps', b3)], w=[sk])
                o.ACT(sq_[:, 512:768], ps[b4][:, 0:256], AF.Square, r=[('ps', b4)], w=[sk])
                o.ACT(sq_[:, 768:800], KVb[sl][:, 256:288], AF.Square, r=[('KVb', sl)], w=[sk])
                o.RED(ssn[:, q, 0:12], sq_[:, 0:768].rearrange("p (h d) -> p h d", d=64), r=[sk], w=['ssn'])
                o.RED(ssn[:, q, 12:13], sq_[:, 768:800].rearrange("p (h d) -> p h d", d=32), r=[sk], w=['ssn'])
            o.TT('dve', ssn[:, :, 0:12], ssn[:, :, 0:12], ssn[:, :, 12:13].to_broadcast([128, 4, 12]), ALU.add, r=['ssn'], w=['ssn'])
            o.ACT(ssn[:, :, 0:12], ssn[:, :, 0:12], AF.Ln, r=['ssn'], w=['ssn'], scale=1.0, bias=96.0 * EPS)
            o.ACT(kinv[:, g * 4:g * 4 + 4, :], ssn[:, :, 0:12], AF.Exp, r=['ssn'], w=['kinv'], scale=-0.5)

        def SE(g):
            kt = g % 2
            s_ = g // ngr
            qcols = slice(s_ * 8, (s_ + 1) * 8)
            for q in range(4):
                oc = ps[BSC][:, q * 96:(q + 1) * 96].rearrange("p (h i) -> p h i", i=8)
                o.MM(oc, KVTc[kt][:, q, 0, :], QL[:, 0, :, qcols], True, False, r=[('KVTc', kt), 'QL'], w=[('ps', BSC)])
                o.MM(oc, KVTc[kt][:, q, 1, :], QL[:, 1, :, qcols], False, False, r=[('KVTc', kt), 'QL'], w=[('ps', BSC)])
                o.MM(oc, KVTr[kt][64:96, q, :], QTs[64:96, :, qcols], False, True, r=[('KVTr', kt), 'QTs'], w=[('ps', BSC)])
            o.TT('dve', tms[:, :, :].rearrange("p q (h i) -> p q h i", i=8),
                 ps[BSC][:, 0:384].rearrange("p (q h i) -> p q h i", h=12, i=8),
                 kinv[:, g * 4:g * 4 + 4, :].unsqueeze(3).to_broadcast([128, 4, 12, 8]), ALU.mult, r=[('ps', BSC), 'kinv'], w=['tms'])
            o.ACT(PTs[kt][:, :, :], tms[:, :, :], AF.Exp, r=['tms'], w=[('PTs', kt)])

        def PV(g):
            kt = g % 2
            gi = g % ngr
            for q in range(4):
                sl = slot(g, q)
                o.MM(ps[BO][0:96, 0:289], PTs[kt][:, q, :], KVb[sl][:, 0:289], gi == 0 and q == 0, False,
                     r=[('PTs', kt), ('KVb', sl)], w=[('ps', BO)])

        def EPI(s_):
            qcols = slice(s_ * 8, (s_ + 1) * 8)
            oc = ps[BSC][0:32, 0:96].rearrange("p (h i) -> p h i", i=8)
            o.MM(oc, cTs[:, 0, :], QL[:, 0, :, qcols], True, False, r=['cTs', 'QL'], w=[('ps', BSC)])
            o.MM(oc, cTs[:, 1, :], QL[:, 1, :, qcols], False, False, r=['cTs', 'QL'], w=[('ps', BSC)])
            o.MM(oc, krb[64:96, :], QTs[64:96, :, qcols], False, True, r=['krb', 'QTs'], w=[('ps', BSC)])
            o.TT('dve', tms[0:32, 0, :].rearrange("p (h i) -> p h i", i=8), ps[BSC][0:32, 0:96].rearrange("p (h i) -> p h i", i=8),
                 kinvn[:, :].unsqueeze(2).to_broadcast([32, 12, 8]), ALU.mult, r=[('ps', BSC), 'kinvn'], w=['tms'])
            o.ACT(tms[0:32, 1, :], tms[0:32, 0, :], AF.Exp, r=['tms'], w=['tms'])
            o.TT('dve', PTn[:, :], tms[0:32, 1, :], smask[:, s_, :], ALU.mult, r=['tms', 'smask'], w=['PTn'])
            o.MM(ps[BO][0:96, 0:289], PTn[:, :], KVbn[:, 0:289], NPG == 0, True, r=['PTn', 'KVbn'], w=[('ps', BO)])
            o.RCP(rcs[:, :], ps[BO][0:96, 288:289], r=[('ps', BO)], w=['rcs'])
            o.TSM('dve', olat[:, :], ps[BO][0:96, 0:256], rcs[:, 0:1], r=[('ps', BO), 'rcs'], w=['olat'])
            b = PSB
            for cc in range(2):
                o.TR(ps[b][:, cc * 96:(cc + 1) * 96], olat[:, cc * 128:(cc + 1) * 128], ident[0:96, 0:96], r=['olat', 'ident'], w=[('ps', b)])
            o.CP('dve', olT[:, :, :], ps[b][:, 0:192].rearrange("p (c n) -> p c n", n=96), r=[('ps', b)], w=['olT'])
            b = PSB + 1
            for h in range(12):
                for cc in range(2):
                    o.MM(ps[b][:, h * 8:(h + 1) * 8], C['wuv'][:, cc, (h // 2) * 128:(h // 2 + 1) * 128], olT[:, cc, h * 8:(h + 1) * 8],
                         cc == 0, cc == 1, r=['wuv', 'olT'], w=[('ps', b)])
            for par in range(2):
                vr = slice(par * 64, par * 64 + 64)
                src = ps[b][vr, 0:96].rearrange("p (hp two i) -> p hp two i", two=2, i=8)[:, :, par, :]
                o.CP('dve', cat[vr, 0:6, qcols], src, r=[('ps', b)], w=['cat'])

        for g in range(min(3, NG)):
            GATHER(g)
        T(0)
        if l == 2:
            KN(0)
        if NG > 1:
            T(1)
        for g in range(NG):
            if g + 3 < NG:
                GATHER(g + 3)
            SE(g)
            if l == 2 and g + 1 < NG:
                KN(g + 1)
            if g + 2 < NG:
                T(g + 2)
            PV(g)
            if g % ngr == ngr - 1:
                EPI(g // ngr)
        mem_attend(l, 6)
        out_proj(l)

    o.DMA('sp', xin[:], I["xs"][:, :], w=['xin'])
    for c in range(8):
        b = C['rA'].next()
        o.TR(ps[b][:, 0:32], xin[:, c * 128:(c + 1) * 128], ident[0:32, 0:32], r=['xin', 'ident'], w=[('ps', b)])
        o.CP('dve', xT[:, c, :], ps[b][:, 0:32], r=[('ps', b)], w=[('xT', c)])
    load_mix_weights(0)
    for l in range(nlayers):
        mem_load(l)
        pre = [load_ffn_group(l, fg) for fg in range(min(NRING, NFG))]
        if l < 2:
            mix_a(l)
        else:
            mix_b(l)
        ffn(l, pre, hook=(lambda l=l: load_mix_weights(l + 1)) if l < nlayers - 1 else None)
        if l == 1:
            kv_stage()
    for c in range(8):
        b = C['rA'].next()
        o.TR(ps[b][0:32, 0:128], xT[:, c, :], ident[:, :], r=[('xT', c), 'ident'], w=[('ps', b)])
        o.CP('dve', xin[:, c * 128:(c + 1) * 128], ps[b][0:32, 0:128], r=[('ps', b)], w=['xin'])
    o.DMA('sp', O["y_s"][:, :], xin[:], r=['xin'], out=True)
    P.finish()
    P.emit('s')
    print("sample program: ops", len(P.ops), "waits", P.nwaits)


def make_in_maps(inputs, cores, NT=2048):
    consts = host_consts()
    maps = []
    ckv = np.ascontiguousarray(inputs["cache_kv"]).reshape(-1, 288)
    shared = {k: np.ascontiguousarray(inputs[k]) for k in
              ("norm_mix", "norm_ffn", "w_out", "w_ffn_in", "w_ffn_out", "norm_mem", "w_mem_kv", "g_mem_q", "g_mem_k",
               "w_in_a", "w_pool_grp", "pool_scale", "w_in_b", "g_q_lora", "w_uq", "g_q", "norm_kv", "w_dkv",
               "g_kv_lora", "w_uk", "w_uv", "g_k_nope", "g_k_rope")}
    for c in cores:
        m = dict(shared)
        m.update(consts)
        m["xp"] = np.ascontiguousarray(inputs["x_prompt"][c][:NT])
        m["xs"] = np.ascontiguousarray(inputs["x_sample"][4 * c:4 * c + 4]).reshape(32, D)
        m["ckv"] = ckv
        m["pt"] = np.ascontiguousarray(inputs["page_table"][4 * c:4 * c + 4]).astype(np.int32)
        m["spool"] = np.ascontiguousarray(inputs["state_pool"][:, 4 * c:4 * c + 4])
        m["cmk"] = np.ascontiguousarray(inputs["cache_mem_k"][:, 4 * c:4 * c + 4]).reshape(4, 4, 256, 256)
        m["cmv"] = np.ascontiguousarray(inputs["cache_mem_v"][:, 4 * c:4 * c + 4]).reshape(4, 4, 256, 256)
        m["memp"] = np.ascontiguousarray(inputs["mem_prompt"][c])
        maps.append(m)
    return maps


def kernel(**inputs):
    nc = build()
    maps = make_in_maps(inputs, list(range(NCORES)))
    res = run_bass_kernel_spmd(nc, maps, core_ids=list(range(NCORES)))
    R = res.results
    y_p = np.stack([R[c]["y_p"] for c in range(NCORES)])
    y_s = np.concatenate([R[c]["y_s"].reshape(4, 8, D) for c in range(NCORES)])
    kv_p = np.stack([R[c]["kv_p"] for c in range(NCORES)])
    kv_s = np.concatenate([R[c]["kv_s"].reshape(4, 8, 288) for c in range(NCORES)])
    pool_p = np.stack([R[c]["pool_p"] for c in range(NCORES)], axis=1)
    pool_s = np.concatenate([R[c]["pool_s"] for c in range(NCORES)], axis=1)
    mk = np.stack([R[c]["mk_p"] for c in range(NCORES)], axis=1).reshape(4, 8, 256, 4, 64)
    mv = np.stack([R[c]["mv_p"] for c in range(NCORES)], axis=1).reshape(4, 8, 256, 4, 64)
    return (y_p, y_s, kv_p, kv_s, pool_p, pool_s, mk, mv)
```

```python
import numpy as np
import concourse.bass as bass
import concourse.mybir as mybir
from concourse.bass_utils import run_bass_kernel_spmd
from contextlib import ExitStack

F32 = mybir.dt.float32
BF16 = mybir.dt.bfloat16
I32 = mybir.dt.int32
AF = mybir.ActivationFunctionType
ALU = mybir.AluOpType
AX = mybir.AxisListType

D = 1024
DFF = 2816
EPS = 1e-6
TB = 512
NCORES = 8


class Prog:
    ENGS = ('pe', 'act', 'dve', 'pool', 'sp')
    NDMA = 8

    def __init__(self, nc, es, pfx=''):
        self.nc, self.es = nc, es
        self.pfx = pfx
        self.ops = []
        self.lastw = {}
        self.readers = {}
        self.out_dmas = []

    def sb(self, name, shape, dt):
        return self.es.enter_context(self.nc.sbuf_tensor(self.pfx + name, shape, dt))

    def ps(self, name, shape, dt):
        return self.es.enter_context(self.nc.psum_tensor(self.pfx + name, shape, dt))

    def op(self, eng, fn, r=(), w=(), dma=False, out=False):
        i = len(self.ops)
        deps = set()
        pr = [k for k in r if isinstance(k, tuple) and k[0] == 'ps']
        if pr:
            w = list(w) + [k for k in pr if k not in w]
        for k in r:
            lw = self.lastw.get(k)
            if lw is not None:
                deps.add(lw)
        for k in w:
            lw = self.lastw.get(k)
            if lw is not None:
                deps.add(lw)
            for rd in self.readers.get(k, ()):
                deps.add(rd)
        keep = []
        for d in deps:
            de, ddma = self.ops[d][0], self.ops[d][3]
            if de == eng and not dma and not ddma and eng == 'pe':
                continue
            keep.append(d)
        for k in r:
            self.readers.setdefault(k, []).append(i)
        for k in w:
            self.lastw[k] = i
            self.readers[k] = []
        self.ops.append([eng, fn, keep, dma])
        if out:
            self.out_dmas.append(i)
        return i

    def finish(self):
        import os
        stop = int(os.environ.get("K_STOP", "0"))
        if stop:
            self.ops = self.ops[:stop]
            self.out_dmas = [i for i in range(len(self.ops)) if self.ops[i][3]]
        self.ops.append(['sp', None, list(self.out_dmas), False])

    def emit(self, tag=''):
        nc, es = self.nc, self.es
        ops = self.ops
        n = len(ops)
        needed = [False] * n
        for o in ops:
            for d in o[2]:
                needed[d] = True
        csem = {e: es.enter_context(nc.semaphore(tag + 'c_' + e)) for e in self.ENGS}
        ccnt = {e: 0 for e in self.ENGS}
        dsem = {e: [es.enter_context(nc.semaphore('%sd_%s%d' % (tag, e, j))) for j in range(self.NDMA)]
                for e in ('sp', 'pool', 'act')}
        dcnt = {e: [0] * self.NDMA for e in dsem}
        dlast = {e: [None] * self.NDMA for e in dsem}
        drr = {e: 0 for e in dsem}
        sig = [None] * n
        extra = [None] * n
        streams = {e: [] for e in self.ENGS}
        for i, o in enumerate(ops):
            eng, fn, deps, dma = o
            streams[eng].append(i)
            if dma:
                j = drr[eng]
                drr[eng] = (j + 1) % self.NDMA
                dcnt[eng][j] += 1
                sig[i] = (dsem[eng][j], 16 * dcnt[eng][j])
                extra[i] = dlast[eng][j]
                dlast[eng][j] = i
            elif needed[i]:
                ccnt[eng] += 1
                sig[i] = (csem[eng], ccnt[eng])
        self.nwaits = 0

        def body(e, eng):
            known = {}
            for i in streams[eng]:
                _, fn, deps, dma = ops[i]
                req = {}
                dl = list(deps)
                if extra[i] is not None:
                    dl.append(extra[i])
                for d in dl:
                    s, v = sig[d]
                    if req.get(id(s), (None, 0))[1] < v:
                        req[id(s)] = (s, v)
                for s, v in req.values():
                    if known.get(id(s), 0) < v:
                        e.wait_ge(s, v)
                        known[id(s)] = v
                        self.nwaits += 1
                if fn is None:
                    continue
                ins = fn(e)
                if sig[i] is not None:
                    ins.then_inc(sig[i][0], 16 if dma else 1)

        with nc.Block() as block:
            @block.tensor
            def _(e):
                body(e, 'pe')

            @block.scalar
            def _(e):
                body(e, 'act')

            @block.vector
            def _(e):
                body(e, 'dve')

            @block.gpsimd
            def _(e):
                body(e, 'pool')

            @block.sync
            def _(e):
                body(e, 'sp')


class Rot:
    def __init__(self, items):
        self.items, self.i = items, 0

    def next(self):
        x = self.items[self.i % len(self.items)]
        self.i += 1
        return x


class Ops:
    def __init__(self, P):
        self.P = P

    def MM(self, o, l, rh, st, sp, r, w):
        self.P.op('pe', lambda e: e.matmul(o, lhsT=l, rhs=rh, start=st, stop=sp), r=r, w=w)

    def TR(self, o, i, idn, r, w):
        self.P.op('pe', lambda e: e.transpose(out=o, in_=i, identity=idn), r=r, w=w)

    def ACT(self, o, i, f, r, w, **kw):
        self.P.op('act', lambda e: e.activation(out=o, in_=i, func=f, **kw), r=r, w=w)

    def TT(self, eng, o, a, b, op, r, w):
        self.P.op(eng, lambda e: e.tensor_tensor(out=o, in0=a, in1=b, op=op), r=r, w=w)

    def STT(self, eng, o, a, s, b, op0, op1, r, w):
        self.P.op(eng, lambda e: e.scalar_tensor_tensor(out=o, in0=a, scalar=s, in1=b, op0=op0, op1=op1), r=r, w=w)

    def TS(self, eng, o, a, s1, s2, op0, op1, r, w):
        self.P.op(eng, lambda e: e.tensor_scalar(out=o, in0=a, scalar1=s1, scalar2=s2, op0=op0, op1=op1), r=r, w=w)

    def CP(self, eng, o, i, r, w):
        if eng == 'act':
            self.P.op('act', lambda e: e.activation(out=o, in_=i, func=AF.Copy), r=r, w=w)
        else:
            self.P.op(eng, lambda e: e.tensor_copy(out=o, in_=i), r=r, w=w)

    def TSA(self, eng, o, a, s1, r, w):
        self.P.op(eng, lambda e: e.tensor_scalar_add(out=o, in0=a, scalar1=s1), r=r, w=w)

    def TSM(self, eng, o, a, s1, r, w):
        self.P.op(eng, lambda e: e.tensor_scalar_mul(out=o, in0=a, scalar1=s1), r=r, w=w)

    def RCP(self, o, i, r, w):
        self.P.op('dve', lambda e: e.reciprocal(out=o, in_=i), r=r, w=w)

    def MSET(self, eng, o, v, w):
        self.P.op(eng, lambda e: e.memset(o, v), w=w)

    def RED(self, o, i, r, w):
        self.P.op('dve', lambda e: e.tensor_reduce(out=o, in_=i, axis=AX.X, op=ALU.add), r=r, w=w)

    def DMA(self, eng, o, i, r=(), w=(), out=False, slow=False):
        if slow:
            self.P.op(eng, lambda e: e.dma_start(out=o, in_=i, allow_slow_non_contiguous=True), r=r, w=w, dma=True, out=out)
        else:
            self.P.op(eng, lambda e: e.dma_start(out=o, in_=i), r=r, w=w, dma=True, out=out)


CKV_ROWS = [5120 * 128]
PT_COLS = [128]


def in_specs(NT):
    return [
        ("xp", [NT, D], F32), ("xs", [32, D], F32), ("ckv", [CKV_ROWS[0], 288], F32), ("pt", [4, PT_COLS[0]], I32),
        ("spool", [2, 4, 15, 768], F32), ("cmk", [4, 4, 256, 256], F32), ("cmv", [4, 4, 256, 256], F32),
        ("memp", [256, D], F32),
        ("norm_mix", [4, D], F32), ("norm_ffn", [4, D], F32), ("w_out", [4, D, D], F32),
        ("w_ffn_in", [4, D, 2 * DFF], F32), ("w_ffn_out", [4, DFF, D], F32), ("norm_mem", [4, D], F32),
        ("w_mem_kv", [4, D, 512], F32), ("g_mem_q", [4, 64], F32), ("g_mem_k", [4, 64], F32),
        ("w_in_a", [2, D, D], F32), ("w_pool_grp", [2, 4, 192, 192], F32), ("pool_scale", [2, 768], F32),
        ("w_in_b", [2, D, 640], F32), ("g_q_lora", [2, 384], F32), ("w_uq", [2, 384, 1152], F32), ("g_q", [2, 96], F32),
        ("norm_kv", [D], F32), ("w_dkv", [D, 288], F32), ("g_kv_lora", [256], F32), ("w_uk", [256, 768], F32),
        ("w_uv", [256, 768], F32), ("g_k_nope", [64], F32), ("g_k_rope", [16], F32),
        ("c_ident", [128, 128], F32), ("c_perm", [96, 96], F32), ("c_cos", [96, 2048 + 32], F32),
        ("c_sin", [96, 2048 + 32], F32), ("c_tri", [128, 128], F32), ("c_rcnt", [96, 8, 16], F32),
        ("c_smask", [32, 4, 96], F32),
    ]


def out_specs(NT):
    return [
        ("y_p", [NT, D]), ("y_s", [32, D]), ("kv_p", [NT, 288]), ("kv_s", [32, 288]),
        ("pool_p", [2, 15, 768]), ("pool_s", [2, 4, 15, 768]), ("mk_p", [4, 256, 256]), ("mv_p", [4, 256, 256]),
    ]


def host_consts():
    c = {}
    c["c_ident"] = np.eye(128, dtype=np.float32)
    perm = np.zeros((96, 96), np.float32)
    for i in range(16):
        perm[80 + i, 64 + i] = -1.0
        perm[64 + i, 80 + i] = 1.0
    c["c_perm"] = perm
    half = 16
    inv_freq = (np.float32(10000.0) ** (-np.arange(half, dtype=np.float32) / np.float32(half))).astype(np.float32)
    pos = np.concatenate([np.arange(2048), np.tile(16384 + np.arange(8), 4)]).astype(np.float32)
    ang = (pos[None, :] * inv_freq[:, None]).astype(np.float32)
    cos = np.ones((96, 2080), np.float32)
    sin = np.zeros((96, 2080), np.float32)
    cos[64:80] = np.cos(ang)
    cos[80:96] = np.cos(ang)
    sin[64:80] = np.sin(ang)
    sin[80:96] = np.sin(ang)
    c["c_cos"], c["c_sin"] = cos, sin
    kk = np.arange(128)
    c["c_tri"] = (kk[None, :] >= kk[:, None]).astype(np.float32)
    rc = np.zeros((96, 8, 16), np.float32)
    for ch in range(8):
        w = 2 ** (ch // 2 + 1)
        rc[:, ch, :] = (1.0 / np.minimum(w, np.arange(16) + 1)).astype(np.float32)[None, :]
    c["c_rcnt"] = rc
    sm = np.zeros((32, 4, 12, 8), np.float32)
    for k in range(32):
        s = k // 8
        for q in range(8):
            if k % 8 <= q:
                sm[k, s, :, q] = 1.0
    c["c_smask"] = sm.reshape(32, 4, 96)
    return c


def build(NT=2048, SEG=512, do_prompt=True, do_sample=True, nlayers=4, NPG=128):
    nc = bass.Bass("TRN2", target_bir_lowering=False)
    I = {n: nc.dram_tensor(n, s, dt, kind="ExternalInput").ap() for n, s, dt in in_specs(NT)}
    O = {n: nc.dram_tensor(n, s, F32, kind="ExternalOutput").ap() for n, s in out_specs(NT)}
    SC = None
    if do_prompt and do_sample:
        SC = {n: nc.dram_tensor("sc_" + n, shp, BF16).ap() for n, shp in (
            ("w_ffn_in", [4, D, 2 * DFF]), ("w_ffn_out", [4, DFF, D]), ("w_in_a", [2, D, D]), ("w_in_b", [2, D, 640]),
            ("w_out", [4, D, D]), ("w_pool_grp", [2, 4, 192, 192]), ("w_uq", [2, 384, 1152]))}
    if do_sample:
        with ExitStack() as es:
            sample_program(nc, es, I, O, nlayers, NPG, SC)
    if do_prompt:
        with ExitStack() as es:
            prompt_program(nc, es, I, O, NT, SEG, nlayers, SC)
    return nc


def load_small(P, o, I, C):
    g1 = P.sb("gst1", [128, 128], F32)
    g2 = P.sb("gst2", [32, 128], F32)
    gT1 = P.sb("gT1", [128, 128], F32)
    gT2 = P.sb("gT2", [128, 32], F32)
    o.MSET('dve', g1[:], 0.0, w=['gst1'])
    o.MSET('dve', g2[:], 0.0, w=['gst2'])
    o.DMA('sp', g1[0:32, :], I["norm_mix"].rearrange("l (k p) -> (l k) p", p=128), w=['gst1'])
    o.DMA('sp', g1[32:64, :], I["norm_ffn"].rearrange("l (k p) -> (l k) p", p=128), w=['gst1'])
    o.DMA('sp', g1[64:96, :], I["norm_mem"].rearrange("l (k p) -> (l k) p", p=128), w=['gst1'])
    o.DMA('sp', g1[96:104, :], I["norm_kv"].rearrange("(k p) -> k p", p=128), w=['gst1'])
    o.DMA('sp', g1[104:106, :], I["g_kv_lora"].rearrange("(k p) -> k p", p=128), w=['gst1'])
    o.DMA('sp', g1[106:112, :], I["g_q_lora"].rearrange("l (k p) -> (l k) p", p=128), w=['gst1'])
    o.DMA('sp', g2[0:16, 0:96], I["pool_scale"].rearrange("l (k p) -> (l k) p", p=96), w=['gst2'])
    o.DMA('sp', g2[16:20, 0:64], I["g_mem_q"][:, :], w=['gst2'])
    o.DMA('sp', g2[16:20, 64:128], I["g_mem_q"][:, :], w=['gst2'])
    o.DMA('sp', g2[20:24, 0:64], I["g_mem_k"][:, :], w=['gst2'])
    o.DMA('sp', g2[20:24, 64:128], I["g_mem_k"][:, :], w=['gst2'])
    o.DMA('sp', g2[24:26, 0:96], I["g_q"][:, :], w=['gst2'])
    o.DMA('sp', g2[26:27, 0:64], I["g_k_nope"].rearrange("(o p) -> o p", o=1), w=['gst2'])
    o.DMA('sp', g2[26:27, 64:80], I["g_k_rope"].rearrange("(o p) -> o p", o=1), w=['gst2'])
    o.DMA('sp', g2[26:27, 80:96], I["g_k_rope"].rearrange("(o p) -> o p", o=1), w=['gst2'])
    ps = C['ps']
    o.TR(ps[0][:, 0:128], g1[:, :], C['ident'][:, :], r=['gst1', 'ident'], w=[('ps', 0)])
    o.CP('dve', gT1[:, :], ps[0][:, 0:128], r=[('ps', 0)], w=['gains'])
    o.TR(ps[1][:, 0:32], g2[:, :], C['ident'][0:32, 0:32], r=['gst2', 'ident'], w=[('ps', 1)])
    o.CP('dve', gT2[:, :], ps[1][:, 0:32], r=[('ps', 1)], w=['gains'])
    G = {'gmix': gT1[:, 0:32], 'gffn': gT1[:, 32:64], 'gmem': gT1[:, 64:96], 'gkv': gT1[:, 96:104], 'gkvl': gT1[:, 104:106],
         'gql': gT1[:, 106:112], 'pscale': gT2[0:96, 0:16], 'gmq': gT2[:, 16:20], 'gmk': gT2[:, 20:24], 'gq': gT2[0:96, 24:26],
         'gk': gT2[0:96, 26:27]}
    return G


def alloc_common(P, o, I):
    C = {}
    C['ident'] = P.sb("ident", [128, 128], F32)
    C['ones'] = P.sb("ones", [128, 128], BF16)
    C['blk2'] = P.sb("blk2", [128, 128], BF16)
    C['perm'] = P.sb("perm", [96, 96], BF16)
    C['tri'] = P.sb("tri", [128, 128], BF16)
    C['rcnt'] = P.sb("rcnt", [96, 8, 16], F32)
    o.DMA('sp', C['ident'][:], I["c_ident"][:, :], w=['ident'])
    o.DMA('pool', C['perm'][:], I["c_perm"][:, :], w=['perm'])
    o.DMA('pool', C['tri'][:], I["c_tri"][:, :], w=['tri'])
    o.DMA('sp', C['rcnt'][:], I["c_rcnt"][:, :, :], w=['rcnt'])
    o.MSET('dve', C['ones'][:], 1.0, w=['ones'])
    o.MSET('dve', C['blk2'][:], 0.0, w=['blk2'])
    o.MSET('dve', C['blk2'][0:64, 0:64], 1.0, w=['blk2'])
    o.MSET('dve', C['blk2'][64:128, 64:128], 1.0, w=['blk2'])
    C['wuk'] = P.sb("wuk", [128, 2, 768], BF16)
    C['wuv'] = P.sb("wuv", [128, 2, 768], BF16)
    C['wdkv'] = P.sb("wdkv", [128, 8, 352], BF16)
    o.DMA('pool', C['wuk'][:], I["w_uk"].rearrange("(k p) n -> p k n", p=128), w=['wuk'])
    o.DMA('pool', C['wuv'][:], I["w_uv"].rearrange("(k p) n -> p k n", p=128), w=['wuv'])
    o.MSET('pool', C['wdkv'][:, :, 256:320], 0.0, w=['wdkv'])
    o.DMA('pool', C['wdkv'][:, :, 0:256], I["w_dkv"].rearrange("(k p) n -> p k n", p=128)[:, :, 0:256], w=['wdkv'])
    o.DMA('pool', C['wdkv'][:, :, 320:352], I["w_dkv"].rearrange("(k p) n -> p k n", p=128)[:, :, 256:288], w=['wdkv'])
    ps = [P.ps("ps%d" % i, [128, 512], F32) for i in range(8)]
    C['ps'] = ps
    C['G'] = load_small(P, o, I, C)
    C['rA'] = Rot([0, 1])
    C['rS'] = Rot([2, 3])
    C['rO'] = Rot([4, 5])
    C['rG'] = Rot([6, 7])
    C['tmp'] = P.sb("tmpn", [128, 512], F32)
    C['rstd'] = P.sb("rstd", [128, 512], F32)
    C['rtmp'] = Rot([0])
    return C


def norm_fm(o, C, xaps, gcols, haps, np_, Dn, tb, rkeys, hkey, ones, eps=EPS, sq=None, split=False):
    ps = C['ps']
    b = C['rA'].next()
    pk = ('ps', b)
    nch = len(xaps)
    sqaps, sqkey = (haps, hkey) if sq is None else sq
    if split and nch == 8:
        order = [0, 2, 4, 6, 1, 3, 5, 7]
        for c in order:
            rk_c = [rkeys[c]] if len(rkeys) == 8 else rkeys
            if c % 2 == 0:
                o.ACT(sqaps[c], xaps[c], AF.Square, r=rk_c, w=[(sqkey, 'sq', c)])
            else:
                o.TT('dve', sqaps[c], xaps[c], xaps[c], ALU.mult, r=rk_c, w=[(sqkey, 'sq', c)])
        for i, c in enumerate(order):
            o.MM(ps[b][0:np_, 0:tb], ones, sqaps[c], i == 0, i == nch - 1, r=[(sqkey, 'sq', c), 'ones', 'blk2'], w=[pk])
        sqdeps = [(sqkey, 'sq', c) for c in range(8)]
    else:
        for c in range(nch):
            o.ACT(sqaps[c], xaps[c], AF.Square, r=rkeys, w=[sqkey])
        for c in range(nch):
            o.MM(ps[b][0:np_, 0:tb], ones, sqaps[c], c == 0, c == nch - 1, r=[sqkey, 'ones', 'blk2'], w=[pk])
        sqdeps = [sqkey]
    o.ACT(C['tmp'][0:np_, 0:tb], ps[b][0:np_, 0:tb], AF.Ln, r=[pk], w=['tmpn'], scale=1.0 / Dn, bias=eps)
    o.ACT(C['rstd'][0:np_, 0:tb], C['tmp'][0:np_, 0:tb], AF.Exp, r=['tmpn'], w=['rstd'], scale=-0.5)
    for c in range(nch):
        o.STT('dve', haps[c], xaps[c], gcols[c], C['rstd'][0:np_, 0:tb], ALU.mult, ALU.mult,
              r=list(rkeys) + ['rstd', 'gains'] + sqdeps, w=[hkey])


def prompt_program(nc, es, I, O, NT, SEG, nlayers, SC=None):
    P = Prog(nc, es)
    o = Ops(P)
    C = alloc_common(P, o, I)
    ps = C['ps']
    G = C['G']
    ident, ones, blk2, perm, tri = C['ident'], C['ones'], C['blk2'], C['perm'], C['tri']
    SCM = {"k": nc.dram_tensor("sc_memKT", [4, 128, 512], BF16).ap(), "v": nc.dram_tensor("sc_Vaug", [4, 128, 1024], BF16).ap()}
    NB = SEG // TB
    NSEG = NT // SEG
    NKC = NT // 128
    tb = TB

    xT = P.sb("xT", [128, 8, SEG], F32)
    hT = P.sb("hT", [128, 8, SEG], BF16)
    win = P.sb("win", [128, 8, 1024], BF16)
    wout = P.sb("wout", [128, 10, 1024], BF16)
    wmisc = P.sb("wmisc", [128, 3456], BF16)
    NRG, NRO = 2, 3
    ringG = [P.sb("ringG%d" % i, [128, 4096], BF16) for i in range(NRG)]
    ringO = [P.sb("ringO%d" % i, [128, 2048], BF16) for i in range(NRO)]
    cT = P.sb("cT", [128, 2, NT], BF16)
    KT = [P.sb("KT%d" % i, [96, NT], BF16) for i in range(2)]
    Vh = [P.sb("Vh%d" % i, [128, NKC, 128], BF16) for i in range(2)]
    kinv = P.sb("kinv", [128, NKC, 12], F32)
    cat = P.sb("cat", [128, 10, tb], BF16)
    qm = P.sb("qm", [128, 2, tb], F32)
    qn = P.sb("qn", [128, 2, tb], BF16)
    PT = [P.sb("PT%d" % i, [128, tb], BF16) for i in range(4)]
    rPT = Rot([0, 1, 2, 3])
    rc = P.sb("rc", [128, tb], F32)
    cosb = P.sb("cosb", [96, tb], F32)
    sinb = P.sb("sinb", [96, tb], F32)
    memKT = P.sb("memKT", [128, 2, 256], BF16)
    Vaug = P.sb("Vaug", [128, 2, 4, 128], BF16)
    halo = P.sb("halo", [96, 2, 8, 16], F32)
    AFN, ABN = 7616, 4096
    arf = P.sb("arena_f", [128, AFN], F32)
    arb = P.sb("arena_b", [128, ABN], BF16)

    def cv(ar, off, parts, n, pat=None, **kw):
        v = ar[0:parts, off:off + n]
        return v.rearrange(pat, **kw) if pat else v
    xin = cv(arf, 0, 128, 1024)
    xinb = [xin, cv(arf, 4608, 128, 1024)]
    memT = cv(arf, 1024, 128, 2048, "p (c t) -> p c t", t=256)
    kf = cv(arf, 3072, 128, 512, "p (c t) -> p c t", t=256)
    kout = cv(arf, 3584, 128, 512, "p (c t) -> p c t", t=256)
    vout = cv(arf, 4096, 128, 512, "p (c t) -> p c t", t=256)
    hmT = cv(arb, 0, 128, 2048, "p (c t) -> p c t", t=256)
    E_ = 16 + tb
    U = cv(arf, 0, 96, 8 * E_, "p (c t) -> p c t", t=E_)
    tA = cv(arf, 8 * E_, 96, 2 * E_, "p (c t) -> p c t", t=E_)
    tB_ = cv(arf, 10 * E_, 96, 2 * E_, "p (c t) -> p c t", t=E_)
    t1a = cv(arf, 12 * E_, 96, 512)
    pso = cv(arf, 12 * E_ + 512, 16, 768)
    dif = cv(arb, 0, 96, 4096, "p (c t) -> p c t", t=tb)
    cq = cv(arf, 0, 128, 1536, "p (c t) -> p c t", t=tb)
    ckf = cv(arf, 0, 128, 1024, "p (c t) -> p c t", t=tb)
    krf = cv(arf, 1024, 96, 512)
    t1 = cv(arf, 1536, 96, 512)
    t2 = cv(arf, 2048, 96, 512)
    kvo = cv(arf, 2560, 128, 288)
    ssn = cv(arf, 2848, 128, 16)
    cqn = cv(arb, 0, 128, 1536, "p (c t) -> p c t", t=tb)
    QT = [cv(arb, 1536, 96, 512), cv(arb, 2048, 96, 512)]
    qg = cv(arb, 2560, 96, 512)
    krb = cv(arb, 1536, 96, 512)
    sqn = cv(arb, 2048, 128, 768)
    sqr = cv(arb, 2816, 128, 32)
    sg = [cv(arf, 0, 128, 512), cv(arf, 512, 128, 512)]
    stg = [cv(arf, 1024, 128, 512), cv(arf, 1536, 128, 512)]
    aT = [cv(arb, 0, 128, 1024, "p (c t) -> p c t", t=tb), cv(arb, 1024, 128, 1024, "p (c t) -> p c t", t=tb)]
    wmem = ringG[0]
    ARK = ['xin', 'xin1', 'memT', 'kf', 'kout', 'vout', 'hmT', 'U', 'tA', 'tB', 't1', 't2', 'pso', 'dif', 'cq', 'ckf', 'krf', 'kvo',
           'ssn', 'ssn2', 'cqn', ('QT', 0), ('QT', 1), 'qg', 'krb', 'sqn', 'sqr', ('sg', 0), ('sg', 1), ('aT', 0), ('aT', 1), ('stg', 0), ('stg', 1)]
    dummy = P.sb("dummyb", [128, 8], F32)

    def barrier():
        P.op('pool', lambda e: e.memset(dummy[:], 0.0), r=ARK, w=ARK)

    o.MSET('pool', Vaug[:], 1.0, w=['Vaug'])
    for i in range(2):
        o.MSET('pool', Vh[i][:], 1.0, w=[('Vh', i)])

    W = SC if SC is not None else I
    WQ = 'sp' if SC is not None else 'pool'
    wv_ffn_in = [W["w_ffn_in"][l].rearrange("(k p) n -> p k n", p=128) for l in range(4)]
    wv_ffn_out = [W["w_ffn_out"][l].rearrange("(k p) n -> p k n", p=128) for l in range(4)]
    NFG = DFF // 256

    def XK(j):
        return [('xT', j, c) for c in range(8)]

    ring_state = {'g': 0, 'o': 0}

    def load_G(l, fg):
        s = ring_state['g'] % NRG
        ring_state['g'] += 1
        rg = ringG[s]
        k = ('rg', s)
        gv = rg[:, 0:2048].rearrange("p (k n) -> p k n", n=256)
        uv = rg[:, 2048:4096].rearrange("p (k n) -> p k n", n=256)
        o.DMA(WQ, gv, wv_ffn_in[l][:, :, fg * 256:(fg + 1) * 256], w=[k])
        o.DMA(WQ, uv, wv_ffn_in[l][:, :, DFF + fg * 256:DFF + (fg + 1) * 256], w=[k])
        return gv, uv, k

    def load_O(l, fg):
        s = ring_state['o'] % NRO
        ring_state['o'] += 1
        k = ('ro', s)
        ov = ringO[s][:, 0:2048].rearrange("p (k n) -> p k n", n=1024)
        o.DMA(WQ, ov, wv_ffn_out[l][:, fg * 2:fg * 2 + 2, :], w=[k])
        return ov, k

    def load_mix_weights(l):
        if l < 2:
            o.DMA(WQ, win[:], W["w_in_a"][l].rearrange("(k p) n -> p k n", p=128), w=['win'])
            o.DMA(WQ, wout[0:96, 0:8, :], W["w_out"][l][0:768, :].rearrange("(k p) n -> p k n", p=96), w=['wout'])
            o.DMA(WQ, wout[:, 8:10, :], W["w_out"][l][768:1024, :].rearrange("(k p) n -> p k n", p=128), w=['wout'])
            gv = wmisc[0:96, 0:1536].rearrange("p (k n) -> p k n", n=192)
            o.DMA(WQ, gv, W["w_pool_grp"][l].rearrange("g (c p) e -> p (g c) e", p=96), w=['wmisc'])
        else:
            j = l - 2
            o.DMA(WQ, win[:, :, 0:640], W["w_in_b"][j].rearrange("(k p) n -> p k n", p=128), w=['win'])
            o.DMA(WQ, wout[:, 0:8, :], W["w_out"][l].rearrange("(k p) n -> p k n", p=128), w=['wout'])
            uq = wmisc[:, 0:3456].rearrange("p (k n) -> p k n", n=1152)
            o.DMA(WQ, uq, W["w_uq"][j].rearrange("(k p) n -> p k n", p=128), w=['wmisc'])

    def mem_kv(l, write_out):
        if not write_out:
            o.DMA('sp', memKT[:], SCM["k"][l].rearrange("p (c m) -> p c m", m=256), r=[('scm', l)], w=['memKT'])
            o.DMA('sp', Vaug[:], SCM["v"][l].rearrange("p (c h d) -> p c h d", h=4, d=128), r=[('scm', l)], w=['Vaug'])
            return
        barrier()
        o.DMA('pool', wmem[:, 0:4096].rearrange("p (k n) -> p k n", n=512),
              I["w_mem_kv"][l].rearrange("(k p) n -> p k n", p=128), w=[('rg', 0)])
        wm = wmem[:, 0:4096].rearrange("p (k n) -> p k n", n=512)
        for mt in range(2):
            o.DMA('sp', xin[:], I["memp"][mt * 128:(mt + 1) * 128, :], w=['xin'])
            for c in range(8):
                b = C['rA'].next()
                o.TR(ps[b][:, 0:128], xin[:, c * 128:(c + 1) * 128], ident[:], r=['xin', 'ident'], w=[('ps', b)])
                o.CP('dve' if c % 2 else 'act', memT[:, c, mt * 128:(mt + 1) * 128], ps[b][:, 0:128], r=[('ps', b)], w=['memT'])
        norm_fm(o, C, [memT[:, c, :] for c in range(8)], [G['gmem'][:, l * 8 + c:l * 8 + c + 1] for c in range(8)],
                [hmT[:, c, :] for c in range(8)], 128, D, 256, ['memT'], 'hmT', ones[:])
        for hp in range(2):
            b = C['rA'].next()
            for k in range(8):
                o.MM(ps[b][:, 0:256], wm[:, k, hp * 128:(hp + 1) * 128], hmT[:, k, :], k == 0, k == 7,
                     r=['hmT', ('rg', 0)], w=[('ps', b)])
            norm_fm(o, C, [ps[b][:, 0:256]], [G['gmk'][:, l:l + 1]], [kf[:, hp, :]], 128, 64, 256, [('ps', b)], 'kf', blk2[:],
                    sq=([qn[:, 0, 0:256]], 'qn'))
        o.CP('dve', memKT[:], kf[:], r=['kf'], w=['memKT'])
        for mc in range(2):
            b = C['rA'].next()
            for k in range(8):
                o.MM(ps[b][:, 0:256], hmT[:, k, mc * 128:(mc + 1) * 128], wm[:, k, 256:512], k == 0, k == 7,
                     r=['hmT', ('rg', 0)], w=[('ps', b)])
            if write_out:
                o.CP('act', vout[:, mc, :], ps[b][:, 0:256], r=[('ps', b)], w=['vout'])
            for h in range(4):
                off = 0 if h % 2 == 0 else 64
                o.CP('dve', Vaug[:, mc, h, off:off + 64], ps[b][:, h * 64:(h + 1) * 64], r=[('ps', b)], w=['Vaug'])
        o.DMA('sp', SCM["k"][l].rearrange("p (c m) -> p c m", m=256), memKT[:], r=['memKT'], w=[('scm', l)])
        o.DMA('sp', SCM["v"][l].rearrange("p (c h d) -> p c h d", h=4, d=128), Vaug[:], r=['Vaug'], w=[('scm', l)])
        if write_out:
            for hp in range(2):
                for mc in range(2):
                    b = C['rA'].next()
                    o.TR(ps[b][:, 0:128], kf[:, hp, mc * 128:(mc + 1) * 128], ident[:], r=['kf', 'ident'], w=[('ps', b)])
                    o.CP('act', kout[:, mc, hp * 128:(hp + 1) * 128], ps[b][:, 0:128], r=[('ps', b)], w=['kout'])
            o.DMA('sp', O["mk_p"][l].rearrange("(m p) f -> p m f", p=128), kout[:], r=['kout'], out=True)
            o.DMA('sp', O["mv_p"][l].rearrange("(m p) f -> p m f", p=128), vout[:], r=['vout'], out=True)

    def mem_attend(l, j, catbase):
        for hp in range(2):
            norm_fm(o, C, [qm[:, hp, :]], [G['gmq'][:, l:l + 1]], [qn[:, hp, :]], 128, 64, tb, ['qm'], 'qn', blk2[:])
        st = {}

        def SS(h):
            hp, base = h // 2, (h % 2) * 64
            pts = []
            for mc in range(2):
                bsc = C['rS'].next()
                o.MM(ps[bsc][:, 0:tb], memKT[base:base + 64, hp, mc * 128:(mc + 1) * 128], qn[base:base + 64, hp, :], True, True,
                     r=['memKT', 'qn'], w=[('ps', bsc)])
                pi = rPT.next()
                o.ACT(PT[pi][:, :], ps[bsc][:, 0:tb], AF.Exp, r=[('ps', bsc)], w=[('PT', pi)], scale=0.125)
                pts.append(pi)
            st[h] = pts

        def PVV(h):
            hp, base = h // 2, (h % 2) * 64
            pts = st[h]
            bo = C['rO'].next()
            for mc in range(2):
                o.MM(ps[bo][:, 0:tb], Vaug[:, mc, h, :], PT[pts[mc]][:, :], mc == 0, mc == 1,
                     r=['Vaug', ('PT', pts[mc])], w=[('ps', bo)])
            vr = slice(base, base + 64)
            dr = slice(64 - base, 128 - base)
            o.ACT(rc[dr, :], ps[bo][dr, 0:tb], AF.Ln, r=[('ps', bo)], w=['rc'])
            o.ACT(rc[dr, :], rc[dr, :], AF.Exp, r=['rc'], w=['rc'], scale=-1.0)
            o.TT('dve', cat[vr, catbase + hp, :], ps[bo][vr, 0:tb], rc[dr, :], ALU.mult, r=[('ps', bo), 'rc'], w=['cat'])

        SS(0)
        for h in range(4):
            if h + 1 < 4:
                SS(h + 1)
            PVV(h)

    def out_proj(l, j):
        blk = slice(j * tb, (j + 1) * tb)
        if l < 2:
            chunks = [(96, c) for c in range(8)] + [(128, 8), (128, 9)]
        else:
            chunks = [(128, c) for c in range(8)]
        for oc in range(8):
            b = C['rA'].next()
            for i, (kk, c) in enumerate(chunks):
                o.MM(ps[b][:, 0:tb], wout[0:kk, c, oc * 128:(oc + 1) * 128], cat[0:kk, c, :], i == 0, i == len(chunks) - 1,
                     r=['wout', 'cat'], w=[('ps', b)])
            o.TT('dve', xT[:, oc, blk], xT[:, oc, blk], ps[b][:, 0:tb], ALU.add, r=[('ps', b), ('xT', j, oc)], w=[('xT', j, oc)])

    def mix_a(l, sg_, j):
        barrier()
        blk = slice(j * tb, (j + 1) * tb)
        jg = sg_ * NB + j
        norm_fm(o, C, [xT[:, c, blk] for c in range(8)], [G['gmix'][:, l * 8 + c:l * 8 + c + 1] for c in range(8)],
                [hT[:, c, blk] for c in range(8)], 128, D, tb, XK(j), ('hT', j), ones[:], split=True)
        if jg == 0:
            o.MSET('pool', U[:, :, 0:16], 0.0, w=['U'])
        elif j == 0:
            o.CP('pool', U[:, :, 0:16], halo[:, l, :, :], r=[('halo', l)], w=['U'])
        for c in range(8):
            b = C['rA'].next()
            for k in range(8):
                o.MM(ps[b][0:96, 0:tb], win[:, k, c * 96:(c + 1) * 96], hT[:, k, blk], k == 0, k == 7,
                     r=['win', ('hT', j)], w=[('ps', b)])
            o.CP('act', U[:, c, 16:16 + tb], ps[b][0:96, 0:tb], r=[('ps', b)], w=['U'])
        for hp in range(2):
            b = C['rA'].next()
            for k in range(8):
                o.MM(ps[b][:, 0:tb], win[:, k, 768 + hp * 128:768 + (hp + 1) * 128], hT[:, k, blk], k == 0, k == 7,
                     r=['win', ('hT', j)], w=[('ps', b)])
            o.CP('act', qm[:, hp, :], ps[b][:, 0:tb], r=[('ps', b)], w=['qm'])
        gw = wmisc[0:96, 0:1536].rearrange("p (k n) -> p k n", n=192)
        E = 16 + tb
        for g in range(4):
            w_ = 2 ** (g + 1)
            src = U[:, 2 * g:2 * g + 2, :]
            bufs = [tA, tB_]
            d = 1
            i = 0
            srck = 'U'
            while d < w_:
                dst = bufs[i % 2]
                dk = 'tA' if i % 2 == 0 else 'tB'
                lo = 2 * d
                o.TT('pool', dst[:, :, lo:E], src[:, :, lo:E], src[:, :, lo - d:E - d], ALU.add, r=[srck], w=[dk])
                src, srck = dst, dk
                d *= 2
                i += 1
            o.STT('dve', dif[:, 2 * g:2 * g + 2, :], src[:, :, 16:E], 1.0 / w_, U[:, 2 * g:2 * g + 2, 16:E], ALU.mult, ALU.subtract,
                  r=[srck, 'U'], w=['dif'])
            if jg == 0:
                o.TT('dve', t1a[:, 0:32].rearrange("p (c t) -> p c t", t=16), src[:, :, 16:32], C['rcnt'][:, 2 * g:2 * g + 2, :],
                     ALU.mult, r=[srck, 'rcnt'], w=['t1'])
                o.TT('dve', dif[:, 2 * g:2 * g + 2, 0:16], t1a[:, 0:32].rearrange("p (c t) -> p c t", t=16),
                     U[:, 2 * g:2 * g + 2, 16:32], ALU.subtract, r=['t1', 'U', 'dif'], w=['dif'])
            for ec in range(2):
                b = C['rA'].next()
                for cc in range(2):
                    o.MM(ps[b][0:96, 0:tb], gw[:, 2 * g + cc, ec * 96:(ec + 1) * 96], dif[:, 2 * g + cc, :], cc == 0, cc == 1,
                         r=['wmisc', 'dif'], w=[('ps', b)])
                ch = 2 * g + ec
                o.TSM('dve', cat[0:96, ch, :], ps[b][0:96, 0:tb], G['pscale'][:, l * 8 + ch:l * 8 + ch + 1],
                      r=[('ps', b), 'gains'], w=['cat'])
        if jg == NSEG * NB - 1:
            for c in range(8):
                b = C['rA'].next()
                o.TR(ps[b][0:15, 0:96], U[:, c, E - 15:E], ident[0:96, 0:96], r=['U', 'ident'], w=[('ps', b)])
                o.CP('act', pso[0:15, c * 96:(c + 1) * 96], ps[b][0:15, 0:96], r=[('ps', b)], w=['pso'])
            o.DMA('sp', O["pool_p"][l], pso[0:15, :], r=['pso'], out=True)
        elif j == NB - 1:
            o.CP('pool', halo[:, l, :, :], U[:, :, tb:tb + 16], r=['U'], w=[('halo', l)])
        else:
            o.CP('pool', tA[:, 0, 0:128].rearrange("p (c t) -> p c t", t=16), U[:, :, tb:tb + 16], r=['U'], w=['tA'])
            o.CP('pool', U[:, :, 0:16], tA[:, 0, 0:128].rearrange("p (c t) -> p c t", t=16), r=['tA'], w=['U'])
        mem_attend(l, j, 8)
        out_proj(l, j)

    def kv_stage(sg_, j):
        barrier()
        blk = slice(j * tb, (j + 1) * tb)
        jg = sg_ * NB + j
        gb = slice(jg * tb, (jg + 1) * tb)
        wd = C['wdkv']
        norm_fm(o, C, [xT[:, c, blk] for c in range(8)], [G['gkv'][:, c:c + 1] for c in range(8)],
                [hT[:, c, blk] for c in range(8)], 128, D, tb, XK(j), ('hT', j), ones[:], split=True)
        o.DMA('sp', cosb[:], I["c_cos"][:, gb], w=['cosb'])
        o.DMA('sp', sinb[:], I["c_sin"][:, gb], w=['sinb'])
        for cc in range(2):
            b = C['rA'].next()
            for k in range(8):
                o.MM(ps[b][:, 0:tb], wd[:, k, cc * 128:(cc + 1) * 128], hT[:, k, blk], k == 0, k == 7,
                     r=['wdkv', ('hT', j)], w=[('ps', b)])
            o.CP('act', ckf[:, cc, :], ps[b][:, 0:tb], r=[('ps', b)], w=['ckf'])
        br = C['rG'].next()
        for k in range(8):
            o.MM(ps[br][0:96, 0:tb], wd[:, k, 256:352], hT[:, k, blk], k == 0, k == 7, r=['wdkv', ('hT', j)], w=[('ps', br)])
        o.MSET('pool', krb[0:64, :], 0.0, w=['krb'])
        o.CP('act', krb[64:96, :], ps[br][64:96, 0:tb], r=[('ps', br)], w=['krb'])
        b2 = C['rG'].next()
        o.MM(ps[b2][0:96, 0:tb], perm[0:96, 0:96], krb[0:96, :], True, True, r=['perm', 'krb'], w=[('ps', b2)])
        o.TT('dve', t1[64:96, :], ps[br][64:96, 0:tb], cosb[64:96, :], ALU.mult, r=[('ps', br), 'cosb'], w=['t1'])
        o.TT('dve', t2[64:96, :], ps[b2][64:96, 0:tb], sinb[64:96, :], ALU.mult, r=[('ps', b2), 'sinb'], w=['t2'])
        o.TT('dve', krf[64:96, :], t1[64:96, :], t2[64:96, :], ALU.add, r=['t1', 't2'], w=['krf'])
        for i in range(2):
            o.CP('pool', KT[i][64:96, gb], krf[64:96, :], r=['krf'], w=[('KT', i)])
        norm_fm(o, C, [ckf[:, cc, :] for cc in range(2)], [G['gkvl'][:, cc:cc + 1] for cc in range(2)],
                [cqn[:, cc, :] for cc in range(2)], 128, 256, tb, ['ckf'], 'cqn', ones[:])
        o.CP('pool', cT[:, :, gb], cqn[:, 0:2, :], r=['cqn'], w=['cT'])
        for cc in range(2):
            o.STT('dve', ckf[:, cc, :], ckf[:, cc, :], G['gkvl'][:, cc:cc + 1], C['rstd'][:, 0:tb], ALU.mult, ALU.mult,
                  r=['ckf', 'rstd', 'cqn'], w=['ckf'])
        for tt in range(tb // 128):
            ts_ = slice(tt * 128, (tt + 1) * 128)
            kc = jg * 4 + tt
            b = C['rA'].next()
            o.TR(ps[b][:, 0:128], ckf[:, 0, ts_], ident[:], r=['ckf', 'ident'], w=[('ps', b)])
            o.TR(ps[b][:, 128:256], ckf[:, 1, ts_], ident[:], r=['ckf', 'ident'], w=[('ps', b)])
            o.TR(ps[b][:, 256:288], krf[64:96, ts_], ident[64:96, 64:96], r=['krf', 'ident'], w=[('ps', b)])
            o.CP('act', kvo[:, :], ps[b][:, 0:288], r=[('ps', b)], w=['kvo'])
            o.DMA('sp', O["kv_p"][jg * tb + tt * 128: jg * tb + (tt + 1) * 128, :], kvo[:, :], r=['kvo'], out=True)
            b3, b4 = C['rG'].next(), C['rG'].next()
            for cc in range(2):
                o.MM(ps[b3][:, 0:512], cT[:, cc, jg * tb + tt * 128: jg * tb + (tt + 1) * 128], C['wuk'][:, cc, 0:512], cc == 0, cc == 1,
                     r=['cT', 'wuk'], w=[('ps', b3)])
            for cc in range(2):
                o.MM(ps[b4][:, 0:256], cT[:, cc, jg * tb + tt * 128: jg * tb + (tt + 1) * 128], C['wuk'][:, cc, 512:768], cc == 0, cc == 1,
                     r=['cT', 'wuk'], w=[('ps', b4)])
            o.ACT(sqn[:, 0:512], ps[b3][:, 0:512], AF.Square, r=[('ps', b3)], w=['sqn'])
            o.ACT(sqn[:, 512:768], ps[b4][:, 0:256], AF.Square, r=[('ps', b4)], w=['sqn'])
            o.RED(ssn[:, 0:12], sqn[:, :].rearrange("p (h d) -> p h d", d=64), r=['sqn'], w=['ssn'])
            o.ACT(sqr[:, 0:32], kvo[:, 256:288], AF.Square, r=['kvo'], w=['sqr'])
            o.RED(ssn[:, 12:13], sqr[:, 0:32].rearrange("p (h d) -> p h d", d=32), r=['sqr'], w=['ssn2'])
            o.TSA('dve', ssn[:, 0:12], ssn[:, 0:12], ssn[:, 12:13], r=['ssn', 'ssn2'], w=['ssn'])
            o.ACT(ssn[:, 0:12], ssn[:, 0:12], AF.Ln, r=['ssn'], w=['ssn'], scale=1.0, bias=96.0 * EPS)
            o.ACT(kinv[:, kc, :], ssn[:, 0:12], AF.Exp, r=['ssn'], w=['kinv'], scale=-0.5)

    def mix_b(l, sg_, j):
        barrier()
        jl = l - 2
        blk = slice(j * tb, (j + 1) * tb)
        jg = sg_ * NB + j
        gb = slice(jg * tb, (jg + 1) * tb)
        norm_fm(o, C, [xT[:, c, blk] for c in range(8)], [G['gmix'][:, l * 8 + c:l * 8 + c + 1] for c in range(8)],
                [hT[:, c, blk] for c in range(8)], 128, D, tb, XK(j), ('hT', j), ones[:], split=True)
        o.DMA('sp', cosb[:], I["c_cos"][:, gb], w=['cosb'])
        o.DMA('sp', sinb[:], I["c_sin"][:, gb], w=['sinb'])
        for c in range(3):
            b = C['rA'].next()
            for k in range(8):
                o.MM(ps[b][:, 0:tb], win[:, k, c * 128:(c + 1) * 128], hT[:, k, blk], k == 0, k == 7,
                     r=['win', ('hT', j)], w=[('ps', b)])
            o.CP('act', cq[:, c, :], ps[b][:, 0:tb], r=[('ps', b)], w=['cq'])
        for hp in range(2):
            b = C['rA'].next()
            for k in range(8):
                o.MM(ps[b][:, 0:tb], win[:, k, 384 + hp * 128:384 + (hp + 1) * 128], hT[:, k, blk], k == 0, k == 7,
                     r=['win', ('hT', j)], w=[('ps', b)])
            o.CP('act', qm[:, hp, :], ps[b][:, 0:tb], r=[('ps', b)], w=['qm'])
        norm_fm(o, C, [cq[:, c, :] for c in range(3)], [G['gql'][:, jl * 3 + c:jl * 3 + c + 1] for c in range(3)],
                [cqn[:, c, :] for c in range(3)], 128, 384, tb, ['cq'], 'cqn', ones[:])
        uq = wmisc[:, 0:3456].rearrange("p (k n) -> p k n", n=1152)
        nk = (jg + 1) * 4
        BQ, BN = 6, 7

        def prepA(h):
            ki = h % 2
            for k in range(3):
                o.MM(ps[BQ][0:96, 0:tb], uq[:, k, h * 96:(h + 1) * 96], cqn[:, k, :], k == 0, k == 2,
                     r=['wmisc', 'cqn'], w=[('ps', BQ)])
            o.ACT(qg[:, :], ps[BQ][0:96, 0:tb], AF.Square, r=[('ps', BQ)], w=['qg'])
            for kb in range(jg + 1):
                bk = C['rA'].next()
                for cc in range(2):
                    o.MM(ps[bk][0:64, 0:tb], C['wuk'][:, cc, h * 64:(h + 1) * 64], cT[:, cc, kb * tb:(kb + 1) * tb], cc == 0, cc == 1,
                         r=['wuk', 'cT'], w=[('ps', bk)])
                o.CP('dve', KT[ki][0:64, kb * tb:(kb + 1) * tb], ps[bk][0:64, 0:tb], r=[('ps', bk)], w=[('KT', ki)])
            off = 0 if h % 2 == 0 else 64
            for kg in range(nk // 8 + (1 if nk % 8 else 0)):
                bk = C['rA'].next()
                n8 = min(8, nk - kg * 8)
                for q8 in range(n8):
                    kc = kg * 8 + q8
                    for cc in range(2):
                        o.MM(ps[bk][:, q8 * 64:(q8 + 1) * 64], cT[:, cc, kc * 128:(kc + 1) * 128], C['wuv'][:, cc, h * 64:(h + 1) * 64],
                             cc == 0, cc == 1, r=['wuv', 'cT'], w=[('ps', bk)])
                o.CP('dve', Vh[ki][:, kg * 8:kg * 8 + n8, off:off + 64],
                     ps[bk][:, 0:n8 * 64].rearrange("p (k d) -> p k d", d=64), r=[('ps', bk)], w=[('Vh', ki)])

        def prepN1(h):
            o.MM(ps[BN][0:96, 0:tb], ones[0:96, 0:96], qg[:, :], True, True, r=['qg', 'ones'], w=[('ps', BN)])
            o.ACT(C['tmp'][0:96, 0:tb], ps[BN][0:96, 0:tb], AF.Ln, r=[('ps', BN)], w=['tmpn'], scale=1.0 / 96, bias=EPS)
            o.ACT(C['rstd'][0:96, 0:tb], C['tmp'][0:96, 0:tb], AF.Exp, r=['tmpn'], w=['rstd'], scale=-0.5)
            o.STT('dve', qg[:, :], ps[BQ][0:96, 0:tb], G['gq'][:, jl:jl + 1], C['rstd'][0:96, 0:tb], ALU.mult, ALU.mult,
                  r=[('ps', BQ), 'rstd', 'gains', 'qg'], w=['qg'])

        def prepN2(h):
            qi = h % 2
            o.MM(ps[BN][0:96, 0:tb], perm[:, :], qg[:, :], True, True, r=['perm', 'qg'], w=[('ps', BN)])
            o.STT('dve', t1[:, :], qg[:, :], G['gk'][:, 0:1], cosb[:, :], ALU.mult, ALU.mult, r=['qg', 'cosb', 'gains'], w=['t1'])
            o.STT('dve', t2[:, :], ps[BN][0:96, 0:tb], G['gk'][:, 0:1], sinb[:, :], ALU.mult, ALU.mult,
                  r=[('ps', BN), 'sinb', 'gains'], w=['t2'])
            o.TT('pool', QT[qi][:, :], t1[:, :], t2[:, :], ALU.add, r=['t1', 't2'], w=[('QT', qi)])

        def attend(h):
            qi = ki = h % 2
            off = 0 if h % 2 == 0 else 64
            bo = C['rO'].next()
            pend = {}

            def S(kc):
                r_ = kc - 4 * jg
                q0 = 128 * max(r_, 0)
                bs = C['rS'].next()
                o.MM(ps[bs][:, q0:tb], KT[ki][0:96, kc * 128:(kc + 1) * 128], QT[qi][:, q0:tb], True, True,
                     r=[('KT', ki), ('QT', qi)], w=[('ps', bs)])
                pi = rPT.next()
                o.ACT(PT[pi][:, q0:tb], ps[bs][:, q0:tb], AF.Exp, r=[('ps', bs), 'kinv'], w=[('PT', pi)],
                      scale=kinv[:, kc, h:h + 1])
                if r_ >= 0:
                    o.TT('pool', PT[pi][:, q0:q0 + 128], PT[pi][:, q0:q0 + 128], tri[:, :], ALU.mult, r=[('PT', pi), 'tri'], w=[('PT', pi)])
                pend[kc] = (pi, q0)

            def PV(kc):
                pi, q0 = pend.pop(kc)
                o.MM(ps[bo][:, q0:tb], Vh[ki][:, kc, :], PT[pi][:, q0:tb], kc == 0, kc == nk - 1,
                     r=[('Vh', ki), ('PT', pi)], w=[('ps', bo)])

            hook2 = min(3, nk - 1)
            S(0)
            if h + 1 < 12:
                prepA(h + 1)
            for kc in range(nk):
                if kc + 1 < nk:
                    S(kc + 1)
                PV(kc)
                if h + 1 < 12:
                    if kc == 0:
                        prepN1(h + 1)
                    if kc == hook2:
                        prepN2(h + 1)
            vr = slice(off, off + 64)
            dr = slice(64 - off, 128 - off)
            o.ACT(rc[dr, :], ps[bo][dr, 0:tb], AF.Ln, r=[('ps', bo)], w=['rc'])
            o.ACT(rc[dr, :], rc[dr, :], AF.Exp, r=['rc'], w=['rc'], scale=-1.0)
            o.TT('dve', cat[vr, h // 2, :], ps[bo][vr, 0:tb], rc[dr, :], ALU.mult, r=[('ps', bo), 'rc'], w=['cat'])

        prepA(0)
        prepN1(0)
        prepN2(0)
        for h in range(12):
            attend(h)
        mem_attend(l, j, 6)
        out_proj(l, j)

    def ffn(l, pre, hook=None):
        barrier()
        for j in range(NB):
            blk = slice(j * tb, (j + 1) * tb)
            norm_fm(o, C, [xT[:, c, blk] for c in range(8)], [G['gffn'][:, l * 8 + c:l * 8 + c + 1] for c in range(8)],
                    [hT[:, c, blk] for c in range(8)], 128, D, tb, XK(j), ('hT', j), ones[:], split=True)
        rOut = Rot([0, 1, 6, 7])
        rStg = Rot([0, 1])
        pendG, pendO = list(pre[0]), list(pre[1])
        nx = {'g': len(pendG), 'o': len(pendO)}
        units = [(fg, j) for fg in range(NFG) for j in range(NB)]
        grpG, grpO = {}, {}

        def GU(u):
            fg, j = units[u]
            if fg not in grpG:
                grpG[fg] = pendG.pop(0)
            gv, uv, rk = grpG[fg]
            blk = slice(j * tb, (j + 1) * tb)
            ai = u % 2
            for cc in range(2):
                bg, bu = 2 + cc, 4 + cc
                for k in range(8):
                    o.MM(ps[bg][:, 0:tb], gv[:, k, cc * 128:(cc + 1) * 128], hT[:, k, blk], k == 0, k == 7,
                         r=[rk, ('hT', j)], w=[('ps', bg)])
                for k in range(8):
                    o.MM(ps[bu][:, 0:tb], uv[:, k, cc * 128:(cc + 1) * 128], hT[:, k, blk], k == 0, k == 7,
                         r=[rk, ('hT', j)], w=[('ps', bu)])
                o.ACT(sg[cc][:, :], ps[bg][:, 0:tb], AF.Silu, r=[('ps', bg)], w=[('sg', cc)])
                o.TT('dve', aT[ai][:, cc, :], sg[cc][:, :], ps[bu][:, 0:tb], ALU.mult, r=[('sg', cc), ('ps', bu)], w=[('aT', ai)])
            if j == NB - 1 and nx['g'] < NFG:
                pendG.append(load_G(l, nx['g']))
                nx['g'] += 1

        def OUT(u):
            fg, j = units[u]
            if fg not in grpO:
                grpO[fg] = pendO.pop(0)
            ov, rk = grpO[fg]
            blk = slice(j * tb, (j + 1) * tb)
            ai = u % 2
            for oc in range(8):
                b = rOut.next()
                for cc in range(2):
                    o.MM(ps[b][:, 0:tb], ov[:, cc, oc * 128:(oc + 1) * 128], aT[ai][:, cc, :], cc == 0, cc == 1,
                         r=[rk, ('aT', ai)], w=[('ps', b)])
                if oc % 2 == 0:
                    o.TT('dve', xT[:, oc, blk], xT[:, oc, blk], ps[b][:, 0:tb], ALU.add, r=[('ps', b), ('xT', j, oc)], w=[('xT', j, oc)])
                else:
                    si = rStg.next()
                    o.CP('act', stg[si][:, :], ps[b][:, 0:tb], r=[('ps', b)], w=[('stg', si)])
                    o.TT('pool', xT[:, oc, blk], xT[:, oc, blk], stg[si][:, :], ALU.add, r=[('stg', si), ('xT', j, oc)], w=[('xT', j, oc)])
            if j == NB - 1:
                if nx['o'] < NFG:
                    pendO.append(load_O(l, nx['o']))
                    nx['o'] += 1
                if fg == 1 and hook is not None:
                    hook()

        GU(0)
        for u in range(len(units)):
            if u + 1 < len(units):
                GU(u + 1)
            OUT(u)

    for sg_ in range(NSEG):
        barrier()
        for tt in range(SEG // 128):
            xb, xk = xinb[tt % 2], ('xin' if tt % 2 == 0 else 'xin1')
            o.DMA('sp', xb[:], I["xp"][sg_ * SEG + tt * 128: sg_ * SEG + (tt + 1) * 128, :], w=[xk])
            for c in range(8):
                b = C['rA'].next()
                o.TR(ps[b][:, 0:128], xb[:, c * 128:(c + 1) * 128], ident[:], r=[xk, 'ident'], w=[('ps', b)])
                o.CP('dve' if c % 2 else 'act', xT[:, c, tt * 128:(tt + 1) * 128], ps[b][:, 0:128], r=[('ps', b)], w=[('xT', tt // 4, c)])
        for l in range(nlayers):
            mem_kv(l, write_out=(sg_ == 0))
            if l == 0 and sg_ == 0:
                load_mix_weights(0)
            pre = []
            for j in range(NB):
                if l < 2:
                    mix_a(l, sg_, j)
                else:
                    mix_b(l, sg_, j)
                if j == 0:
                    pre = ([load_G(l, fg) for fg in range(NRG)], [load_O(l, fg) for fg in range(NRO)])
            ffn(l, pre, hook=(lambda l=l: load_mix_weights((l + 1) % nlayers)) if not (l == nlayers - 1 and sg_ == NSEG - 1) else None)
            if l == 1:
                for j in range(NB):
                    kv_stage(sg_, j)
        barrier()
        for tt in range(SEG // 128):
            xb, xk = xinb[tt % 2], ('xin' if tt % 2 == 0 else 'xin1')
            for c in range(8):
                b = C['rA'].next()
                o.TR(ps[b][:, 0:128], xT[:, c, tt * 128:(tt + 1) * 128], ident[:], r=[('xT', tt // 4, c), 'ident'], w=[('ps', b)])
                o.CP('dve' if c % 2 else 'act', xb[:, c * 128:(c + 1) * 128], ps[b][:, 0:128], r=[('ps', b)], w=[xk])
            o.DMA('sp', O["y_p"][sg_ * SEG + tt * 128: sg_ * SEG + (tt + 1) * 128, :], xb[:], r=[xk], out=True)
    P.finish()
    P.emit('p')
    print("prompt program: ops", len(P.ops), "waits", P.nwaits)


def sample_program(nc, es, I, O, nlayers, NPG=128, SC=None):
    P = Prog(nc, es, 'S_')
    o = Ops(P)
    C = alloc_common(P, o, I)
    ps = C['ps']
    G = C['G']
    ident, ones, blk2, perm = C['ident'], C['ones'], C['blk2'], C['perm']
    tb = 32
    NS = 4
    NPT = NS * NPG

    xT = P.sb("s_xT", [128, 8, tb], F32)
    hT = P.sb("s_hT", [128, 8, tb], BF16)
    win = P.sb("s_win", [128, 8, 1024], BF16)
    wout = P.sb("s_wout", [128, 10, 1024], BF16)
    wmisc = P.sb("s_wmisc", [128, 3456], BF16)
    NRING = 3
    ring = [P.sb("s_ring%d" % i, [128, 6144], BF16) for i in range(NRING)]
    cat = P.sb("s_cat", [128, 10, tb], BF16)
    qm = P.sb("s_qm", [128, 2, tb], F32)
    qn = P.sb("s_qn", [128, 2, tb], BF16)
    xin = P.sb("s_xin", [32, 1024], F32)
    memKT = P.sb("s_memKT", [128, NS, 2, 256], BF16)
    Vaug = P.sb("s_Vaug", [128, NS, 2, 4, 128], BF16)
    mst = P.sb("s_mst", [128, 2, 256], F32)
    Us = P.sb("s_U", [96, 32, 24], F32)
    tA = P.sb("s_tA", [96, 8, 24], F32)
    tB_ = P.sb("s_tB", [96, 8, 24], F32)
    dif = P.sb("s_dif", [96, 8, tb], BF16)
    spt = P.sb("s_spt", [16, 768], F32)
    pso = P.sb("s_pso", [16, 768], F32)
    PTm = P.sb("s_PTm", [128, 256], BF16)
    rc = P.sb("s_rc", [128, 128], F32)
    sg = [P.sb("s_sg%d" % i, [128, tb], F32) for i in range(2)]
    aT = [P.sb("s_aT%d" % i, [128, 2, tb], BF16) for i in range(2)]
    cosb = P.sb("s_cos", [96, tb], F32)
    sinb = P.sb("s_sin", [96, tb], F32)
    ckf = P.sb("s_ckf", [128, 2, tb], F32)
    cTs = P.sb("s_cT", [128, 2, tb], BF16)
    krf = P.sb("s_krf", [96, tb], F32)
    krb = P.sb("s_krb", [96, tb], BF16)
    t1 = P.sb("s_t1", [96, tb], F32)
    t2 = P.sb("s_t2", [96, tb], F32)
    kvo = P.sb("s_kvo", [32, 288], F32)
    KVbn = P.sb("s_KVbn", [32, 289], BF16)
    sqn = P.sb("s_sqn", [128, 768], BF16)
    sqr = P.sb("s_sqr", [128, 32], BF16)
    ssn = P.sb("s_ssn", [128, 4, 16], F32)
    kinvn = P.sb("s_kinvn", [32, 12], F32)
    cq = P.sb("s_cq", [128, 3, tb], F32)
    cqn = P.sb("s_cqn", [128, 3, tb], BF16)
    qg = P.sb("s_qg", [96, tb], BF16)
    QTs = P.sb("s_QT", [96, 12, tb], BF16)
    QL = P.sb("s_QL", [128, 2, 12, tb], BF16)
    wukf = P.sb("s_wukf", [128, 2, 768], F32)
    wukT = P.sb("s_wukT", [64, 12, 256], BF16)
    identb = P.sb("s_identb", [128, 128], BF16)
    smask = P.sb("s_smask", [32, 4, 96], BF16)
    ptb = P.sb("s_ptb", [128, NPT], I32)
    idxall = P.sb("s_idx", [128, NPT], I32)
    iot = P.sb("s_iota", [128, 1], I32)
    kinv = P.sb("s_kinv", [128, NPT, 12], F32)
    NKB = 16
    sqn2 = [P.sb("s_sqn2_%d" % i, [128, 800], BF16) for i in range(2)]
    KVb = [P.sb("s_KVb%d" % i, [128, 289], BF16) for i in range(NKB)]
    KVTc = [P.sb("s_KVTc%d" % i, [128, 4, 2, 128], BF16) for i in range(2)]
    KVTr = [P.sb("s_KVTr%d" % i, [96, 4, 128], BF16) for i in range(2)]
    tms = P.sb("s_tms", [128, 4, 96], F32)
    PTs = [P.sb("s_PTs%d" % i, [128, 4, 96], BF16) for i in range(2)]
    PTn = P.sb("s_PTn", [32, 96], BF16)
    olat = P.sb("s_olat", [96, 256], F32)
    rcs = P.sb("s_rcs", [96, 1], F32)
    olT = P.sb("s_olT", [128, 2, 96], BF16)

    o.MSET('pool', Vaug[:], 1.0, w=['Vaug'])
    for i in range(NKB):
        o.MSET('pool', KVb[i][:, 288:289], 1.0, w=[('KVb', i)])
    o.MSET('pool', KVbn[:, 288:289], 1.0, w=['KVbn'])
    o.CP('pool', identb[:], ident[:], r=['ident'], w=['identb'])
    o.DMA('pool', smask[:], I["c_smask"][:, :, :], w=['smask'])
    o.DMA('sp', wukf[:], I["w_uk"].rearrange("(k p) n -> p k n", p=128), w=['wukf'])
    for h in range(12):
        b = C['rA'].next()
        for cc in range(2):
            o.TR(ps[b][0:64, cc * 128:(cc + 1) * 128], wukf[:, cc, h * 64:(h + 1) * 64], ident[:, :], r=['wukf', 'ident'], w=[('ps', b)])
        o.CP('dve', wukT[:, h, :], ps[b][0:64, 0:256], r=[('ps', b)], w=['wukT'])
    o.DMA('sp', ptb[:], I["pt"].rearrange("s p -> (s p)").partition_broadcast(128), w=['ptb'])
    P.op('pool', lambda e: e.iota(iot[:], pattern=[[0, 1]], base=0, channel_multiplier=1), w=['iota'])
    ptf = P.sb("s_ptf", [128, NPT], F32)
    iotf = P.sb("s_iotf", [128, 1], F32)
    o.CP('dve', ptf[:], ptb[:], r=['ptb'], w=['ptf'])
    o.CP('dve', iotf[:], iot[:], r=['iota'], w=['iotf'])
    o.TS('dve', ptf[:], ptf[:], 128.0, iotf[:, 0:1], ALU.mult, ALU.add, r=['ptf', 'iotf'], w=['ptf'])
    o.CP('dve', idxall[:], ptf[:], r=['ptf'], w=['idx'])

    wv_ffn_in = [I["w_ffn_in"][l].rearrange("(k p) n -> p k n", p=128) for l in range(4)]
    wv_ffn_out = [I["w_ffn_out"][l].rearrange("(k p) n -> p k n", p=128) for l in range(4)]
    NFG = DFF // 256
    XK = [('xT', c) for c in range(8)]
    ring_state = {'n': 0}

    def load_ffn_group(l, fg):
        s = ring_state['n'] % NRING
        ring_state['n'] += 1
        rg = ring[s]
        k = ('ring', s)
        gv = rg[:, 0:2048].rearrange("p (k n) -> p k n", n=256)
        uv = rg[:, 2048:4096].rearrange("p (k n) -> p k n", n=256)
        ov = rg[:, 4096:6144].rearrange("p (k n) -> p k n", n=1024)
        o.DMA('pool', gv, wv_ffn_in[l][:, :, fg * 256:(fg + 1) * 256], w=[k])
        o.DMA('pool', uv, wv_ffn_in[l][:, :, DFF + fg * 256:DFF + (fg + 1) * 256], w=[k])
        o.DMA('pool', ov, wv_ffn_out[l][:, fg * 2:fg * 2 + 2, :], w=[k])
        if SC is not None:
            si = SC["w_ffn_in"][l].rearrange("(k p) n -> p k n", p=128)
            so = SC["w_ffn_out"][l].rearrange("(k p) n -> p k n", p=128)
            o.DMA('sp', si[:, :, fg * 256:(fg + 1) * 256], gv, r=[k], out=True)
            o.DMA('sp', si[:, :, DFF + fg * 256:DFF + (fg + 1) * 256], uv, r=[k], out=True)
            o.DMA('sp', so[:, fg * 2:fg * 2 + 2, :], ov, r=[k], out=True)
        return gv, uv, ov, k

    def load_mix_weights(l):
        if l < 2:
            o.DMA('pool', win[:], I["w_in_a"][l].rearrange("(k p) n -> p k n", p=128), w=['win'])
            o.DMA('pool', wout[0:96, 0:8, :], I["w_out"][l][0:768, :].rearrange("(k p) n -> p k n", p=96), w=['wout'])
            o.DMA('pool', wout[:, 8:10, :], I["w_out"][l][768:1024, :].rearrange("(k p) n -> p k n", p=128), w=['wout'])
            gv = wmisc[0:96, 0:1536].rearrange("p (k n) -> p k n", n=192)
            o.DMA('pool', gv, I["w_pool_grp"][l].rearrange("g (c p) e -> p (g c) e", p=96), w=['wmisc'])
            if SC is not None:
                o.DMA('sp', SC["w_in_a"][l].rearrange("(k p) n -> p k n", p=128), win[:], r=['win'], out=True)
                o.DMA('sp', SC["w_out"][l][0:768, :].rearrange("(k p) n -> p k n", p=96), wout[0:96, 0:8, :], r=['wout'], out=True)
                o.DMA('sp', SC["w_out"][l][768:1024, :].rearrange("(k p) n -> p k n", p=128), wout[:, 8:10, :], r=['wout'], out=True)
                o.DMA('sp', SC["w_pool_grp"][l].rearrange("g (c p) e -> p (g c) e", p=96), gv, r=['wmisc'], out=True)
        else:
            j = l - 2
            o.DMA('pool', win[:, :, 0:640], I["w_in_b"][j].rearrange("(k p) n -> p k n", p=128), w=['win'])
            o.DMA('pool', wout[:, 0:8, :], I["w_out"][l].rearrange("(k p) n -> p k n", p=128), w=['wout'])
            uq = wmisc[:, 0:3456].rearrange("p (k n) -> p k n", n=1152)
            o.DMA('pool', uq, I["w_uq"][j].rearrange("(k p) n -> p k n", p=128), w=['wmisc'])
            if SC is not None:
                o.DMA('sp', SC["w_in_b"][j].rearrange("(k p) n -> p k n", p=128), win[:, :, 0:640], r=['win'], out=True)
                o.DMA('sp', SC["w_out"][l].rearrange("(k p) n -> p k n", p=128), wout[:, 0:8, :], r=['wout'], out=True)
                o.DMA('sp', SC["w_uq"][j].rearrange("(k p) n -> p k n", p=128), uq, r=['wmisc'], out=True)

    def xnorm(gkey, l):
        norm_fm(o, C, [xT[:, c, :] for c in range(8)], [G[gkey][:, l * 8 + c:l * 8 + c + 1] for c in range(8)],
                [hT[:, c, :] for c in range(8)], 128, D, tb, XK, 'hT', ones[:])

    def mem_load(l):
        for s_ in range(NS):
            o.DMA('sp', mst[:], I["cmk"][l, s_].rearrange("(m p) f -> p m f", p=128), w=['mst'])
            for mc in range(2):
                b = C['rA'].next()
                for hp in range(2):
                    o.TR(ps[b][:, hp * 128:(hp + 1) * 128], mst[:, mc, hp * 128:(hp + 1) * 128], ident[:], r=['mst', 'ident'], w=[('ps', b)])
                o.CP('dve', memKT[:, s_, :, mc * 128:(mc + 1) * 128], ps[b][:, 0:256].rearrange("p (h m) -> p h m", m=128),
                     r=[('ps', b)], w=['memKT'])
            o.DMA('sp', mst[:], I["cmv"][l, s_].rearrange("(m p) f -> p m f", p=128), r=['mst'], w=['mst'])
            for mc in range(2):
                for h in range(4):
                    off = 0 if h % 2 == 0 else 64
                    o.CP('pool', Vaug[:, s_, mc, h, off:off + 64], mst[:, mc, h * 64:(h + 1) * 64], r=['mst'], w=['Vaug'])

    def mem_attend(l, catbase):
        for hp in range(2):
            norm_fm(o, C, [qm[:, hp, :]], [G['gmq'][:, l:l + 1]], [qn[:, hp, :]], 128, 64, tb, ['qm'], 'qn', blk2[:])
        bsb = [2, 3]
        for mc in range(2):
            for s_ in range(NS):
                for h in range(4):
                    hp, par = h // 2, h % 2
                    base = par * 64
                    col = ((mc * NS + s_) * 2 + hp) * 8
                    o.MM(ps[bsb[par]][:, col:col + 8], memKT[base:base + 64, s_, hp, mc * 128:(mc + 1) * 128],
                         qn[base:base + 64, hp, s_ * 8:(s_ + 1) * 8], True, True, r=['memKT', 'qn'], w=[('ps', bsb[par])])
        for par in range(2):
            o.ACT(PTm[:, par * 128:(par + 1) * 128], ps[bsb[par]][:, 0:128], AF.Exp, r=[('ps', bsb[par])], w=['PTm'], scale=0.125)
        bo = C['rO'].next()
        for s_ in range(NS):
            for h in range(4):
                hp, par = h // 2, h % 2
                colo = (s_ * 4 + h) * 8
                for mc in range(2):
                    col = par * 128 + ((mc * NS + s_) * 2 + hp) * 8
                    o.MM(ps[bo][:, colo:colo + 8], Vaug[:, s_, mc, h, :], PTm[:, col:col + 8], mc == 0, mc == 1,
                         r=['Vaug', 'PTm'], w=[('ps', bo)])
        o.RCP(rc[:, 0:128], ps[bo][:, 0:128], r=[('ps', bo)], w=['rc'])
        for h in range(4):
            hp, base = h // 2, (h % 2) * 64
            vr = slice(base, base + 64)
            dr = slice(64 - base, 128 - base)
            pv = ps[bo][vr, 0:128].rearrange("p (s h i) -> p s h i", h=4, i=8)[:, :, h, :]
            rv = rc[dr, 0:128].rearrange("p (s h i) -> p s h i", h=4, i=8)[:, :, h, :]
            o.TT('dve', cat[vr, catbase + hp, :].rearrange("p (s i) -> p s i", i=8), pv, rv, ALU.mult, r=[('ps', bo), 'rc'], w=['cat'])

    def out_proj(l):
        if l < 2:
            chunks = [(96, c) for c in range(8)] + [(128, 8), (128, 9)]
        else:
            chunks = [(128, c) for c in range(8)]
        for oc in range(8):
            b = C['rA'].next()
            for i, (kk, c) in enumerate(chunks):
                o.MM(ps[b][:, 0:tb], wout[0:kk, c, oc * 128:(oc + 1) * 128], cat[0:kk, c, :], i == 0, i == len(chunks) - 1,
                     r=['wout', 'cat'], w=[('ps', b)])
            o.TT('dve', xT[:, oc, :], xT[:, oc, :], ps[b][:, 0:tb], ALU.add, r=[('ps', b), ('xT', oc)], w=[('xT', oc)])

    def mix_a(l):
        xnorm('gmix', l)
        for s_ in range(NS):
            o.DMA('sp', spt[0:15, :], I["spool"][l, s_], w=['spt'])
            b = C['rA'].next()
            for c in range(8):
                o.TR(ps[b][0:96, c * 16:c * 16 + 15], spt[0:15, c * 96:(c + 1) * 96], ident[0:15, 0:15], r=['spt', 'ident'], w=[('ps', b)])
            o.CP('dve', Us[:, :, 1:16].rearrange("p (c s) t -> p c s t", s=NS)[:, :, s_, :],
                 ps[b][0:96, 0:128].rearrange("p (c t) -> p c t", t=16)[:, :, 0:15], r=[('ps', b)], w=['U'])
        for c in range(8):
            b = C['rA'].next()
            for k in range(8):
                o.MM(ps[b][0:96, 0:tb], win[:, k, c * 96:(c + 1) * 96], hT[:, k, :], k == 0, k == 7, r=['win', 'hT'], w=[('ps', b)])
            o.CP('act', Us[:, c * NS:(c + 1) * NS, 16:24], ps[b][0:96, 0:tb].rearrange("p (s i) -> p s i", i=8), r=[('ps', b)], w=['U'])
        for hp in range(2):
            b = C['rA'].next()
            for k in range(8):
                o.MM(ps[b][:, 0:tb], win[:, k, 768 + hp * 128:768 + (hp + 1) * 128], hT[:, k, :], k == 0, k == 7,
                     r=['win', 'hT'], w=[('ps', b)])
            o.CP('act', qm[:, hp, :], ps[b][:, 0:tb], r=[('ps', b)], w=['qm'])
        gw = wmisc[0:96, 0:1536].rearrange("p (k n) -> p k n", n=192)
        E = 24
        for g in range(4):
            w_ = 2 ** (g + 1)
            src = Us[:, 8 * g:8 * g + 8, :]
            bufs = [tA, tB_]
            d, i, srck = 1, 0, 'U'
            while d < w_:
                dst = bufs[i % 2]
                dk = 'tA' if i % 2 == 0 else 'tB'
                lo = 2 * d
                o.TT('pool', dst[:, :, lo:E], src[:, :, lo:E], src[:, :, lo - d:E - d], ALU.add, r=[srck], w=[dk])
                src, srck = dst, dk
                d *= 2
                i += 1
            o.STT('dve', dif[:, 2 * g:2 * g + 2, :].rearrange("p c (s i) -> p (c s) i", i=8), src[:, :, 16:E], 1.0 / w_,
                  Us[:, 8 * g:8 * g + 8, 16:E], ALU.mult, ALU.subtract, r=[srck, 'U'], w=['dif'])
            for ec in range(2):
                b = C['rA'].next()
                for cc in range(2):
                    o.MM(ps[b][0:96, 0:tb], gw[:, 2 * g + cc, ec * 96:(ec + 1) * 96], dif[:, 2 * g + cc, :], cc == 0, cc == 1,
                         r=['wmisc', 'dif'], w=[('ps', b)])
                ch = 2 * g + ec
                o.TSM('dve', cat[0:96, ch, :], ps[b][0:96, 0:tb], G['pscale'][:, l * 8 + ch:l * 8 + ch + 1], r=[('ps', b), 'gains'], w=['cat'])
        for s_ in range(NS):
            b1, b2 = C['rA'].next(), C['rA'].next()
            for c in range(8):
                bb = b1 if c < 4 else b2
                o.TR(ps[bb][0:15, (c % 4) * 96:(c % 4 + 1) * 96], Us[:, c * NS + s_, 9:24], ident[0:96, 0:96], r=['U', 'ident'], w=[('ps', bb)])
            o.CP('act', pso[0:15, 0:384], ps[b1][0:15, 0:384], r=[('ps', b1)], w=['pso'])
            o.CP('act', pso[0:15, 384:768], ps[b2][0:15, 0:384], r=[('ps', b2)], w=['pso'])
            o.DMA('sp', O["pool_s"][l, s_], pso[0:15, :], r=['pso'], out=True)
        mem_attend(l, 8)
        out_proj(l)

    def ffn(l, pre, hook=None):
        xnorm('gffn', l)
        rOut = Rot([0, 1, 6, 7])
        pend = list(pre)
        nxt = len(pre)
        for fg in range(NFG):
            gv, uv, ov, rk = pend.pop(0)
            ai = fg % 2
            for cc in range(2):
                bg, bu = 2 + cc, 4 + cc
                for k in range(8):
                    o.MM(ps[bg][:, 0:tb], gv[:, k, cc * 128:(cc + 1) * 128], hT[:, k, :], k == 0, k == 7, r=[rk, 'hT'], w=[('ps', bg)])
                for k in range(8):
                    o.MM(ps[bu][:, 0:tb], uv[:, k, cc * 128:(cc + 1) * 128], hT[:, k, :], k == 0, k == 7, r=[rk, 'hT'], w=[('ps', bu)])
                o.ACT(sg[cc][:, :], ps[bg][:, 0:tb], AF.Silu, r=[('ps', bg)], w=[('sg', cc)])
                o.TT('dve', aT[ai][:, cc, :], sg[cc][:, :], ps[bu][:, 0:tb], ALU.mult, r=[('sg', cc), ('ps', bu)], w=[('aT', ai)])
            for oc in range(8):
                b = rOut.next()
                for cc in range(2):
                    o.MM(ps[b][:, 0:tb], ov[:, cc, oc * 128:(oc + 1) * 128], aT[ai][:, cc, :], cc == 0, cc == 1,
                         r=[rk, ('aT', ai)], w=[('ps', b)])
                o.TT('dve', xT[:, oc, :], xT[:, oc, :], ps[b][:, 0:tb], ALU.add, r=[('ps', b), ('xT', oc)], w=[('xT', oc)])
            if nxt < NFG:
                pend.append(load_ffn_group(l, nxt))
                nxt += 1
            if fg == 1 and hook is not None:
                hook()

    def kv_stage():
        wd = C['wdkv']
        norm_fm(o, C, [xT[:, c, :] for c in range(8)], [G['gkv'][:, c:c + 1] for c in range(8)],
                [hT[:, c, :] for c in range(8)], 128, D, tb, XK, 'hT', ones[:])
        o.DMA('sp', cosb[:], I["c_cos"][:, 2048:2080], w=['cosb'])
        o.DMA('sp', sinb[:], I["c_sin"][:, 2048:2080], w=['sinb'])
        for cc in range(2):
            b = C['rA'].next()
            for k in range(8):
                o.MM(ps[b][:, 0:tb], wd[:, k, cc * 128:(cc + 1) * 128], hT[:, k, :], k == 0, k == 7, r=['wdkv', 'hT'], w=[('ps', b)])
            o.CP('act', ckf[:, cc, :], ps[b][:, 0:tb], r=[('ps', b)], w=['ckf'])
        br = C['rG'].next()
        for k in range(8):
            o.MM(ps[br][0:96, 0:tb], wd[:, k, 256:352], hT[:, k, :], k == 0, k == 7, r=['wdkv', 'hT'], w=[('ps', br)])
        o.CP('act', krb[64:96, :], ps[br][64:96, 0:tb], r=[('ps', br)], w=['krb'])
        b2 = C['rG'].next()
        o.MM(ps[b2][0:96, 0:tb], perm[64:96, 0:96], krb[64:96, :], True, True, r=['perm', 'krb'], w=[('ps', b2)])
        o.TT('dve', t1[64:96, :], ps[br][64:96, 0:tb], cosb[64:96, :], ALU.mult, r=[('ps', br), 'cosb'], w=['t1'])
        o.TT('dve', t2[64:96, :], ps[b2][64:96, 0:tb], sinb[64:96, :], ALU.mult, r=[('ps', b2), 'sinb'], w=['t2'])
        o.TT('dve', krf[64:96, :], t1[64:96, :], t2[64:96, :], ALU.add, r=['t1', 't2'], w=['krf'])
        o.CP('pool', krb[64:96, :], krf[64:96, :], r=['krf'], w=['krb'])
        norm_fm(o, C, [ckf[:, cc, :] for cc in range(2)], [G['gkvl'][:, cc:cc + 1] for cc in range(2)],
                [cTs[:, cc, :] for cc in range(2)], 128, 256, tb, ['ckf'], 'cTs', ones[:])
        for cc in range(2):
            o.STT('dve', ckf[:, cc, :], ckf[:, cc, :], G['gkvl'][:, cc:cc + 1], C['rstd'][:, 0:tb], ALU.mult, ALU.mult,
                  r=['ckf', 'rstd', 'cTs'], w=['ckf'])
        b = C['rA'].next()
        o.TR(ps[b][0:32, 0:128], ckf[:, 0, :], ident[:], r=['ckf', 'ident'], w=[('ps', b)])
        o.TR(ps[b][0:32, 128:256], ckf[:, 1, :], ident[:], r=['ckf', 'ident'], w=[('ps', b)])
        o.TR(ps[b][0:32, 256:288], krf[64:96, :], ident[64:96, 64:96], r=['krf', 'ident'], w=[('ps', b)])
        o.CP('act', kvo[:, :], ps[b][0:32, 0:288], r=[('ps', b)], w=['kvo'])
        o.DMA('sp', O["kv_s"][:, :], kvo[:, :], r=['kvo'], out=True)
        o.CP('dve', KVbn[:, 0:288], kvo[:, :], r=['kvo'], w=['KVbn'])
        b3, b4 = C['rG'].next(), C['rG'].next()
        for cc in range(2):
            o.MM(ps[b3][0:32, 0:512], cTs[:, cc, :], C['wuk'][:, cc, 0:512], cc == 0, cc == 1, r=['cTs', 'wuk'], w=[('ps', b3)])
        for cc in range(2):
            o.MM(ps[b4][0:32, 0:256], cTs[:, cc, :], C['wuk'][:, cc, 512:768], cc == 0, cc == 1, r=['cTs', 'wuk'], w=[('ps', b4)])
        o.ACT(sqn[0:32, 0:512], ps[b3][0:32, 0:512], AF.Square, r=[('ps', b3)], w=['sqn'])
        o.ACT(sqn[0:32, 512:768], ps[b4][0:32, 0:256], AF.Square, r=[('ps', b4)], w=['sqn'])
        o.RED(ssn[0:32, 0, 0:12], sqn[0:32, :].rearrange("p (h d) -> p h d", d=64), r=['sqn'], w=['ssn'])
        o.ACT(sqr[0:32, 0:32], kvo[:, 256:288], AF.Square, r=['kvo'], w=['sqr'])
        o.RED(ssn[0:32, 0, 12:13], sqr[0:32, 0:32].rearrange("p (h d) -> p h d", d=32), r=['sqr'], w=['ssn2'])
        o.TSA('dve', ssn[0:32, 0, 0:12], ssn[0:32, 0, 0:12], ssn[0:32, 0, 12:13], r=['ssn', 'ssn2'], w=['ssn'])
        o.ACT(ssn[0:32, 0, 0:12], ssn[0:32, 0, 0:12], AF.Ln, r=['ssn'], w=['ssn'], scale=1.0, bias=96.0 * EPS)
        o.ACT(kinvn[:, :], ssn[0:32, 0, 0:12], AF.Exp, r=['ssn'], w=['kinvn'], scale=-0.5)

    def mix_b(l):
        jl = l - 2
        xnorm('gmix', l)
        for c in range(3):
            b = C['rA'].next()
            for k in range(8):
                o.MM(ps[b][:, 0:tb], win[:, k, c * 128:(c + 1) * 128], hT[:, k, :], k == 0, k == 7, r=['win', 'hT'], w=[('ps', b)])
            o.CP('act', cq[:, c, :], ps[b][:, 0:tb], r=[('ps', b)], w=['cq'])
        for hp in range(2):
            b = C['rA'].next()
            for k in range(8):
                o.MM(ps[b][:, 0:tb], win[:, k, 384 + hp * 128:384 + (hp + 1) * 128], hT[:, k, :], k == 0, k == 7,
                     r=['win', 'hT'], w=[('ps', b)])
            o.CP('act', qm[:, hp, :], ps[b][:, 0:tb], r=[('ps', b)], w=['qm'])
        norm_fm(o, C, [cq[:, c, :] for c in range(3)], [G['gql'][:, jl * 3 + c:jl * 3 + c + 1] for c in range(3)],
                [cqn[:, c, :] for c in range(3)], 128, 384, tb, ['cq'], 'cqn', ones[:])
        uq = wmisc[:, 0:3456].rearrange("p (k n) -> p k n", n=1152)
        for h in range(12):
            bq = C['rG'].next()
            for k in range(3):
                o.MM(ps[bq][0:96, 0:tb], uq[:, k, h * 96:(h + 1) * 96], cqn[:, k, :], k == 0, k == 2, r=['wmisc', 'cqn'], w=[('ps', bq)])
            norm_fm(o, C, [ps[bq][0:96, 0:tb]], [G['gq'][:, jl:jl + 1]], [qg[:, :]], 96, 96, tb, [('ps', bq)], 'qg', ones[0:96, 0:96])
            b2 = C['rG'].next()
            o.MM(ps[b2][0:96, 0:tb], perm[:, :], qg[:, :], True, True, r=['perm', 'qg'], w=[('ps', b2)])
            o.STT('dve', t1[:, :], qg[:, :], G['gk'][:, 0:1], cosb[:, :], ALU.mult, ALU.mult, r=['qg', 'cosb', 'gains'], w=['t1'])
            o.STT('dve', t2[:, :], ps[b2][0:96, 0:tb], G['gk'][:, 0:1], sinb[:, :], ALU.mult, ALU.mult,
                  r=[('ps', b2), 'sinb', 'gains'], w=['t2'])
            o.TT('pool', QTs[:, h, :], t1[:, :], t2[:, :], ALU.add, r=['t1', 't2'], w=['QTs'])
        for cc in range(2):
            b = C['rA'].next()
            for h in range(12):
                o.MM(ps[b][:, h * tb:(h + 1) * tb], wukT[0:64, h, cc * 128:(cc + 1) * 128], QTs[0:64, h, :], True, True,
                     r=['wukT', 'QTs'], w=[('ps', b)])
            o.CP('dve', QL[:, cc, :, :], ps[b][:, 0:12 * tb].rearrange("p (h t) -> p h t", t=tb), r=[('ps', b)], w=['QL'])
        BO, PSB, BSC = 4, 5, 3
        ngr = NPG // 4
        NG = NS * ngr
        psc = ps[PSB][:, :].bitcast(BF16)
        psr = ps[PSB + 1][:, :].bitcast(BF16)

        def slot(g, q):
            return (g % 4) * 4 + q

        def GATHER(g):
            for q in range(4):
                pg = g * 4 + q
                sl = slot(g, q)
                P.op('pool', lambda e, sl=sl, pg=pg: e.indirect_dma_start(
                    out=KVb[sl][:, 0:288], out_offset=None, in_=I["ckv"][:, :],
                    in_offset=bass.IndirectOffsetOnAxis(ap=idxall[:, pg:pg + 1], axis=0)),
                    r=['idx'], w=[('KVb', sl)], dma=True)

        def T(g):
            kt = g % 2
            for q in range(4):
                sl = slot(g, q)
                for cc in range(2):
                    o.TR(psc[:, (q * 2 + cc) * 128:(q * 2 + cc + 1) * 128], KVb[sl][:, cc * 128:(cc + 1) * 128], identb[:, :],
                         r=[('KVb', sl), 'identb'], w=[('ps', PSB)])
                o.TR(psr[0:96, q * 128:(q + 1) * 128], KVb[sl][:, 192:288], identb[:, :], r=[('KVb', sl), 'identb'], w=[('ps', PSB + 1)])
            o.CP('dve', KVTc[kt][:, :, :, :], psc[:, 0:1024].rearrange("p (q c k) -> p q c k", c=2, k=128), r=[('ps', PSB)], w=[('KVTc', kt)])
            o.CP('act', KVTr[kt][64:96, :, :], psr[64:96, 0:512].rearrange("p (q k) -> p q k", k=128), r=[('ps', PSB + 1)], w=[('KVTr', kt)])

        def KN(g):
            kt = g % 2
            for q in range(4):
                sl = slot(g, q)
                b3, b4 = (7, 0) if q % 2 == 0 else (1, 2)
                sq_ = sqn2[q % 2]
                sk = ('sqn', q % 2)
                for cc in range(2):
                    o.MM(ps[b3][:, 0:512], KVTc[kt][:, q, cc, :], C['wuk'][:, cc, 0:512], cc == 0, cc == 1,
                         r=[('KVTc', kt), 'wuk'], w=[('ps', b3)])
                for cc in range(2):
                    o.MM(ps[b4][:, 0:256], KVTc[kt][:, q, cc, :], C['wuk'][:, cc, 512:768], cc == 0, cc == 1,
                         r=[('KVTc', kt), 'wuk'], w=[('ps', b4)])
                o.ACT(sq_[:, 0:512], ps[b3][:, 0:512], AF.Square, r=[('ps', b3)], w=[sk])
                o.ACT(sq_[:, 512:768], ps[b4][:, 0:256], AF.Square, r=[('ps', b4)], w=[sk])
                o.ACT(sq_[:, 768:800], KVb[sl][:, 256:288], AF.Square, r=[('KVb', sl)], w=[sk])
                o.RED(ssn[:, q, 0:12], sq_[:, 0:768].rearrange("p (h d) -> p h d", d=64), r=[sk], w=['ssn'])
                o.RED(ssn[:, q, 12:13], sq_[:, 768:800].rearrange("p (h d) -> p h d", d=32), r=[sk], w=['ssn'])
            o.TT('dve', ssn[:, :, 0:12], ssn[:, :, 0:12], ssn[:, :, 12:13].to_broadcast([128, 4, 12]), ALU.add, r=['ssn'], w=['ssn'])
            o.ACT(ssn[:, :, 0:12], ssn[:, :, 0:12], AF.Ln, r=['ssn'], w=['ssn'], scale=1.0, bias=96.0 * EPS)
            o.ACT(kinv[:, g * 4:g * 4 + 4, :], ssn[:, :, 0:12], AF.Exp, r=['ssn'], w=['kinv'], scale=-0.5)

        def SE(g):
            kt = g % 2
            s_ = g // ngr
            qcols = slice(s_ * 8, (s_ + 1) * 8)
            for q in range(4):
                oc = ps[BSC][:, q * 96:(q + 1) * 96].rearrange("p (h i) -> p h i", i=8)
                o.MM(oc, KVTc[kt][:, q, 0, :], QL[:, 0, :, qcols], True, False, r=[('KVTc', kt), 'QL'], w=[('ps', BSC)])
                o.MM(oc, KVTc[kt][:, q, 1, :], QL[:, 1, :, qcols], False, False, r=[('KVTc', kt), 'QL'], w=[('ps', BSC)])
                o.MM(oc, KVTr[kt][64:96, q, :], QTs[64:96, :, qcols], False, True, r=[('KVTr', kt), 'QTs'], w=[('ps', BSC)])
            o.TT('dve', tms[:, :, :].rearrange("p q (h i) -> p q h i", i=8),
                 ps[BSC][:, 0:384].rearrange("p (q h i) -> p q h i", h=12, i=8),
                 kinv[:, g * 4:g * 4 + 4, :].unsqueeze(3).to_broadcast([128, 4, 12, 8]), ALU.mult, r=[('ps', BSC), 'kinv'], w=['tms'])
            o.ACT(PTs[kt][:, :, :], tms[:, :, :], AF.Exp, r=['tms'], w=[('PTs', kt)])

        def PV(g):
            kt = g % 2
            gi = g % ngr
            for q in range(4):
                sl = slot(g, q)
                o.MM(ps[BO][0:96, 0:289], PTs[kt][:, q, :], KVb[sl][:, 0:289], gi == 0 and q == 0, False,
                     r=[('PTs', kt), ('KVb', sl)], w=[('ps', BO)])

        def EPI(s_):
            qcols = slice(s_ * 8, (s_ + 1) * 8)
            oc = ps[BSC][0:32, 0:96].rearrange("p (h i) -> p h i", i=8)
            o.MM(oc, cTs[:, 0, :], QL[:, 0, :, qcols], True, False, r=['cTs', 'QL'], w=[('ps', BSC)])
            o.MM(oc, cTs[:, 1, :], QL[:, 1, :, qcols], False, False, r=['cTs', 'QL'], w=[('ps', BSC)])
            o.MM(oc, krb[64:96, :], QTs[64:96, :, qcols], False, True, r=['krb', 'QTs'], w=[('ps', BSC)])
            o.TT('dve', tms[0:32, 0, :].rearrange("p (h i) -> p h i", i=8), ps[BSC][0:32, 0:96].rearrange("p (h i) -> p h i", i=8),
                 kinvn[:, :].unsqueeze(2).to_broadcast([32, 12, 8]), ALU.mult, r=[('ps', BSC), 'kinvn'], w=['tms'])
            o.ACT(tms[0:32, 1, :], tms[0:32, 0, :], AF.Exp, r=['tms'], w=['tms'])
            o.TT('dve', PTn[:, :], tms[0:32, 1, :], smask[:, s_, :], ALU.mult, r=['tms', 'smask'], w=['PTn'])
            o.MM(ps[BO][0:96, 0:289], PTn[:, :], KVbn[:, 0:289], NPG == 0, True, r=['PTn', 'KVbn'], w=[('ps', BO)])
            o.RCP(rcs[:, :], ps[BO][0:96, 288:289], r=[('ps', BO)], w=['rcs'])
            o.TSM('dve', olat[:, :], ps[BO][0:96, 0:256], rcs[:, 0:1], r=[('ps', BO), 'rcs'], w=['olat'])
            b = PSB
            for cc in range(2):
                o.TR(ps[b][:, cc * 96:(cc + 1) * 96], olat[:, cc * 128:(cc + 1) * 128], ident[0:96, 0:96], r=['olat', 'ident'], w=[('ps', b)])
            o.CP('dve', olT[:, :, :], ps[b][:, 0:192].rearrange("p (c n) -> p c n", n=96), r=[('ps', b)], w=['olT'])
            b = PSB + 1
            for h in range(12):
                for cc in range(2):
                    o.MM(ps[b][:, h * 8:(h + 1) * 8], C['wuv'][:, cc, (h // 2) * 128:(h // 2 + 1) * 128], olT[:, cc, h * 8:(h + 1) * 8],
                         cc == 0, cc == 1, r=['wuv', 'olT'], w=[('ps', b)])
            for par in range(2):
                vr = slice(par * 64, par * 64 + 64)
                src = ps[b][vr, 0:96].rearrange("p (hp two i) -> p hp two i", two=2, i=8)[:, :, par, :]
                o.CP('dve', cat[vr, 0:6, qcols], src, r=[('ps', b)], w=['cat'])

        for g in range(min(3, NG)):
            GATHER(g)
        T(0)
        if l == 2:
            KN(0)
        if NG > 1:
            T(1)
        for g in range(NG):
            if g + 3 < NG:
                GATHER(g + 3)
            SE(g)
            if l == 2 and g + 1 < NG:
                KN(g + 1)
            if g + 2 < NG:
                T(g + 2)
            PV(g)
            if g % ngr == ngr - 1:
                EPI(g // ngr)
        mem_attend(l, 6)
        out_proj(l)

    o.DMA('sp', xin[:], I["xs"][:, :], w=['xin'])
    for c in range(8):
        b = C['rA'].next()
        o.TR(ps[b][:, 0:32], xin[:, c * 128:(c + 1) * 128], ident[0:32, 0:32], r=['xin', 'ident'], w=[('ps', b)])
        o.CP('dve', xT[:, c, :], ps[b][:, 0:32], r=[('ps', b)], w=[('xT', c)])
    load_mix_weights(0)
    for l in range(nlayers):
        mem_load(l)
        pre = [load_ffn_group(l, fg) for fg in range(min(NRING, NFG))]
        if l < 2:
            mix_a(l)
        else:
            mix_b(l)
        ffn(l, pre, hook=(lambda l=l: load_mix_weights(l + 1)) if l < nlayers - 1 else None)
        if l == 1:
            kv_stage()
    for c in range(8):
        b = C['rA'].next()
        o.TR(ps[b][0:32, 0:128], xT[:, c, :], ident[:, :], r=[('xT', c), 'ident'], w=[('ps', b)])
        o.CP('dve', xin[:, c * 128:(c + 1) * 128], ps[b][0:32, 0:128], r=[('ps', b)], w=['xin'])
    o.DMA('sp', O["y_s"][:, :], xin[:], r=['xin'], out=True)
    P.finish()
    P.emit('s')
    print("sample program: ops", len(P.ops), "waits", P.nwaits)


def make_in_maps(inputs, cores, NT=2048):
    consts = host_consts()
    maps = []
    ckv = np.ascontiguousarray(inputs["cache_kv"]).reshape(-1, 288)
    shared = {k: np.ascontiguousarray(inputs[k]) for k in
              ("norm_mix", "norm_ffn", "w_out", "w_ffn_in", "w_ffn_out", "norm_mem", "w_mem_kv", "g_mem_q", "g_mem_k",
               "w_in_a", "w_pool_grp", "pool_scale", "w_in_b", "g_q_lora", "w_uq", "g_q", "norm_kv", "w_dkv",
               "g_kv_lora", "w_uk", "w_uv", "g_k_nope", "g_k_rope")}
    for c in cores:
        m = dict(shared)
        m.update(consts)
        m["xp"] = np.ascontiguousarray(inputs["x_prompt"][c][:NT])
        m["xs"] = np.ascontiguousarray(inputs["x_sample"][4 * c:4 * c + 4]).reshape(32, D)
        m["ckv"] = ckv
        m["pt"] = np.ascontiguousarray(inputs["page_table"][4 * c:4 * c + 4]).astype(np.int32)
        m["spool"] = np.ascontiguousarray(inputs["state_pool"][:, 4 * c:4 * c + 4])
        m["cmk"] = np.ascontiguousarray(inputs["cache_mem_k"][:, 4 * c:4 * c + 4]).reshape(4, 4, 256, 256)
        m["cmv"] = np.ascontiguousarray(inputs["cache_mem_v"][:, 4 * c:4 * c + 4]).reshape(4, 4, 256, 256)
        m["memp"] = np.ascontiguousarray(inputs["mem_prompt"][c])
        maps.append(m)
    return maps


def kernel(**inputs):
    nc = build()
    maps = make_in_maps(inputs, list(range(NCORES)))
    res = run_bass_kernel_spmd(nc, maps, core_ids=list(range(NCORES)))
    R = res.results
    y_p = np.stack([R[c]["y_p"] for c in range(NCORES)])
    y_s = np.concatenate([R[c]["y_s"].reshape(4, 8, D) for c in range(NCORES)])
    kv_p = np.stack([R[c]["kv_p"] for c in range(NCORES)])
    kv_s = np.concatenate([R[c]["kv_s"].reshape(4, 8, 288) for c in range(NCORES)])
    pool_p = np.stack([R[c]["pool_p"] for c in range(NCORES)], axis=1)
    pool_s = np.concatenate([R[c]["pool_s"] for c in range(NCORES)], axis=1)
    mk = np.stack([R[c]["mk_p"] for c in range(NCORES)], axis=1).reshape(4, 8, 256, 4, 64)
    mv = np.stack([R[c]["mv_p"] for c in range(NCORES)], axis=1).reshape(4, 8, 256, 4, 64)
    return (y_p, y_s, kv_p, kv_s, pool_p, pool_s, mk, mv)
```

```python
import numpy as np
import concourse.bass as bass
import concourse.mybir as mybir
from concourse.bass_utils import run_bass_kernel_spmd
from contextlib import ExitStack

F32 = mybir.dt.float32
BF16 = mybir.dt.bfloat16
I32 = mybir.dt.int32
AF = mybir.ActivationFunctionType
ALU = mybir.AluOpType
AX = mybir.AxisListType

D = 1024
DFF = 2816
EPS = 1e-6
TB = 512
NCORES = 8


class Prog:
    ENGS = ('pe', 'act', 'dve', 'pool', 'sp')
    NDMA = 8

    def __init__(self, nc, es, pfx=''):
        self.nc, self.es = nc, es
        self.pfx = pfx
        self.ops = []
        self.lastw = {}
        self.readers = {}
        self.out_dmas = []

    def sb(self, name, shape, dt):
        return self.es.enter_context(self.nc.sbuf_tensor(self.pfx + name, shape, dt))

    def ps(self, name, shape, dt):
        return self.es.enter_context(self.nc.psum_tensor(self.pfx + name, shape, dt))

    def op(self, eng, fn, r=(), w=(), dma=False, out=False):
        i = len(self.ops)
        deps = set()
        pr = [k for k in r if isinstance(k, tuple) and k[0] == 'ps']
        if pr:
            w = list(w) + [k for k in pr if k not in w]
        for k in r:
            lw = self.lastw.get(k)
            if lw is not None:
                deps.add(lw)
        for k in w:
            lw = self.lastw.get(k)
            if lw is not None:
                deps.add(lw)
            for rd in self.readers.get(k, ()):
                deps.add(rd)
        keep = []
        for d in deps:
            de, ddma = self.ops[d][0], self.ops[d][3]
            if de == eng and not dma and not ddma and eng == 'pe':
                continue
            keep.append(d)
        for k in r:
            self.readers.setdefault(k, []).append(i)
        for k in w:
            self.lastw[k] = i
            self.readers[k] = []
        self.ops.append([eng, fn, keep, dma])
        if out:
            self.out_dmas.append(i)
        return i

    def finish(self):
        import os
        stop = int(os.environ.get("K_STOP", "0"))
        if stop:
            self.ops = self.ops[:stop]
            self.out_dmas = [i for i in range(len(self.ops)) if self.ops[i][3]]
        self.ops.append(['sp', None, list(self.out_dmas), False])

    def emit(self, tag=''):
        nc, es = self.nc, self.es
        ops = self.ops
        n = len(ops)
        needed = [False] * n
        for o in ops:
            for d in o[2]:
                needed[d] = True
        csem = {e: es.enter_context(nc.semaphore(tag + 'c_' + e)) for e in self.ENGS}
        ccnt = {e: 0 for e in self.ENGS}
        dsem = {e: [es.enter_context(nc.semaphore('%sd_%s%d' % (tag, e, j))) for j in range(self.NDMA)]
                for e in ('sp', 'pool', 'act')}
        dcnt = {e: [0] * self.NDMA for e in dsem}
        dlast = {e: [None] * self.NDMA for e in dsem}
        drr = {e: 0 for e in dsem}
        sig = [None] * n
        extra = [None] * n
        streams = {e: [] for e in self.ENGS}
        for i, o in enumerate(ops):
            eng, fn, deps, dma = o
            streams[eng].append(i)
            if dma:
                j = drr[eng]
                drr[eng] = (j + 1) % self.NDMA
                dcnt[eng][j] += 1
                sig[i] = (dsem[eng][j], 16 * dcnt[eng][j])
                extra[i] = dlast[eng][j]
                dlast[eng][j] = i
            elif needed[i]:
                ccnt[eng] += 1
                sig[i] = (csem[eng], ccnt[eng])
        self.nwaits = 0

        def body(e, eng):
            known = {}
            for i in streams[eng]:
                _, fn, deps, dma = ops[i]
                req = {}
                dl = list(deps)
                if extra[i] is not None:
                    dl.append(extra[i])
                for d in dl:
                    s, v = sig[d]
                    if req.get(id(s), (None, 0))[1] < v:
                        req[id(s)] = (s, v)
                for s, v in req.values():
                    if known.get(id(s), 0) < v:
                        e.wait_ge(s, v)
                        known[id(s)] = v
                        self.nwaits += 1
                if fn is None:
                    continue
                ins = fn(e)
                if sig[i] is not None:
                    ins.then_inc(sig[i][0], 16 if dma else 1)

        with nc.Block() as block:
            @block.tensor
            def _(e):
                body(e, 'pe')

            @block.scalar
            def _(e):
                body(e, 'act')

            @block.vector
            def _(e):
                body(e, 'dve')

            @block.gpsimd
            def _(e):
                body(e, 'pool')

            @block.sync
            def _(e):
                body(e, 'sp')


class Rot:
    def __init__(self, items):
        self.items, self.i = items, 0

    def next(self):
        x = self.items[self.i % len(self.items)]
        self.i += 1
        return x


class Ops:
    def __init__(self, P):
        self.P = P

    def MM(self, o, l, rh, st, sp, r, w):
        self.P.op('pe', lambda e: e.matmul(o, lhsT=l, rhs=rh, start=st, stop=sp), r=r, w=w)

    def TR(self, o, i, idn, r, w):
        self.P.op('pe', lambda e: e.transpose(out=o, in_=i, identity=idn), r=r, w=w)

    def ACT(self, o, i, f, r, w, **kw):
        self.P.op('act', lambda e: e.activation(out=o, in_=i, func=f, **kw), r=r, w=w)

    def TT(self, eng, o, a, b, op, r, w):
        self.P.op(eng, lambda e: e.tensor_tensor(out=o, in0=a, in1=b, op=op), r=r, w=w)

    def STT(self, eng, o, a, s, b, op0, op1, r, w):
        self.P.op(eng, lambda e: e.scalar_tensor_tensor(out=o, in0=a, scalar=s, in1=b, op0=op0, op1=op1), r=r, w=w)

    def TS(self, eng, o, a, s1, s2, op0, op1, r, w):
        self.P.op(eng, lambda e: e.tensor_scalar(out=o, in0=a, scalar1=s1, scalar2=s2, op0=op0, op1=op1), r=r, w=w)

    def CP(self, eng, o, i, r, w):
        if eng == 'act':
            self.P.op('act', lambda e: e.activation(out=o, in_=i, func=AF.Copy), r=r, w=w)
        else:
            self.P.op(eng, lambda e: e.tensor_copy(out=o, in_=i), r=r, w=w)

    def TSA(self, eng, o, a, s1, r, w):
        self.P.op(eng, lambda e: e.tensor_scalar_add(out=o, in0=a, scalar1=s1), r=r, w=w)

    def TSM(self, eng, o, a, s1, r, w):
        self.P.op(eng, lambda e: e.tensor_scalar_mul(out=o, in0=a, scalar1=s1), r=r, w=w)

    def RCP(self, o, i, r, w):
        self.P.op('dve', lambda e: e.reciprocal(out=o, in_=i), r=r, w=w)

    def MSET(self, eng, o, v, w):
        self.P.op(eng, lambda e: e.memset(o, v), w=w)

    def RED(self, o, i, r, w):
        self.P.op('dve', lambda e: e.tensor_reduce(out=o, in_=i, axis=AX.X, op=ALU.add), r=r, w=w)

    def DMA(self, eng, o, i, r=(), w=(), out=False, slow=False):
        if slow:
            self.P.op(eng, lambda e: e.dma_start(out=o, in_=i, allow_slow_non_contiguous=True), r=r, w=w, dma=True, out=out)
        else:
            self.P.op(eng, lambda e: e.dma_start(out=o, in_=i), r=r, w=w, dma=True, out=out)


CKV_ROWS = [5120 * 128]
PT_COLS = [128]


def in_specs(NT):
    return [
        ("xp", [NT, D], F32), ("xs", [32, D], F32), ("ckv", [CKV_ROWS[0], 288], F32), ("pt", [4, PT_COLS[0]], I32),
        ("spool", [2, 4, 15, 768], F32), ("cmk", [4, 4, 256, 256], F32), ("cmv", [4, 4, 256, 256], F32),
        ("memp", [256, D], F32),
        ("norm_mix", [4, D], F32), ("norm_ffn", [4, D], F32), ("w_out", [4, D, D], F32),
        ("w_ffn_in", [4, D, 2 * DFF], F32), ("w_ffn_out", [4, DFF, D], F32), ("norm_mem", [4, D], F32),
        ("w_mem_kv", [4, D, 512], F32), ("g_mem_q", [4, 64], F32), ("g_mem_k", [4, 64], F32),
        ("w_in_a", [2, D, D], F32), ("w_pool_grp", [2, 4, 192, 192], F32), ("pool_scale", [2, 768], F32),
        ("w_in_b", [2, D, 640], F32), ("g_q_lora", [2, 384], F32), ("w_uq", [2, 384, 1152], F32), ("g_q", [2, 96], F32),
        ("norm_kv", [D], F32), ("w_dkv", [D, 288], F32), ("g_kv_lora", [256], F32), ("w_uk", [256, 768], F32),
        ("w_uv", [256, 768], F32), ("g_k_nope", [64], F32), ("g_k_rope", [16], F32),
        ("c_ident", [128, 128], F32), ("c_perm", [96, 96], F32), ("c_cos", [96, 2048 + 32], F32),
        ("c_sin", [96, 2048 + 32], F32), ("c_tri", [128, 128], F32), ("c_rcnt", [96, 8, 16], F32),
        ("c_smask", [32, 4, 96], F32),
    ]


def out_specs(NT):
    return [
        ("y_p", [NT, D]), ("y_s", [32, D]), ("kv_p", [NT, 288]), ("kv_s", [32, 288]),
        ("pool_p", [2, 15, 768]), ("pool_s", [2, 4, 15, 768]), ("mk_p", [4, 256, 256]), ("mv_p", [4, 256, 256]),
    ]


def host_consts():
    c = {}
    c["c_ident"] = np.eye(128, dtype=np.float32)
    perm = np.zeros((96, 96), np.float32)
    for i in range(16):
        perm[80 + i, 64 + i] = -1.0
        perm[64 + i, 80 + i] = 1.0
    c["c_perm"] = perm
    half = 16
    inv_freq = (np.float32(10000.0) ** (-np.arange(half, dtype=np.float32) / np.float32(half))).astype(np.float32)
    pos = np.concatenate([np.arange(2048), np.tile(16384 + np.arange(8), 4)]).astype(np.float32)
    ang = (pos[None, :] * inv_freq[:, None]).astype(np.float32)
    cos = np.ones((96, 2080), np.float32)
    sin = np.zeros((96, 2080), np.float32)
    cos[64:80] = np.cos(ang)
    cos[80:96] = np.cos(ang)
    sin[64:80] = np.sin(ang)
    sin[80:96] = np.sin(ang)
    c["c_cos"], c["c_sin"] = cos, sin
    kk = np.arange(128)
    c["c_tri"] = (kk[None, :] >= kk[:, None]).astype(np.float32)
    rc = np.zeros((96, 8, 16), np.float32)
    for ch in range(8):
        w = 2 ** (ch // 2 + 1)
        rc[:, ch, :] = (1.0 / np.minimum(w, np.arange(16) + 1)).astype(np.float32)[None, :]
    c["c_rcnt"] = rc
    sm = np.zeros((32, 4, 12, 8), np.float32)
    for k in range(32):
        s = k // 8
        for q in range(8):
            if k % 8 <= q:
                sm[k, s, :, q] = 1.0
    c["c_smask"] = sm.reshape(32, 4, 96)
    return c


def build(NT=2048, SEG=512, do_prompt=True, do_sample=True, nlayers=4, NPG=128):
    nc = bass.Bass("TRN2", target_bir_lowering=False)
    I = {n: nc.dram_tensor(n, s, dt, kind="ExternalInput").ap() for n, s, dt in in_specs(NT)}
    O = {n: nc.dram_tensor(n, s, F32, kind="ExternalOutput").ap() for n, s in out_specs(NT)}
    SC = None
    if do_prompt and do_sample:
        SC = {n: nc.dram_tensor("sc_" + n, shp, BF16).ap() for n, shp in (
            ("w_ffn_in", [4, D, 2 * DFF]), ("w_ffn_out", [4, DFF, D]), ("w_in_a", [2, D, D]), ("w_in_b", [2, D, 640]),
            ("w_out", [4, D, D]), ("w_pool_grp", [2, 4, 192, 192]), ("w_uq", [2, 384, 1152]))}
    if do_prompt:
        with ExitStack() as es:
            prompt_program(nc, es, I, O, NT, SEG, nlayers, SC)
    if do_sample:
        with ExitStack() as es:
            sample_program(nc, es, I, O, nlayers, NPG, SC)
    return nc


def load_small(P, o, I, C):
    g1 = P.sb("gst1", [128, 128], F32)
    g2 = P.sb("gst2", [32, 128], F32)
    gT1 = P.sb("gT1", [128, 128], F32)
    gT2 = P.sb("gT2", [128, 32], F32)
    o.MSET('dve', g1[:], 0.0, w=['gst1'])
    o.MSET('dve', g2[:], 0.0, w=['gst2'])
    o.DMA('sp', g1[0:32, :], I["norm_mix"].rearrange("l (k p) -> (l k) p", p=128), w=['gst1'])
    o.DMA('sp', g1[32:64, :], I["norm_ffn"].rearrange("l (k p) -> (l k) p", p=128), w=['gst1'])
    o.DMA('sp', g1[64:96, :], I["norm_mem"].rearrange("l (k p) -> (l k) p", p=128), w=['gst1'])
    o.DMA('sp', g1[96:104, :], I["norm_kv"].rearrange("(k p) -> k p", p=128), w=['gst1'])
    o.DMA('sp', g1[104:106, :], I["g_kv_lora"].rearrange("(k p) -> k p", p=128), w=['gst1'])
    o.DMA('sp', g1[106:112, :], I["g_q_lora"].rearrange("l (k p) -> (l k) p", p=128), w=['gst1'])
    o.DMA('sp', g2[0:16, 0:96], I["pool_scale"].rearrange("l (k p) -> (l k) p", p=96), w=['gst2'])
    o.DMA('sp', g2[16:20, 0:64], I["g_mem_q"][:, :], w=['gst2'])
    o.DMA('sp', g2[16:20, 64:128], I["g_mem_q"][:, :], w=['gst2'])
    o.DMA('sp', g2[20:24, 0:64], I["g_mem_k"][:, :], w=['gst2'])
    o.DMA('sp', g2[20:24, 64:128], I["g_mem_k"][:, :], w=['gst2'])
    o.DMA('sp', g2[24:26, 0:96], I["g_q"][:, :], w=['gst2'])
    o.DMA('sp', g2[26:27, 0:64], I["g_k_nope"].rearrange("(o p) -> o p", o=1), w=['gst2'])
    o.DMA('sp', g2[26:27, 64:80], I["g_k_rope"].rearrange("(o p) -> o p", o=1), w=['gst2'])
    o.DMA('sp', g2[26:27, 80:96], I["g_k_rope"].rearrange("(o p) -> o p", o=1), w=['gst2'])
    ps = C['ps']
    o.TR(ps[0][:, 0:128], g1[:, :], C['ident'][:, :], r=['gst1', 'ident'], w=[('ps', 0)])
    o.CP('dve', gT1[:, :], ps[0][:, 0:128], r=[('ps', 0)], w=['gains'])
    o.TR(ps[1][:, 0:32], g2[:, :], C['ident'][0:32, 0:32], r=['gst2', 'ident'], w=[('ps', 1)])
    o.CP('dve', gT2[:, :], ps[1][:, 0:32], r=[('ps', 1)], w=['gains'])
    G = {'gmix': gT1[:, 0:32], 'gffn': gT1[:, 32:64], 'gmem': gT1[:, 64:96], 'gkv': gT1[:, 96:104], 'gkvl': gT1[:, 104:106],
         'gql': gT1[:, 106:112], 'pscale': gT2[0:96, 0:16], 'gmq': gT2[:, 16:20], 'gmk': gT2[:, 20:24], 'gq': gT2[0:96, 24:26],
         'gk': gT2[0:96, 26:27]}
    return G


def alloc_common(P, o, I):
    C = {}
    C['ident'] = P.sb("ident", [128, 128], F32)
    C['ones'] = P.sb("ones", [128, 128], BF16)
    C['blk2'] = P.sb("blk2", [128, 128], BF16)
    C['perm'] = P.sb("perm", [96, 96], BF16)
    C['tri'] = P.sb("tri", [128, 128], BF16)
    C['rcnt'] = P.sb("rcnt", [96, 8, 16], F32)
    o.DMA('sp', C['ident'][:], I["c_ident"][:, :], w=['ident'])
    o.DMA('pool', C['perm'][:], I["c_perm"][:, :], w=['perm'])
    o.DMA('pool', C['tri'][:], I["c_tri"][:, :], w=['tri'])
    o.DMA('sp', C['rcnt'][:], I["c_rcnt"][:, :, :], w=['rcnt'])
    o.MSET('dve', C['ones'][:], 1.0, w=['ones'])
    o.MSET('dve', C['blk2'][:], 0.0, w=['blk2'])
    o.MSET('dve', C['blk2'][0:64, 0:64], 1.0, w=['blk2'])
    o.MSET('dve', C['blk2'][64:128, 64:128], 1.0, w=['blk2'])
    C['wuk'] = P.sb("wuk", [128, 2, 768], BF16)
    C['wuv'] = P.sb("wuv", [128, 2, 768], BF16)
    C['wdkv'] = P.sb("wdkv", [128, 8, 352], BF16)
    o.DMA('pool', C['wuk'][:], I["w_uk"].rearrange("(k p) n -> p k n", p=128), w=['wuk'])
    o.DMA('pool', C['wuv'][:], I["w_uv"].rearrange("(k p) n -> p k n", p=128), w=['wuv'])
    o.MSET('pool', C['wdkv'][:, :, 256:320], 0.0, w=['wdkv'])
    o.DMA('pool', C['wdkv'][:, :, 0:256], I["w_dkv"].rearrange("(k p) n -> p k n", p=128)[:, :, 0:256], w=['wdkv'])
    o.DMA('pool', C['wdkv'][:, :, 320:352], I["w_dkv"].rearrange("(k p) n -> p k n", p=128)[:, :, 256:288], w=['wdkv'])
    ps = [P.ps("ps%d" % i, [128, 512], F32) for i in range(8)]
    C['ps'] = ps
    C['G'] = load_small(P, o, I, C)
    C['rA'] = Rot([0, 1])
    C['rS'] = Rot([2, 3])
    C['rO'] = Rot([4, 5])
    C['rG'] = Rot([6, 7])
    C['tmp'] = P.sb("tmpn", [128, 512], F32)
    C['rstd'] = P.sb("rstd", [128, 512], F32)
    C['rtmp'] = Rot([0])
    return C


def norm_fm(o, C, xaps, gcols, haps, np_, Dn, tb, rkeys, hkey, ones, eps=EPS, sq=None, split=False):
    ps = C['ps']
    b = C['rA'].next()
    pk = ('ps', b)
    nch = len(xaps)
    sqaps, sqkey = (haps, hkey) if sq is None else sq
    if split and nch == 8:
        order = [0, 2, 4, 6, 1, 3, 5, 7]
        for c in order:
            rk_c = [rkeys[c]] if len(rkeys) == 8 else rkeys
            if c % 2 == 0:
                o.ACT(sqaps[c], xaps[c], AF.Square, r=rk_c, w=[(sqkey, 'sq', c)])
            else:
                o.TT('dve', sqaps[c], xaps[c], xaps[c], ALU.mult, r=rk_c, w=[(sqkey, 'sq', c)])
        for i, c in enumerate(order):
            o.MM(ps[b][0:np_, 0:tb], ones, sqaps[c], i == 0, i == nch - 1, r=[(sqkey, 'sq', c), 'ones', 'blk2'], w=[pk])
        sqdeps = [(sqkey, 'sq', c) for c in range(8)]
    else:
        for c in range(nch):
            o.ACT(sqaps[c], xaps[c], AF.Square, r=rkeys, w=[sqkey])
        for c in range(nch):
            o.MM(ps[b][0:np_, 0:tb], ones, sqaps[c], c == 0, c == nch - 1, r=[sqkey, 'ones', 'blk2'], w=[pk])
        sqdeps = [sqkey]
    o.ACT(C['tmp'][0:np_, 0:tb], ps[b][0:np_, 0:tb], AF.Ln, r=[pk], w=['tmpn'], scale=1.0 / Dn, bias=eps)
    o.ACT(C['rstd'][0:np_, 0:tb], C['tmp'][0:np_, 0:tb], AF.Exp, r=['tmpn'], w=['rstd'], scale=-0.5)
    for c in range(nch):
        o.STT('dve', haps[c], xaps[c], gcols[c], C['rstd'][0:np_, 0:tb], ALU.mult, ALU.mult,
              r=list(rkeys) + ['rstd', 'gains'] + sqdeps, w=[hkey])


def prompt_program(nc, es, I, O, NT, SEG, nlayers, SC=None):
    P = Prog(nc, es)
    o = Ops(P)
    C = alloc_common(P, o, I)
    ps = C['ps']
    G = C['G']
    ident, ones, blk2, perm, tri = C['ident'], C['ones'], C['blk2'], C['perm'], C['tri']
    SCM = {"k": nc.dram_tensor("sc_memKT", [4, 128, 512], BF16).ap(), "v": nc.dram_tensor("sc_Vaug", [4, 128, 1024], BF16).ap()}
    NB = SEG // TB
    NSEG = NT // SEG
    NKC = NT // 128
    tb = TB

    xT = P.sb("xT", [128, 8, SEG], F32)
    hT = P.sb("hT", [128, 8, SEG], BF16)
    win = P.sb("win", [128, 8, 1024], BF16)
    wout = P.sb("wout", [128, 10, 1024], BF16)
    wmisc = P.sb("wmisc", [128, 3456], BF16)
    NRG, NRO = 2, 3
    ringG = [P.sb("ringG%d" % i, [128, 4096], BF16) for i in range(NRG)]
    ringO = [P.sb("ringO%d" % i, [128, 2048], BF16) for i in range(NRO)]
    cT = P.sb("cT", [128, 2, NT], BF16)
    KT = [P.sb("KT%d" % i, [96, NT], BF16) for i in range(2)]
    Vh = [P.sb("Vh%d" % i, [128, NKC, 128], BF16) for i in range(2)]
    kinv = P.sb("kinv", [128, NKC, 12], F32)
    cat = P.sb("cat", [128, 10, tb], BF16)
    qm = P.sb("qm", [128, 2, tb], F32)
    qn = P.sb("qn", [128, 2, tb], BF16)
    PT = [P.sb("PT%d" % i, [128, tb], BF16) for i in range(4)]
    rPT = Rot([0, 1, 2, 3])
    rc = P.sb("rc", [128, tb], F32)
    cosb = P.sb("cosb", [96, tb], F32)
    sinb = P.sb("sinb", [96, tb], F32)
    memKT = P.sb("memKT", [128, 2, 256], BF16)
    Vaug = P.sb("Vaug", [128, 2, 4, 128], BF16)
    halo = P.sb("halo", [96, 2, 8, 16], F32)
    AFN, ABN = 7616, 4096
    arf = P.sb("arena_f", [128, AFN], F32)
    arb = P.sb("arena_b", [128, ABN], BF16)

    def cv(ar, off, parts, n, pat=None, **kw):
        v = ar[0:parts, off:off + n]
        return v.rearrange(pat, **kw) if pat else v
    xin = cv(arf, 0, 128, 1024)
    xinb = [xin, cv(arf, 4608, 128, 1024)]
    memT = cv(arf, 1024, 128, 2048, "p (c t) -> p c t", t=256)
    kf = cv(arf, 3072, 128, 512, "p (c t) -> p c t", t=256)
    kout = cv(arf, 3584, 128, 512, "p (c t) -> p c t", t=256)
    vout = cv(arf, 4096, 128, 512, "p (c t) -> p c t", t=256)
    hmT = cv(arb, 0, 128, 2048, "p (c t) -> p c t", t=256)
    E_ = 16 + tb
    U = cv(arf, 0, 96, 8 * E_, "p (c t) -> p c t", t=E_)
    tA = cv(arf, 8 * E_, 96, 2 * E_, "p (c t) -> p c t", t=E_)
    tB_ = cv(arf, 10 * E_, 96, 2 * E_, "p (c t) -> p c t", t=E_)
    t1a = cv(arf, 12 * E_, 96, 512)
    pso = cv(arf, 12 * E_ + 512, 16, 768)
    dif = cv(arb, 0, 96, 4096, "p (c t) -> p c t", t=tb)
    cq = cv(arf, 0, 128, 1536, "p (c t) -> p c t", t=tb)
    ckf = cv(arf, 0, 128, 1024, "p (c t) -> p c t", t=tb)
    krf = cv(arf, 1024, 96, 512)
    t1 = cv(arf, 1536, 96, 512)
    t2 = cv(arf, 2048, 96, 512)
    kvo = cv(arf, 2560, 128, 288)
    ssn = cv(arf, 2848, 128, 16)
    cqn = cv(arb, 0, 128, 1536, "p (c t) -> p c t", t=tb)
    QT = [cv(arb, 1536, 96, 512), cv(arb, 2048, 96, 512)]
    qg = cv(arb, 2560, 96, 512)
    krb = cv(arb, 1536, 96, 512)
    sqn = cv(arb, 2048, 128, 768)
    sqr = cv(arb, 2816, 128, 32)
    sg = [cv(arf, 0, 128, 512), cv(arf, 512, 128, 512)]
    stg = [cv(arf, 1024, 128, 512), cv(arf, 1536, 128, 512)]
    aT = [cv(arb, 0, 128, 1024, "p (c t) -> p c t", t=tb), cv(arb, 1024, 128, 1024, "p (c t) -> p c t", t=tb)]
    wmem = ringG[0]
    ARK = ['xin', 'xin1', 'memT', 'kf', 'kout', 'vout', 'hmT', 'U', 'tA', 'tB', 't1', 't2', 'pso', 'dif', 'cq', 'ckf', 'krf', 'kvo',
           'ssn', 'ssn2', 'cqn', ('QT', 0), ('QT', 1), 'qg', 'krb', 'sqn', 'sqr', ('sg', 0), ('sg', 1), ('aT', 0), ('aT', 1), ('stg', 0), ('stg', 1)]
    dummy = P.sb("dummyb", [128, 8], F32)

    def barrier():
        P.op('pool', lambda e: e.memset(dummy[:], 0.0), r=ARK, w=ARK)

    o.MSET('pool', Vaug[:], 1.0, w=['Vaug'])
    for i in range(2):
        o.MSET('pool', Vh[i][:], 1.0, w=[('Vh', i)])

    def wsrc(first):
        return (I, 'pool') if (SC is None or first) else (SC, 'sp')

    def wb(first, dst, src_sb, rkey, wkey):
        if SC is not None and first:
            o.DMA('sp', dst, src_sb, r=[rkey], w=[wkey], out=True)
    NFG = DFF // 256

    def XK(j):
        return [('xT', j, c) for c in range(8)]

    ring_state = {'g': 0, 'o': 0}

    def load_G(l, fg, first):
        src, q = wsrc(first)
        s_ = ring_state['g'] % NRG
        ring_state['g'] += 1
        rg = ringG[s_]
        k = ('rg', s_)
        gv = rg[:, 0:2048].rearrange("p (k n) -> p k n", n=256)
        uv = rg[:, 2048:4096].rearrange("p (k n) -> p k n", n=256)
        wi = src["w_ffn_in"][l].rearrange("(k p) n -> p k n", p=128)
        rk = [] if src is I else [('scw', 'g', l, fg)]
        o.DMA(q, gv, wi[:, :, fg * 256:(fg + 1) * 256], r=rk, w=[k])
        o.DMA(q, uv, wi[:, :, DFF + fg * 256:DFF + (fg + 1) * 256], r=rk, w=[k])
        if SC is not None and first:
            so = SC["w_ffn_in"][l].rearrange("(k p) n -> p k n", p=128)
            wb(first, so[:, :, fg * 256:(fg + 1) * 256], gv, k, ('scw', 'g', l, fg))
            wb(first, so[:, :, DFF + fg * 256:DFF + (fg + 1) * 256], uv, k, ('scw', 'g', l, fg))
        return gv, uv, k

    def load_O(l, fg, first):
        src, q = wsrc(first)
        s_ = ring_state['o'] % NRO
        ring_state['o'] += 1
        k = ('ro', s_)
        ov = ringO[s_][:, 0:2048].rearrange("p (k n) -> p k n", n=1024)
        wo = src["w_ffn_out"][l].rearrange("(k p) n -> p k n", p=128)
        rk = [] if src is I else [('scw', 'o', l, fg)]
        o.DMA(q, ov, wo[:, fg * 2:fg * 2 + 2, :], r=rk, w=[k])
        if SC is not None and first:
            so = SC["w_ffn_out"][l].rearrange("(k p) n -> p k n", p=128)
            wb(first, so[:, fg * 2:fg * 2 + 2, :], ov, k, ('scw', 'o', l, fg))
        return ov, k

    def load_mix_weights(l, first):
        W, WQ = wsrc(first)
        rk = [] if W is I else [('scw', 'mix', l)]
        wk = ('scw', 'mix', l)
        if l < 2:
            o.DMA(WQ, win[:], W["w_in_a"][l].rearrange("(k p) n -> p k n", p=128), r=rk, w=['win'])
            o.DMA(WQ, wout[0:96, 0:8, :], W["w_out"][l][0:768, :].rearrange("(k p) n -> p k n", p=96), r=rk, w=['wout'])
            o.DMA(WQ, wout[:, 8:10, :], W["w_out"][l][768:1024, :].rearrange("(k p) n -> p k n", p=128), r=rk, w=['wout'])
            gv = wmisc[0:96, 0:1536].rearrange("p (k n) -> p k n", n=192)
            o.DMA(WQ, gv, W["w_pool_grp"][l].rearrange("g (c p) e -> p (g c) e", p=96), r=rk, w=['wmisc'])
            if SC is not None and first:
                wb(first, SC["w_in_a"][l].rearrange("(k p) n -> p k n", p=128), win[:], 'win', wk)
                wb(first, SC["w_out"][l][0:768, :].rearrange("(k p) n -> p k n", p=96), wout[0:96, 0:8, :], 'wout', wk)
                wb(first, SC["w_out"][l][768:1024, :].rearrange("(k p) n -> p k n", p=128), wout[:, 8:10, :], 'wout', wk)
                wb(first, SC["w_pool_grp"][l].rearrange("g (c p) e -> p (g c) e", p=96), gv, 'wmisc', wk)
        else:
            j = l - 2
            o.DMA(WQ, win[:, :, 0:640], W["w_in_b"][j].rearrange("(k p) n -> p k n", p=128), r=rk, w=['win'])
            o.DMA(WQ, wout[:, 0:8, :], W["w_out"][l].rearrange("(k p) n -> p k n", p=128), r=rk, w=['wout'])
            uq = wmisc[:, 0:3456].rearrange("p (k n) -> p k n", n=1152)
            o.DMA(WQ, uq, W["w_uq"][j].rearrange("(k p) n -> p k n", p=128), r=rk, w=['wmisc'])
            if SC is not None and first:
                wb(first, SC["w_in_b"][j].rearrange("(k p) n -> p k n", p=128), win[:, :, 0:640], 'win', wk)
                wb(first, SC["w_out"][l].rearrange("(k p) n -> p k n", p=128), wout[:, 0:8, :], 'wout', wk)
                wb(first, SC["w_uq"][j].rearrange("(k p) n -> p k n", p=128), uq, 'wmisc', wk)

    def mem_kv(l, write_out):
        if not write_out:
            o.DMA('sp', memKT[:], SCM["k"][l].rearrange("p (c m) -> p c m", m=256), r=[('scm', l)], w=['memKT'])
            o.DMA('sp', Vaug[:], SCM["v"][l].rearrange("p (c h d) -> p c h d", h=4, d=128), r=[('scm', l)], w=['Vaug'])
            return
        barrier()
        o.DMA('pool', wmem[:, 0:4096].rearrange("p (k n) -> p k n", n=512),
              I["w_mem_kv"][l].rearrange("(k p) n -> p k n", p=128), w=[('rg', 0)])
        wm = wmem[:, 0:4096].rearrange("p (k n) -> p k n", n=512)
        for mt in range(2):
            o.DMA('sp', xin[:], I["memp"][mt * 128:(mt + 1) * 128, :], w=['xin'])
            for c in range(8):
                b = C['rA'].next()
                o.TR(ps[b][:, 0:128], xin[:, c * 128:(c + 1) * 128], ident[:], r=['xin', 'ident'], w=[('ps', b)])
                o.CP('dve' if c % 2 else 'act', memT[:, c, mt * 128:(mt + 1) * 128], ps[b][:, 0:128], r=[('ps', b)], w=['memT'])
        norm_fm(o, C, [memT[:, c, :] for c in range(8)], [G['gmem'][:, l * 8 + c:l * 8 + c + 1] for c in range(8)],
                [hmT[:, c, :] for c in range(8)], 128, D, 256, ['memT'], 'hmT', ones[:])
        for hp in range(2):
            b = C['rA'].next()
            for k in range(8):
                o.MM(ps[b][:, 0:256], wm[:, k, hp * 128:(hp + 1) * 128], hmT[:, k, :], k == 0, k == 7,
                     r=['hmT', ('rg', 0)], w=[('ps', b)])
            norm_fm(o, C, [ps[b][:, 0:256]], [G['gmk'][:, l:l + 1]], [kf[:, hp, :]], 128, 64, 256, [('ps', b)], 'kf', blk2[:],
                    sq=([qn[:, 0, 0:256]], 'qn'))
        o.CP('dve', memKT[:], kf[:], r=['kf'], w=['memKT'])
        for mc in range(2):
            b = C['rA'].next()
            for k in range(8):
                o.MM(ps[b][:, 0:256], hmT[:, k, mc * 128:(mc + 1) * 128], wm[:, k, 256:512], k == 0, k == 7,
                     r=['hmT', ('rg', 0)], w=[('ps', b)])
            if write_out:
                o.CP('act', vout[:, mc, :], ps[b][:, 0:256], r=[('ps', b)], w=['vout'])
            for h in range(4):
                off = 0 if h % 2 == 0 else 64
                o.CP('dve', Vaug[:, mc, h, off:off + 64], ps[b][:, h * 64:(h + 1) * 64], r=[('ps', b)], w=['Vaug'])
        o.DMA('sp', SCM["k"][l].rearrange("p (c m) -> p c m", m=256), memKT[:], r=['memKT'], w=[('scm', l)])
        o.DMA('sp', SCM["v"][l].rearrange("p (c h d) -> p c h d", h=4, d=128), Vaug[:], r=['Vaug'], w=[('scm', l)])
        if write_out:
            for hp in range(2):
                for mc in range(2):
                    b = C['rA'].next()
                    o.TR(ps[b][:, 0:128], kf[:, hp, mc * 128:(mc + 1) * 128], ident[:], r=['kf', 'ident'], w=[('ps', b)])
                    o.CP('act', kout[:, mc, hp * 128:(hp + 1) * 128], ps[b][:, 0:128], r=[('ps', b)], w=['kout'])
            o.DMA('sp', O["mk_p"][l].rearrange("(m p) f -> p m f", p=128), kout[:], r=['kout'], out=True)
            o.DMA('sp', O["mv_p"][l].rearrange("(m p) f -> p m f", p=128), vout[:], r=['vout'], out=True)

    def mem_attend(l, j, catbase):
        for hp in range(2):
            norm_fm(o, C, [qm[:, hp, :]], [G['gmq'][:, l:l + 1]], [qn[:, hp, :]], 128, 64, tb, ['qm'], 'qn', blk2[:])
        st = {}

        def SS(h):
            hp, base = h // 2, (h % 2) * 64
            pts = []
            for mc in range(2):
                bsc = C['rS'].next()
                o.MM(ps[bsc][:, 0:tb], memKT[base:base + 64, hp, mc * 128:(mc + 1) * 128], qn[base:base + 64, hp, :], True, True,
                     r=['memKT', 'qn'], w=[('ps', bsc)])
                pi = rPT.next()
                o.ACT(PT[pi][:, :], ps[bsc][:, 0:tb], AF.Exp, r=[('ps', bsc)], w=[('PT', pi)], scale=0.125)
                pts.append(pi)
            st[h] = pts

        def PVV(h):
            hp, base = h // 2, (h % 2) * 64
            pts = st[h]
            bo = C['rO'].next()
            for mc in range(2):
                o.MM(ps[bo][:, 0:tb], Vaug[:, mc, h, :], PT[pts[mc]][:, :], mc == 0, mc == 1,
                     r=['Vaug', ('PT', pts[mc])], w=[('ps', bo)])
            vr = slice(base, base + 64)
            dr = slice(64 - base, 128 - base)
            o.ACT(rc[dr, :], ps[bo][dr, 0:tb], AF.Ln, r=[('ps', bo)], w=['rc'])
            o.ACT(rc[dr, :], rc[dr, :], AF.Exp, r=['rc'], w=['rc'], scale=-1.0)
            o.TT('dve', cat[vr, catbase + hp, :], ps[bo][vr, 0:tb], rc[dr, :], ALU.mult, r=[('ps', bo), 'rc'], w=['cat'])

        SS(0)
        for h in range(4):
            if h + 1 < 4:
                SS(h + 1)
            PVV(h)

    def out_proj(l, j):
        blk = slice(j * tb, (j + 1) * tb)
        if l < 2:
            chunks = [(96, c) for c in range(8)] + [(128, 8), (128, 9)]
        else:
            chunks = [(128, c) for c in range(8)]
        for oc in range(8):
            b = C['rA'].next()
            for i, (kk, c) in enumerate(chunks):
                o.MM(ps[b][:, 0:tb], wout[0:kk, c, oc * 128:(oc + 1) * 128], cat[0:kk, c, :], i == 0, i == len(chunks) - 1,
                     r=['wout', 'cat'], w=[('ps', b)])
            o.TT('dve', xT[:, oc, blk], xT[:, oc, blk], ps[b][:, 0:tb], ALU.add, r=[('ps', b), ('xT', j, oc)], w=[('xT', j, oc)])

    def mix_a(l, sg_, j):
        barrier()
        blk = slice(j * tb, (j + 1) * tb)
        jg = sg_ * NB + j
        norm_fm(o, C, [xT[:, c, blk] for c in range(8)], [G['gmix'][:, l * 8 + c:l * 8 + c + 1] for c in range(8)],
                [hT[:, c, blk] for c in range(8)], 128, D, tb, XK(j), ('hT', j), ones[:], split=True)
        if jg == 0:
            o.MSET('pool', U[:, :, 0:16], 0.0, w=['U'])
        elif j == 0:
            o.CP('pool', U[:, :, 0:16], halo[:, l, :, :], r=[('halo', l)], w=['U'])
        for c in range(8):
            b = C['rA'].next()
            for k in range(8):
                o.MM(ps[b][0:96, 0:tb], win[:, k, c * 96:(c + 1) * 96], hT[:, k, blk], k == 0, k == 7,
                     r=['win', ('hT', j)], w=[('ps', b)])
            o.CP('act', U[:, c, 16:16 + tb], ps[b][0:96, 0:tb], r=[('ps', b)], w=['U'])
        for hp in range(2):
            b = C['rA'].next()
            for k in range(8):
                o.MM(ps[b][:, 0:tb], win[:, k, 768 + hp * 128:768 + (hp + 1) * 128], hT[:, k, blk], k == 0, k == 7,
                     r=['win', ('hT', j)], w=[('ps', b)])
            o.CP('act', qm[:, hp, :], ps[b][:, 0:tb], r=[('ps', b)], w=['qm'])
        gw = wmisc[0:96, 0:1536].rearrange("p (k n) -> p k n", n=192)
        E = 16 + tb
        for g in range(4):
            w_ = 2 ** (g + 1)
            src = U[:, 2 * g:2 * g + 2, :]
            bufs = [tA, tB_]
            d = 1
            i = 0
            srck = 'U'
            while d < w_:
                dst = bufs[i % 2]
                dk = 'tA' if i % 2 == 0 else 'tB'
                lo = 2 * d
                o.TT('pool', dst[:, :, lo:E], src[:, :, lo:E], src[:, :, lo - d:E - d], ALU.add, r=[srck], w=[dk])
                src, srck = dst, dk
                d *= 2
                i += 1
            o.STT('dve', dif[:, 2 * g:2 * g + 2, :], src[:, :, 16:E], 1.0 / w_, U[:, 2 * g:2 * g + 2, 16:E], ALU.mult, ALU.subtract,
                  r=[srck, 'U'], w=['dif'])
            if jg == 0:
                o.TT('dve', t1a[:, 0:32].rearrange("p (c t) -> p c t", t=16), src[:, :, 16:32], C['rcnt'][:, 2 * g:2 * g + 2, :],
                     ALU.mult, r=[srck, 'rcnt'], w=['t1'])
                o.TT('dve', dif[:, 2 * g:2 * g + 2, 0:16], t1a[:, 0:32].rearrange("p (c t) -> p c t", t=16),
                     U[:, 2 * g:2 * g + 2, 16:32], ALU.subtract, r=['t1', 'U', 'dif'], w=['dif'])
            for ec in range(2):
                b = C['rA'].next()
                for cc in range(2):
                    o.MM(ps[b][0:96, 0:tb], gw[:, 2 * g + cc, ec * 96:(ec + 1) * 96], dif[:, 2 * g + cc, :], cc == 0, cc == 1,
                         r=['wmisc', 'dif'], w=[('ps', b)])
                ch = 2 * g + ec
                o.TSM('dve', cat[0:96, ch, :], ps[b][0:96, 0:tb], G['pscale'][:, l * 8 + ch:l * 8 + ch + 1],
                      r=[('ps', b), 'gains'], w=['cat'])
        if jg == NSEG * NB - 1:
            for c in range(8):
                b = C['rA'].next()
                o.TR(ps[b][0:15, 0:96], U[:, c, E - 15:E], ident[0:96, 0:96], r=['U', 'ident'], w=[('ps', b)])
                o.CP('act', pso[0:15, c * 96:(c + 1) * 96], ps[b][0:15, 0:96], r=[('ps', b)], w=['pso'])
            o.DMA('sp', O["pool_p"][l], pso[0:15, :], r=['pso'], out=True)
        elif j == NB - 1:
            o.CP('pool', halo[:, l, :, :], U[:, :, tb:tb + 16], r=['U'], w=[('halo', l)])
        else:
            o.CP('pool', tA[:, 0, 0:128].rearrange("p (c t) -> p c t", t=16), U[:, :, tb:tb + 16], r=['U'], w=['tA'])
            o.CP('pool', U[:, :, 0:16], tA[:, 0, 0:128].rearrange("p (c t) -> p c t", t=16), r=['tA'], w=['U'])
        mem_attend(l, j, 8)
        out_proj(l, j)

    def kv_stage(sg_, j):
        barrier()
        blk = slice(j * tb, (j + 1) * tb)
        jg = sg_ * NB + j
        gb = slice(jg * tb, (jg + 1) * tb)
        wd = C['wdkv']
        norm_fm(o, C, [xT[:, c, blk] for c in range(8)], [G['gkv'][:, c:c + 1] for c in range(8)],
                [hT[:, c, blk] for c in range(8)], 128, D, tb, XK(j), ('hT', j), ones[:], split=True)
        o.DMA('sp', cosb[:], I["c_cos"][:, gb], w=['cosb'])
        o.DMA('sp', sinb[:], I["c_sin"][:, gb], w=['sinb'])
        for cc in range(2):
            b = C['rA'].next()
            for k in range(8):
                o.MM(ps[b][:, 0:tb], wd[:, k, cc * 128:(cc + 1) * 128], hT[:, k, blk], k == 0, k == 7,
                     r=['wdkv', ('hT', j)], w=[('ps', b)])
            o.CP('act', ckf[:, cc, :], ps[b][:, 0:tb], r=[('ps', b)], w=['ckf'])
        br = C['rG'].next()
        for k in range(8):
            o.MM(ps[br][0:96, 0:tb], wd[:, k, 256:352], hT[:, k, blk], k == 0, k == 7, r=['wdkv', ('hT', j)], w=[('ps', br)])
        o.MSET('pool', krb[0:64, :], 0.0, w=['krb'])
        o.CP('act', krb[64:96, :], ps[br][64:96, 0:tb], r=[('ps', br)], w=['krb'])
        b2 = C['rG'].next()
        o.MM(ps[b2][0:96, 0:tb], perm[0:96, 0:96], krb[0:96, :], True, True, r=['perm', 'krb'], w=[('ps', b2)])
        o.TT('dve', t1[64:96, :], ps[br][64:96, 0:tb], cosb[64:96, :], ALU.mult, r=[('ps', br), 'cosb'], w=['t1'])
        o.TT('dve', t2[64:96, :], ps[b2][64:96, 0:tb], sinb[64:96, :], ALU.mult, r=[('ps', b2), 'sinb'], w=['t2'])
        o.TT('dve', krf[64:96, :], t1[64:96, :], t2[64:96, :], ALU.add, r=['t1', 't2'], w=['krf'])
        for i in range(2):
            o.CP('pool', KT[i][64:96, gb], krf[64:96, :], r=['krf'], w=[('KT', i)])
        norm_fm(o, C, [ckf[:, cc, :] for cc in range(2)], [G['gkvl'][:, cc:cc + 1] for cc in range(2)],
                [cqn[:, cc, :] for cc in range(2)], 128, 256, tb, ['ckf'], 'cqn', ones[:])
        o.CP('pool', cT[:, :, gb], cqn[:, 0:2, :], r=['cqn'], w=['cT'])
        for cc in range(2):
            o.STT('dve', ckf[:, cc, :], ckf[:, cc, :], G['gkvl'][:, cc:cc + 1], C['rstd'][:, 0:tb], ALU.mult, ALU.mult,
                  r=['ckf', 'rstd', 'cqn'], w=['ckf'])
        for tt in range(tb // 128):
            ts_ = slice(tt * 128, (tt + 1) * 128)
            kc = jg * 4 + tt
            b = C['rA'].next()
            o.TR(ps[b][:, 0:128], ckf[:, 0, ts_], ident[:], r=['ckf', 'ident'], w=[('ps', b)])
            o.TR(ps[b][:, 128:256], ckf[:, 1, ts_], ident[:], r=['ckf', 'ident'], w=[('ps', b)])
            o.TR(ps[b][:, 256:288], krf[64:96, ts_], ident[64:96, 64:96], r=['krf', 'ident'], w=[('ps', b)])
            o.CP('act', kvo[:, :], ps[b][:, 0:288], r=[('ps', b)], w=['kvo'])
            o.DMA('sp', O["kv_p"][jg * tb + tt * 128: jg * tb + (tt + 1) * 128, :], kvo[:, :], r=['kvo'], out=True)
            b3, b4 = C['rG'].next(), C['rG'].next()
            for cc in range(2):
                o.MM(ps[b3][:, 0:512], cT[:, cc, jg * tb + tt * 128: jg * tb + (tt + 1) * 128], C['wuk'][:, cc, 0:512], cc == 0, cc == 1,
                     r=['cT', 'wuk'], w=[('ps', b3)])
            for cc in range(2):
                o.MM(ps[b4][:, 0:256], cT[:, cc, jg * tb + tt * 128: jg * tb + (tt + 1) * 128], C['wuk'][:, cc, 512:768], cc == 0, cc == 1,
                     r=['cT', 'wuk'], w=[('ps', b4)])
            o.ACT(sqn[:, 0:512], ps[b3][:, 0:512], AF.Square, r=[('ps', b3)], w=['sqn'])
            o.ACT(sqn[:, 512:768], ps[b4][:, 0:256], AF.Square, r=[('ps', b4)], w=['sqn'])
            o.RED(ssn[:, 0:12], sqn[:, :].rearrange("p (h d) -> p h d", d=64), r=['sqn'], w=['ssn'])
            o.ACT(sqr[:, 0:32], kvo[:, 256:288], AF.Square, r=['kvo'], w=['sqr'])
            o.RED(ssn[:, 12:13], sqr[:, 0:32].rearrange("p (h d) -> p h d", d=32), r=['sqr'], w=['ssn2'])
            o.TSA('dve', ssn[:, 0:12], ssn[:, 0:12], ssn[:, 12:13], r=['ssn', 'ssn2'], w=['ssn'])
            o.ACT(ssn[:, 0:12], ssn[:, 0:12], AF.Ln, r=['ssn'], w=['ssn'], scale=1.0, bias=96.0 * EPS)
            o.ACT(kinv[:, kc, :], ssn[:, 0:12], AF.Exp, r=['ssn'], w=['kinv'], scale=-0.5)

    def mix_b(l, sg_, j):
        barrier()
        jl = l - 2
        blk = slice(j * tb, (j + 1) * tb)
        jg = sg_ * NB + j
        gb = slice(jg * tb, (jg + 1) * tb)
        norm_fm(o, C, [xT[:, c, blk] for c in range(8)], [G['gmix'][:, l * 8 + c:l * 8 + c + 1] for c in range(8)],
                [hT[:, c, blk] for c in range(8)], 128, D, tb, XK(j), ('hT', j), ones[:], split=True)
        o.DMA('sp', cosb[:], I["c_cos"][:, gb], w=['cosb'])
        o.DMA('sp', sinb[:], I["c_sin"][:, gb], w=['sinb'])
        for c in range(3):
            b = C['rA'].next()
            for k in range(8):
                o.MM(ps[b][:, 0:tb], win[:, k, c * 128:(c + 1) * 128], hT[:, k, blk], k == 0, k == 7,
                     r=['win', ('hT', j)], w=[('ps', b)])
            o.CP('act', cq[:, c, :], ps[b][:, 0:tb], r=[('ps', b)], w=['cq'])
        for hp in range(2):
            b = C['rA'].next()
            for k in range(8):
                o.MM(ps[b][:, 0:tb], win[:, k, 384 + hp * 128:384 + (hp + 1) * 128], hT[:, k, blk], k == 0, k == 7,
                     r=['win', ('hT', j)], w=[('ps', b)])
            o.CP('act', qm[:, hp, :], ps[b][:, 0:tb], r=[('ps', b)], w=['qm'])
        norm_fm(o, C, [cq[:, c, :] for c in range(3)], [G['gql'][:, jl * 3 + c:jl * 3 + c + 1] for c in range(3)],
                [cqn[:, c, :] for c in range(3)], 128, 384, tb, ['cq'], 'cqn', ones[:])
        uq = wmisc[:, 0:3456].rearrange("p (k n) -> p k n", n=1152)
        nk = (jg + 1) * 4
        BQ, BN = 6, 7

        def prepA(h):
            ki = h % 2
            for k in range(3):
                o.MM(ps[BQ][0:96, 0:tb], uq[:, k, h * 96:(h + 1) * 96], cqn[:, k, :], k == 0, k == 2,
                     r=['wmisc', 'cqn'], w=[('ps', BQ)])
            o.ACT(qg[:, :], ps[BQ][0:96, 0:tb], AF.Square, r=[('ps', BQ)], w=['qg'])
            for kb in range(jg + 1):
                bk = C['rA'].next()
                for cc in range(2):
                    o.MM(ps[bk][0:64, 0:tb], C['wuk'][:, cc, h * 64:(h + 1) * 64], cT[:, cc, kb * tb:(kb + 1) * tb], cc == 0, cc == 1,
                         r=['wuk', 'cT'], w=[('ps', bk)])
                o.CP('dve', KT[ki][0:64, kb * tb:(kb + 1) * tb], ps[bk][0:64, 0:tb], r=[('ps', bk)], w=[('KT', ki)])
            off = 0 if h % 2 == 0 else 64
            for kg in range(nk // 8 + (1 if nk % 8 else 0)):
                bk = C['rA'].next()
                n8 = min(8, nk - kg * 8)
                for q8 in range(n8):
                    kc = kg * 8 + q8
                    for cc in range(2):
                        o.MM(ps[bk][:, q8 * 64:(q8 + 1) * 64], cT[:, cc, kc * 128:(kc + 1) * 128], C['wuv'][:, cc, h * 64:(h + 1) * 64],
                             cc == 0, cc == 1, r=['wuv', 'cT'], w=[('ps', bk)])
                o.CP('dve', Vh[ki][:, kg * 8:kg * 8 + n8, off:off + 64],
                     ps[bk][:, 0:n8 * 64].rearrange("p (k d) -> p k d", d=64), r=[('ps', bk)], w=[('Vh', ki)])

        def prepN1(h):
            o.MM(ps[BN][0:96, 0:tb], ones[0:96, 0:96], qg[:, :], True, True, r=['qg', 'ones'], w=[('ps', BN)])
            o.ACT(C['tmp'][0:96, 0:tb], ps[BN][0:96, 0:tb], AF.Ln, r=[('ps', BN)], w=['tmpn'], scale=1.0 / 96, bias=EPS)
            o.ACT(C['rstd'][0:96, 0:tb], C['tmp'][0:96, 0:tb], AF.Exp, r=['tmpn'], w=['rstd'], scale=-0.5)
            o.STT('dve', qg[:, :], ps[BQ][0:96, 0:tb], G['gq'][:, jl:jl + 1], C['rstd'][0:96, 0:tb], ALU.mult, ALU.mult,
                  r=[('ps', BQ), 'rstd', 'gains', 'qg'], w=['qg'])

        def prepN2(h):
            qi = h % 2
            o.MM(ps[BN][0:96, 0:tb], perm[:, :], qg[:, :], True, True, r=['perm', 'qg'], w=[('ps', BN)])
            o.STT('dve', t1[:, :], qg[:, :], G['gk'][:, 0:1], cosb[:, :], ALU.mult, ALU.mult, r=['qg', 'cosb', 'gains'], w=['t1'])
            o.STT('dve', t2[:, :], ps[BN][0:96, 0:tb], G['gk'][:, 0:1], sinb[:, :], ALU.mult, ALU.mult,
                  r=[('ps', BN), 'sinb', 'gains'], w=['t2'])
            o.TT('pool', QT[qi][:, :], t1[:, :], t2[:, :], ALU.add, r=['t1', 't2'], w=[('QT', qi)])

        def attend(h):
            qi = ki = h % 2
            off = 0 if h % 2 == 0 else 64
            bo = C['rO'].next()
            pend = {}

            def S(kc):
                r_ = kc - 4 * jg
                q0 = 128 * max(r_, 0)
                bs = C['rS'].next()
                o.MM(ps[bs][:, q0:tb], KT[ki][0:96, kc * 128:(kc + 1) * 128], QT[qi][:, q0:tb], True, True,
                     r=[('KT', ki), ('QT', qi)], w=[('ps', bs)])
                pi = rPT.next()
                o.ACT(PT[pi][:, q0:tb], ps[bs][:, q0:tb], AF.Exp, r=[('ps', bs), 'kinv'], w=[('PT', pi)],
                      scale=kinv[:, kc, h:h + 1])
                if r_ >= 0:
                    o.TT('pool', PT[pi][:, q0:q0 + 128], PT[pi][:, q0:q0 + 128], tri[:, :], ALU.mult, r=[('PT', pi), 'tri'], w=[('PT', pi)])
                pend[kc] = (pi, q0)

            def PV(kc):
                pi, q0 = pend.pop(kc)
                o.MM(ps[bo][:, q0:tb], Vh[ki][:, kc, :], PT[pi][:, q0:tb], kc == 0, kc == nk - 1,
                     r=[('Vh', ki), ('PT', pi)], w=[('ps', bo)])

            hook2 = min(3, nk - 1)
            S(0)
            if h + 1 < 12:
                prepA(h + 1)
            for kc in range(nk):
                if kc + 1 < nk:
                    S(kc + 1)
                PV(kc)
                if h + 1 < 12:
                    if kc == 0:
                        prepN1(h + 1)
                    if kc == hook2:
                        prepN2(h + 1)
            vr = slice(off, off + 64)
            dr = slice(64 - off, 128 - off)
            o.ACT(rc[dr, :], ps[bo][dr, 0:tb], AF.Ln, r=[('ps', bo)], w=['rc'])
            o.ACT(rc[dr, :], rc[dr, :], AF.Exp, r=['rc'], w=['rc'], scale=-1.0)
            o.TT('dve', cat[vr, h // 2, :], ps[bo][vr, 0:tb], rc[dr, :], ALU.mult, r=[('ps', bo), 'rc'], w=['cat'])

        prepA(0)
        prepN1(0)
        prepN2(0)
        for h in range(12):
            attend(h)
        mem_attend(l, j, 6)
        out_proj(l, j)

    def ffn(l, pre, hook=None, first=False):
        barrier()
        for j in range(NB):
            blk = slice(j * tb, (j + 1) * tb)
            norm_fm(o, C, [xT[:, c, blk] for c in range(8)], [G['gffn'][:, l * 8 + c:l * 8 + c + 1] for c in range(8)],
                    [hT[:, c, blk] for c in range(8)], 128, D, tb, XK(j), ('hT', j), ones[:], split=True)
        rOut = Rot([0, 1, 6, 7])
        rStg = Rot([0, 1])
        pendG, pendO = list(pre[0]), list(pre[1])
        nx = {'g': len(pendG), 'o': len(pendO)}
        units = [(fg, j) for fg in range(NFG) for j in range(NB)]
        grpG, grpO = {}, {}

        def GU(u):
            fg, j = units[u]
            if fg not in grpG:
                grpG[fg] = pendG.pop(0)
            gv, uv, rk = grpG[fg]
            blk = slice(j * tb, (j + 1) * tb)
            ai = u % 2
            for cc in range(2):
                bg, bu = 2 + cc, 4 + cc
                for k in range(8):
                    o.MM(ps[bg][:, 0:tb], gv[:, k, cc * 128:(cc + 1) * 128], hT[:, k, blk], k == 0, k == 7,
                         r=[rk, ('hT', j)], w=[('ps', bg)])
                for k in range(8):
                    o.MM(ps[bu][:, 0:tb], uv[:, k, cc * 128:(cc + 1) * 128], hT[:, k, blk], k == 0, k == 7,
                         r=[rk, ('hT', j)], w=[('ps', bu)])
                o.ACT(sg[cc][:, :], ps[bg][:, 0:tb], AF.Silu, r=[('ps', bg)], w=[('sg', cc)])
                o.TT('dve', aT[ai][:, cc, :], sg[cc][:, :], ps[bu][:, 0:tb], ALU.mult, r=[('sg', cc), ('ps', bu)], w=[('aT', ai)])
            if j == NB - 1 and nx['g'] < NFG:
                pendG.append(load_G(l, nx['g'], first))
                nx['g'] += 1

        def OUT(u):
            fg, j = units[u]
            if fg not in grpO:
                grpO[fg] = pendO.pop(0)
            ov, rk = grpO[fg]
            blk = slice(j * tb, (j + 1) * tb)
            ai = u % 2
            for oc in range(8):
                b = rOut.next()
                for cc in range(2):
                    o.MM(ps[b][:, 0:tb], ov[:, cc, oc * 128:(oc + 1) * 128], aT[ai][:, cc, :], cc == 0, cc == 1,
                         r=[rk, ('aT', ai)], w=[('ps', b)])
                if oc % 2 == 0:
                    o.TT('dve', xT[:, oc, blk], xT[:, oc, blk], ps[b][:, 0:tb], ALU.add, r=[('ps', b), ('xT', j, oc)], w=[('xT', j, oc)])
                else:
                    si = rStg.next()
                    o.CP('act', stg[si][:, :], ps[b][:, 0:tb], r=[('ps', b)], w=[('stg', si)])
                    o.TT('pool', xT[:, oc, blk], xT[:, oc, blk], stg[si][:, :], ALU.add, r=[('stg', si), ('xT', j, oc)], w=[('xT', j, oc)])
            if j == NB - 1:
                if nx['o'] < NFG:
                    pendO.append(load_O(l, nx['o'], first))
                    nx['o'] += 1
                if fg == 1 and hook is not None:
                    hook()

        GU(0)
        for u in range(len(units)):
            if u + 1 < len(units):
                GU(u + 1)
            OUT(u)

    for sg_ in range(NSEG):
        barrier()
        for tt in range(SEG // 128):
            xb, xk = xinb[tt % 2], ('xin' if tt % 2 == 0 else 'xin1')
            o.DMA('sp', xb[:], I["xp"][sg_ * SEG + tt * 128: sg_ * SEG + (tt + 1) * 128, :], w=[xk])
            for c in range(8):
                b = C['rA'].next()
                o.TR(ps[b][:, 0:128], xb[:, c * 128:(c + 1) * 128], ident[:], r=[xk, 'ident'], w=[('ps', b)])
                o.CP('dve' if c % 2 else 'act', xT[:, c, tt * 128:(tt + 1) * 128], ps[b][:, 0:128], r=[('ps', b)], w=[('xT', tt // 4, c)])
        for l in range(nlayers):
            mem_kv(l, write_out=(sg_ == 0))
            if l == 0 and sg_ == 0:
                load_mix_weights(0, True)
            pre = []
            for j in range(NB):
                if l < 2:
                    mix_a(l, sg_, j)
                else:
                    mix_b(l, sg_, j)
                if j == 0:
                    pre = ([load_G(l, fg, sg_ == 0) for fg in range(NRG)], [load_O(l, fg, sg_ == 0) for fg in range(NRO)])
            ffn(l, pre, hook=(lambda l=l, sg_=sg_: load_mix_weights((l + 1) % nlayers, sg_ == 0 and l + 1 < nlayers))
                if not (l == nlayers - 1 and sg_ == NSEG - 1) else None, first=(sg_ == 0))
            if l == 1:
                for j in range(NB):
                    kv_stage(sg_, j)
        barrier()
        for tt in range(SEG // 128):
            xb, xk = xinb[tt % 2], ('xin' if tt % 2 == 0 else 'xin1')
            for c in range(8):
                b = C['rA'].next()
                o.TR(ps[b][:, 0:128], xT[:, c, tt * 128:(tt + 1) * 128], ident[:], r=[('xT', tt // 4, c), 'ident'], w=[('ps', b)])
                o.CP('dve' if c % 2 else 'act', xb[:, c * 128:(c + 1) * 128], ps[b][:, 0:128], r=[('ps', b)], w=[xk])
            o.DMA('sp', O["y_p"][sg_ * SEG + tt * 128: sg_ * SEG + (tt + 1) * 128, :], xb[:], r=[xk], out=True)
    P.finish()
    P.emit('p')
    print("prompt program: ops", len(P.ops), "waits", P.nwaits)


def sample_program(nc, es, I, O, nlayers, NPG=128, SC=None):
    P = Prog(nc, es, 'S_')
    o = Ops(P)
    C = alloc_common(P, o, I)
    ps = C['ps']
    G = C['G']
    ident, ones, blk2, perm = C['ident'], C['ones'], C['blk2'], C['perm']
    tb = 32
    NS = 4
    NPT = NS * NPG

    xT = P.sb("s_xT", [128, 8, tb], F32)
    hT = P.sb("s_hT", [128, 8, tb], BF16)
    win = P.sb("s_win", [128, 8, 1024], BF16)
    wout = P.sb("s_wout", [128, 10, 1024], BF16)
    wmisc = P.sb("s_wmisc", [128, 3456], BF16)
    NRING = 3
    ring = [P.sb("s_ring%d" % i, [128, 6144], BF16) for i in range(NRING)]
    cat = P.sb("s_cat", [128, 10, tb], BF16)
    qm = P.sb("s_qm", [128, 2, tb], F32)
    qn = P.sb("s_qn", [128, 2, tb], BF16)
    xin = P.sb("s_xin", [32, 1024], F32)
    memKT = P.sb("s_memKT", [128, NS, 2, 256], BF16)
    Vaug = P.sb("s_Vaug", [128, NS, 2, 4, 128], BF16)
    mst = P.sb("s_mst", [128, 2, 256], F32)
    Us = P.sb("s_U", [96, 32, 24], F32)
    tA = P.sb("s_tA", [96, 8, 24], F32)
    tB_ = P.sb("s_tB", [96, 8, 24], F32)
    dif = P.sb("s_dif", [96, 8, tb], BF16)
    spt = P.sb("s_spt", [16, 768], F32)
    pso = P.sb("s_pso", [16, 768], F32)
    PTm = P.sb("s_PTm", [128, 256], BF16)
    rc = P.sb("s_rc", [128, 128], F32)
    sg = [P.sb("s_sg%d" % i, [128, tb], F32) for i in range(2)]
    aT = [P.sb("s_aT%d" % i, [128, 2, tb], BF16) for i in range(2)]
    cosb = P.sb("s_cos", [96, tb], F32)
    sinb = P.sb("s_sin", [96, tb], F32)
    ckf = P.sb("s_ckf", [128, 2, tb], F32)
    cTs = P.sb("s_cT", [128, 2, tb], BF16)
    krf = P.sb("s_krf", [96, tb], F32)
    krb = P.sb("s_krb", [96, tb], BF16)
    t1 = P.sb("s_t1", [96, tb], F32)
    t2 = P.sb("s_t2", [96, tb], F32)
    kvo = P.sb("s_kvo", [32, 288], F32)
    KVbn = P.sb("s_KVbn", [32, 289], BF16)
    sqn = P.sb("s_sqn", [128, 768], BF16)
    sqr = P.sb("s_sqr", [128, 32], BF16)
    ssn = P.sb("s_ssn", [128, 4, 16], F32)
    kinvn = P.sb("s_kinvn", [32, 12], F32)
    cq = P.sb("s_cq", [128, 3, tb], F32)
    cqn = P.sb("s_cqn", [128, 3, tb], BF16)
    qg = P.sb("s_qg", [96, tb], BF16)
    QTs = P.sb("s_QT", [96, 12, tb], BF16)
    QL = P.sb("s_QL", [128, 2, 12, tb], BF16)
    wukf = P.sb("s_wukf", [128, 2, 768], F32)
    wukT = P.sb("s_wukT", [64, 12, 256], BF16)
    identb = P.sb("s_identb", [128, 128], BF16)
    smask = P.sb("s_smask", [32, 4, 96], BF16)
    ptb = P.sb("s_ptb", [128, NPT], I32)
    idxall = P.sb("s_idx", [128, NPT], I32)
    iot = P.sb("s_iota", [128, 1], I32)
    kinv = P.sb("s_kinv", [128, NPT, 12], F32)
    NKB = 16
    sqn2 = [P.sb("s_sqn2_%d" % i, [128, 800], BF16) for i in range(2)]
    KVb = [P.sb("s_KVb%d" % i, [128, 289], BF16) for i in range(NKB)]
    KVTc = [P.sb("s_KVTc%d" % i, [128, 4, 2, 128], BF16) for i in range(2)]
    KVTr = [P.sb("s_KVTr%d" % i, [96, 4, 128], BF16) for i in range(2)]
    tms = P.sb("s_tms", [128, 4, 96], F32)
    PTs = [P.sb("s_PTs%d" % i, [128, 4, 96], BF16) for i in range(2)]
    PTn = P.sb("s_PTn", [32, 96], BF16)
    olat = P.sb("s_olat", [96, 256], F32)
    rcs = P.sb("s_rcs", [96, 1], F32)
    olT = P.sb("s_olT", [128, 2, 96], BF16)

    o.MSET('pool', Vaug[:], 1.0, w=['Vaug'])
    for i in range(NKB):
        o.MSET('pool', KVb[i][:, 288:289], 1.0, w=[('KVb', i)])
    o.MSET('pool', KVbn[:, 288:289], 1.0, w=['KVbn'])
    o.CP('pool', identb[:], ident[:], r=['ident'], w=['identb'])
    o.DMA('pool', smask[:], I["c_smask"][:, :, :], w=['smask'])
    o.DMA('sp', wukf[:], I["w_uk"].rearrange("(k p) n -> p k n", p=128), w=['wukf'])
    for h in range(12):
        b = C['rA'].next()
        for cc in range(2):
            o.TR(ps[b][0:64, cc * 128:(cc + 1) * 128], wukf[:, cc, h * 64:(h + 1) * 64], ident[:, :], r=['wukf', 'ident'], w=[('ps', b)])
        o.CP('dve', wukT[:, h, :], ps[b][0:64, 0:256], r=[('ps', b)], w=['wukT'])
    o.DMA('sp', ptb[:], I["pt"].rearrange("s p -> (s p)").partition_broadcast(128), w=['ptb'])
    P.op('pool', lambda e: e.iota(iot[:], pattern=[[0, 1]], base=0, channel_multiplier=1), w=['iota'])
    ptf = P.sb("s_ptf", [128, NPT], F32)
    iotf = P.sb("s_iotf", [128, 1], F32)
    o.CP('dve', ptf[:], ptb[:], r=['ptb'], w=['ptf'])
    o.CP('dve', iotf[:], iot[:], r=['iota'], w=['iotf'])
    o.TS('dve', ptf[:], ptf[:], 128.0, iotf[:, 0:1], ALU.mult, ALU.add, r=['ptf', 'iotf'], w=['ptf'])
    o.CP('dve', idxall[:], ptf[:], r=['ptf'], w=['idx'])

    WS = SC if SC is not None else I
    WQ = 'sp' if SC is not None else 'pool'
    wv_ffn_in = [WS["w_ffn_in"][l].rearrange("(k p) n -> p k n", p=128) for l in range(4)]
    wv_ffn_out = [WS["w_ffn_out"][l].rearrange("(k p) n -> p k n", p=128) for l in range(4)]
    NFG = DFF // 256
    XK = [('xT', c) for c in range(8)]
    ring_state = {'n': 0}

    def load_ffn_group(l, fg):
        s = ring_state['n'] % NRING
        ring_state['n'] += 1
        rg = ring[s]
        k = ('ring', s)
        gv = rg[:, 0:2048].rearrange("p (k n) -> p k n", n=256)
        uv = rg[:, 2048:4096].rearrange("p (k n) -> p k n", n=256)
        ov = rg[:, 4096:6144].rearrange("p (k n) -> p k n", n=1024)
        o.DMA(WQ, gv, wv_ffn_in[l][:, :, fg * 256:(fg + 1) * 256], w=[k])
        o.DMA(WQ, uv, wv_ffn_in[l][:, :, DFF + fg * 256:DFF + (fg + 1) * 256], w=[k])
        o.DMA(WQ, ov, wv_ffn_out[l][:, fg * 2:fg * 2 + 2, :], w=[k])
        if False:
            si = SC["w_ffn_in"][l].rearrange("(k p) n -> p k n", p=128)
            so = SC["w_ffn_out"][l].rearrange("(k p) n -> p k n", p=128)
            o.DMA('sp', si[:, :, fg * 256:(fg + 1) * 256], gv, r=[k], out=True)
            o.DMA('sp', si[:, :, DFF + fg * 256:DFF + (fg + 1) * 256], uv, r=[k], out=True)
            o.DMA('sp', so[:, fg * 2:fg * 2 + 2, :], ov, r=[k], out=True)
        return gv, uv, ov, k

    def load_mix_weights(l):
        if l < 2:
            o.DMA(WQ, win[:], WS["w_in_a"][l].rearrange("(k p) n -> p k n", p=128), w=['win'])
            o.DMA(WQ, wout[0:96, 0:8, :], WS["w_out"][l][0:768, :].rearrange("(k p) n -> p k n", p=96), w=['wout'])
            o.DMA(WQ, wout[:, 8:10, :], WS["w_out"][l][768:1024, :].rearrange("(k p) n -> p k n", p=128), w=['wout'])
            gv = wmisc[0:96, 0:1536].rearrange("p (k n) -> p k n", n=192)
            o.DMA(WQ, gv, WS["w_pool_grp"][l].rearrange("g (c p) e -> p (g c) e", p=96), w=['wmisc'])
            if False:
                o.DMA('sp', SC["w_in_a"][l].rearrange("(k p) n -> p k n", p=128), win[:], r=['win'], out=True)
                o.DMA('sp', SC["w_out"][l][0:768, :].rearrange("(k p) n -> p k n", p=96), wout[0:96, 0:8, :], r=['wout'], out=True)
                o.DMA('sp', SC["w_out"][l][768:1024, :].rearrange("(k p) n -> p k n", p=128), wout[:, 8:10, :], r=['wout'], out=True)
                o.DMA('sp', SC["w_pool_grp"][l].rearrange("g (c p) e -> p (g c) e", p=96), gv, r=['wmisc'], out=True)
        else:
            j = l - 2
            o.DMA(WQ, win[:, :, 0:640], WS["w_in_b"][j].rearrange("(k p) n -> p k n", p=128), w=['win'])
            o.DMA(WQ, wout[:, 0:8, :], WS["w_out"][l].rearrange("(k p) n -> p k n", p=128), w=['wout'])
            uq = wmisc[:, 0:3456].rearrange("p (k n) -> p k n", n=1152)
            o.DMA(WQ, uq, WS["w_uq"][j].rearrange("(k p) n -> p k n", p=128), w=['wmisc'])
            if False:
                o.DMA('sp', SC["w_in_b"][j].rearrange("(k p) n -> p k n", p=128), win[:, :, 0:640], r=['win'], out=True)
                o.DMA('sp', SC["w_out"][l].rearrange("(k p) n -> p k n", p=128), wout[:, 0:8, :], r=['wout'], out=True)
                o.DMA('sp', SC["w_uq"][j].rearrange("(k p) n -> p k n", p=128), uq, r=['wmisc'], out=True)

    def xnorm(gkey, l):
        norm_fm(o, C, [xT[:, c, :] for c in range(8)], [G[gkey][:, l * 8 + c:l * 8 + c + 1] for c in range(8)],
                [hT[:, c, :] for c in range(8)], 128, D, tb, XK, 'hT', ones[:])

    def mem_load(l):
        for s_ in range(NS):
            o.DMA('sp', mst[:], I["cmk"][l, s_].rearrange("(m p) f -> p m f", p=128), w=['mst'])
            for mc in range(2):
                b = C['rA'].next()
                for hp in range(2):
                    o.TR(ps[b][:, hp * 128:(hp + 1) * 128], mst[:, mc, hp * 128:(hp + 1) * 128], ident[:], r=['mst', 'ident'], w=[('ps', b)])
                o.CP('dve', memKT[:, s_, :, mc * 128:(mc + 1) * 128], ps[b][:, 0:256].rearrange("p (h m) -> p h m", m=128),
                     r=[('ps', b)], w=['memKT'])
            o.DMA('sp', mst[:], I["cmv"][l, s_].rearrange("(m p) f -> p m f", p=128), r=['mst'], w=['mst'])
            for mc in range(2):
                for h in range(4):
                    off = 0 if h % 2 == 0 else 64
                    o.CP('pool', Vaug[:, s_, mc, h, off:off + 64], mst[:, mc, h * 64:(h + 1) * 64], r=['mst'], w=['Vaug'])

    def mem_attend(l, catbase):
        for hp in range(2):
            norm_fm(o, C, [qm[:, hp, :]], [G['gmq'][:, l:l + 1]], [qn[:, hp, :]], 128, 64, tb, ['qm'], 'qn', blk2[:])
        bsb = [2, 3]
        for mc in range(2):
            for s_ in range(NS):
                for h in range(4):
                    hp, par = h // 2, h % 2
                    base = par * 64
                    col = ((mc * NS + s_) * 2 + hp) * 8
                    o.MM(ps[bsb[par]][:, col:col + 8], memKT[base:base + 64, s_, hp, mc * 128:(mc + 1) * 128],
                         qn[base:base + 64, hp, s_ * 8:(s_ + 1) * 8], True, True, r=['memKT', 'qn'], w=[('ps', bsb[par])])
        for par in range(2):
            o.ACT(PTm[:, par * 128:(par + 1) * 128], ps[bsb[par]][:, 0:128], AF.Exp, r=[('ps', bsb[par])], w=['PTm'], scale=0.125)
        bo = C['rO'].next()
        for s_ in range(NS):
            for h in range(4):
                hp, par = h // 2, h % 2
                colo = (s_ * 4 + h) * 8
                for mc in range(2):
                    col = par * 128 + ((mc * NS + s_) * 2 + hp) * 8
                    o.MM(ps[bo][:, colo:colo + 8], Vaug[:, s_, mc, h, :], PTm[:, col:col + 8], mc == 0, mc == 1,
                         r=['Vaug', 'PTm'], w=[('ps', bo)])
        o.RCP(rc[:, 0:128], ps[bo][:, 0:128], r=[('ps', bo)], w=['rc'])
        for h in range(4):
            hp, base = h // 2, (h % 2) * 64
            vr = slice(base, base + 64)
            dr = slice(64 - base, 128 - base)
            pv = ps[bo][vr, 0:128].rearrange("p (s h i) -> p s h i", h=4, i=8)[:, :, h, :]
            rv = rc[dr, 0:128].rearrange("p (s h i) -> p s h i", h=4, i=8)[:, :, h, :]
            o.TT('dve', cat[vr, catbase + hp, :].rearrange("p (s i) -> p s i", i=8), pv, rv, ALU.mult, r=[('ps', bo), 'rc'], w=['cat'])

    def out_proj(l):
        if l < 2:
            chunks = [(96, c) for c in range(8)] + [(128, 8), (128, 9)]
        else:
            chunks = [(128, c) for c in range(8)]
        for oc in range(8):
            b = C['rA'].next()
            for i, (kk, c) in enumerate(chunks):
                o.MM(ps[b][:, 0:tb], wout[0:kk, c, oc * 128:(oc + 1) * 128], cat[0:kk, c, :], i == 0, i == len(chunks) - 1,
                     r=['wout', 'cat'], w=[('ps', b)])
            o.TT('dve', xT[:, oc, :], xT[:, oc, :], ps[b][:, 0:tb], ALU.add, r=[('ps', b), ('xT', oc)], w=[('xT', oc)])

    def mix_a(l):
        xnorm('gmix', l)
        for s_ in range(NS):
            o.DMA('sp', spt[0:15, :], I["spool"][l, s_], w=['spt'])
            b = C['rA'].next()
            for c in range(8):
                o.TR(ps[b][0:96, c * 16:c * 16 + 15], spt[0:15, c * 96:(c + 1) * 96], ident[0:15, 0:15], r=['spt', 'ident'], w=[('ps', b)])
            o.CP('dve', Us[:, :, 1:16].rearrange("p (c s) t -> p c s t", s=NS)[:, :, s_, :],
                 ps[b][0:96, 0:128].rearrange("p (c t) -> p c t", t=16)[:, :, 0:15], r=[('ps', b)], w=['U'])
        for c in range(8):
            b = C['rA'].next()
            for k in range(8):
                o.MM(ps[b][0:96, 0:tb], win[:, k, c * 96:(c + 1) * 96], hT[:, k, :], k == 0, k == 7, r=['win', 'hT'], w=[('ps', b)])
            o.CP('act', Us[:, c * NS:(c + 1) * NS, 16:24], ps[b][0:96, 0:tb].rearrange("p (s i) -> p s i", i=8), r=[('ps', b)], w=['U'])
        for hp in range(2):
            b = C['rA'].next()
            for k in range(8):
                o.MM(ps[b][:, 0:tb], win[:, k, 768 + hp * 128:768 + (hp + 1) * 128], hT[:, k, :], k == 0, k == 7,
                     r=['win', 'hT'], w=[('ps', b)])
            o.CP('act', qm[:, hp, :], ps[b][:, 0:tb], r=[('ps', b)], w=['qm'])
        gw = wmisc[0:96, 0:1536].rearrange("p (k n) -> p k n", n=192)
        E = 24
        for g in range(4):
            w_ = 2 ** (g + 1)
            src = Us[:, 8 * g:8 * g + 8, :]
            bufs = [tA, tB_]
            d, i, srck = 1, 0, 'U'
            while d < w_:
                dst = bufs[i % 2]
                dk = 'tA' if i % 2 == 0 else 'tB'
                lo = 2 * d
                o.TT('pool', dst[:, :, lo:E], src[:, :, lo:E], src[:, :, lo - d:E - d], ALU.add, r=[srck], w=[dk])
                src, srck = dst, dk
                d *= 2
                i += 1
            o.STT('dve', dif[:, 2 * g:2 * g + 2, :].rearrange("p c (s i) -> p (c s) i", i=8), src[:, :, 16:E], 1.0 / w_,
                  Us[:, 8 * g:8 * g + 8, 16:E], ALU.mult, ALU.subtract, r=[srck, 'U'], w=['dif'])
            for ec in range(2):
                b = C['rA'].next()
                for cc in range(2):
                    o.MM(ps[b][0:96, 0:tb], gw[:, 2 * g + cc, ec * 96:(ec + 1) * 96], dif[:, 2 * g + cc, :], cc == 0, cc == 1,
                         r=['wmisc', 'dif'], w=[('ps', b)])
                ch = 2 * g + ec
                o.TSM('dve', cat[0:96, ch, :], ps[b][0:96, 0:tb], G['pscale'][:, l * 8 + ch:l * 8 + ch + 1], r=[('ps', b), 'gains'], w=['cat'])
        for s_ in range(NS):
            b1, b2 = C['rA'].next(), C['rA'].next()
            for c in range(8):
                bb = b1 if c < 4 else b2
                o.TR(ps[bb][0:15, (c % 4) * 96:(c % 4 + 1) * 96], Us[:, c * NS + s_, 9:24], ident[0:96, 0:96], r=['U', 'ident'], w=[('ps', bb)])
            o.CP('act', pso[0:15, 0:384], ps[b1][0:15, 0:384], r=[('ps', b1)], w=['pso'])
            o.CP('act', pso[0:15, 384:768], ps[b2][0:15, 0:384], r=[('ps', b2)], w=['pso'])
            o.DMA('sp', O["pool_s"][l, s_], pso[0:15, :], r=['pso'], out=True)
        mem_attend(l, 8)
        out_proj(l)

    def ffn(l, pre, hook=None):
        xnorm('gffn', l)
        rOut = Rot([0, 1, 6, 7])
        pend = list(pre)
        nxt = len(pre)
        for fg in range(NFG):
            gv, uv, ov, rk = pend.pop(0)
            ai = fg % 2
            for cc in range(2):
                bg, bu = 2 + cc, 4 + cc
                for k in range(8):
                    o.MM(ps[bg][:, 0:tb], gv[:, k, cc * 128:(cc + 1) * 128], hT[:, k, :], k == 0, k == 7, r=[rk, 'hT'], w=[('ps', bg)])
                for k in range(8):
                    o.MM(ps[bu][:, 0:tb], uv[:, k, cc * 128:(cc + 1) * 128], hT[:, k, :], k == 0, k == 7, r=[rk, 'hT'], w=[('ps', bu)])
                o.ACT(sg[cc][:, :], ps[bg][:, 0:tb], AF.Silu, r=[('ps', bg)], w=[('sg', cc)])
                o.TT('dve', aT[ai][:, cc, :], sg[cc][:, :], ps[bu][:, 0:tb], ALU.mult, r=[('sg', cc), ('ps', bu)], w=[('aT', ai)])
            for oc in range(8):
                b = rOut.next()
                for cc in range(2):
                    o.MM(ps[b][:, 0:tb], ov[:, cc, oc * 128:(oc + 1) * 128], aT[ai][:, cc, :], cc == 0, cc == 1,
                         r=[rk, ('aT', ai)], w=[('ps', b)])
                o.TT('dve', xT[:, oc, :], xT[:, oc, :], ps[b][:, 0:tb], ALU.add, r=[('ps', b), ('xT', oc)], w=[('xT', oc)])
            if nxt < NFG:
                pend.append(load_ffn_group(l, nxt))
                nxt += 1
            if fg == 1 and hook is not None:
                hook()

    def kv_stage():
        wd = C['wdkv']
        norm_fm(o, C, [xT[:, c, :] for c in range(8)], [G['gkv'][:, c:c + 1] for c in range(8)],
                [hT[:, c, :] for c in range(8)], 128, D, tb, XK, 'hT', ones[:])
        o.DMA('sp', cosb[:], I["c_cos"][:, 2048:2080], w=['cosb'])
        o.DMA('sp', sinb[:], I["c_sin"][:, 2048:2080], w=['sinb'])
        for cc in range(2):
            b = C['rA'].next()
            for k in range(8):
                o.MM(ps[b][:, 0:tb], wd[:, k, cc * 128:(cc + 1) * 128], hT[:, k, :], k == 0, k == 7, r=['wdkv', 'hT'], w=[('ps', b)])
            o.CP('act', ckf[:, cc, :], ps[b][:, 0:tb], r=[('ps', b)], w=['ckf'])
        br = C['rG'].next()
        for k in range(8):
            o.MM(ps[br][0:96, 0:tb], wd[:, k, 256:352], hT[:, k, :], k == 0, k == 7, r=['wdkv', 'hT'], w=[('ps', br)])
        o.CP('act', krb[64:96, :], ps[br][64:96, 0:tb], r=[('ps', br)], w=['krb'])
        b2 = C['rG'].next()
        o.MM(ps[b2][0:96, 0:tb], perm[64:96, 0:96], krb[64:96, :], True, True, r=['perm', 'krb'], w=[('ps', b2)])
        o.TT('dve', t1[64:96, :], ps[br][64:96, 0:tb], cosb[64:96, :], ALU.mult, r=[('ps', br), 'cosb'], w=['t1'])
        o.TT('dve', t2[64:96, :], ps[b2][64:96, 0:tb], sinb[64:96, :], ALU.mult, r=[('ps', b2), 'sinb'], w=['t2'])
        o.TT('dve', krf[64:96, :], t1[64:96, :], t2[64:96, :], ALU.add, r=['t1', 't2'], w=['krf'])
        o.CP('pool', krb[64:96, :], krf[64:96, :], r=['krf'], w=['krb'])
        norm_fm(o, C, [ckf[:, cc, :] for cc in range(2)], [G['gkvl'][:, cc:cc + 1] for cc in range(2)],
                [cTs[:, cc, :] for cc in range(2)], 128, 256, tb, ['ckf'], 'cTs', ones[:])
        for cc in range(2):
            o.STT('dve', ckf[:, cc, :], ckf[:, cc, :], G['gkvl'][:, cc:cc + 1], C['rstd'][:, 0:tb], ALU.mult, ALU.mult,
                  r=['ckf', 'rstd', 'cTs'], w=['ckf'])
        b = C['rA'].next()
        o.TR(ps[b][0:32, 0:128], ckf[:, 0, :], ident[:], r=['ckf', 'ident'], w=[('ps', b)])
        o.TR(ps[b][0:32, 128:256], ckf[:, 1, :], ident[:], r=['ckf', 'ident'], w=[('ps', b)])
        o.TR(ps[b][0:32, 256:288], krf[64:96, :], ident[64:96, 64:96], r=['krf', 'ident'], w=[('ps', b)])
        o.CP('act', kvo[:, :], ps[b][0:32, 0:288], r=[('ps', b)], w=['kvo'])
        o.DMA('sp', O["kv_s"][:, :], kvo[:, :], r=['kvo'], out=True)
        o.CP('dve', KVbn[:, 0:288], kvo[:, :], r=['kvo'], w=['KVbn'])
        b3, b4 = C['rG'].next(), C['rG'].next()
        for cc in range(2):
            o.MM(ps[b3][0:32, 0:512], cTs[:, cc, :], C['wuk'][:, cc, 0:512], cc == 0, cc == 1, r=['cTs', 'wuk'], w=[('ps', b3)])
        for cc in range(2):
            o.MM(ps[b4][0:32, 0:256], cTs[:, cc, :], C['wuk'][:, cc, 512:768], cc == 0, cc == 1, r=['cTs', 'wuk'], w=[('ps', b4)])
        o.ACT(sqn[0:32, 0:512], ps[b3][0:32, 0:512], AF.Square, r=[('ps', b3)], w=['sqn'])
        o.ACT(sqn[0:32, 512:768], ps[b4][0:32, 0:256], AF.Square, r=[('ps', b4)], w=['sqn'])
        o.RED(ssn[0:32, 0, 0:12], sqn[0:32, :].rearrange("p (h d) -> p h d", d=64), r=['sqn'], w=['ssn'])
        o.ACT(sqr[0:32, 0:32], kvo[:, 256:288], AF.Square, r=['kvo'], w=['sqr'])
        o.RED(ssn[0:32, 0, 12:13], sqr[0:32, 0:32].rearrange("p (h d) -> p h d", d=32), r=['sqr'], w=['ssn2'])
        o.TSA('dve', ssn[0:32, 0, 0:12], ssn[0:32, 0, 0:12], ssn[0:32, 0, 12:13], r=['ssn', 'ssn2'], w=['ssn'])
        o.ACT(ssn[0:32, 0, 0:12], ssn[0:32, 0, 0:12], AF.Ln, r=['ssn'], w=['ssn'], scale=1.0, bias=96.0 * EPS)
        o.ACT(kinvn[:, :], ssn[0:32, 0, 0:12], AF.Exp, r=['ssn'], w=['kinvn'], scale=-0.5)

    def mix_b(l):
        jl = l - 2
        xnorm('gmix', l)
        for c in range(3):
            b = C['rA'].next()
            for k in range(8):
                o.MM(ps[b][:, 0:tb], win[:, k, c * 128:(c + 1) * 128], hT[:, k, :], k == 0, k == 7, r=['win', 'hT'], w=[('ps', b)])
            o.CP('act', cq[:, c, :], ps[b][:, 0:tb], r=[('ps', b)], w=['cq'])
        for hp in range(2):
            b = C['rA'].next()
            for k in range(8):
                o.MM(ps[b][:, 0:tb], win[:, k, 384 + hp * 128:384 + (hp + 1) * 128], hT[:, k, :], k == 0, k == 7,
                     r=['win', 'hT'], w=[('ps', b)])
            o.CP('act', qm[:, hp, :], ps[b][:, 0:tb], r=[('ps', b)], w=['qm'])
        norm_fm(o, C, [cq[:, c, :] for c in range(3)], [G['gql'][:, jl * 3 + c:jl * 3 + c + 1] for c in range(3)],
                [cqn[:, c, :] for c in range(3)], 128, 384, tb, ['cq'], 'cqn', ones[:])
        uq = wmisc[:, 0:3456].rearrange("p (k n) -> p k n", n=1152)
        for h in range(12):
            bq = C['rG'].next()
            for k in range(3):
                o.MM(ps[bq][0:96, 0:tb], uq[:, k, h * 96:(h + 1) * 96], cqn[:, k, :], k == 0, k == 2, r=['wmisc', 'cqn'], w=[('ps', bq)])
            norm_fm(o, C, [ps[bq][0:96, 0:tb]], [G['gq'][:, jl:jl + 1]], [qg[:, :]], 96, 96, tb, [('ps', bq)], 'qg', ones[0:96, 0:96])
            b2 = C['rG'].next()
            o.MM(ps[b2][0:96, 0:tb], perm[:, :], qg[:, :], True, True, r=['perm', 'qg'], w=[('ps', b2)])
            o.STT('dve', t1[:, :], qg[:, :], G['gk'][:, 0:1], cosb[:, :], ALU.mult, ALU.mult, r=['qg', 'cosb', 'gains'], w=['t1'])
            o.STT('dve', t2[:, :], ps[b2][0:96, 0:tb], G['gk'][:, 0:1], sinb[:, :], ALU.mult, ALU.mult,
                  r=[('ps', b2), 'sinb', 'gains'], w=['t2'])
            o.TT('pool', QTs[:, h, :], t1[:, :], t2[:, :], ALU.add, r=['t1', 't2'], w=['QTs'])
        for cc in range(2):
            b = C['rA'].next()
            for h in range(12):
                o.MM(ps[b][:, h * tb:(h + 1) * tb], wukT[0:64, h, cc * 128:(cc + 1) * 128], QTs[0:64, h, :], True, True,
                     r=['wukT', 'QTs'], w=[('ps', b)])
            o.CP('dve', QL[:, cc, :, :], ps[b][:, 0:12 * tb].rearrange("p (h t) -> p h t", t=tb), r=[('ps', b)], w=['QL'])
        BO, PSB, BSC = 4, 5, 3
        ngr = NPG // 4
        NG = NS * ngr
        psc = ps[PSB][:, :].bitcast(BF16)
        psr = ps[PSB + 1][:, :].bitcast(BF16)

        def slot(g, q):
            return (g % 4) * 4 + q

        def GATHER(g):
            for q in range(4):
                pg = g * 4 + q
                sl = slot(g, q)
                P.op('pool', lambda e, sl=sl, pg=pg: e.indirect_dma_start(
                    out=KVb[sl][:, 0:288], out_offset=None, in_=I["ckv"][:, :],
                    in_offset=bass.IndirectOffsetOnAxis(ap=idxall[:, pg:pg + 1], axis=0)),
                    r=['idx'], w=[('KVb', sl)], dma=True)

        def T(g):
            kt = g % 2
            for q in range(4):
                sl = slot(g, q)
                for cc in range(2):
                    o.TR(psc[:, (q * 2 + cc) * 128:(q * 2 + cc + 1) * 128], KVb[sl][:, cc * 128:(cc + 1) * 128], identb[:, :],
                         r=[('KVb', sl), 'identb'], w=[('ps', PSB)])
                o.TR(psr[0:96, q * 128:(q + 1) * 128], KVb[sl][:, 192:288], identb[:, :], r=[('KVb', sl), 'identb'], w=[('ps', PSB + 1)])
            o.CP('dve', KVTc[kt][:, :, :, :], psc[:, 0:1024].rearrange("p (q c k) -> p q c k", c=2, k=128), r=[('ps', PSB)], w=[('KVTc', kt)])
            o.CP('act', KVTr[kt][64:96, :, :], psr[64:96, 0:512].rearrange("p (q k) -> p q k", k=128), r=[('ps', PSB + 1)], w=[('KVTr', kt)])

        def KN(g):
            kt = g % 2
            for q in range(4):
                sl = slot(g, q)
                b3, b4 = (7, 0) if q % 2 == 0 else (1, 2)
                sq_ = sqn2[q % 2]
                sk = ('sqn', q % 2)
                for cc in range(2):
                    o.MM(ps[b3][:, 0:512], KVTc[kt][:, q, cc, :], C['wuk'][:, cc, 0:512], cc == 0, cc == 1,
                         r=[('KVTc', kt), 'wuk'], w=[('ps', b3)])
                for cc in range(2):
                    o.MM(ps[b4][:, 0:256], KVTc[kt][:, q, cc, :], C['wuk'][:, cc, 512:768], cc == 0, cc == 1,
                         r=[('KVTc', kt), 'wuk'], w=[('ps', b4)])
                o.ACT(sq_[:, 0:512], ps[b3][:, 0:512], AF.Square, r=[('ps', b3)], w=[sk])
                o.ACT(sq_[:, 512:768], ps[b4][:, 0:256], AF.Square, r=[('ps', b4)], w=[sk])
                o.ACT(sq_[:, 768:800], KVb[sl][:, 256:288], AF.Square, r=[('KVb', sl)], w=[sk])
                o.RED(ssn[:, q, 0:12], sq_[:, 0:768].rearrange("p (h d) -> p h d", d=64), r=[sk], w=['ssn'])
                o.RED(ssn[:, q, 12:13], sq_[:, 768:800].rearrange("p (h d) -> p h d", d=32), r=[sk], w=['ssn'])
            o.TT('dve', ssn[:, :, 0:12], ssn[:, :, 0:12], ssn[:, :, 12:13].to_broadcast([128, 4, 12]), ALU.add, r=['ssn'], w=['ssn'])
            o.ACT(ssn[:, :, 0:12], ssn[:, :, 0:12], AF.Ln, r=['ssn'], w=['ssn'], scale=1.0, bias=96.0 * EPS)
            o.ACT(kinv[:, g * 4:g * 4 + 4, :], ssn[:, :, 0:12], AF.Exp, r=['ssn'], w=['kinv'], scale=-0.5)

        def SE(g):
            kt = g % 2
            s_ = g // ngr
            qcols = slice(s_ * 8, (s_ + 1) * 8)
            for q in range(4):
                oc = ps[BSC][:, q * 96:(q + 1) * 96].rearrange("p (h i) -> p h i", i=8)
                o.MM(oc, KVTc[kt][:, q, 0, :], QL[:, 0, :, qcols], True, False, r=[('KVTc', kt), 'QL'], w=[('ps', BSC)])
                o.MM(oc, KVTc[kt][:, q, 1, :], QL[:, 1, :, qcols], False, False, r=[('KVTc', kt), 'QL'], w=[('ps', BSC)])
                o.MM(oc, KVTr[kt][64:96, q, :], QTs[64:96, :, qcols], False, True, r=[('KVTr', kt), 'QTs'], w=[('ps', BSC)])
            o.TT('dve', tms[:, :, :].rearrange("p q (h i) -> p q h i", i=8),
                 ps[BSC][:, 0:384].rearrange("p (q h i) -> p q h i", h=12, i=8),
                 kinv[:, g * 4:g * 4 + 4, :].unsqueeze(3).to_broadcast([128, 4, 12, 8]), ALU.mult, r=[('ps', BSC), 'kinv'], w=['tms'])
            o.ACT(PTs[kt][:, :, :], tms[:, :, :], AF.Exp, r=['tms'], w=[('PTs', kt)])

        def PV(g):
            kt = g % 2
            gi = g % ngr
            for q in range(4):
                sl = slot(g, q)
                o.MM(ps[BO][0:96, 0:289], PTs[kt][:, q, :], KVb[sl][:, 0:289], gi == 0 and q == 0, False,
                     r=[('PTs', kt), ('KVb', sl)], w=[('ps', BO)])

        def EPI(s_):
            qcols = slice(s_ * 8, (s_ + 1) * 8)
            oc = ps[BSC][0:32, 0:96].rearrange("p (h i) -> p h i", i=8)
            o.MM(oc, cTs[:, 0, :], QL[:, 0, :, qcols], True, False, r=['cTs', 'QL'], w=[('ps', BSC)])
            o.MM(oc, cTs[:, 1, :], QL[:, 1, :, qcols], False, False, r=['cTs', 'QL'], w=[('ps', BSC)])
            o.MM(oc, krb[64:96, :], QTs[64:96, :, qcols], False, True, r=['krb', 'QTs'], w=[('ps', BSC)])
            o.TT('dve', tms[0:32, 0, :].rearrange("p (h i) -> p h i", i=8), ps[BSC][0:32, 0:96].rearrange("p (h i) -> p h i", i=8),
                 kinvn[:, :].unsqueeze(2).to_broadcast([32, 12, 8]), ALU.mult, r=[('ps', BSC), 'kinvn'], w=['tms'])
            o.ACT(tms[0:32, 1, :], tms[0:32, 0, :], AF.Exp, r=['tms'], w=['tms'])
            o.TT('dve', PTn[:, :], tms[0:32, 1, :], smask[:, s_, :], ALU.mult, r=['tms', 'smask'], w=['PTn'])
            o.MM(ps[BO][0:96, 0:289], PTn[:, :], KVbn[:, 0:289], NPG == 0, True, r=['PTn', 'KVbn'], w=[('ps', BO)])
            o.RCP(rcs[:, :], ps[BO][0:96, 288:289], r=[('ps', BO)], w=['rcs'])
            o.TSM('dve', olat[:, :], ps[BO][0:96, 0:256], rcs[:, 0:1], r=[('ps', BO), 'rcs'], w=['olat'])
            b = PSB
            for cc in range(2):
                o.TR(ps[b][:, cc * 96:(cc + 1) * 96], olat[:, cc * 128:(cc + 1) * 128], ident[0:96, 0:96], r=['olat', 'ident'], w=[('ps', b)])
            o.CP('dve', olT[:, :, :], ps[b][:, 0:192].rearrange("p (c n) -> p c n", n=96), r=[('ps', b)], w=['olT'])
            b = PSB + 1
            for h in range(12):
                for cc in range(2):
                    o.MM(ps[b][:, h * 8:(h + 1) * 8], C['wuv'][:, cc, (h // 2) * 128:(h // 2 + 1) * 128], olT[:, cc, h * 8:(h + 1) * 8],
                         cc == 0, cc == 1, r=['wuv', 'olT'], w=[('ps', b)])
            for par in range(2):
                vr = slice(par * 64, par * 64 + 64)
                src = ps[b][vr, 0:96].rearrange("p (hp two i) -> p hp two i", two=2, i=8)[:, :, par, :]
                o.CP('dve', cat[vr, 0:6, qcols], src, r=[('ps', b)], w=['cat'])

        for g in range(min(3, NG)):
            GATHER(g)
        T(0)
        if l == 2:
            KN(0)
        if NG > 1:
            T(1)
        for g in range(NG):
            if g + 3 < NG:
                GATHER(g + 3)
            SE(g)
            if g + 2 < NG:
                T(g + 2)
            if l == 2 and g + 1 < NG:
                KN(g + 1)
            PV(g)
            if g % ngr == ngr - 1:
                EPI(g // ngr)
        mem_attend(l, 6)
        out_proj(l)

    o.DMA('sp', xin[:], I["xs"][:, :], w=['xin'])
    for c in range(8):
        b = C['rA'].next()
        o.TR(ps[b][:, 0:32], xin[:, c * 128:(c + 1) * 128], ident[0:32, 0:32], r=['xin', 'ident'], w=[('ps', b)])
        o.CP('dve', xT[:, c, :], ps[b][:, 0:32], r=[('ps', b)], w=[('xT', c)])
    load_mix_weights(0)
    for l in range(nlayers):
        mem_load(l)
        pre = [load_ffn_group(l, fg) for fg in range(min(NRING, NFG))]
        if l < 2:
            mix_a(l)
        else:
            mix_b(l)
        ffn(l, pre, hook=(lambda l=l: load_mix_weights(l + 1)) if l < nlayers - 1 else None)
        if l == 1:
            kv_stage()
    for c in range(8):
        b = C['rA'].next()
        o.TR(ps[b][0:32, 0:128], xT[:, c, :], ident[:, :], r=[('xT', c), 'ident'], w=[('ps', b)])
        o.CP('dve', xin[:, c * 128:(c + 1) * 128], ps[b][0:32, 0:128], r=[('ps', b)], w=['xin'])
    o.DMA('sp', O["y_s"][:, :], xin[:], r=['xin'], out=True)
    P.finish()
    P.emit('s')
    print("sample program: ops", len(P.ops), "waits", P.nwaits)


def make_in_maps(inputs, cores, NT=2048):
    consts = host_consts()
    maps = []
    ckv = np.ascontiguousarray(inputs["cache_kv"]).reshape(-1, 288)
    shared = {k: np.ascontiguousarray(inputs[k]) for k in
              ("norm_mix", "norm_ffn", "w_out", "w_ffn_in", "w_ffn_out", "norm_mem", "w_mem_kv", "g_mem_q", "g_mem_k",
               "w_in_a", "w_pool_grp", "pool_scale", "w_in_b", "g_q_lora", "w_uq", "g_q", "norm_kv", "w_dkv",
               "g_kv_lora", "w_uk", "w_uv", "g_k_nope", "g_k_rope")}
    for c in cores:
        m = dict(shared)
        m.update(consts)
        m["xp"] = np.ascontiguousarray(inputs["x_prompt"][c][:NT])
        m["xs"] = np.ascontiguousarray(inputs["x_sample"][4 * c:4 * c + 4]).reshape(32, D)
        m["ckv"] = ckv
        m["pt"] = np.ascontiguousarray(inputs["page_table"][4 * c:4 * c + 4]).astype(np.int32)
        m["spool"] = np.ascontiguousarray(inputs["state_pool"][:, 4 * c:4 * c + 4])
        m["cmk"] = np.ascontiguousarray(inputs["cache_mem_k"][:, 4 * c:4 * c + 4]).reshape(4, 4, 256, 256)
        m["cmv"] = np.ascontiguousarray(inputs["cache_mem_v"][:, 4 * c:4 * c + 4]).reshape(4, 4, 256, 256)
        m["memp"] = np.ascontiguousarray(inputs["mem_prompt"][c])
        maps.append(m)
    return maps


def kernel(**inputs):
    nc = build()
    maps = make_in_maps(inputs, list(range(NCORES)))
    res = run_bass_kernel_spmd(nc, maps, core_ids=list(range(NCORES)))
    R = res.results
    y_p = np.stack([R[c]["y_p"] for c in range(NCORES)])
    y_s = np.concatenate([R[c]["y_s"].reshape(4, 8, D) for c in range(NCORES)])
    kv_p = np.stack([R[c]["kv_p"] for c in range(NCORES)])
    kv_s = np.concatenate([R[c]["kv_s"].reshape(4, 8, 288) for c in range(NCORES)])
    pool_p = np.stack([R[c]["pool_p"] for c in range(NCORES)], axis=1)
    pool_s = np.concatenate([R[c]["pool_s"] for c in range(NCORES)], axis=1)
    mk = np.stack([R[c]["mk_p"] for c in range(NCORES)], axis=1).reshape(4, 8, 256, 4, 64)
    mv = np.stack([R[c]["mv_p"] for c in range(NCORES)], axis=1).reshape(4, 8, 256, 4, 64)
    return (y_p, y_s, kv_p, kv_s, pool_p, pool_s, mk, mv)
```

```python
import numpy as np
import concourse.bass as bass
import concourse.mybir as mybir
from concourse.bass_utils import run_bass_kernel_spmd
from contextlib import ExitStack

F32 = mybir.dt.float32
BF16 = mybir.dt.bfloat16
I32 = mybir.dt.int32
AF = mybir.ActivationFunctionType
ALU = mybir.AluOpType
AX = mybir.AxisListType

D = 1024
DFF = 2816
EPS = 1e-6
TB = 512
NCORES = 8


class Prog:
    ENGS = ('pe', 'act', 'dve', 'pool', 'sp')
    NDMA = 8

    def __init__(self, nc, es, pfx=''):
        self.nc, self.es = nc, es
        self.pfx = pfx
        self.ops = []
        self.lastw = {}
        self.readers = {}
        self.out_dmas = []

    def sb(self, name, shape, dt):
        return self.es.enter_context(self.nc.sbuf_tensor(self.pfx + name, shape, dt))

    def ps(self, name, shape, dt):
        return self.es.enter_context(self.nc.psum_tensor(self.pfx + name, shape, dt))

    def op(self, eng, fn, r=(), w=(), dma=False, out=False):
        i = len(self.ops)
        deps = set()
        pr = [k for k in r if isinstance(k, tuple) and k[0] == 'ps']
        if pr:
            w = list(w) + [k for k in pr if k not in w]
        for k in r:
            lw = self.lastw.get(k)
            if lw is not None:
                deps.add(lw)
        for k in w:
            lw = self.lastw.get(k)
            if lw is not None:
                deps.add(lw)
            for rd in self.readers.get(k, ()):
                deps.add(rd)
        keep = []
        for d in deps:
            de, ddma = self.ops[d][0], self.ops[d][3]
            if de == eng and not dma and not ddma and eng == 'pe':
                continue
            keep.append(d)
        for k in r:
            self.readers.setdefault(k, []).append(i)
        for k in w:
            self.lastw[k] = i
            self.readers[k] = []
        self.ops.append([eng, fn, keep, dma])
        if out:
            self.out_dmas.append(i)
        return i

    def finish(self):
        import os
        stop = int(os.environ.get("K_STOP", "0"))
        if stop:
            self.ops = self.ops[:stop]
            self.out_dmas = [i for i in range(len(self.ops)) if self.ops[i][3]]
        self.ops.append(['sp', None, list(self.out_dmas), False])

    def emit(self, tag=''):
        nc, es = self.nc, self.es
        ops = self.ops
        n = len(ops)
        needed = [False] * n
        for o in ops:
            for d in o[2]:
                needed[d] = True
        csem = {e: es.enter_context(nc.semaphore(tag + 'c_' + e)) for e in self.ENGS}
        ccnt = {e: 0 for e in self.ENGS}
        dsem = {e: [es.enter_context(nc.semaphore('%sd_%s%d' % (tag, e, j))) for j in range(self.NDMA)]
                for e in ('sp', 'pool', 'act')}
        dcnt = {e: [0] * self.NDMA for e in dsem}
        dlast = {e: [None] * self.NDMA for e in dsem}
        drr = {e: 0 for e in dsem}
        sig = [None] * n
        extra = [None] * n
        streams = {e: [] for e in self.ENGS}
        for i, o in enumerate(ops):
            eng, fn, deps, dma = o
            streams[eng].append(i)
            if dma:
                j = drr[eng]
                drr[eng] = (j + 1) % self.NDMA
                dcnt[eng][j] += 1
                sig[i] = (dsem[eng][j], 16 * dcnt[eng][j])
                extra[i] = dlast[eng][j]
                dlast[eng][j] = i
            elif needed[i]:
                ccnt[eng] += 1
                sig[i] = (csem[eng], ccnt[eng])
        self.nwaits = 0

        def body(e, eng):
            known = {}
            for i in streams[eng]:
                _, fn, deps, dma = ops[i]
                req = {}
                dl = list(deps)
                if extra[i] is not None:
                    dl.append(extra[i])
                for d in dl:
                    s, v = sig[d]
                    if req.get(id(s), (None, 0))[1] < v:
                        req[id(s)] = (s, v)
                for s, v in req.values():
                    if known.get(id(s), 0) < v:
                        e.wait_ge(s, v)
                        known[id(s)] = v
                        self.nwaits += 1
                if fn is None:
                    continue
                ins = fn(e)
                if sig[i] is not None:
                    ins.then_inc(sig[i][0], 16 if dma else 1)

        with nc.Block() as block:
            @block.tensor
            def _(e):
                body(e, 'pe')

            @block.scalar
            def _(e):
                body(e, 'act')

            @block.vector
            def _(e):
                body(e, 'dve')

            @block.gpsimd
            def _(e):
                body(e, 'pool')

            @block.sync
            def _(e):
                body(e, 'sp')


class Rot:
    def __init__(self, items):
        self.items, self.i = items, 0

    def next(self):
        x = self.items[self.i % len(self.items)]
        self.i += 1
        return x


class Ops:
    def __init__(self, P):
        self.P = P

    def MM(self, o, l, rh, st, sp, r, w):
        self.P.op('pe', lambda e: e.matmul(o, lhsT=l, rhs=rh, start=st, stop=sp), r=r, w=w)

    def TR(self, o, i, idn, r, w):
        self.P.op('pe', lambda e: e.transpose(out=o, in_=i, identity=idn), r=r, w=w)

    def ACT(self, o, i, f, r, w, **kw):
        self.P.op('act', lambda e: e.activation(out=o, in_=i, func=f, **kw), r=r, w=w)

    def TT(self, eng, o, a, b, op, r, w):
        self.P.op(eng, lambda e: e.tensor_tensor(out=o, in0=a, in1=b, op=op), r=r, w=w)

    def STT(self, eng, o, a, s, b, op0, op1, r, w):
        self.P.op(eng, lambda e: e.scalar_tensor_tensor(out=o, in0=a, scalar=s, in1=b, op0=op0, op1=op1), r=r, w=w)

    def TS(self, eng, o, a, s1, s2, op0, op1, r, w):
        self.P.op(eng, lambda e: e.tensor_scalar(out=o, in0=a, scalar1=s1, scalar2=s2, op0=op0, op1=op1), r=r, w=w)

    def CP(self, eng, o, i, r, w):
        if eng == 'act':
            self.P.op('act', lambda e: e.activation(out=o, in_=i, func=AF.Copy), r=r, w=w)
        else:
            self.P.op(eng, lambda e: e.tensor_copy(out=o, in_=i), r=r, w=w)

    def TSA(self, eng, o, a, s1, r, w):
        self.P.op(eng, lambda e: e.tensor_scalar_add(out=o, in0=a, scalar1=s1), r=r, w=w)

    def TSM(self, eng, o, a, s1, r, w):
        self.P.op(eng, lambda e: e.tensor_scalar_mul(out=o, in0=a, scalar1=s1), r=r, w=w)

    def RCP(self, o, i, r, w):
        self.P.op('dve', lambda e: e.reciprocal(out=o, in_=i), r=r, w=w)

    def MSET(self, eng, o, v, w):
        self.P.op(eng, lambda e: e.memset(o, v), w=w)

    def RED(self, o, i, r, w):
        self.P.op('dve', lambda e: e.tensor_reduce(out=o, in_=i, axis=AX.X, op=ALU.add), r=r, w=w)

    def DMA(self, eng, o, i, r=(), w=(), out=False, slow=False):
        if slow:
            self.P.op(eng, lambda e: e.dma_start(out=o, in_=i, allow_slow_non_contiguous=True), r=r, w=w, dma=True, out=out)
        else:
            self.P.op(eng, lambda e: e.dma_start(out=o, in_=i), r=r, w=w, dma=True, out=out)


CKV_ROWS = [5120 * 128]
PT_COLS = [128]


def in_specs(NT):
    return [
        ("xp", [NT, D], F32), ("xs", [32, D], F32), ("ckv", [CKV_ROWS[0], 288], F32), ("pt", [4, PT_COLS[0]], I32),
        ("spool", [2, 4, 15, 768], F32), ("cmk", [4, 4, 256, 256], F32), ("cmv", [4, 4, 256, 256], F32),
        ("memp", [256, D], F32),
        ("norm_mix", [4, D], F32), ("norm_ffn", [4, D], F32), ("w_out", [4, D, D], F32),
        ("w_ffn_in", [4, D, 2 * DFF], F32), ("w_ffn_out", [4, DFF, D], F32), ("norm_mem", [4, D], F32),
        ("w_mem_kv", [4, D, 512], F32), ("g_mem_q", [4, 64], F32), ("g_mem_k", [4, 64], F32),
        ("w_in_a", [2, D, D], F32), ("w_pool_grp", [2, 4, 192, 192], F32), ("pool_scale", [2, 768], F32),
        ("w_in_b", [2, D, 640], F32), ("g_q_lora", [2, 384], F32), ("w_uq", [2, 384, 1152], F32), ("g_q", [2, 96], F32),
        ("norm_kv", [D], F32), ("w_dkv", [D, 288], F32), ("g_kv_lora", [256], F32), ("w_uk", [256, 768], F32),
        ("w_uv", [256, 768], F32), ("g_k_nope", [64], F32), ("g_k_rope", [16], F32),
        ("c_ident", [128, 128], F32), ("c_perm", [96, 96], F32), ("c_cos", [96, 2048 + 32], F32),
        ("c_sin", [96, 2048 + 32], F32), ("c_tri", [128, 128], F32), ("c_rcnt", [96, 8, 16], F32),
        ("c_smask", [32, 4, 96], F32),
    ]


def out_specs(NT):
    return [
        ("y_p", [NT, D]), ("y_s", [32, D]), ("kv_p", [NT, 288]), ("kv_s", [32, 288]),
        ("pool_p", [2, 15, 768]), ("pool_s", [2, 4, 15, 768]), ("mk_p", [4, 256, 256]), ("mv_p", [4, 256, 256]),
    ]


def host_consts():
    c = {}
    c["c_ident"] = np.eye(128, dtype=np.float32)
    perm = np.zeros((96, 96), np.float32)
    for i in range(16):
        perm[80 + i, 64 + i] = -1.0
        perm[64 + i, 80 + i] = 1.0
    c["c_perm"] = perm
    half = 16
    inv_freq = (np.float32(10000.0) ** (-np.arange(half, dtype=np.float32) / np.float32(half))).astype(np.float32)
    pos = np.concatenate([np.arange(2048), np.tile(16384 + np.arange(8), 4)]).astype(np.float32)
    ang = (pos[None, :] * inv_freq[:, None]).astype(np.float32)
    cos = np.ones((96, 2080), np.float32)
    sin = np.zeros((96, 2080), np.float32)
    cos[64:80] = np.cos(ang)
    cos[80:96] = np.cos(ang)
    sin[64:80] = np.sin(ang)
    sin[80:96] = np.sin(ang)
    c["c_cos"], c["c_sin"] = cos, sin
    kk = np.arange(128)
    c["c_tri"] = (kk[None, :] >= kk[:, None]).astype(np.float32)
    rc = np.zeros((96, 8, 16), np.float32)
    for ch in range(8):
        w = 2 ** (ch // 2 + 1)
        rc[:, ch, :] = (1.0 / np.minimum(w, np.arange(16) + 1)).astype(np.float32)[None, :]
    c["c_rcnt"] = rc
    sm = np.zeros((32, 4, 12, 8), np.float32)
    for k in range(32):
        s = k // 8
        for q in range(8):
            if k % 8 <= q:
                sm[k, s, :, q] = 1.0
    c["c_smask"] = sm.reshape(32, 4, 96)
    return c


def build(NT=2048, SEG=512, do_prompt=True, do_sample=True, nlayers=4, NPG=128):
    nc = bass.Bass("TRN2", target_bir_lowering=False)
    I = {n: nc.dram_tensor(n, s, dt, kind="ExternalInput").ap() for n, s, dt in in_specs(NT)}
    O = {n: nc.dram_tensor(n, s, F32, kind="ExternalOutput").ap() for n, s in out_specs(NT)}
    SC = None
    if do_prompt and do_sample:
        SC = {n: nc.dram_tensor("sc_" + n, shp, BF16).ap() for n, shp in (
            ("w_ffn_in", [4, D, 2 * DFF]), ("w_ffn_out", [4, DFF, D]), ("w_in_a", [2, D, D]), ("w_in_b", [2, D, 640]),
            ("w_out", [4, D, D]), ("w_pool_grp", [2, 4, 192, 192]), ("w_uq", [2, 384, 1152]))}
    if do_prompt:
        with ExitStack() as es:
            prompt_program(nc, es, I, O, NT, SEG, nlayers, SC)
    if do_sample:
        with ExitStack() as es:
            sample_program(nc, es, I, O, nlayers, NPG, SC)
    return nc


def load_small(P, o, I, C):
    g1 = P.sb("gst1", [128, 128], F32)
    g2 = P.sb("gst2", [32, 128], F32)
    gT1 = P.sb("gT1", [128, 128], F32)
    gT2 = P.sb("gT2", [128, 32], F32)
    o.MSET('dve', g1[:], 0.0, w=['gst1'])
    o.MSET('dve', g2[:], 0.0, w=['gst2'])
    o.DMA('sp', g1[0:32, :], I["norm_mix"].rearrange("l (k p) -> (l k) p", p=128), w=['gst1'])
    o.DMA('sp', g1[32:64, :], I["norm_ffn"].rearrange("l (k p) -> (l k) p", p=128), w=['gst1'])
    o.DMA('sp', g1[64:96, :], I["norm_mem"].rearrange("l (k p) -> (l k) p", p=128), w=['gst1'])
    o.DMA('sp', g1[96:104, :], I["norm_kv"].rearrange("(k p) -> k p", p=128), w=['gst1'])
    o.DMA('sp', g1[104:106, :], I["g_kv_lora"].rearrange("(k p) -> k p", p=128), w=['gst1'])
    o.DMA('sp', g1[106:112, :], I["g_q_lora"].rearrange("l (k p) -> (l k) p", p=128), w=['gst1'])
    o.DMA('sp', g2[0:16, 0:96], I["pool_scale"].rearrange("l (k p) -> (l k) p", p=96), w=['gst2'])
    o.DMA('sp', g2[16:20, 0:64], I["g_mem_q"][:, :], w=['gst2'])
    o.DMA('sp', g2[16:20, 64:128], I["g_mem_q"][:, :], w=['gst2'])
    o.DMA('sp', g2[20:24, 0:64], I["g_mem_k"][:, :], w=['gst2'])
    o.DMA('sp', g2[20:24, 64:128], I["g_mem_k"][:, :], w=['gst2'])
    o.DMA('sp', g2[24:26, 0:96], I["g_q"][:, :], w=['gst2'])
    o.DMA('sp', g2[26:27, 0:64], I["g_k_nope"].rearrange("(o p) -> o p", o=1), w=['gst2'])
    o.DMA('sp', g2[26:27, 64:80], I["g_k_rope"].rearrange("(o p) -> o p", o=1), w=['gst2'])
    o.DMA('sp', g2[26:27, 80:96], I["g_k_rope"].rearrange("(o p) -> o p", o=1), w=['gst2'])
    ps = C['ps']
    o.TR(ps[0][:, 0:128], g1[:, :], C['ident'][:, :], r=['gst1', 'ident'], w=[('ps', 0)])
    o.CP('dve', gT1[:, :], ps[0][:, 0:128], r=[('ps', 0)], w=['gains'])
    o.TR(ps[1][:, 0:32], g2[:, :], C['ident'][0:32, 0:32], r=['gst2', 'ident'], w=[('ps', 1)])
    o.CP('dve', gT2[:, :], ps[1][:, 0:32], r=[('ps', 1)], w=['gains'])
    G = {'gmix': gT1[:, 0:32], 'gffn': gT1[:, 32:64], 'gmem': gT1[:, 64:96], 'gkv': gT1[:, 96:104], 'gkvl': gT1[:, 104:106],
         'gql': gT1[:, 106:112], 'pscale': gT2[0:96, 0:16], 'gmq': gT2[:, 16:20], 'gmk': gT2[:, 20:24], 'gq': gT2[0:96, 24:26],
         'gk': gT2[0:96, 26:27]}
    return G


def alloc_common(P, o, I):
    C = {}
    C['ident'] = P.sb("ident", [128, 128], F32)
    C['ones'] = P.sb("ones", [128, 128], BF16)
    C['blk2'] = P.sb("blk2", [128, 128], BF16)
    C['perm'] = P.sb("perm", [96, 96], BF16)
    C['tri'] = P.sb("tri", [128, 128], BF16)
    C['rcnt'] = P.sb("rcnt", [96, 8, 16], F32)
    o.DMA('sp', C['ident'][:], I["c_ident"][:, :], w=['ident'])
    o.DMA('pool', C['perm'][:], I["c_perm"][:, :], w=['perm'])
    o.DMA('pool', C['tri'][:], I["c_tri"][:, :], w=['tri'])
    o.DMA('sp', C['rcnt'][:], I["c_rcnt"][:, :, :], w=['rcnt'])
    o.MSET('dve', C['ones'][:], 1.0, w=['ones'])
    o.MSET('dve', C['blk2'][:], 0.0, w=['blk2'])
    o.MSET('dve', C['blk2'][0:64, 0:64], 1.0, w=['blk2'])
    o.MSET('dve', C['blk2'][64:128, 64:128], 1.0, w=['blk2'])
    C['wuk'] = P.sb("wuk", [128, 2, 768], BF16)
    C['wuv'] = P.sb("wuv", [128, 2, 768], BF16)
    C['wdkv'] = P.sb("wdkv", [128, 8, 352], BF16)
    o.DMA('pool', C['wuk'][:], I["w_uk"].rearrange("(k p) n -> p k n", p=128), w=['wuk'])
    o.DMA('pool', C['wuv'][:], I["w_uv"].rearrange("(k p) n -> p k n", p=128), w=['wuv'])
    o.MSET('pool', C['wdkv'][:, :, 256:320], 0.0, w=['wdkv'])
    o.DMA('pool', C['wdkv'][:, :, 0:256], I["w_dkv"].rearrange("(k p) n -> p k n", p=128)[:, :, 0:256], w=['wdkv'])
    o.DMA('pool', C['wdkv'][:, :, 320:352], I["w_dkv"].rearrange("(k p) n -> p k n", p=128)[:, :, 256:288], w=['wdkv'])
    ps = [P.ps("ps%d" % i, [128, 512], F32) for i in range(8)]
    C['ps'] = ps
    C['G'] = load_small(P, o, I, C)
    C['rA'] = Rot([0, 1])
    C['rS'] = Rot([2, 3])
    C['rO'] = Rot([4, 5])
    C['rG'] = Rot([6, 7])
    C['tmp'] = P.sb("tmpn", [128, 512], F32)
    C['rstd'] = P.sb("rstd", [128, 512], F32)
    C['rtmp'] = Rot([0])
    return C


def norm_fm(o, C, xaps, gcols, haps, np_, Dn, tb, rkeys, hkey, ones, eps=EPS, sq=None, split=False):
    ps = C['ps']
    b = C['rA'].next()
    pk = ('ps', b)
    nch = len(xaps)
    sqaps, sqkey = (haps, hkey) if sq is None else sq
    if split and nch == 8:
        order = [0, 2, 4, 6, 1, 3, 5, 7]
        for c in order:
            rk_c = [rkeys[c]] if len(rkeys) == 8 else rkeys
            if c % 2 == 0:
                o.ACT(sqaps[c], xaps[c], AF.Square, r=rk_c, w=[(sqkey, 'sq', c)])
            else:
                o.TT('dve', sqaps[c], xaps[c], xaps[c], ALU.mult, r=rk_c, w=[(sqkey, 'sq', c)])
        for i, c in enumerate(order):
            o.MM(ps[b][0:np_, 0:tb], ones, sqaps[c], i == 0, i == nch - 1, r=[(sqkey, 'sq', c), 'ones', 'blk2'], w=[pk])
        sqdeps = [(sqkey, 'sq', c) for c in range(8)]
    else:
        for c in range(nch):
            o.ACT(sqaps[c], xaps[c], AF.Square, r=rkeys, w=[sqkey])
        for c in range(nch):
            o.MM(ps[b][0:np_, 0:tb], ones, sqaps[c], c == 0, c == nch - 1, r=[sqkey, 'ones', 'blk2'], w=[pk])
        sqdeps = [sqkey]
    o.ACT(C['tmp'][0:np_, 0:tb], ps[b][0:np_, 0:tb], AF.Ln, r=[pk], w=['tmpn'], scale=1.0 / Dn, bias=eps)
    o.ACT(C['rstd'][0:np_, 0:tb], C['tmp'][0:np_, 0:tb], AF.Exp, r=['tmpn'], w=['rstd'], scale=-0.5)
    for c in range(nch):
        o.STT('dve', haps[c], xaps[c], gcols[c], C['rstd'][0:np_, 0:tb], ALU.mult, ALU.mult,
              r=list(rkeys) + ['rstd', 'gains'] + sqdeps, w=[hkey])


def prompt_program(nc, es, I, O, NT, SEG, nlayers, SC=None):
    P = Prog(nc, es)
    o = Ops(P)
    C = alloc_common(P, o, I)
    ps = C['ps']
    G = C['G']
    ident, ones, blk2, perm, tri = C['ident'], C['ones'], C['blk2'], C['perm'], C['tri']
    SCM = {"k": nc.dram_tensor("sc_memKT", [4, 128, 512], BF16).ap(), "v": nc.dram_tensor("sc_Vaug", [4, 128, 1024], BF16).ap()}
    NB = SEG // TB
    NSEG = NT // SEG
    NKC = NT // 128
    tb = TB

    xT = P.sb("xT", [128, 8, SEG], F32)
    hT = P.sb("hT", [128, 8, SEG], BF16)
    win = P.sb("win", [128, 8, 1024], BF16)
    wout = P.sb("wout", [128, 10, 1024], BF16)
    wmisc = P.sb("wmisc", [128, 3456], BF16)
    NRG, NRO = 2, 3
    ringG = [P.sb("ringG%d" % i, [128, 4096], BF16) for i in range(NRG)]
    ringO = [P.sb("ringO%d" % i, [128, 2048], BF16) for i in range(NRO)]
    cT = P.sb("cT", [128, 2, NT], BF16)
    KT = [P.sb("KT%d" % i, [96, NT], BF16) for i in range(2)]
    Vh = [P.sb("Vh%d" % i, [128, NKC, 128], BF16) for i in range(2)]
    kinv = P.sb("kinv", [128, NKC, 12], F32)
    cat = P.sb("cat", [128, 10, tb], BF16)
    qm = P.sb("qm", [128, 2, tb], F32)
    qn = P.sb("qn", [128, 2, tb], BF16)
    PT = [P.sb("PT%d" % i, [128, tb], BF16) for i in range(4)]
    rPT = Rot([0, 1, 2, 3])
    rc = P.sb("rc", [128, tb], F32)
    cosb = P.sb("cosb", [96, tb], F32)
    sinb = P.sb("sinb", [96, tb], F32)
    memKT = P.sb("memKT", [128, 2, 256], BF16)
    Vaug = P.sb("Vaug", [128, 2, 4, 128], BF16)
    halo = P.sb("halo", [96, 2, 8, 16], F32)
    AFN, ABN = 7616, 4096
    arf = P.sb("arena_f", [128, AFN], F32)
    arb = P.sb("arena_b", [128, ABN], BF16)

    def cv(ar, off, parts, n, pat=None, **kw):
        v = ar[0:parts, off:off + n]
        return v.rearrange(pat, **kw) if pat else v
    xin = cv(arf, 0, 128, 1024)
    xinb = [xin, cv(arf, 4608, 128, 1024)]
    memT = cv(arf, 1024, 128, 2048, "p (c t) -> p c t", t=256)
    kf = cv(arf, 3072, 128, 512, "p (c t) -> p c t", t=256)
    kout = cv(arf, 3584, 128, 512, "p (c t) -> p c t", t=256)
    vout = cv(arf, 4096, 128, 512, "p (c t) -> p c t", t=256)
    hmT = cv(arb, 0, 128, 2048, "p (c t) -> p c t", t=256)
    E_ = 16 + tb
    U = cv(arf, 0, 96, 8 * E_, "p (c t) -> p c t", t=E_)
    tA = cv(arf, 8 * E_, 96, 2 * E_, "p (c t) -> p c t", t=E_)
    tB_ = cv(arf, 10 * E_, 96, 2 * E_, "p (c t) -> p c t", t=E_)
    t1a = cv(arf, 12 * E_, 96, 512)
    pso = cv(arf, 12 * E_ + 512, 16, 768)
    dif = cv(arb, 0, 96, 4096, "p (c t) -> p c t", t=tb)
    cq = cv(arf, 0, 128, 1536, "p (c t) -> p c t", t=tb)
    ckf = cv(arf, 0, 128, 1024, "p (c t) -> p c t", t=tb)
    krf = cv(arf, 1024, 96, 512)
    t1 = cv(arf, 1536, 96, 512)
    t2 = cv(arf, 2048, 96, 512)
    kvo = cv(arf, 2560, 128, 288)
    ssn = cv(arf, 2848, 128, 16)
    cqn = cv(arb, 0, 128, 1536, "p (c t) -> p c t", t=tb)
    QT = [cv(arb, 1536, 96, 512), cv(arb, 2048, 96, 512)]
    qg = cv(arb, 2560, 96, 512)
    krb = cv(arb, 1536, 96, 512)
    sqn = cv(arb, 2048, 128, 768)
    sqr = cv(arb, 2816, 128, 32)
    sg = [cv(arf, 0, 128, 512), cv(arf, 512, 128, 512)]
    stg = [cv(arf, 1024, 128, 512), cv(arf, 1536, 128, 512)]
    aT = [cv(arb, 0, 128, 1024, "p (c t) -> p c t", t=tb), cv(arb, 1024, 128, 1024, "p (c t) -> p c t", t=tb)]
    wmem = ringG[0]
    ARK = ['xin', 'xin1', 'memT', 'kf', 'kout', 'vout', 'hmT', 'U', 'tA', 'tB', 't1', 't2', 'pso', 'dif', 'cq', 'ckf', 'krf', 'kvo',
           'ssn', 'ssn2', 'cqn', ('QT', 0), ('QT', 1), 'qg', 'krb', 'sqn', 'sqr', ('sg', 0), ('sg', 1), ('aT', 0), ('aT', 1), ('stg', 0), ('stg', 1)]
    dummy = P.sb("dummyb", [128, 8], F32)

    def barrier():
        P.op('pool', lambda e: e.memset(dummy[:], 0.0), r=ARK, w=ARK)

    o.MSET('pool', Vaug[:], 1.0, w=['Vaug'])
    for i in range(2):
        o.MSET('pool', Vh[i][:], 1.0, w=[('Vh', i)])

    def wsrc(first):
        return (I, 'pool') if (SC is None or first) else (SC, 'sp')

    def wb(first, dst, src_sb, rkey, wkey):
        if SC is not None and first:
            o.DMA('sp', dst, src_sb, r=[rkey], w=[wkey], out=True)
    NFG = DFF // 256

    def XK(j):
        return [('xT', j, c) for c in range(8)]

    ring_state = {'g': 0, 'o': 0}

    def load_G(l, fg, first):
        src, q = wsrc(first)
        s_ = ring_state['g'] % NRG
        ring_state['g'] += 1
        rg = ringG[s_]
        k = ('rg', s_)
        gv = rg[:, 0:2048].rearrange("p (k n) -> p k n", n=256)
        uv = rg[:, 2048:4096].rearrange("p (k n) -> p k n", n=256)
        wi = src["w_ffn_in"][l].rearrange("(k p) n -> p k n", p=128)
        rk = [] if src is I else [('scw', 'g', l, fg)]
        o.DMA(q, gv, wi[:, :, fg * 256:(fg + 1) * 256], r=rk, w=[k])
        o.DMA(q, uv, wi[:, :, DFF + fg * 256:DFF + (fg + 1) * 256], r=rk, w=[k])
        if SC is not None and first:
            so = SC["w_ffn_in"][l].rearrange("(k p) n -> p k n", p=128)
            wb(first, so[:, :, fg * 256:(fg + 1) * 256], gv, k, ('scw', 'g', l, fg))
            wb(first, so[:, :, DFF + fg * 256:DFF + (fg + 1) * 256], uv, k, ('scw', 'g', l, fg))
        return gv, uv, k

    def load_O(l, fg, first):
        src, q = wsrc(first)
        s_ = ring_state['o'] % NRO
        ring_state['o'] += 1
        k = ('ro', s_)
        ov = ringO[s_][:, 0:2048].rearrange("p (k n) -> p k n", n=1024)
        wo = src["w_ffn_out"][l].rearrange("(k p) n -> p k n", p=128)
        rk = [] if src is I else [('scw', 'o', l, fg)]
        o.DMA(q, ov, wo[:, fg * 2:fg * 2 + 2, :], r=rk, w=[k])
        if SC is not None and first:
            so = SC["w_ffn_out"][l].rearrange("(k p) n -> p k n", p=128)
            wb(first, so[:, fg * 2:fg * 2 + 2, :], ov, k, ('scw', 'o', l, fg))
        return ov, k

    def load_mix_weights(l, first):
        W, WQ = wsrc(first)
        rk = [] if W is I else [('scw', 'mix', l)]
        wk = ('scw', 'mix', l)
        if l < 2:
            o.DMA(WQ, win[:], W["w_in_a"][l].rearrange("(k p) n -> p k n", p=128), r=rk, w=['win'])
            o.DMA(WQ, wout[0:96, 0:8, :], W["w_out"][l][0:768, :].rearrange("(k p) n -> p k n", p=96), r=rk, w=['wout'])
            o.DMA(WQ, wout[:, 8:10, :], W["w_out"][l][768:1024, :].rearrange("(k p) n -> p k n", p=128), r=rk, w=['wout'])
            gv = wmisc[0:96, 0:1536].rearrange("p (k n) -> p k n", n=192)
            o.DMA(WQ, gv, W["w_pool_grp"][l].rearrange("g (c p) e -> p (g c) e", p=96), r=rk, w=['wmisc'])
            if SC is not None and first:
                wb(first, SC["w_in_a"][l].rearrange("(k p) n -> p k n", p=128), win[:], 'win', wk)
                wb(first, SC["w_out"][l][0:768, :].rearrange("(k p) n -> p k n", p=96), wout[0:96, 0:8, :], 'wout', wk)
                wb(first, SC["w_out"][l][768:1024, :].rearrange("(k p) n -> p k n", p=128), wout[:, 8:10, :], 'wout', wk)
                wb(first, SC["w_pool_grp"][l].rearrange("g (c p) e -> p (g c) e", p=96), gv, 'wmisc', wk)
        else:
            j = l - 2
            o.DMA(WQ, win[:, :, 0:640], W["w_in_b"][j].rearrange("(k p) n -> p k n", p=128), r=rk, w=['win'])
            o.DMA(WQ, wout[:, 0:8, :], W["w_out"][l].rearrange("(k p) n -> p k n", p=128), r=rk, w=['wout'])
            uq = wmisc[:, 0:3456].rearrange("p (k n) -> p k n", n=1152)
            o.DMA(WQ, uq, W["w_uq"][j].rearrange("(k p) n -> p k n", p=128), r=rk, w=['wmisc'])
            if SC is not None and first:
                wb(first, SC["w_in_b"][j].rearrange("(k p) n -> p k n", p=128), win[:, :, 0:640], 'win', wk)
                wb(first, SC["w_out"][l].rearrange("(k p) n -> p k n", p=128), wout[:, 0:8, :], 'wout', wk)
                wb(first, SC["w_uq"][j].rearrange("(k p) n -> p k n", p=128), uq, 'wmisc', wk)

    def mem_kv(l, write_out):
        if not write_out:
            o.DMA('sp', memKT[:], SCM["k"][l].rearrange("p (c m) -> p c m", m=256), r=[('scm', l)], w=['memKT'])
            o.DMA('sp', Vaug[:], SCM["v"][l].rearrange("p (c h d) -> p c h d", h=4, d=128), r=[('scm', l)], w=['Vaug'])
            return
        barrier()
        o.DMA('pool', wmem[:, 0:4096].rearrange("p (k n) -> p k n", n=512),
              I["w_mem_kv"][l].rearrange("(k p) n -> p k n", p=128), w=[('rg', 0)])
        wm = wmem[:, 0:4096].rearrange("p (k n) -> p k n", n=512)
        for mt in range(2):
            o.DMA('sp', xin[:], I["memp"][mt * 128:(mt + 1) * 128, :], w=['xin'])
            for c in range(8):
                b = C['rA'].next()
                o.TR(ps[b][:, 0:128], xin[:, c * 128:(c + 1) * 128], ident[:], r=['xin', 'ident'], w=[('ps', b)])
                o.CP('dve' if c % 2 else 'act', memT[:, c, mt * 128:(mt + 1) * 128], ps[b][:, 0:128], r=[('ps', b)], w=['memT'])
        norm_fm(o, C, [memT[:, c, :] for c in range(8)], [G['gmem'][:, l * 8 + c:l * 8 + c + 1] for c in range(8)],
                [hmT[:, c, :] for c in range(8)], 128, D, 256, ['memT'], 'hmT', ones[:])
        for hp in range(2):
            b = C['rA'].next()
            for k in range(8):
                o.MM(ps[b][:, 0:256], wm[:, k, hp * 128:(hp + 1) * 128], hmT[:, k, :], k == 0, k == 7,
                     r=['hmT', ('rg', 0)], w=[('ps', b)])
            norm_fm(o, C, [ps[b][:, 0:256]], [G['gmk'][:, l:l + 1]], [kf[:, hp, :]], 128, 64, 256, [('ps', b)], 'kf', blk2[:],
                    sq=([qn[:, 0, 0:256]], 'qn'))
        o.CP('dve', memKT[:], kf[:], r=['kf'], w=['memKT'])
        for mc in range(2):
            b = C['rA'].next()
            for k in range(8):
                o.MM(ps[b][:, 0:256], hmT[:, k, mc * 128:(mc + 1) * 128], wm[:, k, 256:512], k == 0, k == 7,
                     r=['hmT', ('rg', 0)], w=[('ps', b)])
            if write_out:
                o.CP('act', vout[:, mc, :], ps[b][:, 0:256], r=[('ps', b)], w=['vout'])
            for h in range(4):
                off = 0 if h % 2 == 0 else 64
                o.CP('dve', Vaug[:, mc, h, off:off + 64], ps[b][:, h * 64:(h + 1) * 64], r=[('ps', b)], w=['Vaug'])
        o.DMA('sp', SCM["k"][l].rearrange("p (c m) -> p c m", m=256), memKT[:], r=['memKT'], w=[('scm', l)])
        o.DMA('sp', SCM["v"][l].rearrange("p (c h d) -> p c h d", h=4, d=128), Vaug[:], r=['Vaug'], w=[('scm', l)])
        if write_out:
            for hp in range(2):
                for mc in range(2):
                    b = C['rA'].next()
                    o.TR(ps[b][:, 0:128], kf[:, hp, mc * 128:(mc + 1) * 128], ident[:], r=['kf', 'ident'], w=[('ps', b)])
                    o.CP('act', kout[:, mc, hp * 128:(hp + 1) * 128], ps[b][:, 0:128], r=[('ps', b)], w=['kout'])
            o.DMA('sp', O["mk_p"][l].rearrange("(m p) f -> p m f", p=128), kout[:], r=['kout'], out=True)
            o.DMA('sp', O["mv_p"][l].rearrange("(m p) f -> p m f", p=128), vout[:], r=['vout'], out=True)

    def mem_attend(l, j, catbase):
        for hp in range(2):
            norm_fm(o, C, [qm[:, hp, :]], [G['gmq'][:, l:l + 1]], [qn[:, hp, :]], 128, 64, tb, ['qm'], 'qn', blk2[:])
        st = {}

        def SS(h):
            hp, base = h // 2, (h % 2) * 64
            pts = []
            for mc in range(2):
                bsc = C['rS'].next()
                o.MM(ps[bsc][:, 0:tb], memKT[base:base + 64, hp, mc * 128:(mc + 1) * 128], qn[base:base + 64, hp, :], True, True,
                     r=['memKT', 'qn'], w=[('ps', bsc)])
                pi = rPT.next()
                o.ACT(PT[pi][:, :], ps[bsc][:, 0:tb], AF.Exp, r=[('ps', bsc)], w=[('PT', pi)], scale=0.125)
                pts.append(pi)
            st[h] = pts

        def PVV(h):
            hp, base = h // 2, (h % 2) * 64
            pts = st[h]
            bo = C['rO'].next()
            for mc in range(2):
                o.MM(ps[bo][:, 0:tb], Vaug[:, mc, h, :], PT[pts[mc]][:, :], mc == 0, mc == 1,
                     r=['Vaug', ('PT', pts[mc])], w=[('ps', bo)])
            vr = slice(base, base + 64)
            dr = slice(64 - base, 128 - base)
            o.ACT(rc[dr, :], ps[bo][dr, 0:tb], AF.Ln, r=[('ps', bo)], w=['rc'])
            o.ACT(rc[dr, :], rc[dr, :], AF.Exp, r=['rc'], w=['rc'], scale=-1.0)
            o.TT('dve', cat[vr, catbase + hp, :], ps[bo][vr, 0:tb], rc[dr, :], ALU.mult, r=[('ps', bo), 'rc'], w=['cat'])

        SS(0)
        for h in range(4):
            if h + 1 < 4:
                SS(h + 1)
            PVV(h)

    def out_proj(l, j):
        blk = slice(j * tb, (j + 1) * tb)
        if l < 2:
            chunks = [(96, c) for c in range(8)] + [(128, 8), (128, 9)]
        else:
            chunks = [(128, c) for c in range(8)]
        for oc in range(8):
            b = C['rA'].next()
            for i, (kk, c) in enumerate(chunks):
                o.MM(ps[b][:, 0:tb], wout[0:kk, c, oc * 128:(oc + 1) * 128], cat[0:kk, c, :], i == 0, i == len(chunks) - 1,
                     r=['wout', 'cat'], w=[('ps', b)])
            o.TT('dve', xT[:, oc, blk], xT[:, oc, blk], ps[b][:, 0:tb], ALU.add, r=[('ps', b), ('xT', j, oc)], w=[('xT', j, oc)])

    def mix_a(l, sg_, j):
        barrier()
        blk = slice(j * tb, (j + 1) * tb)
        jg = sg_ * NB + j
        norm_fm(o, C, [xT[:, c, blk] for c in range(8)], [G['gmix'][:, l * 8 + c:l * 8 + c + 1] for c in range(8)],
                [hT[:, c, blk] for c in range(8)], 128, D, tb, XK(j), ('hT', j), ones[:], split=True)
        if jg == 0:
            o.MSET('pool', U[:, :, 0:16], 0.0, w=['U'])
        elif j == 0:
            o.CP('pool', U[:, :, 0:16], halo[:, l, :, :], r=[('halo', l)], w=['U'])
        for c in range(8):
            b = C['rA'].next()
            for k in range(8):
                o.MM(ps[b][0:96, 0:tb], win[:, k, c * 96:(c + 1) * 96], hT[:, k, blk], k == 0, k == 7,
                     r=['win', ('hT', j)], w=[('ps', b)])
            o.CP('act', U[:, c, 16:16 + tb], ps[b][0:96, 0:tb], r=[('ps', b)], w=['U'])
        for hp in range(2):
            b = C['rA'].next()
            for k in range(8):
                o.MM(ps[b][:, 0:tb], win[:, k, 768 + hp * 128:768 + (hp + 1) * 128], hT[:, k, blk], k == 0, k == 7,
                     r=['win', ('hT', j)], w=[('ps', b)])
            o.CP('act', qm[:, hp, :], ps[b][:, 0:tb], r=[('ps', b)], w=['qm'])
        gw = wmisc[0:96, 0:1536].rearrange("p (k n) -> p k n", n=192)
        E = 16 + tb
        for g in range(4):
            w_ = 2 ** (g + 1)
            src = U[:, 2 * g:2 * g + 2, :]
            bufs = [tA, tB_]
            d = 1
            i = 0
            srck = 'U'
            while d < w_:
                dst = bufs[i % 2]
                dk = 'tA' if i % 2 == 0 else 'tB'
                lo = 2 * d
                o.TT('pool', dst[:, :, lo:E], src[:, :, lo:E], src[:, :, lo - d:E - d], ALU.add, r=[srck], w=[dk])
                src, srck = dst, dk
                d *= 2
                i += 1
            o.STT('dve', dif[:, 2 * g:2 * g + 2, :], src[:, :, 16:E], 1.0 / w_, U[:, 2 * g:2 * g + 2, 16:E], ALU.mult, ALU.subtract,
                  r=[srck, 'U'], w=['dif'])
            if jg == 0:
                o.TT('dve', t1a[:, 0:32].rearrange("p (c t) -> p c t", t=16), src[:, :, 16:32], C['rcnt'][:, 2 * g:2 * g + 2, :],
                     ALU.mult, r=[srck, 'rcnt'], w=['t1'])
                o.TT('dve', dif[:, 2 * g:2 * g + 2, 0:16], t1a[:, 0:32].rearrange("p (c t) -> p c t", t=16),
                     U[:, 2 * g:2 * g + 2, 16:32], ALU.subtract, r=['t1', 'U', 'dif'], w=['dif'])
            for ec in range(2):
                b = C['rA'].next()
                for cc in range(2):
                    o.MM(ps[b][0:96, 0:tb], gw[:, 2 * g + cc, ec * 96:(ec + 1) * 96], dif[:, 2 * g + cc, :], cc == 0, cc == 1,
                         r=['wmisc', 'dif'], w=[('ps', b)])
                ch = 2 * g + ec
                o.TSM('dve', cat[0:96, ch, :], ps[b][0:96, 0:tb], G['pscale'][:, l * 8 + ch:l * 8 + ch + 1],
                      r=[('ps', b), 'gains'], w=['cat'])
        if jg == NSEG * NB - 1:
            for c in range(8):
                b = C['rA'].next()
                o.TR(ps[b][0:15, 0:96], U[:, c, E - 15:E], ident[0:96, 0:96], r=['U', 'ident'], w=[('ps', b)])
                o.CP('act', pso[0:15, c * 96:(c + 1) * 96], ps[b][0:15, 0:96], r=[('ps', b)], w=['pso'])
            o.DMA('sp', O["pool_p"][l], pso[0:15, :], r=['pso'], out=True)
        elif j == NB - 1:
            o.CP('pool', halo[:, l, :, :], U[:, :, tb:tb + 16], r=['U'], w=[('halo', l)])
        else:
            o.CP('pool', tA[:, 0, 0:128].rearrange("p (c t) -> p c t", t=16), U[:, :, tb:tb + 16], r=['U'], w=['tA'])
            o.CP('pool', U[:, :, 0:16], tA[:, 0, 0:128].rearrange("p (c t) -> p c t", t=16), r=['tA'], w=['U'])
        mem_attend(l, j, 8)
        out_proj(l, j)

    def kv_stage(sg_, j):
        barrier()
        blk = slice(j * tb, (j + 1) * tb)
        jg = sg_ * NB + j
        gb = slice(jg * tb, (jg + 1) * tb)
        wd = C['wdkv']
        norm_fm(o, C, [xT[:, c, blk] for c in range(8)], [G['gkv'][:, c:c + 1] for c in range(8)],
                [hT[:, c, blk] for c in range(8)], 128, D, tb, XK(j), ('hT', j), ones[:], split=True)
        o.DMA('sp', cosb[:], I["c_cos"][:, gb], w=['cosb'])
        o.DMA('sp', sinb[:], I["c_sin"][:, gb], w=['sinb'])
        for cc in range(2):
            b = C['rA'].next()
            for k in range(8):
                o.MM(ps[b][:, 0:tb], wd[:, k, cc * 128:(cc + 1) * 128], hT[:, k, blk], k == 0, k == 7,
                     r=['wdkv', ('hT', j)], w=[('ps', b)])
            o.CP('act', ckf[:, cc, :], ps[b][:, 0:tb], r=[('ps', b)], w=['ckf'])
        br = C['rG'].next()
        for k in range(8):
            o.MM(ps[br][0:96, 0:tb], wd[:, k, 256:352], hT[:, k, blk], k == 0, k == 7, r=['wdkv', ('hT', j)], w=[('ps', br)])
        o.MSET('pool', krb[0:64, :], 0.0, w=['krb'])
        o.CP('act', krb[64:96, :], ps[br][64:96, 0:tb], r=[('ps', br)], w=['krb'])
        b2 = C['rG'].next()
        o.MM(ps[b2][0:96, 0:tb], perm[0:96, 0:96], krb[0:96, :], True, True, r=['perm', 'krb'], w=[('ps', b2)])
        o.TT('dve', t1[64:96, :], ps[br][64:96, 0:tb], cosb[64:96, :], ALU.mult, r=[('ps', br), 'cosb'], w=['t1'])
        o.TT('dve', t2[64:96, :], ps[b2][64:96, 0:tb], sinb[64:96, :], ALU.mult, r=[('ps', b2), 'sinb'], w=['t2'])
        o.TT('dve', krf[64:96, :], t1[64:96, :], t2[64:96, :], ALU.add, r=['t1', 't2'], w=['krf'])
        for i in range(2):
            o.CP('pool', KT[i][64:96, gb], krf[64:96, :], r=['krf'], w=[('KT', i)])
        norm_fm(o, C, [ckf[:, cc, :] for cc in range(2)], [G['gkvl'][:, cc:cc + 1] for cc in range(2)],
                [cqn[:, cc, :] for cc in range(2)], 128, 256, tb, ['ckf'], 'cqn', ones[:])
        o.CP('pool', cT[:, :, gb], cqn[:, 0:2, :], r=['cqn'], w=['cT'])
        for cc in range(2):
            o.STT('dve', ckf[:, cc, :], ckf[:, cc, :], G['gkvl'][:, cc:cc + 1], C['rstd'][:, 0:tb], ALU.mult, ALU.mult,
                  r=['ckf', 'rstd', 'cqn'], w=['ckf'])
        for tt in range(tb // 128):
            ts_ = slice(tt * 128, (tt + 1) * 128)
            kc = jg * 4 + tt
            b = C['rA'].next()
            o.TR(ps[b][:, 0:128], ckf[:, 0, ts_], ident[:], r=['ckf', 'ident'], w=[('ps', b)])
            o.TR(ps[b][:, 128:256], ckf[:, 1, ts_], ident[:], r=['ckf', 'ident'], w=[('ps', b)])
            o.TR(ps[b][:, 256:288], krf[64:96, ts_], ident[64:96, 64:96], r=['krf', 'ident'], w=[('ps', b)])
            o.CP('act', kvo[:, :], ps[b][:, 0:288], r=[('ps', b)], w=['kvo'])
            o.DMA('sp', O["kv_p"][jg * tb + tt * 128: jg * tb + (tt + 1) * 128, :], kvo[:, :], r=['kvo'], out=True)
            b3, b4 = C['rG'].next(), C['rG'].next()
            for cc in range(2):
                o.MM(ps[b3][:, 0:512], cT[:, cc, jg * tb + tt * 128: jg * tb + (tt + 1) * 128], C['wuk'][:, cc, 0:512], cc == 0, cc == 1,
                     r=['cT', 'wuk'], w=[('ps', b3)])
            for cc in range(2):
                o.MM(ps[b4][:, 0:256], cT[:, cc, jg * tb + tt * 128: jg * tb + (tt + 1) * 128], C['wuk'][:, cc, 512:768], cc == 0, cc == 1,
                     r=['cT', 'wuk'], w=[('ps', b4)])
            o.ACT(sqn[:, 0:512], ps[b3][:, 0:512], AF.Square, r=[('ps', b3)], w=['sqn'])
            o.ACT(sqn[:, 512:768], ps[b4][:, 0:256], AF.Square, r=[('ps', b4)], w=['sqn'])
            o.RED(ssn[:, 0:12], sqn[:, :].rearrange("p (h d) -> p h d", d=64), r=['sqn'], w=['ssn'])
            o.ACT(sqr[:, 0:32], kvo[:, 256:288], AF.Square, r=['kvo'], w=['sqr'])
            o.RED(ssn[:, 12:13], sqr[:, 0:32].rearrange("p (h d) -> p h d", d=32), r=['sqr'], w=['ssn2'])
            o.TSA('dve', ssn[:, 0:12], ssn[:, 0:12], ssn[:, 12:13], r=['ssn', 'ssn2'], w=['ssn'])
            o.ACT(ssn[:, 0:12], ssn[:, 0:12], AF.Ln, r=['ssn'], w=['ssn'], scale=1.0, bias=96.0 * EPS)
            o.ACT(kinv[:, kc, :], ssn[:, 0:12], AF.Exp, r=['ssn'], w=['kinv'], scale=-0.5)

    def mix_b(l, sg_, j):
        barrier()
        jl = l - 2
        blk = slice(j * tb, (j + 1) * tb)
        jg = sg_ * NB + j
        gb = slice(jg * tb, (jg + 1) * tb)
        norm_fm(o, C, [xT[:, c, blk] for c in range(8)], [G['gmix'][:, l * 8 + c:l * 8 + c + 1] for c in range(8)],
                [hT[:, c, blk] for c in range(8)], 128, D, tb, XK(j), ('hT', j), ones[:], split=True)
        o.DMA('sp', cosb[:], I["c_cos"][:, gb], w=['cosb'])
        o.DMA('sp', sinb[:], I["c_sin"][:, gb], w=['sinb'])
        for c in range(3):
            b = C['rA'].next()
            for k in range(8):
                o.MM(ps[b][:, 0:tb], win[:, k, c * 128:(c + 1) * 128], hT[:, k, blk], k == 0, k == 7,
                     r=['win', ('hT', j)], w=[('ps', b)])
            o.CP('act', cq[:, c, :], ps[b][:, 0:tb], r=[('ps', b)], w=['cq'])
        for hp in range(2):
            b = C['rA'].next()
            for k in range(8):
                o.MM(ps[b][:, 0:tb], win[:, k, 384 + hp * 128:384 + (hp + 1) * 128], hT[:, k, blk], k == 0, k == 7,
                     r=['win', ('hT', j)], w=[('ps', b)])
            o.CP('act', qm[:, hp, :], ps[b][:, 0:tb], r=[('ps', b)], w=['qm'])
        norm_fm(o, C, [cq[:, c, :] for c in range(3)], [G['gql'][:, jl * 3 + c:jl * 3 + c + 1] for c in range(3)],
                [cqn[:, c, :] for c in range(3)], 128, 384, tb, ['cq'], 'cqn', ones[:])
        uq = wmisc[:, 0:3456].rearrange("p (k n) -> p k n", n=1152)
        nk = (jg + 1) * 4
        BQ, BN = 6, 7

        def prepA(h):
            ki = h % 2
            for k in range(3):
                o.MM(ps[BQ][0:96, 0:tb], uq[:, k, h * 96:(h + 1) * 96], cqn[:, k, :], k == 0, k == 2,
                     r=['wmisc', 'cqn'], w=[('ps', BQ)])
            o.ACT(qg[:, :], ps[BQ][0:96, 0:tb], AF.Square, r=[('ps', BQ)], w=['qg'])
            for kb in range(jg + 1):
                bk = C['rA'].next()
                for cc in range(2):
                    o.MM(ps[bk][0:64, 0:tb], C['wuk'][:, cc, h * 64:(h + 1) * 64], cT[:, cc, kb * tb:(kb + 1) * tb], cc == 0, cc == 1,
                         r=['wuk', 'cT'], w=[('ps', bk)])
                o.CP('dve', KT[ki][0:64, kb * tb:(kb + 1) * tb], ps[bk][0:64, 0:tb], r=[('ps', bk)], w=[('KT', ki)])
            off = 0 if h % 2 == 0 else 64
            for kg in range(nk // 8 + (1 if nk % 8 else 0)):
                bk = C['rA'].next()
                n8 = min(8, nk - kg * 8)
                for q8 in range(n8):
                    kc = kg * 8 + q8
                    for cc in range(2):
                        o.MM(ps[bk][:, q8 * 64:(q8 + 1) * 64], cT[:, cc, kc * 128:(kc + 1) * 128], C['wuv'][:, cc, h * 64:(h + 1) * 64],
                             cc == 0, cc == 1, r=['wuv', 'cT'], w=[('ps', bk)])
                o.CP('dve', Vh[ki][:, kg * 8:kg * 8 + n8, off:off + 64],
                     ps[bk][:, 0:n8 * 64].rearrange("p (k d) -> p k d", d=64), r=[('ps', bk)], w=[('Vh', ki)])

        def prepN1(h):
            o.MM(ps[BN][0:96, 0:tb], ones[0:96, 0:96], qg[:, :], True, True, r=['qg', 'ones'], w=[('ps', BN)])
            o.ACT(C['tmp'][0:96, 0:tb], ps[BN][0:96, 0:tb], AF.Ln, r=[('ps', BN)], w=['tmpn'], scale=1.0 / 96, bias=EPS)
            o.ACT(C['rstd'][0:96, 0:tb], C['tmp'][0:96, 0:tb], AF.Exp, r=['tmpn'], w=['rstd'], scale=-0.5)
            o.STT('dve', qg[:, :], ps[BQ][0:96, 0:tb], G['gq'][:, jl:jl + 1], C['rstd'][0:96, 0:tb], ALU.mult, ALU.mult,
                  r=[('ps', BQ), 'rstd', 'gains', 'qg'], w=['qg'])

        def prepN2(h):
            qi = h % 2
            o.MM(ps[BN][0:96, 0:tb], perm[:, :], qg[:, :], True, True, r=['perm', 'qg'], w=[('ps', BN)])
            o.STT('dve', t1[:, :], qg[:, :], G['gk'][:, 0:1], cosb[:, :], ALU.mult, ALU.mult, r=['qg', 'cosb', 'gains'], w=['t1'])
            o.STT('dve', t2[:, :], ps[BN][0:96, 0:tb], G['gk'][:, 0:1], sinb[:, :], ALU.mult, ALU.mult,
                  r=[('ps', BN), 'sinb', 'gains'], w=['t2'])
            o.TT('pool', QT[qi][:, :], t1[:, :], t2[:, :], ALU.add, r=['t1', 't2'], w=[('QT', qi)])

        def attend(h):
            qi = ki = h % 2
            off = 0 if h % 2 == 0 else 64
            bo = C['rO'].next()
            pend = {}

            def S(kc):
                r_ = kc - 4 * jg
                q0 = 128 * max(r_, 0)
                bs = C['rS'].next()
                o.MM(ps[bs][:, q0:tb], KT[ki][0:96, kc * 128:(kc + 1) * 128], QT[qi][:, q0:tb], True, True,
                     r=[('KT', ki), ('QT', qi)], w=[('ps', bs)])
                pi = rPT.next()
                o.ACT(PT[pi][:, q0:tb], ps[bs][:, q0:tb], AF.Exp, r=[('ps', bs), 'kinv'], w=[('PT', pi)],
                      scale=kinv[:, kc, h:h + 1])
                if r_ >= 0:
                    o.TT('pool', PT[pi][:, q0:q0 + 128], PT[pi][:, q0:q0 + 128], tri[:, :], ALU.mult, r=[('PT', pi), 'tri'], w=[('PT', pi)])
                pend[kc] = (pi, q0)

            def PV(kc):
                pi, q0 = pend.pop(kc)
                o.MM(ps[bo][:, q0:tb], Vh[ki][:, kc, :], PT[pi][:, q0:tb], kc == 0, kc == nk - 1,
                     r=[('Vh', ki), ('PT', pi)], w=[('ps', bo)])

            hook2 = (nk - 3) if nk >= 8 else (nk - 1)
            S(0)
            if h + 1 < 12:
                prepA(h + 1)
            for kc in range(nk):
                if kc + 1 < nk:
                    S(kc + 1)
                PV(kc)
                if h + 1 < 12:
                    if kc == 0:
                        prepN1(h + 1)
                    if kc == hook2:
                        prepN2(h + 1)
            vr = slice(off, off + 64)
            dr = slice(64 - off, 128 - off)
            o.ACT(rc[dr, :], ps[bo][dr, 0:tb], AF.Ln, r=[('ps', bo)], w=['rc'])
            o.ACT(rc[dr, :], rc[dr, :], AF.Exp, r=['rc'], w=['rc'], scale=-1.0)
            o.TT('dve', cat[vr, h // 2, :], ps[bo][vr, 0:tb], rc[dr, :], ALU.mult, r=[('ps', bo), 'rc'], w=['cat'])

        prepA(0)
        prepN1(0)
        prepN2(0)
        for h in range(12):
            attend(h)
        mem_attend(l, j, 6)
        out_proj(l, j)

    def ffn(l, pre, hook=None, first=False):
        barrier()
        for j in range(NB):
            blk = slice(j * tb, (j + 1) * tb)
            norm_fm(o, C, [xT[:, c, blk] for c in range(8)], [G['gffn'][:, l * 8 + c:l * 8 + c + 1] for c in range(8)],
                    [hT[:, c, blk] for c in range(8)], 128, D, tb, XK(j), ('hT', j), ones[:], split=True)
        rOut = Rot([0, 1, 6, 7])
        rStg = Rot([0, 1])
        pendG, pendO = list(pre[0]), list(pre[1])
        nx = {'g': len(pendG), 'o': len(pendO)}
        units = [(fg, j) for fg in range(NFG) for j in range(NB)]
        grpG, grpO = {}, {}

        def GU(u):
            fg, j = units[u]
            if fg not in grpG:
                grpG[fg] = pendG.pop(0)
            gv, uv, rk = grpG[fg]
            blk = slice(j * tb, (j + 1) * tb)
            ai = u % 2
            for cc in range(2):
                bg, bu = 2 + cc, 4 + cc
                for k in range(8):
                    o.MM(ps[bg][:, 0:tb], gv[:, k, cc * 128:(cc + 1) * 128], hT[:, k, blk], k == 0, k == 7,
                         r=[rk, ('hT', j)], w=[('ps', bg)])
                for k in range(8):
                    o.MM(ps[bu][:, 0:tb], uv[:, k, cc * 128:(cc + 1) * 128], hT[:, k, blk], k == 0, k == 7,
                         r=[rk, ('hT', j)], w=[('ps', bu)])
                o.ACT(sg[cc][:, :], ps[bg][:, 0:tb], AF.Silu, r=[('ps', bg)], w=[('sg', cc)])
                o.TT('dve', aT[ai][:, cc, :], sg[cc][:, :], ps[bu][:, 0:tb], ALU.mult, r=[('sg', cc), ('ps', bu)], w=[('aT', ai)])
            if j == NB - 1 and nx['g'] < NFG:
                pendG.append(load_G(l, nx['g'], first))
                nx['g'] += 1

        def OUT(u):
            fg, j = units[u]
            if fg not in grpO:
                grpO[fg] = pendO.pop(0)
            ov, rk = grpO[fg]
            blk = slice(j * tb, (j + 1) * tb)
            ai = u % 2
            for oc in range(8):
                b = rOut.next()
                for cc in range(2):
                    o.MM(ps[b][:, 0:tb], ov[:, cc, oc * 128:(oc + 1) * 128], aT[ai][:, cc, :], cc == 0, cc == 1,
                         r=[rk, ('aT', ai)], w=[('ps', b)])
                if oc % 2 == 0:
                    o.TT('dve', xT[:, oc, blk], xT[:, oc, blk], ps[b][:, 0:tb], ALU.add, r=[('ps', b), ('xT', j, oc)], w=[('xT', j, oc)])
                else:
                    si = rStg.next()
                    o.CP('act', stg[si][:, :], ps[b][:, 0:tb], r=[('ps', b)], w=[('stg', si)])
                    o.TT('pool', xT[:, oc, blk], xT[:, oc, blk], stg[si][:, :], ALU.add, r=[('stg', si), ('xT', j, oc)], w=[('xT', j, oc)])
            if j == NB - 1:
                if nx['o'] < NFG:
                    pendO.append(load_O(l, nx['o'], first))
                    nx['o'] += 1
                if fg == 1 and hook is not None:
                    hook()

        GU(0)
        for u in range(len(units)):
            if u + 1 < len(units):
                GU(u + 1)
            OUT(u)

    for sg_ in range(NSEG):
        barrier()
        for tt in range(SEG // 128):
            xb, xk = xinb[tt % 2], ('xin' if tt % 2 == 0 else 'xin1')
            o.DMA('sp', xb[:], I["xp"][sg_ * SEG + tt * 128: sg_ * SEG + (tt + 1) * 128, :], w=[xk])
            for c in range(8):
                b = C['rA'].next()
                o.TR(ps[b][:, 0:128], xb[:, c * 128:(c + 1) * 128], ident[:], r=[xk, 'ident'], w=[('ps', b)])
                o.CP('dve' if c % 2 else 'act', xT[:, c, tt * 128:(tt + 1) * 128], ps[b][:, 0:128], r=[('ps', b)], w=[('xT', tt // 4, c)])
        for l in range(nlayers):
            mem_kv(l, write_out=(sg_ == 0))
            if l == 0 and sg_ == 0:
                load_mix_weights(0, True)
            pre = []
            for j in range(NB):
                if l < 2:
                    mix_a(l, sg_, j)
                else:
                    mix_b(l, sg_, j)
                if j == 0:
                    pre = ([load_G(l, fg, sg_ == 0) for fg in range(NRG)], [load_O(l, fg, sg_ == 0) for fg in range(NRO)])
            ffn(l, pre, hook=(lambda l=l, sg_=sg_: load_mix_weights((l + 1) % nlayers, sg_ == 0 and l + 1 < nlayers))
                if not (l == nlayers - 1 and sg_ == NSEG - 1) else None, first=(sg_ == 0))
            if l == 1:
                for j in range(NB):
                    kv_stage(sg_, j)
        barrier()
        for tt in range(SEG // 128):
            xb, xk = xinb[tt % 2], ('xin' if tt % 2 == 0 else 'xin1')
            for c in range(8):
                b = C['rA'].next()
                o.TR(ps[b][:, 0:128], xT[:, c, tt * 128:(tt + 1) * 128], ident[:], r=[('xT', tt // 4, c), 'ident'], w=[('ps', b)])
                o.CP('dve' if c % 2 else 'act', xb[:, c * 128:(c + 1) * 128], ps[b][:, 0:128], r=[('ps', b)], w=[xk])
            o.DMA('sp', O["y_p"][sg_ * SEG + tt * 128: sg_ * SEG + (tt + 1) * 128, :], xb[:], r=[xk], out=True)
    P.finish()
    P.emit('p')
    print("prompt program: ops", len(P.ops), "waits", P.nwaits)


def sample_program(nc, es, I, O, nlayers, NPG=128, SC=None):
    P = Prog(nc, es, 'S_')
    o = Ops(P)
    C = alloc_common(P, o, I)
    ps = C['ps']
    G = C['G']
    ident, ones, blk2, perm = C['ident'], C['ones'], C['blk2'], C['perm']
    tb = 32
    NS = 4
    NPT = NS * NPG

    xT = P.sb("s_xT", [128, 8, tb], F32)
    hT = P.sb("s_hT", [128, 8, tb], BF16)
    win = P.sb("s_win", [128, 8, 1024], BF16)
    wout = P.sb("s_wout", [128, 10, 1024], BF16)
    wmisc = P.sb("s_wmisc", [128, 3456], BF16)
    NRING = 3
    ring = [P.sb("s_ring%d" % i, [128, 6144], BF16) for i in range(NRING)]
    cat = P.sb("s_cat", [128, 10, tb], BF16)
    qm = P.sb("s_qm", [128, 2, tb], F32)
    qn = P.sb("s_qn", [128, 2, tb], BF16)
    xin = P.sb("s_xin", [32, 1024], F32)
    memKT = P.sb("s_memKT", [128, NS, 2, 256], BF16)
    Vaug = P.sb("s_Vaug", [128, NS, 2, 4, 128], BF16)
    mst = P.sb("s_mst", [128, 2, 256], F32)
    Us = P.sb("s_U", [96, 32, 24], F32)
    tA = P.sb("s_tA", [96, 8, 24], F32)
    tB_ = P.sb("s_tB", [96, 8, 24], F32)
    dif = P.sb("s_dif", [96, 8, tb], BF16)
    spt = P.sb("s_spt", [16, 768], F32)
    pso = P.sb("s_pso", [16, 768], F32)
    PTm = P.sb("s_PTm", [128, 256], BF16)
    rc = P.sb("s_rc", [128, 128], F32)
    sg = [P.sb("s_sg%d" % i, [128, tb], F32) for i in range(2)]
    aT = [P.sb("s_aT%d" % i, [128, 2, tb], BF16) for i in range(2)]
    cosb = P.sb("s_cos", [96, tb], F32)
    sinb = P.sb("s_sin", [96, tb], F32)
    ckf = P.sb("s_ckf", [128, 2, tb], F32)
    cTs = P.sb("s_cT", [128, 2, tb], BF16)
    krf = P.sb("s_krf", [96, tb], F32)
    krb = P.sb("s_krb", [96, tb], BF16)
    t1 = P.sb("s_t1", [96, tb], F32)
    t2 = P.sb("s_t2", [96, tb], F32)
    kvo = P.sb("s_kvo", [32, 288], F32)
    KVbn = P.sb("s_KVbn", [32, 289], BF16)
    sqn = P.sb("s_sqn", [128, 768], BF16)
    sqr = P.sb("s_sqr", [128, 32], BF16)
    ssn = P.sb("s_ssn", [128, 4, 16], F32)
    kinvn = P.sb("s_kinvn", [32, 12], F32)
    cq = P.sb("s_cq", [128, 3, tb], F32)
    cqn = P.sb("s_cqn", [128, 3, tb], BF16)
    qg = P.sb("s_qg", [96, tb], BF16)
    QTs = P.sb("s_QT", [96, 12, tb], BF16)
    QL = P.sb("s_QL", [128, 2, 12, tb], BF16)
    wukf = P.sb("s_wukf", [128, 2, 768], F32)
    wukT = P.sb("s_wukT", [64, 12, 256], BF16)
    identb = P.sb("s_identb", [128, 128], BF16)
    smask = P.sb("s_smask", [32, 4, 96], BF16)
    ptb = P.sb("s_ptb", [128, NPT], I32)
    idxall = P.sb("s_idx", [128, NPT], I32)
    iot = P.sb("s_iota", [128, 1], I32)
    kinv = P.sb("s_kinv", [128, NPT, 12], F32)
    NKB = 16
    sqn2 = [P.sb("s_sqn2_%d" % i, [128, 800], BF16) for i in range(2)]
    KVb = [P.sb("s_KVb%d" % i, [128, 289], BF16) for i in range(NKB)]
    KVTc = [P.sb("s_KVTc%d" % i, [128, 4, 2, 128], BF16) for i in range(2)]
    KVTr = [P.sb("s_KVTr%d" % i, [96, 4, 128], BF16) for i in range(2)]
    tms = P.sb("s_tms", [128, 4, 96], F32)
    PTs = [P.sb("s_PTs%d" % i, [128, 4, 96], BF16) for i in range(2)]
    PTn = P.sb("s_PTn", [32, 96], BF16)
    olat = P.sb("s_olat", [96, 256], F32)
    rcs = P.sb("s_rcs", [96, 1], F32)
    olT = P.sb("s_olT", [128, 2, 96], BF16)

    o.MSET('pool', Vaug[:], 1.0, w=['Vaug'])
    for i in range(NKB):
        o.MSET('pool', KVb[i][:, 288:289], 1.0, w=[('KVb', i)])
    o.MSET('pool', KVbn[:, 288:289], 1.0, w=['KVbn'])
    o.CP('pool', identb[:], ident[:], r=['ident'], w=['identb'])
    o.DMA('pool', smask[:], I["c_smask"][:, :, :], w=['smask'])
    o.DMA('sp', wukf[:], I["w_uk"].rearrange("(k p) n -> p k n", p=128), w=['wukf'])
    for h in range(12):
        b = C['rA'].next()
        for cc in range(2):
            o.TR(ps[b][0:64, cc * 128:(cc + 1) * 128], wukf[:, cc, h * 64:(h + 1) * 64], ident[:, :], r=['wukf', 'ident'], w=[('ps', b)])
        o.CP('dve', wukT[:, h, :], ps[b][0:64, 0:256], r=[('ps', b)], w=['wukT'])
    o.DMA('sp', ptb[:], I["pt"].rearrange("s p -> (s p)").partition_broadcast(128), w=['ptb'])
    P.op('pool', lambda e: e.iota(iot[:], pattern=[[0, 1]], base=0, channel_multiplier=1), w=['iota'])
    ptf = P.sb("s_ptf", [128, NPT], F32)
    iotf = P.sb("s_iotf", [128, 1], F32)
    o.CP('dve', ptf[:], ptb[:], r=['ptb'], w=['ptf'])
    o.CP('dve', iotf[:], iot[:], r=['iota'], w=['iotf'])
    o.TS('dve', ptf[:], ptf[:], 128.0, iotf[:, 0:1], ALU.mult, ALU.add, r=['ptf', 'iotf'], w=['ptf'])
    o.CP('dve', idxall[:], ptf[:], r=['ptf'], w=['idx'])

    WS = SC if SC is not None else I
    WQ = 'sp' if SC is not None else 'pool'
    wv_ffn_in = [WS["w_ffn_in"][l].rearrange("(k p) n -> p k n", p=128) for l in range(4)]
    wv_ffn_out = [WS["w_ffn_out"][l].rearrange("(k p) n -> p k n", p=128) for l in range(4)]
    NFG = DFF // 256
    XK = [('xT', c) for c in range(8)]
    ring_state = {'n': 0}

    def load_ffn_group(l, fg):
        s = ring_state['n'] % NRING
        ring_state['n'] += 1
        rg = ring[s]
        k = ('ring', s)
        gv = rg[:, 0:2048].rearrange("p (k n) -> p k n", n=256)
        uv = rg[:, 2048:4096].rearrange("p (k n) -> p k n", n=256)
        ov = rg[:, 4096:6144].rearrange("p (k n) -> p k n", n=1024)
        o.DMA(WQ, gv, wv_ffn_in[l][:, :, fg * 256:(fg + 1) * 256], w=[k])
        o.DMA(WQ, uv, wv_ffn_in[l][:, :, DFF + fg * 256:DFF + (fg + 1) * 256], w=[k])
        o.DMA(WQ, ov, wv_ffn_out[l][:, fg * 2:fg * 2 + 2, :], w=[k])
        if False:
            si = SC["w_ffn_in"][l].rearrange("(k p) n -> p k n", p=128)
            so = SC["w_ffn_out"][l].rearrange("(k p) n -> p k n", p=128)
            o.DMA('sp', si[:, :, fg * 256:(fg + 1) * 256], gv, r=[k], out=True)
            o.DMA('sp', si[:, :, DFF + fg * 256:DFF + (fg + 1) * 256], uv, r=[k], out=True)
            o.DMA('sp', so[:, fg * 2:fg * 2 + 2, :], ov, r=[k], out=True)
        return gv, uv, ov, k

    def load_mix_weights(l):
        if l < 2:
            o.DMA(WQ, win[:], WS["w_in_a"][l].rearrange("(k p) n -> p k n", p=128), w=['win'])
            o.DMA(WQ, wout[0:96, 0:8, :], WS["w_out"][l][0:768, :].rearrange("(k p) n -> p k n", p=96), w=['wout'])
            o.DMA(WQ, wout[:, 8:10, :], WS["w_out"][l][768:1024, :].rearrange("(k p) n -> p k n", p=128), w=['wout'])
            gv = wmisc[0:96, 0:1536].rearrange("p (k n) -> p k n", n=192)
            o.DMA(WQ, gv, WS["w_pool_grp"][l].rearrange("g (c p) e -> p (g c) e", p=96), w=['wmisc'])
            if False:
                o.DMA('sp', SC["w_in_a"][l].rearrange("(k p) n -> p k n", p=128), win[:], r=['win'], out=True)
                o.DMA('sp', SC["w_out"][l][0:768, :].rearrange("(k p) n -> p k n", p=96), wout[0:96, 0:8, :], r=['wout'], out=True)
                o.DMA('sp', SC["w_out"][l][768:1024, :].rearrange("(k p) n -> p k n", p=128), wout[:, 8:10, :], r=['wout'], out=True)
                o.DMA('sp', SC["w_pool_grp"][l].rearrange("g (c p) e -> p (g c) e", p=96), gv, r=['wmisc'], out=True)
        else:
            j = l - 2
            o.DMA(WQ, win[:, :, 0:640], WS["w_in_b"][j].rearrange("(k p) n -> p k n", p=128), w=['win'])
            o.DMA(WQ, wout[:, 0:8, :], WS["w_out"][l].rearrange("(k p) n -> p k n", p=128), w=['wout'])
            uq = wmisc[:, 0:3456].rearrange("p (k n) -> p k n", n=1152)
            o.DMA(WQ, uq, WS["w_uq"][j].rearrange("(k p) n -> p k n", p=128), w=['wmisc'])
            if False:
                o.DMA('sp', SC["w_in_b"][j].rearrange("(k p) n -> p k n", p=128), win[:, :, 0:640], r=['win'], out=True)
                o.DMA('sp', SC["w_out"][l].rearrange("(k p) n -> p k n", p=128), wout[:, 0:8, :], r=['wout'], out=True)
                o.DMA('sp', SC["w_uq"][j].rearrange("(k p) n -> p k n", p=128), uq, r=['wmisc'], out=True)

    def xnorm(gkey, l):
        norm_fm(o, C, [xT[:, c, :] for c in range(8)], [G[gkey][:, l * 8 + c:l * 8 + c + 1] for c in range(8)],
                [hT[:, c, :] for c in range(8)], 128, D, tb, XK, 'hT', ones[:])

    def mem_load(l, seqs=None):
        for s_ in (range(NS) if seqs is None else seqs):
            o.DMA('pool', mst[:], I["cmk"][l, s_].rearrange("(m p) f -> p m f", p=128), w=['mst'])
            for mc in range(2):
                b = C['rA'].next()
                for hp in range(2):
                    o.TR(ps[b][:, hp * 128:(hp + 1) * 128], mst[:, mc, hp * 128:(hp + 1) * 128], ident[:], r=['mst', 'ident'], w=[('ps', b)])
                o.CP('dve', memKT[:, s_, :, mc * 128:(mc + 1) * 128], ps[b][:, 0:256].rearrange("p (h m) -> p h m", m=128),
                     r=[('ps', b)], w=['memKT'])
            o.DMA('pool', mst[:], I["cmv"][l, s_].rearrange("(m p) f -> p m f", p=128), r=['mst'], w=['mst'])
            for mc in range(2):
                for h in range(4):
                    off = 0 if h % 2 == 0 else 64
                    o.CP('pool', Vaug[:, s_, mc, h, off:off + 64], mst[:, mc, h * 64:(h + 1) * 64], r=['mst'], w=['Vaug'])

    def mem_attend(l, catbase):
        for hp in range(2):
            norm_fm(o, C, [qm[:, hp, :]], [G['gmq'][:, l:l + 1]], [qn[:, hp, :]], 128, 64, tb, ['qm'], 'qn', blk2[:])
        bsb = [2, 3]
        for mc in range(2):
            for s_ in range(NS):
                for h in range(4):
                    hp, par = h // 2, h % 2
                    base = par * 64
                    col = ((mc * NS + s_) * 2 + hp) * 8
                    o.MM(ps[bsb[par]][:, col:col + 8], memKT[base:base + 64, s_, hp, mc * 128:(mc + 1) * 128],
                         qn[base:base + 64, hp, s_ * 8:(s_ + 1) * 8], True, True, r=['memKT', 'qn'], w=[('ps', bsb[par])])
        for par in range(2):
            o.ACT(PTm[:, par * 128:(par + 1) * 128], ps[bsb[par]][:, 0:128], AF.Exp, r=[('ps', bsb[par])], w=['PTm'], scale=0.125)
        bo = C['rO'].next()
        for s_ in range(NS):
            for h in range(4):
                hp, par = h // 2, h % 2
                colo = (s_ * 4 + h) * 8
                for mc in range(2):
                    col = par * 128 + ((mc * NS + s_) * 2 + hp) * 8
                    o.MM(ps[bo][:, colo:colo + 8], Vaug[:, s_, mc, h, :], PTm[:, col:col + 8], mc == 0, mc == 1,
                         r=['Vaug', 'PTm'], w=[('ps', bo)])
        o.RCP(rc[:, 0:128], ps[bo][:, 0:128], r=[('ps', bo)], w=['rc'])
        for h in range(4):
            hp, base = h // 2, (h % 2) * 64
            vr = slice(base, base + 64)
            dr = slice(64 - base, 128 - base)
            pv = ps[bo][vr, 0:128].rearrange("p (s h i) -> p s h i", h=4, i=8)[:, :, h, :]
            rv = rc[dr, 0:128].rearrange("p (s h i) -> p s h i", h=4, i=8)[:, :, h, :]
            o.TT('dve', cat[vr, catbase + hp, :].rearrange("p (s i) -> p s i", i=8), pv, rv, ALU.mult, r=[('ps', bo), 'rc'], w=['cat'])

    def out_proj(l):
        if l < 2:
            chunks = [(96, c) for c in range(8)] + [(128, 8), (128, 9)]
        else:
            chunks = [(128, c) for c in range(8)]
        for oc in range(8):
            b = C['rA'].next()
            for i, (kk, c) in enumerate(chunks):
                o.MM(ps[b][:, 0:tb], wout[0:kk, c, oc * 128:(oc + 1) * 128], cat[0:kk, c, :], i == 0, i == len(chunks) - 1,
                     r=['wout', 'cat'], w=[('ps', b)])
            o.TT('dve', xT[:, oc, :], xT[:, oc, :], ps[b][:, 0:tb], ALU.add, r=[('ps', b), ('xT', oc)], w=[('xT', oc)])

    def halo_load(l, seqs=None):
        for s_ in (range(NS) if seqs is None else seqs):
            o.DMA('pool', spt[0:15, :], I["spool"][l, s_], w=['spt'])
            b = C['rA'].next()
            for c in range(8):
                o.TR(ps[b][0:96, c * 16:c * 16 + 15], spt[0:15, c * 96:(c + 1) * 96], ident[0:15, 0:15], r=['spt', 'ident'], w=[('ps', b)])
            o.CP('dve', Us[:, :, 1:16].rearrange("p (c s) t -> p c s t", s=NS)[:, :, s_, :],
                 ps[b][0:96, 0:128].rearrange("p (c t) -> p c t", t=16)[:, :, 0:15], r=[('ps', b)], w=['U'])

    def mix_a(l):
        xnorm('gmix', l)
        for c in range(8):
            b = C['rA'].next()
            for k in range(8):
                o.MM(ps[b][0:96, 0:tb], win[:, k, c * 96:(c + 1) * 96], hT[:, k, :], k == 0, k == 7, r=['win', 'hT'], w=[('ps', b)])
            o.CP('act', Us[:, c * NS:(c + 1) * NS, 16:24], ps[b][0:96, 0:tb].rearrange("p (s i) -> p s i", i=8), r=[('ps', b)], w=['U'])
        for hp in range(2):
            b = C['rA'].next()
            for k in range(8):
                o.MM(ps[b][:, 0:tb], win[:, k, 768 + hp * 128:768 + (hp + 1) * 128], hT[:, k, :], k == 0, k == 7,
                     r=['win', 'hT'], w=[('ps', b)])
            o.CP('act', qm[:, hp, :], ps[b][:, 0:tb], r=[('ps', b)], w=['qm'])
        gw = wmisc[0:96, 0:1536].rearrange("p (k n) -> p k n", n=192)
        E = 24
        for g in range(4):
            w_ = 2 ** (g + 1)
            src = Us[:, 8 * g:8 * g + 8, :]
            bufs = [tA, tB_]
            d, i, srck = 1, 0, 'U'
            while d < w_:
                dst = bufs[i % 2]
                dk = 'tA' if i % 2 == 0 else 'tB'
                lo = 2 * d
                o.TT('pool', dst[:, :, lo:E], src[:, :, lo:E], src[:, :, lo - d:E - d], ALU.add, r=[srck], w=[dk])
                src, srck = dst, dk
                d *= 2
                i += 1
            o.STT('dve', dif[:, 2 * g:2 * g + 2, :].rearrange("p c (s i) -> p (c s) i", i=8), src[:, :, 16:E], 1.0 / w_,
                  Us[:, 8 * g:8 * g + 8, 16:E], ALU.mult, ALU.subtract, r=[srck, 'U'], w=['dif'])
            for ec in range(2):
                b = C['rA'].next()
                for cc in range(2):
                    o.MM(ps[b][0:96, 0:tb], gw[:, 2 * g + cc, ec * 96:(ec + 1) * 96], dif[:, 2 * g + cc, :], cc == 0, cc == 1,
                         r=['wmisc', 'dif'], w=[('ps', b)])
                ch = 2 * g + ec
                o.TSM('dve', cat[0:96, ch, :], ps[b][0:96, 0:tb], G['pscale'][:, l * 8 + ch:l * 8 + ch + 1], r=[('ps', b), 'gains'], w=['cat'])
        for s_ in range(NS):
            b1, b2 = C['rA'].next(), C['rA'].next()
            for c in range(8):
                bb = b1 if c < 4 else b2
                o.TR(ps[bb][0:15, (c % 4) * 96:(c % 4 + 1) * 96], Us[:, c * NS + s_, 9:24], ident[0:96, 0:96], r=['U', 'ident'], w=[('ps', bb)])
            o.CP('act', pso[0:15, 0:384], ps[b1][0:15, 0:384], r=[('ps', b1)], w=['pso'])
            o.CP('act', pso[0:15, 384:768], ps[b2][0:15, 0:384], r=[('ps', b2)], w=['pso'])
            o.DMA('sp', O["pool_s"][l, s_], pso[0:15, :], r=['pso'], out=True)
        mem_attend(l, 8)
        out_proj(l)

    def ffn(l, pre, hook=None):
        xnorm('gffn', l)
        rOut = Rot([0, 1, 6, 7])
        pend = list(pre)
        nxt = len(pre)
        for fg in range(NFG):
            gv, uv, ov, rk = pend.pop(0)
            ai = fg % 2
            for cc in range(2):
                bg, bu = 2 + cc, 4 + cc
                for k in range(8):
                    o.MM(ps[bg][:, 0:tb], gv[:, k, cc * 128:(cc + 1) * 128], hT[:, k, :], k == 0, k == 7, r=[rk, 'hT'], w=[('ps', bg)])
                for k in range(8):
                    o.MM(ps[bu][:, 0:tb], uv[:, k, cc * 128:(cc + 1) * 128], hT[:, k, :], k == 0, k == 7, r=[rk, 'hT'], w=[('ps', bu)])
                o.ACT(sg[cc][:, :], ps[bg][:, 0:tb], AF.Silu, r=[('ps', bg)], w=[('sg', cc)])
                o.TT('dve', aT[ai][:, cc, :], sg[cc][:, :], ps[bu][:, 0:tb], ALU.mult, r=[('sg', cc), ('ps', bu)], w=[('aT', ai)])
            for oc in range(8):
                b = rOut.next()
                for cc in range(2):
                    o.MM(ps[b][:, 0:tb], ov[:, cc, oc * 128:(oc + 1) * 128], aT[ai][:, cc, :], cc == 0, cc == 1,
                         r=[rk, ('aT', ai)], w=[('ps', b)])
                o.TT('dve', xT[:, oc, :], xT[:, oc, :], ps[b][:, 0:tb], ALU.add, r=[('ps', b), ('xT', oc)], w=[('xT', oc)])
            if nxt < NFG:
                pend.append(load_ffn_group(l, nxt))
                nxt += 1
            if hook is not None:
                hook(fg)

    def kv_stage():
        wd = C['wdkv']
        norm_fm(o, C, [xT[:, c, :] for c in range(8)], [G['gkv'][:, c:c + 1] for c in range(8)],
                [hT[:, c, :] for c in range(8)], 128, D, tb, XK, 'hT', ones[:])
        o.DMA('sp', cosb[:], I["c_cos"][:, 2048:2080], w=['cosb'])
        o.DMA('sp', sinb[:], I["c_sin"][:, 2048:2080], w=['sinb'])
        for cc in range(2):
            b = C['rA'].next()
            for k in range(8):
                o.MM(ps[b][:, 0:tb], wd[:, k, cc * 128:(cc + 1) * 128], hT[:, k, :], k == 0, k == 7, r=['wdkv', 'hT'], w=[('ps', b)])
            o.CP('act', ckf[:, cc, :], ps[b][:, 0:tb], r=[('ps', b)], w=['ckf'])
        br = C['rG'].next()
        for k in range(8):
            o.MM(ps[br][0:96, 0:tb], wd[:, k, 256:352], hT[:, k, :], k == 0, k == 7, r=['wdkv', 'hT'], w=[('ps', br)])
        o.CP('act', krb[64:96, :], ps[br][64:96, 0:tb], r=[('ps', br)], w=['krb'])
        b2 = C['rG'].next()
        o.MM(ps[b2][0:96, 0:tb], perm[64:96, 0:96], krb[64:96, :], True, True, r=['perm', 'krb'], w=[('ps', b2)])
        o.TT('dve', t1[64:96, :], ps[br][64:96, 0:tb], cosb[64:96, :], ALU.mult, r=[('ps', br), 'cosb'], w=['t1'])
        o.TT('dve', t2[64:96, :], ps[b2][64:96, 0:tb], sinb[64:96, :], ALU.mult, r=[('ps', b2), 'sinb'], w=['t2'])
        o.TT('dve', krf[64:96, :], t1[64:96, :], t2[64:96, :], ALU.add, r=['t1', 't2'], w=['krf'])
        o.CP('pool', krb[64:96, :], krf[64:96, :], r=['krf'], w=['krb'])
        norm_fm(o, C, [ckf[:, cc, :] for cc in range(2)], [G['gkvl'][:, cc:cc + 1] for cc in range(2)],
                [cTs[:, cc, :] for cc in range(2)], 128, 256, tb, ['ckf'], 'cTs', ones[:])
        for cc in range(2):
            o.STT('dve', ckf[:, cc, :], ckf[:, cc, :], G['gkvl'][:, cc:cc + 1], C['rstd'][:, 0:tb], ALU.mult, ALU.mult,
                  r=['ckf', 'rstd', 'cTs'], w=['ckf'])
        b = C['rA'].next()
        o.TR(ps[b][0:32, 0:128], ckf[:, 0, :], ident[:], r=['ckf', 'ident'], w=[('ps', b)])
        o.TR(ps[b][0:32, 128:256], ckf[:, 1, :], ident[:], r=['ckf', 'ident'], w=[('ps', b)])
        o.TR(ps[b][0:32, 256:288], krf[64:96, :], ident[64:96, 64:96], r=['krf', 'ident'], w=[('ps', b)])
        o.CP('act', kvo[:, :], ps[b][0:32, 0:288], r=[('ps', b)], w=['kvo'])
        o.DMA('sp', O["kv_s"][:, :], kvo[:, :], r=['kvo'], out=True)
        o.CP('dve', KVbn[:, 0:288], kvo[:, :], r=['kvo'], w=['KVbn'])
        b3, b4 = C['rG'].next(), C['rG'].next()
        for cc in range(2):
            o.MM(ps[b3][0:32, 0:512], cTs[:, cc, :], C['wuk'][:, cc, 0:512], cc == 0, cc == 1, r=['cTs', 'wuk'], w=[('ps', b3)])
        for cc in range(2):
            o.MM(ps[b4][0:32, 0:256], cTs[:, cc, :], C['wuk'][:, cc, 512:768], cc == 0, cc == 1, r=['cTs', 'wuk'], w=[('ps', b4)])
        o.ACT(sqn[0:32, 0:512], ps[b3][0:32, 0:512], AF.Square, r=[('ps', b3)], w=['sqn'])
        o.ACT(sqn[0:32, 512:768], ps[b4][0:32, 0:256], AF.Square, r=[('ps', b4)], w=['sqn'])
        o.RED(ssn[0:32, 0, 0:12], sqn[0:32, :].rearrange("p (h d) -> p h d", d=64), r=['sqn'], w=['ssn'])
        o.ACT(sqr[0:32, 0:32], kvo[:, 256:288], AF.Square, r=['kvo'], w=['sqr'])
        o.RED(ssn[0:32, 0, 12:13], sqr[0:32, 0:32].rearrange("p (h d) -> p h d", d=32), r=['sqr'], w=['ssn2'])
        o.TSA('dve', ssn[0:32, 0, 0:12], ssn[0:32, 0, 0:12], ssn[0:32, 0, 12:13], r=['ssn', 'ssn2'], w=['ssn'])
        o.ACT(ssn[0:32, 0, 0:12], ssn[0:32, 0, 0:12], AF.Ln, r=['ssn'], w=['ssn'], scale=1.0, bias=96.0 * EPS)
        o.ACT(kinvn[:, :], ssn[0:32, 0, 0:12], AF.Exp, r=['ssn'], w=['kinvn'], scale=-0.5)

    def mix_b(l):
        jl = l - 2
        xnorm('gmix', l)
        for c in range(3):
            b = C['rA'].next()
            for k in range(8):
                o.MM(ps[b][:, 0:tb], win[:, k, c * 128:(c + 1) * 128], hT[:, k, :], k == 0, k == 7, r=['win', 'hT'], w=[('ps', b)])
            o.CP('act', cq[:, c, :], ps[b][:, 0:tb], r=[('ps', b)], w=['cq'])
        for hp in range(2):
            b = C['rA'].next()
            for k in range(8):
                o.MM(ps[b][:, 0:tb], win[:, k, 384 + hp * 128:384 + (hp + 1) * 128], hT[:, k, :], k == 0, k == 7,
                     r=['win', 'hT'], w=[('ps', b)])
            o.CP('act', qm[:, hp, :], ps[b][:, 0:tb], r=[('ps', b)], w=['qm'])
        norm_fm(o, C, [cq[:, c, :] for c in range(3)], [G['gql'][:, jl * 3 + c:jl * 3 + c + 1] for c in range(3)],
                [cqn[:, c, :] for c in range(3)], 128, 384, tb, ['cq'], 'cqn', ones[:])
        uq = wmisc[:, 0:3456].rearrange("p (k n) -> p k n", n=1152)
        for h in range(12):
            bq = C['rG'].next()
            for k in range(3):
                o.MM(ps[bq][0:96, 0:tb], uq[:, k, h * 96:(h + 1) * 96], cqn[:, k, :], k == 0, k == 2, r=['wmisc', 'cqn'], w=[('ps', bq)])
            norm_fm(o, C, [ps[bq][0:96, 0:tb]], [G['gq'][:, jl:jl + 1]], [qg[:, :]], 96, 96, tb, [('ps', bq)], 'qg', ones[0:96, 0:96])
            b2 = C['rG'].next()
            o.MM(ps[b2][0:96, 0:tb], perm[:, :], qg[:, :], True, True, r=['perm', 'qg'], w=[('ps', b2)])
            o.STT('dve', t1[:, :], qg[:, :], G['gk'][:, 0:1], cosb[:, :], ALU.mult, ALU.mult, r=['qg', 'cosb', 'gains'], w=['t1'])
            o.STT('dve', t2[:, :], ps[b2][0:96, 0:tb], G['gk'][:, 0:1], sinb[:, :], ALU.mult, ALU.mult,
                  r=[('ps', b2), 'sinb', 'gains'], w=['t2'])
            o.TT('pool', QTs[:, h, :], t1[:, :], t2[:, :], ALU.add, r=['t1', 't2'], w=['QTs'])
        for cc in range(2):
            b = C['rA'].next()
            for h in range(12):
                o.MM(ps[b][:, h * tb:(h + 1) * tb], wukT[0:64, h, cc * 128:(cc + 1) * 128], QTs[0:64, h, :], True, True,
                     r=['wukT', 'QTs'], w=[('ps', b)])
            o.CP('dve', QL[:, cc, :, :], ps[b][:, 0:12 * tb].rearrange("p (h t) -> p h t", t=tb), r=[('ps', b)], w=['QL'])
        BO, PSB, BSC = 4, 5, 3
        ngr = NPG // 4
        NG = NS * ngr
        psc = ps[PSB][:, :].bitcast(BF16)
        psr = ps[PSB + 1][:, :].bitcast(BF16)

        def slot(g, q):
            return (g % 4) * 4 + q

        def GATHER(g):
            for q in range(4):
                pg = g * 4 + q
                sl = slot(g, q)
                P.op('pool', lambda e, sl=sl, pg=pg: e.indirect_dma_start(
                    out=KVb[sl][:, 0:288], out_offset=None, in_=I["ckv"][:, :],
                    in_offset=bass.IndirectOffsetOnAxis(ap=idxall[:, pg:pg + 1], axis=0)),
                    r=['idx'], w=[('KVb', sl)], dma=True)

        def T(g):
            kt = g % 2
            for q in range(4):
                sl = slot(g, q)
                for cc in range(2):
                    o.TR(psc[:, (q * 2 + cc) * 128:(q * 2 + cc + 1) * 128], KVb[sl][:, cc * 128:(cc + 1) * 128], identb[:, :],
                         r=[('KVb', sl), 'identb'], w=[('ps', PSB)])
                o.TR(psr[0:96, q * 128:(q + 1) * 128], KVb[sl][:, 192:288], identb[:, :], r=[('KVb', sl), 'identb'], w=[('ps', PSB + 1)])
            o.CP('dve', KVTc[kt][:, :, :, :], psc[:, 0:1024].rearrange("p (q c k) -> p q c k", c=2, k=128), r=[('ps', PSB)], w=[('KVTc', kt)])
            o.CP('act', KVTr[kt][64:96, :, :], psr[64:96, 0:512].rearrange("p (q k) -> p q k", k=128), r=[('ps', PSB + 1)], w=[('KVTr', kt)])

        def KN(g):
            kt = g % 2
            for q in range(4):
                sl = slot(g, q)
                b3, b4 = (7, 0) if q % 2 == 0 else (1, 2)
                sq_ = sqn2[q % 2]
                sk = ('sqn', q % 2)
                for cc in range(2):
                    o.MM(ps[b3][:, 0:512], KVTc[kt][:, q, cc, :], C['wuk'][:, cc, 0:512], cc == 0, cc == 1,
                         r=[('KVTc', kt), 'wuk'], w=[('ps', b3)])
                for cc in range(2):
                    o.MM(ps[b4][:, 0:256], KVTc[kt][:, q, cc, :], C['wuk'][:, cc, 512:768], cc == 0, cc == 1,
                         r=[('KVTc', kt), 'wuk'], w=[('ps', b4)])
                o.ACT(sq_[:, 0:512], ps[b3][:, 0:512], AF.Square, r=[('ps', b3)], w=[sk])
                o.ACT(sq_[:, 512:768], ps[b4][:, 0:256], AF.Square, r=[('ps', b4)], w=[sk])
                o.ACT(sq_[:, 768:800], KVb[sl][:, 256:288], AF.Square, r=[('KVb', sl)], w=[sk])
                o.RED(ssn[:, q, 0:12], sq_[:, 0:768].rearrange("p (h d) -> p h d", d=64), r=[sk], w=['ssn'])
                o.RED(ssn[:, q, 12:13], sq_[:, 768:800].rearrange("p (h d) -> p h d", d=32), r=[sk], w=['ssn'])
            o.TT('dve', ssn[:, :, 0:12], ssn[:, :, 0:12], ssn[:, :, 12:13].to_broadcast([128, 4, 12]), ALU.add, r=['ssn'], w=['ssn'])
            o.ACT(ssn[:, :, 0:12], ssn[:, :, 0:12], AF.Ln, r=['ssn'], w=['ssn'], scale=1.0, bias=96.0 * EPS)
            o.ACT(kinv[:, g * 4:g * 4 + 4, :], ssn[:, :, 0:12], AF.Exp, r=['ssn'], w=['kinv'], scale=-0.5)

        def SE(g):
            kt = g % 2
            s_ = g // ngr
            qcols = slice(s_ * 8, (s_ + 1) * 8)
            for q in range(4):
                oc = ps[BSC][:, q * 96:(q + 1) * 96].rearrange("p (h i) -> p h i", i=8)
                o.MM(oc, KVTc[kt][:, q, 0, :], QL[:, 0, :, qcols], True, False, r=[('KVTc', kt), 'QL'], w=[('ps', BSC)])
                o.MM(oc, KVTc[kt][:, q, 1, :], QL[:, 1, :, qcols], False, False, r=[('KVTc', kt), 'QL'], w=[('ps', BSC)])
                o.MM(oc, KVTr[kt][64:96, q, :], QTs[64:96, :, qcols], False, True, r=[('KVTr', kt), 'QTs'], w=[('ps', BSC)])
            o.TT('dve', tms[:, :, :].rearrange("p q (h i) -> p q h i", i=8),
                 ps[BSC][:, 0:384].rearrange("p (q h i) -> p q h i", h=12, i=8),
                 kinv[:, g * 4:g * 4 + 4, :].unsqueeze(3).to_broadcast([128, 4, 12, 8]), ALU.mult, r=[('ps', BSC), 'kinv'], w=['tms'])
            o.ACT(PTs[kt][:, :, :], tms[:, :, :], AF.Exp, r=['tms'], w=[('PTs', kt)])

        def PV(g):
            kt = g % 2
            gi = g % ngr
            for q in range(4):
                sl = slot(g, q)
                o.MM(ps[BO][0:96, 0:289], PTs[kt][:, q, :], KVb[sl][:, 0:289], gi == 0 and q == 0, False,
                     r=[('PTs', kt), ('KVb', sl)], w=[('ps', BO)])

        def EPI(s_):
            qcols = slice(s_ * 8, (s_ + 1) * 8)
            oc = ps[BSC][0:32, 0:96].rearrange("p (h i) -> p h i", i=8)
            o.MM(oc, cTs[:, 0, :], QL[:, 0, :, qcols], True, False, r=['cTs', 'QL'], w=[('ps', BSC)])
            o.MM(oc, cTs[:, 1, :], QL[:, 1, :, qcols], False, False, r=['cTs', 'QL'], w=[('ps', BSC)])
            o.MM(oc, krb[64:96, :], QTs[64:96, :, qcols], False, True, r=['krb', 'QTs'], w=[('ps', BSC)])
            o.TT('dve', tms[0:32, 0, :].rearrange("p (h i) -> p h i", i=8), ps[BSC][0:32, 0:96].rearrange("p (h i) -> p h i", i=8),
                 kinvn[:, :].unsqueeze(2).to_broadcast([32, 12, 8]), ALU.mult, r=[('ps', BSC), 'kinvn'], w=['tms'])
            o.ACT(tms[0:32, 1, :], tms[0:32, 0, :], AF.Exp, r=['tms'], w=['tms'])
            o.TT('dve', PTn[:, :], tms[0:32, 1, :], smask[:, s_, :], ALU.mult, r=['tms', 'smask'], w=['PTn'])
            o.MM(ps[BO][0:96, 0:289], PTn[:, :], KVbn[:, 0:289], NPG == 0, True, r=['PTn', 'KVbn'], w=[('ps', BO)])
            o.RCP(rcs[:, :], ps[BO][0:96, 288:289], r=[('ps', BO)], w=['rcs'])
            o.TSM('dve', olat[:, :], ps[BO][0:96, 0:256], rcs[:, 0:1], r=[('ps', BO), 'rcs'], w=['olat'])
            b = PSB
            for cc in range(2):
                o.TR(ps[b][:, cc * 96:(cc + 1) * 96], olat[:, cc * 128:(cc + 1) * 128], ident[0:96, 0:96], r=['olat', 'ident'], w=[('ps', b)])
            o.CP('dve', olT[:, :, :], ps[b][:, 0:192].rearrange("p (c n) -> p c n", n=96), r=[('ps', b)], w=['olT'])
            b = PSB + 1
            for h in range(12):
                for cc in range(2):
                    o.MM(ps[b][:, h * 8:(h + 1) * 8], C['wuv'][:, cc, (h // 2) * 128:(h // 2 + 1) * 128], olT[:, cc, h * 8:(h + 1) * 8],
                         cc == 0, cc == 1, r=['wuv', 'olT'], w=[('ps', b)])
            for par in range(2):
                vr = slice(par * 64, par * 64 + 64)
                src = ps[b][vr, 0:96].rearrange("p (hp two i) -> p hp two i", two=2, i=8)[:, :, par, :]
                o.CP('dve', cat[vr, 0:6, qcols], src, r=[('ps', b)], w=['cat'])

        for g in range(min(3, NG)):
            GATHER(g)
        T(0)
        if l == 2:
            KN(0)
        if NG > 1:
            T(1)
        for g in range(NG):
            if g + 3 < NG:
                GATHER(g + 3)
            SE(g)
            if l == 2 and g + 1 < NG:
                KN(g + 1)
            if g + 2 < NG:
                T(g + 2)
            PV(g)
            if g % ngr == ngr - 1:
                EPI(g // ngr)
        mem_attend(l, 6)
        out_proj(l)

    o.DMA('sp', xin[:], I["xs"][:, :], w=['xin'])
    for c in range(8):
        b = C['rA'].next()
        o.TR(ps[b][:, 0:32], xin[:, c * 128:(c + 1) * 128], ident[0:32, 0:32], r=['xin', 'ident'], w=[('ps', b)])
        o.CP('dve', xT[:, c, :], ps[b][:, 0:32], r=[('ps', b)], w=[('xT', c)])
    load_mix_weights(0)
    mem_load(0)
    halo_load(0)

    def mk_hook(l):
        def hook(fg):
            if fg == 1:
                load_mix_weights(l + 1)
            elif fg in (2, 4, 6, 8):
                mem_load(l + 1, [(fg - 2) // 2])
            elif fg in (3, 5, 7, 9) and l + 1 < 2:
                halo_load(l + 1, [(fg - 3) // 2])
        return hook

    for l in range(nlayers):
        pre = [load_ffn_group(l, fg) for fg in range(min(NRING, NFG))]
        if l < 2:
            mix_a(l)
        else:
            mix_b(l)
        ffn(l, pre, hook=mk_hook(l) if l < nlayers - 1 else None)
        if l == 1:
            kv_stage()
    for c in range(8):
        b = C['rA'].next()
        o.TR(ps[b][0:32, 0:128], xT[:, c, :], ident[:, :], r=[('xT', c), 'ident'], w=[('ps', b)])
        o.CP('dve', xin[:, c * 128:(c + 1) * 128], ps[b][0:32, 0:128], r=[('ps', b)], w=['xin'])
    o.DMA('sp', O["y_s"][:, :], xin[:], r=['xin'], out=True)
    P.finish()
    P.emit('s')
    print("sample program: ops", len(P.ops), "waits", P.nwaits)


def make_in_maps(inputs, cores, NT=2048):
    consts = host_consts()
    maps = []
    ckv = np.ascontiguousarray(inputs["cache_kv"]).reshape(-1, 288)
    shared = {k: np.ascontiguousarray(inputs[k]) for k in
              ("norm_mix", "norm_ffn", "w_out", "w_ffn_in", "w_ffn_out", "norm_mem", "w_mem_kv", "g_mem_q", "g_mem_k",
               "w_in_a", "w_pool_grp", "pool_scale", "w_in_b", "g_q_lora", "w_uq", "g_q", "norm_kv", "w_dkv",
               "g_kv_lora", "w_uk", "w_uv", "g_k_nope", "g_k_rope")}
    for c in cores:
        m = dict(shared)
        m.update(consts)
        m["xp"] = np.ascontiguousarray(inputs["x_prompt"][c][:NT])
        m["xs"] = np.ascontiguousarray(inputs["x_sample"][4 * c:4 * c + 4]).reshape(32, D)
        m["ckv"] = ckv
        m["pt"] = np.ascontiguousarray(inputs["page_table"][4 * c:4 * c + 4]).astype(np.int32)
        m["spool"] = np.ascontiguousarray(inputs["state_pool"][:, 4 * c:4 * c + 4])
        m["cmk"] = np.ascontiguousarray(inputs["cache_mem_k"][:, 4 * c:4 * c + 4]).reshape(4, 4, 256, 256)
        m["cmv"] = np.ascontiguousarray(inputs["cache_mem_v"][:, 4 * c:4 * c + 4]).reshape(4, 4, 256, 256)
        m["memp"] = np.ascontiguousarray(inputs["mem_prompt"][c])
        maps.append(m)
    return maps


def kernel(**inputs):
    nc = build()
    maps = make_in_maps(inputs, list(range(NCORES)))
    res = run_bass_kernel_spmd(nc, maps, core_ids=list(range(NCORES)))
    R = res.results
    y_p = np.stack([R[c]["y_p"] for c in range(NCORES)])
    y_s = np.concatenate([R[c]["y_s"].reshape(4, 8, D) for c in range(NCORES)])
    kv_p = np.stack([R[c]["kv_p"] for c in range(NCORES)])
    kv_s = np.concatenate([R[c]["kv_s"].reshape(4, 8, 288) for c in range(NCORES)])
    pool_p = np.stack([R[c]["pool_p"] for c in range(NCORES)], axis=1)
    pool_s = np.concatenate([R[c]["pool_s"] for c in range(NCORES)], axis=1)
    mk = np.stack([R[c]["mk_p"] for c in range(NCORES)], axis=1).reshape(4, 8, 256, 4, 64)
    mv = np.stack([R[c]["mv_p"] for c in range(NCORES)], axis=1).reshape(4, 8, 256, 4, 64)
    return (y_p, y_s, kv_p, kv_s, pool_p, pool_s, mk, mv)
```

```python
import numpy as np
import concourse.bass as bass
import concourse.mybir as mybir
from concourse.bass_utils import run_bass_kernel_spmd
from contextlib import ExitStack

F32 = mybir.dt.float32
BF16 = mybir.dt.bfloat16
I32 = mybir.dt.int32
AF = mybir.ActivationFunctionType
ALU = mybir.AluOpType
AX = mybir.AxisListType

D = 1024
DFF = 2816
EPS = 1e-6
TB = 512
NCORES = 8


class Prog:
    ENGS = ('pe', 'act', 'dve', 'pool', 'sp')
    NDMA = 8

    def __init__(self, nc, es, pfx=''):
        self.nc, self.es = nc, es
        self.pfx = pfx
        self.ops = []
        self.lastw = {}
        self.readers = {}
        self.out_dmas = []

    def sb(self, name, shape, dt):
        return self.es.enter_context(self.nc.sbuf_tensor(self.pfx + name, shape, dt))

    def ps(self, name, shape, dt):
        return self.es.enter_context(self.nc.psum_tensor(self.pfx + name, shape, dt))

    def op(self, eng, fn, r=(), w=(), dma=False, out=False):
        i = len(self.ops)
        deps = set()
        pr = [k for k in r if isinstance(k, tuple) and k[0] == 'ps']
        if pr:
            w = list(w) + [k for k in pr if k not in w]
        for k in r:
            lw = self.lastw.get(k)
            if lw is not None:
                deps.add(lw)
        for k in w:
            lw = self.lastw.get(k)
            if lw is not None:
                deps.add(lw)
            for rd in self.readers.get(k, ()):
                deps.add(rd)
        keep = []
        for d in deps:
            de, ddma = self.ops[d][0], self.ops[d][3]
            if de == eng and not dma and not ddma and eng == 'pe':
                continue
            keep.append(d)
        for k in r:
            self.readers.setdefault(k, []).append(i)
        for k in w:
            self.lastw[k] = i
            self.readers[k] = []
        self.ops.append([eng, fn, keep, dma])
        if out:
            self.out_dmas.append(i)
        return i

    def finish(self):
        import os
        stop = int(os.environ.get("K_STOP", "0"))
        if stop:
            self.ops = self.ops[:stop]
            self.out_dmas = [i for i in range(len(self.ops)) if self.ops[i][3]]
        self.ops.append(['sp', None, list(self.out_dmas), False])

    def emit(self, tag=''):
        nc, es = self.nc, self.es
        ops = self.ops
        n = len(ops)
        needed = [False] * n
        for o in ops:
            for d in o[2]:
                needed[d] = True
        csem = {e: es.enter_context(nc.semaphore(tag + 'c_' + e)) for e in self.ENGS}
        ccnt = {e: 0 for e in self.ENGS}
        dsem = {e: [es.enter_context(nc.semaphore('%sd_%s%d' % (tag, e, j))) for j in range(self.NDMA)]
                for e in ('sp', 'pool', 'act')}
        dcnt = {e: [0] * self.NDMA for e in dsem}
        dlast = {e: [None] * self.NDMA for e in dsem}
        drr = {e: 0 for e in dsem}
        sig = [None] * n
        extra = [None] * n
        streams = {e: [] for e in self.ENGS}
        for i, o in enumerate(ops):
            eng, fn, deps, dma = o
            streams[eng].append(i)
            if dma:
                j = drr[eng]
                drr[eng] = (j + 1) % self.NDMA
                dcnt[eng][j] += 1
                sig[i] = (dsem[eng][j], 16 * dcnt[eng][j])
                extra[i] = dlast[eng][j]
                dlast[eng][j] = i
            elif needed[i]:
                ccnt[eng] += 1
                sig[i] = (csem[eng], ccnt[eng])
        self.nwaits = 0

        def body(e, eng):
            known = {}
            for i in streams[eng]:
                _, fn, deps, dma = ops[i]
                req = {}
                dl = list(deps)
                if extra[i] is not None:
                    dl.append(extra[i])
                for d in dl:
                    s, v = sig[d]
                    if req.get(id(s), (None, 0))[1] < v:
                        req[id(s)] = (s, v)
                for s, v in req.values():
                    if known.get(id(s), 0) < v:
                        e.wait_ge(s, v)
                        known[id(s)] = v
                        self.nwaits += 1
                if fn is None:
                    continue
                ins = fn(e)
                if sig[i] is not None:
                    ins.then_inc(sig[i][0], 16 if dma else 1)

        with nc.Block() as block:
            @block.tensor
            def _(e):
                body(e, 'pe')

            @block.scalar
            def _(e):
                body(e, 'act')

            @block.vector
            def _(e):
                body(e, 'dve')

            @block.gpsimd
            def _(e):
                body(e, 'pool')

            @block.sync
            def _(e):
                body(e, 'sp')


class Rot:
    def __init__(self, items):
        self.items, self.i = items, 0

    def next(self):
        x = self.items[self.i % len(self.items)]
        self.i += 1
        return x


class Ops:
    def __init__(self, P):
        self.P = P

    def MM(self, o, l, rh, st, sp, r, w):
        self.P.op('pe', lambda e: e.matmul(o, lhsT=l, rhs=rh, start=st, stop=sp), r=r, w=w)

    def TR(self, o, i, idn, r, w):
        self.P.op('pe', lambda e: e.transpose(out=o, in_=i, identity=idn), r=r, w=w)

    def ACT(self, o, i, f, r, w, **kw):
        self.P.op('act', lambda e: e.activation(out=o, in_=i, func=f, **kw), r=r, w=w)

    def TT(self, eng, o, a, b, op, r, w):
        self.P.op(eng, lambda e: e.tensor_tensor(out=o, in0=a, in1=b, op=op), r=r, w=w)

    def STT(self, eng, o, a, s, b, op0, op1, r, w):
        self.P.op(eng, lambda e: e.scalar_tensor_tensor(out=o, in0=a, scalar=s, in1=b, op0=op0, op1=op1), r=r, w=w)

    def TS(self, eng, o, a, s1, s2, op0, op1, r, w):
        self.P.op(eng, lambda e: e.tensor_scalar(out=o, in0=a, scalar1=s1, scalar2=s2, op0=op0, op1=op1), r=r, w=w)

    def CP(self, eng, o, i, r, w):
        if eng == 'act':
            self.P.op('act', lambda e: e.activation(out=o, in_=i, func=AF.Copy), r=r, w=w)
        else:
            self.P.op(eng, lambda e: e.tensor_copy(out=o, in_=i), r=r, w=w)

    def TSA(self, eng, o, a, s1, r, w):
        self.P.op(eng, lambda e: e.tensor_scalar_add(out=o, in0=a, scalar1=s1), r=r, w=w)

    def TSM(self, eng, o, a, s1, r, w):
        self.P.op(eng, lambda e: e.tensor_scalar_mul(out=o, in0=a, scalar1=s1), r=r, w=w)

    def RCP(self, o, i, r, w):
        self.P.op('dve', lambda e: e.reciprocal(out=o, in_=i), r=r, w=w)

    def MSET(self, eng, o, v, w):
        self.P.op(eng, lambda e: e.memset(o, v), w=w)

    def RED(self, o, i, r, w):
        self.P.op('dve', lambda e: e.tensor_reduce(out=o, in_=i, axis=AX.X, op=ALU.add), r=r, w=w)

    def DMA(self, eng, o, i, r=(), w=(), out=False, slow=False):
        if slow:
            self.P.op(eng, lambda e: e.dma_start(out=o, in_=i, allow_slow_non_contiguous=True), r=r, w=w, dma=True, out=out)
        else:
            self.P.op(eng, lambda e: e.dma_start(out=o, in_=i), r=r, w=w, dma=True, out=out)


CKV_ROWS = [5120 * 128]
PT_COLS = [128]


def in_specs(NT):
    return [
        ("xp", [NT, D], F32), ("xs", [32, D], F32), ("ckv", [CKV_ROWS[0], 288], F32), ("pt", [4, PT_COLS[0]], I32),
        ("spool", [2, 4, 15, 768], F32), ("cmk", [4, 4, 256, 256], F32), ("cmv", [4, 4, 256, 256], F32),
        ("memp", [256, D], F32),
        ("norm_mix", [4, D], F32), ("norm_ffn", [4, D], F32), ("w_out", [4, D, D], F32),
        ("w_ffn_in", [4, D, 2 * DFF], F32), ("w_ffn_out", [4, DFF, D], F32), ("norm_mem", [4, D], F32),
        ("w_mem_kv", [4, D, 512], F32), ("g_mem_q", [4, 64], F32), ("g_mem_k", [4, 64], F32),
        ("w_in_a", [2, D, D], F32), ("w_pool_grp", [2, 4, 192, 192], F32), ("pool_scale", [2, 768], F32),
        ("w_in_b", [2, D, 640], F32), ("g_q_lora", [2, 384], F32), ("w_uq", [2, 384, 1152], F32), ("g_q", [2, 96], F32),
        ("norm_kv", [D], F32), ("w_dkv", [D, 288], F32), ("g_kv_lora", [256], F32), ("w_uk", [256, 768], F32),
        ("w_uv", [256, 768], F32), ("g_k_nope", [64], F32), ("g_k_rope", [16], F32),
        ("c_ident", [128, 128], F32), ("c_perm", [96, 96], F32), ("c_cos", [96, 2048 + 32], F32),
        ("c_sin", [96, 2048 + 32], F32), ("c_tri", [128, 128], F32), ("c_rcnt", [96, 8, 16], F32),
        ("c_smask", [32, 4, 96], F32),
    ]


def out_specs(NT):
    return [
        ("y_p", [NT, D]), ("y_s", [32, D]), ("kv_p", [NT, 288]), ("kv_s", [32, 288]),
        ("pool_p", [2, 15, 768]), ("pool_s", [2, 4, 15, 768]), ("mk_p", [4, 256, 256]), ("mv_p", [4, 256, 256]),
    ]


def host_consts():
    c = {}
    c["c_ident"] = np.eye(128, dtype=np.float32)
    perm = np.zeros((96, 96), np.float32)
    for i in range(16):
        perm[80 + i, 64 + i] = -1.0
        perm[64 + i, 80 + i] = 1.0
    c["c_perm"] = perm
    half = 16
    inv_freq = (np.float32(10000.0) ** (-np.arange(half, dtype=np.float32) / np.float32(half))).astype(np.float32)
    pos = np.concatenate([np.arange(2048), np.tile(16384 + np.arange(8), 4)]).astype(np.float32)
    ang = (pos[None, :] * inv_freq[:, None]).astype(np.float32)
    cos = np.ones((96, 2080), np.float32)
    sin = np.zeros((96, 2080), np.float32)
    cos[64:80] = np.cos(ang)
    cos[80:96] = np.cos(ang)
    sin[64:80] = np.sin(ang)
    sin[80:96] = np.sin(ang)
    c["c_cos"], c["c_sin"] = cos, sin
    kk = np.arange(128)
    c["c_tri"] = (kk[None, :] >= kk[:, None]).astype(np.float32)
    rc = np.zeros((96, 8, 16), np.float32)
    for ch in range(8):
        w = 2 ** (ch // 2 + 1)
        rc[:, ch, :] = (1.0 / np.minimum(w, np.arange(16) + 1)).astype(np.float32)[None, :]
    c["c_rcnt"] = rc
    sm = np.zeros((32, 4, 12, 8), np.float32)
    for k in range(32):
        s = k // 8
        for q in range(8):
            if k % 8 <= q:
                sm[k, s, :, q] = 1.0
    c["c_smask"] = sm.reshape(32, 4, 96)
    return c


def build(NT=2048, SEG=512, do_prompt=True, do_sample=True, nlayers=4, NPG=128):
    nc = bass.Bass("TRN2", target_bir_lowering=False)
    I = {n: nc.dram_tensor(n, s, dt, kind="ExternalInput").ap() for n, s, dt in in_specs(NT)}
    O = {n: nc.dram_tensor(n, s, F32, kind="ExternalOutput").ap() for n, s in out_specs(NT)}
    SC = None
    if do_prompt and do_sample:
        SC = {n: nc.dram_tensor("sc_" + n, shp, BF16).ap() for n, shp in (
            ("w_ffn_in", [4, D, 2 * DFF]), ("w_ffn_out", [4, DFF, D]), ("w_in_a", [2, D, D]), ("w_in_b", [2, D, 640]),
            ("w_out", [4, D, D]), ("w_pool_grp", [2, 4, 192, 192]), ("w_uq", [2, 384, 1152]))}
    if do_prompt:
        with ExitStack() as es:
            prompt_program(nc, es, I, O, NT, SEG, nlayers, SC)
    if do_sample:
        with ExitStack() as es:
            sample_program(nc, es, I, O, nlayers, NPG, SC)
    return nc


def load_small(P, o, I, C):
    g1 = P.sb("gst1", [128, 128], F32)
    g2 = P.sb("gst2", [32, 128], F32)
    gT1 = P.sb("gT1", [128, 128], F32)
    gT2 = P.sb("gT2", [128, 32], F32)
    o.MSET('dve', g1[:], 0.0, w=['gst1'])
    o.MSET('dve', g2[:], 0.0, w=['gst2'])
    o.DMA('sp', g1[0:32, :], I["norm_mix"].rearrange("l (k p) -> (l k) p", p=128), w=['gst1'])
    o.DMA('sp', g1[32:64, :], I["norm_ffn"].rearrange("l (k p) -> (l k) p", p=128), w=['gst1'])
    o.DMA('sp', g1[64:96, :], I["norm_mem"].rearrange("l (k p) -> (l k) p", p=128), w=['gst1'])
    o.DMA('sp', g1[96:104, :], I["norm_kv"].rearrange("(k p) -> k p", p=128), w=['gst1'])
    o.DMA('sp', g1[104:106, :], I["g_kv_lora"].rearrange("(k p) -> k p", p=128), w=['gst1'])
    o.DMA('sp', g1[106:112, :], I["g_q_lora"].rearrange("l (k p) -> (l k) p", p=128), w=['gst1'])
    o.DMA('sp', g2[0:16, 0:96], I["pool_scale"].rearrange("l (k p) -> (l k) p", p=96), w=['gst2'])
    o.DMA('sp', g2[16:20, 0:64], I["g_mem_q"][:, :], w=['gst2'])
    o.DMA('sp', g2[16:20, 64:128], I["g_mem_q"][:, :], w=['gst2'])
    o.DMA('sp', g2[20:24, 0:64], I["g_mem_k"][:, :], w=['gst2'])
    o.DMA('sp', g2[20:24, 64:128], I["g_mem_k"][:, :], w=['gst2'])
    o.DMA('sp', g2[24:26, 0:96], I["g_q"][:, :], w=['gst2'])
    o.DMA('sp', g2[26:27, 0:64], I["g_k_nope"].rearrange("(o p) -> o p", o=1), w=['gst2'])
    o.DMA('sp', g2[26:27, 64:80], I["g_k_rope"].rearrange("(o p) -> o p", o=1), w=['gst2'])
    o.DMA('sp', g2[26:27, 80:96], I["g_k_rope"].rearrange("(o p) -> o p", o=1), w=['gst2'])
    ps = C['ps']
    o.TR(ps[0][:, 0:128], g1[:, :], C['ident'][:, :], r=['gst1', 'ident'], w=[('ps', 0)])
    o.CP('dve', gT1[:, :], ps[0][:, 0:128], r=[('ps', 0)], w=['gains'])
    o.TR(ps[1][:, 0:32], g2[:, :], C['ident'][0:32, 0:32], r=['gst2', 'ident'], w=[('ps', 1)])
    o.CP('dve', gT2[:, :], ps[1][:, 0:32], r=[('ps', 1)], w=['gains'])
    G = {'gmix': gT1[:, 0:32], 'gffn': gT1[:, 32:64], 'gmem': gT1[:, 64:96], 'gkv': gT1[:, 96:104], 'gkvl': gT1[:, 104:106],
         'gql': gT1[:, 106:112], 'pscale': gT2[0:96, 0:16], 'gmq': gT2[:, 16:20], 'gmk': gT2[:, 20:24], 'gq': gT2[0:96, 24:26],
         'gk': gT2[0:96, 26:27]}
    return G


def alloc_common(P, o, I):
    C = {}
    C['ident'] = P.sb("ident", [128, 128], F32)
    C['ones'] = P.sb("ones", [128, 128], BF16)
    C['blk2'] = P.sb("blk2", [128, 128], BF16)
    C['perm'] = P.sb("perm", [96, 96], BF16)
    C['tri'] = P.sb("tri", [128, 128], BF16)
    C['rcnt'] = P.sb("rcnt", [96, 8, 16], F32)
    o.DMA('sp', C['ident'][:], I["c_ident"][:, :], w=['ident'])
    o.DMA('pool', C['perm'][:], I["c_perm"][:, :], w=['perm'])
    o.DMA('pool', C['tri'][:], I["c_tri"][:, :], w=['tri'])
    o.DMA('sp', C['rcnt'][:], I["c_rcnt"][:, :, :], w=['rcnt'])
    o.MSET('dve', C['ones'][:], 1.0, w=['ones'])
    o.MSET('dve', C['blk2'][:], 0.0, w=['blk2'])
    o.MSET('dve', C['blk2'][0:64, 0:64], 1.0, w=['blk2'])
    o.MSET('dve', C['blk2'][64:128, 64:128], 1.0, w=['blk2'])
    C['wuk'] = P.sb("wuk", [128, 2, 768], BF16)
    C['wuv'] = P.sb("wuv", [128, 2, 768], BF16)
    C['wdkv'] = P.sb("wdkv", [128, 8, 352], BF16)
    o.DMA('pool', C['wuk'][:], I["w_uk"].rearrange("(k p) n -> p k n", p=128), w=['wuk'])
    o.DMA('pool', C['wuv'][:], I["w_uv"].rearrange("(k p) n -> p k n", p=128), w=['wuv'])
    o.MSET('pool', C['wdkv'][:, :, 256:320], 0.0, w=['wdkv'])
    o.DMA('pool', C['wdkv'][:, :, 0:256], I["w_dkv"].rearrange("(k p) n -> p k n", p=128)[:, :, 0:256], w=['wdkv'])
    o.DMA('pool', C['wdkv'][:, :, 320:352], I["w_dkv"].rearrange("(k p) n -> p k n", p=128)[:, :, 256:288], w=['wdkv'])
    ps = [P.ps("ps%d" % i, [128, 512], F32) for i in range(8)]
    C['ps'] = ps
    C['G'] = load_small(P, o, I, C)
    C['rA'] = Rot([0, 1])
    C['rS'] = Rot([2, 3])
    C['rO'] = Rot([4, 5])
    C['rG'] = Rot([6, 7])
    C['tmp'] = P.sb("tmpn", [128, 512], F32)
    C['rstd'] = P.sb("rstd", [128, 512], F32)
    C['rtmp'] = Rot([0])
    return C


def norm_fm(o, C, xaps, gcols, haps, np_, Dn, tb, rkeys, hkey, ones, eps=EPS, sq=None, split=False):
    ps = C['ps']
    b = C['rA'].next()
    pk = ('ps', b)
    nch = len(xaps)
    sqaps, sqkey = (haps, hkey) if sq is None else sq
    if split and nch == 8:
        order = [0, 2, 4, 6, 1, 3, 5, 7]
        for c in order:
            rk_c = [rkeys[c]] if len(rkeys) == 8 else rkeys
            if c % 2 == 0:
                o.ACT(sqaps[c], xaps[c], AF.Square, r=rk_c, w=[(sqkey, 'sq', c)])
            else:
                o.TT('dve', sqaps[c], xaps[c], xaps[c], ALU.mult, r=rk_c, w=[(sqkey, 'sq', c)])
        for i, c in enumerate(order):
            o.MM(ps[b][0:np_, 0:tb], ones, sqaps[c], i == 0, i == nch - 1, r=[(sqkey, 'sq', c), 'ones', 'blk2'], w=[pk])
        sqdeps = [(sqkey, 'sq', c) for c in range(8)]
    else:
        for c in range(nch):
            o.ACT(sqaps[c], xaps[c], AF.Square, r=rkeys, w=[sqkey])
        for c in range(nch):
            o.MM(ps[b][0:np_, 0:tb], ones, sqaps[c], c == 0, c == nch - 1, r=[sqkey, 'ones', 'blk2'], w=[pk])
        sqdeps = [sqkey]
    o.ACT(C['tmp'][0:np_, 0:tb], ps[b][0:np_, 0:tb], AF.Ln, r=[pk], w=['tmpn'], scale=1.0 / Dn, bias=eps)
    o.ACT(C['rstd'][0:np_, 0:tb], C['tmp'][0:np_, 0:tb], AF.Exp, r=['tmpn'], w=['rstd'], scale=-0.5)
    for c in range(nch):
        o.STT('dve', haps[c], xaps[c], gcols[c], C['rstd'][0:np_, 0:tb], ALU.mult, ALU.mult,
              r=list(rkeys) + ['rstd', 'gains'] + sqdeps, w=[hkey])


def prompt_program(nc, es, I, O, NT, SEG, nlayers, SC=None):
    P = Prog(nc, es)
    o = Ops(P)
    C = alloc_common(P, o, I)
    ps = C['ps']
    G = C['G']
    ident, ones, blk2, perm, tri = C['ident'], C['ones'], C['blk2'], C['perm'], C['tri']
    SCM = {"k": nc.dram_tensor("sc_memKT", [4, 128, 512], BF16).ap(), "v": nc.dram_tensor("sc_Vaug", [4, 128, 1024], BF16).ap()}
    NB = SEG // TB
    NSEG = NT // SEG
    NKC = NT // 128
    tb = TB

    xT = P.sb("xT", [128, 8, SEG], F32)
    hT = P.sb("hT", [128, 8, SEG], BF16)
    win = P.sb("win", [128, 8, 1024], BF16)
    wout = P.sb("wout", [128, 10, 1024], BF16)
    wmisc = P.sb("wmisc", [128, 3456], BF16)
    NRG, NRO = 2, 3
    ringG = [P.sb("ringG%d" % i, [128, 4096], BF16) for i in range(NRG)]
    ringO = [P.sb("ringO%d" % i, [128, 2048], BF16) for i in range(NRO)]
    cT = P.sb("cT", [128, 2, NT], BF16)
    KT = [P.sb("KT%d" % i, [96, NT], BF16) for i in range(2)]
    Vh = [P.sb("Vh%d" % i, [128, NKC, 128], BF16) for i in range(2)]
    kinv = P.sb("kinv", [128, NKC, 12], F32)
    cat = P.sb("cat", [128, 10, tb], BF16)
    qm = P.sb("qm", [128, 2, tb], F32)
    qn = P.sb("qn", [128, 2, tb], BF16)
    PT = [P.sb("PT%d" % i, [128, tb], BF16) for i in range(4)]
    rPT = Rot([0, 1, 2, 3])
    rc = P.sb("rc", [128, tb], F32)
    cosb = P.sb("cosb", [96, tb], F32)
    sinb = P.sb("sinb", [96, tb], F32)
    memKT = P.sb("memKT", [128, 2, 256], BF16)
    Vaug = P.sb("Vaug", [128, 2, 4, 128], BF16)
    halo = P.sb("halo", [96, 2, 8, 16], F32)
    AFN, ABN = 7616, 4096
    arf = P.sb("arena_f", [128, AFN], F32)
    arb = P.sb("arena_b", [128, ABN], BF16)

    def cv(ar, off, parts, n, pat=None, **kw):
        v = ar[0:parts, off:off + n]
        return v.rearrange(pat, **kw) if pat else v
    xin = cv(arf, 0, 128, 1024)
    xinb = [xin, cv(arf, 4608, 128, 1024)]
    memT = cv(arf, 1024, 128, 2048, "p (c t) -> p c t", t=256)
    kf = cv(arf, 3072, 128, 512, "p (c t) -> p c t", t=256)
    kout = cv(arf, 3584, 128, 512, "p (c t) -> p c t", t=256)
    vout = cv(arf, 4096, 128, 512, "p (c t) -> p c t", t=256)
    hmT = cv(arb, 0, 128, 2048, "p (c t) -> p c t", t=256)
    E_ = 16 + tb
    U = cv(arf, 0, 96, 8 * E_, "p (c t) -> p c t", t=E_)
    tA = cv(arf, 8 * E_, 96, 2 * E_, "p (c t) -> p c t", t=E_)
    tB_ = cv(arf, 10 * E_, 96, 2 * E_, "p (c t) -> p c t", t=E_)
    t1a = cv(arf, 12 * E_, 96, 512)
    pso = cv(arf, 12 * E_ + 512, 16, 768)
    dif = cv(arb, 0, 96, 4096, "p (c t) -> p c t", t=tb)
    cq = cv(arf, 0, 128, 1536, "p (c t) -> p c t", t=tb)
    ckf = cv(arf, 0, 128, 1024, "p (c t) -> p c t", t=tb)
    krf = cv(arf, 1024, 96, 512)
    t1 = cv(arf, 1536, 96, 512)
    t2 = cv(arf, 2048, 96, 512)
    kvo = cv(arf, 2560, 128, 288)
    ssn = cv(arf, 2848, 128, 16)
    cqn = cv(arb, 0, 128, 1536, "p (c t) -> p c t", t=tb)
    QT = [cv(arb, 1536, 96, 512), cv(arb, 2048, 96, 512)]
    qg = cv(arb, 2560, 96, 512)
    krb = cv(arb, 1536, 96, 512)
    sqn = cv(arb, 2048, 128, 768)
    sqr = cv(arb, 2816, 128, 32)
    sg = [cv(arf, 0, 128, 512), cv(arf, 512, 128, 512)]
    stg = [cv(arf, 1024, 128, 512), cv(arf, 1536, 128, 512)]
    aT = [cv(arb, 0, 128, 1024, "p (c t) -> p c t", t=tb), cv(arb, 1024, 128, 1024, "p (c t) -> p c t", t=tb)]
    wmem = ringG[0]
    ARK = ['xin', 'xin1', 'memT', 'kf', 'kout', 'vout', 'hmT', 'U', 'tA', 'tB', 't1', 't2', 'pso', 'dif', 'cq', 'ckf', 'krf', 'kvo',
           'ssn', 'ssn2', 'cqn', ('QT', 0), ('QT', 1), 'qg', 'krb', 'sqn', 'sqr', ('sg', 0), ('sg', 1), ('aT', 0), ('aT', 1), ('stg', 0), ('stg', 1)]
    dummy = P.sb("dummyb", [128, 8], F32)

    def barrier():
        P.op('pool', lambda e: e.memset(dummy[:], 0.0), r=ARK, w=ARK)

    o.MSET('pool', Vaug[:], 1.0, w=['Vaug'])
    for i in range(2):
        o.MSET('pool', Vh[i][:], 1.0, w=[('Vh', i)])

    def wsrc(first):
        return (I, 'pool') if (SC is None or first) else (SC, 'sp')

    def wb(first, dst, src_sb, rkey, wkey):
        if SC is not None and first:
            o.DMA('sp', dst, src_sb, r=[rkey], w=[wkey], out=True)
    NFG = DFF // 256

    def XK(j):
        return [('xT', j, c) for c in range(8)]

    ring_state = {'g': 0, 'o': 0}

    def load_G(l, fg, first):
        src, q = wsrc(first)
        s_ = ring_state['g'] % NRG
        ring_state['g'] += 1
        rg = ringG[s_]
        k = ('rg', s_)
        gv = rg[:, 0:2048].rearrange("p (k n) -> p k n", n=256)
        uv = rg[:, 2048:4096].rearrange("p (k n) -> p k n", n=256)
        wi = src["w_ffn_in"][l].rearrange("(k p) n -> p k n", p=128)
        rk = [] if src is I else [('scw', 'g', l, fg)]
        o.DMA(q, gv, wi[:, :, fg * 256:(fg + 1) * 256], r=rk, w=[k])
        o.DMA(q, uv, wi[:, :, DFF + fg * 256:DFF + (fg + 1) * 256], r=rk, w=[k])
        if SC is not None and first:
            so = SC["w_ffn_in"][l].rearrange("(k p) n -> p k n", p=128)
            wb(first, so[:, :, fg * 256:(fg + 1) * 256], gv, k, ('scw', 'g', l, fg))
            wb(first, so[:, :, DFF + fg * 256:DFF + (fg + 1) * 256], uv, k, ('scw', 'g', l, fg))
        return gv, uv, k

    def load_O(l, fg, first):
        src, q = wsrc(first)
        s_ = ring_state['o'] % NRO
        ring_state['o'] += 1
        k = ('ro', s_)
        ov = ringO[s_][:, 0:2048].rearrange("p (k n) -> p k n", n=1024)
        wo = src["w_ffn_out"][l].rearrange("(k p) n -> p k n", p=128)
        rk = [] if src is I else [('scw', 'o', l, fg)]
        o.DMA(q, ov, wo[:, fg * 2:fg * 2 + 2, :], r=rk, w=[k])
        if SC is not None and first:
            so = SC["w_ffn_out"][l].rearrange("(k p) n -> p k n", p=128)
            wb(first, so[:, fg * 2:fg * 2 + 2, :], ov, k, ('scw', 'o', l, fg))
        return ov, k

    def load_mix_weights(l, first):
        W, WQ = wsrc(first)
        rk = [] if W is I else [('scw', 'mix', l)]
        wk = ('scw', 'mix', l)
        if l < 2:
            o.DMA(WQ, win[:], W["w_in_a"][l].rearrange("(k p) n -> p k n", p=128), r=rk, w=['win'])
            o.DMA(WQ, wout[0:96, 0:8, :], W["w_out"][l][0:768, :].rearrange("(k p) n -> p k n", p=96), r=rk, w=['wout'])
            o.DMA(WQ, wout[:, 8:10, :], W["w_out"][l][768:1024, :].rearrange("(k p) n -> p k n", p=128), r=rk, w=['wout'])
            gv = wmisc[0:96, 0:1536].rearrange("p (k n) -> p k n", n=192)
            o.DMA(WQ, gv, W["w_pool_grp"][l].rearrange("g (c p) e -> p (g c) e", p=96), r=rk, w=['wmisc'])
            if SC is not None and first:
                wb(first, SC["w_in_a"][l].rearrange("(k p) n -> p k n", p=128), win[:], 'win', wk)
                wb(first, SC["w_out"][l][0:768, :].rearrange("(k p) n -> p k n", p=96), wout[0:96, 0:8, :], 'wout', wk)
                wb(first, SC["w_out"][l][768:1024, :].rearrange("(k p) n -> p k n", p=128), wout[:, 8:10, :], 'wout', wk)
                wb(first, SC["w_pool_grp"][l].rearrange("g (c p) e -> p (g c) e", p=96), gv, 'wmisc', wk)
        else:
            j = l - 2
            o.DMA(WQ, win[:, :, 0:640], W["w_in_b"][j].rearrange("(k p) n -> p k n", p=128), r=rk, w=['win'])
            o.DMA(WQ, wout[:, 0:8, :], W["w_out"][l].rearrange("(k p) n -> p k n", p=128), r=rk, w=['wout'])
            uq = wmisc[:, 0:3456].rearrange("p (k n) -> p k n", n=1152)
            o.DMA(WQ, uq, W["w_uq"][j].rearrange("(k p) n -> p k n", p=128), r=rk, w=['wmisc'])
            if SC is not None and first:
                wb(first, SC["w_in_b"][j].rearrange("(k p) n -> p k n", p=128), win[:, :, 0:640], 'win', wk)
                wb(first, SC["w_out"][l].rearrange("(k p) n -> p k n", p=128), wout[:, 0:8, :], 'wout', wk)
                wb(first, SC["w_uq"][j].rearrange("(k p) n -> p k n", p=128), uq, 'wmisc', wk)

    def mem_kv(l, write_out):
        if not write_out:
            o.DMA('sp', memKT[:], SCM["k"][l].rearrange("p (c m) -> p c m", m=256), r=[('scm', l)], w=['memKT'])
            o.DMA('sp', Vaug[:], SCM["v"][l].rearrange("p (c h d) -> p c h d", h=4, d=128), r=[('scm', l)], w=['Vaug'])
            return
        barrier()
        o.DMA('pool', wmem[:, 0:4096].rearrange("p (k n) -> p k n", n=512),
              I["w_mem_kv"][l].rearrange("(k p) n -> p k n", p=128), w=[('rg', 0)])
        wm = wmem[:, 0:4096].rearrange("p (k n) -> p k n", n=512)
        for mt in range(2):
            o.DMA('sp', xin[:], I["memp"][mt * 128:(mt + 1) * 128, :], w=['xin'])
            for c in range(8):
                b = C['rA'].next()
                o.TR(ps[b][:, 0:128], xin[:, c * 128:(c + 1) * 128], ident[:], r=['xin', 'ident'], w=[('ps', b)])
                o.CP('dve' if c % 2 else 'act', memT[:, c, mt * 128:(mt + 1) * 128], ps[b][:, 0:128], r=[('ps', b)], w=['memT'])
        norm_fm(o, C, [memT[:, c, :] for c in range(8)], [G['gmem'][:, l * 8 + c:l * 8 + c + 1] for c in range(8)],
                [hmT[:, c, :] for c in range(8)], 128, D, 256, ['memT'], 'hmT', ones[:])
        for hp in range(2):
            b = C['rA'].next()
            for k in range(8):
                o.MM(ps[b][:, 0:256], wm[:, k, hp * 128:(hp + 1) * 128], hmT[:, k, :], k == 0, k == 7,
                     r=['hmT', ('rg', 0)], w=[('ps', b)])
            norm_fm(o, C, [ps[b][:, 0:256]], [G['gmk'][:, l:l + 1]], [kf[:, hp, :]], 128, 64, 256, [('ps', b)], 'kf', blk2[:],
                    sq=([qn[:, 0, 0:256]], 'qn'))
        o.CP('dve', memKT[:], kf[:], r=['kf'], w=['memKT'])
        for mc in range(2):
            b = C['rA'].next()
            for k in range(8):
                o.MM(ps[b][:, 0:256], hmT[:, k, mc * 128:(mc + 1) * 128], wm[:, k, 256:512], k == 0, k == 7,
                     r=['hmT', ('rg', 0)], w=[('ps', b)])
            if write_out:
                o.CP('act', vout[:, mc, :], ps[b][:, 0:256], r=[('ps', b)], w=['vout'])
            for h in range(4):
                off = 0 if h % 2 == 0 else 64
                o.CP('dve', Vaug[:, mc, h, off:off + 64], ps[b][:, h * 64:(h + 1) * 64], r=[('ps', b)], w=['Vaug'])
        o.DMA('sp', SCM["k"][l].rearrange("p (c m) -> p c m", m=256), memKT[:], r=['memKT'], w=[('scm', l)])
        o.DMA('sp', SCM["v"][l].rearrange("p (c h d) -> p c h d", h=4, d=128), Vaug[:], r=['Vaug'], w=[('scm', l)])
        if write_out:
            for hp in range(2):
                for mc in range(2):
                    b = C['rA'].next()
                    o.TR(ps[b][:, 0:128], kf[:, hp, mc * 128:(mc + 1) * 128], ident[:], r=['kf', 'ident'], w=[('ps', b)])
                    o.CP('act', kout[:, mc, hp * 128:(hp + 1) * 128], ps[b][:, 0:128], r=[('ps', b)], w=['kout'])
            o.DMA('sp', O["mk_p"][l].rearrange("(m p) f -> p m f", p=128), kout[:], r=['kout'], out=True)
            o.DMA('sp', O["mv_p"][l].rearrange("(m p) f -> p m f", p=128), vout[:], r=['vout'], out=True)

    def mem_qnorm(l):
        for hp in range(2):
            norm_fm(o, C, [qm[:, hp, :]], [G['gmq'][:, l:l + 1]], [qn[:, hp, :]], 128, 64, tb, ['qm'], 'qn', blk2[:])

    def mem_attend(l, j, catbase):
        st = {}

        def SS(h):
            hp, base = h // 2, (h % 2) * 64
            pts = []
            for mc in range(2):
                bsc = C['rS'].next()
                o.MM(ps[bsc][:, 0:tb], memKT[base:base + 64, hp, mc * 128:(mc + 1) * 128], qn[base:base + 64, hp, :], True, True,
                     r=['memKT', 'qn'], w=[('ps', bsc)])
                pi = rPT.next()
                o.ACT(PT[pi][:, :], ps[bsc][:, 0:tb], AF.Exp, r=[('ps', bsc)], w=[('PT', pi)], scale=0.125)
                pts.append(pi)
            st[h] = pts

        def PVV(h):
            hp, base = h // 2, (h % 2) * 64
            pts = st[h]
            bo = C['rO'].next()
            for mc in range(2):
                o.MM(ps[bo][:, 0:tb], Vaug[:, mc, h, :], PT[pts[mc]][:, :], mc == 0, mc == 1,
                     r=['Vaug', ('PT', pts[mc])], w=[('ps', bo)])
            vr = slice(base, base + 64)
            dr = slice(64 - base, 128 - base)
            o.ACT(rc[dr, :], ps[bo][dr, 0:tb], AF.Ln, r=[('ps', bo)], w=['rc'])
            o.ACT(rc[dr, :], rc[dr, :], AF.Exp, r=['rc'], w=['rc'], scale=-1.0)
            o.TT('dve', cat[vr, catbase + hp, :], ps[bo][vr, 0:tb], rc[dr, :], ALU.mult, r=[('ps', bo), 'rc'], w=['cat'])

        SS(0)
        for h in range(4):
            if h + 1 < 4:
                SS(h + 1)
            PVV(h)

    def out_proj(l, j):
        blk = slice(j * tb, (j + 1) * tb)
        if l < 2:
            chunks = [(96, c) for c in range(8)] + [(128, 8), (128, 9)]
        else:
            chunks = [(128, c) for c in range(8)]
        for oc in range(8):
            b = C['rA'].next()
            for i, (kk, c) in enumerate(chunks):
                o.MM(ps[b][:, 0:tb], wout[0:kk, c, oc * 128:(oc + 1) * 128], cat[0:kk, c, :], i == 0, i == len(chunks) - 1,
                     r=['wout', 'cat'], w=[('ps', b)])
            o.TT('dve', xT[:, oc, blk], xT[:, oc, blk], ps[b][:, 0:tb], ALU.add, r=[('ps', b), ('xT', j, oc)], w=[('xT', j, oc)])

    def mix_a(l, sg_, j):
        barrier()
        blk = slice(j * tb, (j + 1) * tb)
        jg = sg_ * NB + j
        norm_fm(o, C, [xT[:, c, blk] for c in range(8)], [G['gmix'][:, l * 8 + c:l * 8 + c + 1] for c in range(8)],
                [hT[:, c, blk] for c in range(8)], 128, D, tb, XK(j), ('hT', j), ones[:], split=True)
        if jg == 0:
            o.MSET('pool', U[:, :, 0:16], 0.0, w=['U'])
        elif j == 0:
            o.CP('pool', U[:, :, 0:16], halo[:, l, :, :], r=[('halo', l)], w=['U'])
        for c in range(8):
            b = C['rA'].next()
            for k in range(8):
                o.MM(ps[b][0:96, 0:tb], win[:, k, c * 96:(c + 1) * 96], hT[:, k, blk], k == 0, k == 7,
                     r=['win', ('hT', j)], w=[('ps', b)])
            o.CP('act', U[:, c, 16:16 + tb], ps[b][0:96, 0:tb], r=[('ps', b)], w=['U'])
        for hp in range(2):
            b = C['rA'].next()
            for k in range(8):
                o.MM(ps[b][:, 0:tb], win[:, k, 768 + hp * 128:768 + (hp + 1) * 128], hT[:, k, blk], k == 0, k == 7,
                     r=['win', ('hT', j)], w=[('ps', b)])
            o.CP('act', qm[:, hp, :], ps[b][:, 0:tb], r=[('ps', b)], w=['qm'])
        mem_qnorm(l)
        gw = wmisc[0:96, 0:1536].rearrange("p (k n) -> p k n", n=192)
        E = 16 + tb
        for g in range(4):
            w_ = 2 ** (g + 1)
            src = U[:, 2 * g:2 * g + 2, :]
            bufs = [tA, tB_]
            d = 1
            i = 0
            srck = 'U'
            while d < w_:
                dst = bufs[i % 2]
                dk = 'tA' if i % 2 == 0 else 'tB'
                lo = 2 * d
                o.TT('pool', dst[:, :, lo:E], src[:, :, lo:E], src[:, :, lo - d:E - d], ALU.add, r=[srck], w=[dk])
                src, srck = dst, dk
                d *= 2
                i += 1
            o.STT('dve', dif[:, 2 * g:2 * g + 2, :], src[:, :, 16:E], 1.0 / w_, U[:, 2 * g:2 * g + 2, 16:E], ALU.mult, ALU.subtract,
                  r=[srck, 'U'], w=['dif'])
            if jg == 0:
                o.TT('dve', t1a[:, 0:32].rearrange("p (c t) -> p c t", t=16), src[:, :, 16:32], C['rcnt'][:, 2 * g:2 * g + 2, :],
                     ALU.mult, r=[srck, 'rcnt'], w=['t1'])
                o.TT('dve', dif[:, 2 * g:2 * g + 2, 0:16], t1a[:, 0:32].rearrange("p (c t) -> p c t", t=16),
                     U[:, 2 * g:2 * g + 2, 16:32], ALU.subtract, r=['t1', 'U', 'dif'], w=['dif'])
            for ec in range(2):
                b = C['rA'].next()
                for cc in range(2):
                    o.MM(ps[b][0:96, 0:tb], gw[:, 2 * g + cc, ec * 96:(ec + 1) * 96], dif[:, 2 * g + cc, :], cc == 0, cc == 1,
                         r=['wmisc', 'dif'], w=[('ps', b)])
                ch = 2 * g + ec
                o.TSM('dve', cat[0:96, ch, :], ps[b][0:96, 0:tb], G['pscale'][:, l * 8 + ch:l * 8 + ch + 1],
                      r=[('ps', b), 'gains'], w=['cat'])
        if jg == NSEG * NB - 1:
            for c in range(8):
                b = C['rA'].next()
                o.TR(ps[b][0:15, 0:96], U[:, c, E - 15:E], ident[0:96, 0:96], r=['U', 'ident'], w=[('ps', b)])
                o.CP('act', pso[0:15, c * 96:(c + 1) * 96], ps[b][0:15, 0:96], r=[('ps', b)], w=['pso'])
            o.DMA('sp', O["pool_p"][l], pso[0:15, :], r=['pso'], out=True)
        elif j == NB - 1:
            o.CP('pool', halo[:, l, :, :], U[:, :, tb:tb + 16], r=['U'], w=[('halo', l)])
        else:
            o.CP('pool', tA[:, 0, 0:128].rearrange("p (c t) -> p c t", t=16), U[:, :, tb:tb + 16], r=['U'], w=['tA'])
            o.CP('pool', U[:, :, 0:16], tA[:, 0, 0:128].rearrange("p (c t) -> p c t", t=16), r=['tA'], w=['U'])
        mem_attend(l, j, 8)
        out_proj(l, j)

    def kv_stage(sg_, j):
        barrier()
        blk = slice(j * tb, (j + 1) * tb)
        jg = sg_ * NB + j
        gb = slice(jg * tb, (jg + 1) * tb)
        wd = C['wdkv']
        norm_fm(o, C, [xT[:, c, blk] for c in range(8)], [G['gkv'][:, c:c + 1] for c in range(8)],
                [hT[:, c, blk] for c in range(8)], 128, D, tb, XK(j), ('hT', j), ones[:], split=True)
        o.DMA('sp', cosb[:], I["c_cos"][:, gb], w=['cosb'])
        o.DMA('sp', sinb[:], I["c_sin"][:, gb], w=['sinb'])
        for cc in range(2):
            b = C['rA'].next()
            for k in range(8):
                o.MM(ps[b][:, 0:tb], wd[:, k, cc * 128:(cc + 1) * 128], hT[:, k, blk], k == 0, k == 7,
                     r=['wdkv', ('hT', j)], w=[('ps', b)])
            o.CP('act', ckf[:, cc, :], ps[b][:, 0:tb], r=[('ps', b)], w=['ckf'])
        br = C['rG'].next()
        for k in range(8):
            o.MM(ps[br][0:96, 0:tb], wd[:, k, 256:352], hT[:, k, blk], k == 0, k == 7, r=['wdkv', ('hT', j)], w=[('ps', br)])
        o.MSET('pool', krb[0:64, :], 0.0, w=['krb'])
        o.CP('act', krb[64:96, :], ps[br][64:96, 0:tb], r=[('ps', br)], w=['krb'])
        b2 = C['rG'].next()
        o.MM(ps[b2][0:96, 0:tb], perm[0:96, 0:96], krb[0:96, :], True, True, r=['perm', 'krb'], w=[('ps', b2)])
        o.TT('dve', t1[64:96, :], ps[br][64:96, 0:tb], cosb[64:96, :], ALU.mult, r=[('ps', br), 'cosb'], w=['t1'])
        o.TT('dve', t2[64:96, :], ps[b2][64:96, 0:tb], sinb[64:96, :], ALU.mult, r=[('ps', b2), 'sinb'], w=['t2'])
        o.TT('dve', krf[64:96, :], t1[64:96, :], t2[64:96, :], ALU.add, r=['t1', 't2'], w=['krf'])
        for i in range(2):
            o.CP('pool', KT[i][64:96, gb], krf[64:96, :], r=['krf'], w=[('KT', i)])
        norm_fm(o, C, [ckf[:, cc, :] for cc in range(2)], [G['gkvl'][:, cc:cc + 1] for cc in range(2)],
                [cqn[:, cc, :] for cc in range(2)], 128, 256, tb, ['ckf'], 'cqn', ones[:])
        o.CP('pool', cT[:, :, gb], cqn[:, 0:2, :], r=['cqn'], w=['cT'])
        for cc in range(2):
            o.STT('dve', ckf[:, cc, :], ckf[:, cc, :], G['gkvl'][:, cc:cc + 1], C['rstd'][:, 0:tb], ALU.mult, ALU.mult,
                  r=['ckf', 'rstd', 'cqn'], w=['ckf'])
        for tt in range(tb // 128):
            ts_ = slice(tt * 128, (tt + 1) * 128)
            kc = jg * 4 + tt
            b = C['rA'].next()
            o.TR(ps[b][:, 0:128], ckf[:, 0, ts_], ident[:], r=['ckf', 'ident'], w=[('ps', b)])
            o.TR(ps[b][:, 128:256], ckf[:, 1, ts_], ident[:], r=['ckf', 'ident'], w=[('ps', b)])
            o.TR(ps[b][:, 256:288], krf[64:96, ts_], ident[64:96, 64:96], r=['krf', 'ident'], w=[('ps', b)])
            o.CP('act', kvo[:, :], ps[b][:, 0:288], r=[('ps', b)], w=['kvo'])
            o.DMA('sp', O["kv_p"][jg * tb + tt * 128: jg * tb + (tt + 1) * 128, :], kvo[:, :], r=['kvo'], out=True)
            b3, b4 = C['rG'].next(), C['rG'].next()
            for cc in range(2):
                o.MM(ps[b3][:, 0:512], cT[:, cc, jg * tb + tt * 128: jg * tb + (tt + 1) * 128], C['wuk'][:, cc, 0:512], cc == 0, cc == 1,
                     r=['cT', 'wuk'], w=[('ps', b3)])
            for cc in range(2):
                o.MM(ps[b4][:, 0:256], cT[:, cc, jg * tb + tt * 128: jg * tb + (tt + 1) * 128], C['wuk'][:, cc, 512:768], cc == 0, cc == 1,
                     r=['cT', 'wuk'], w=[('ps', b4)])
            o.ACT(sqn[:, 0:512], ps[b3][:, 0:512], AF.Square, r=[('ps', b3)], w=['sqn'])
            o.ACT(sqn[:, 512:768], ps[b4][:, 0:256], AF.Square, r=[('ps', b4)], w=['sqn'])
            o.RED(ssn[:, 0:12], sqn[:, :].rearrange("p (h d) -> p h d", d=64), r=['sqn'], w=['ssn'])
            o.ACT(sqr[:, 0:32], kvo[:, 256:288], AF.Square, r=['kvo'], w=['sqr'])
            o.RED(ssn[:, 12:13], sqr[:, 0:32].rearrange("p (h d) -> p h d", d=32), r=['sqr'], w=['ssn2'])
            o.TSA('dve', ssn[:, 0:12], ssn[:, 0:12], ssn[:, 12:13], r=['ssn', 'ssn2'], w=['ssn'])
            o.ACT(ssn[:, 0:12], ssn[:, 0:12], AF.Ln, r=['ssn'], w=['ssn'], scale=1.0, bias=96.0 * EPS)
            o.ACT(kinv[:, kc, :], ssn[:, 0:12], AF.Exp, r=['ssn'], w=['kinv'], scale=-0.5)

    def mix_b(l, sg_, j):
        barrier()
        jl = l - 2
        blk = slice(j * tb, (j + 1) * tb)
        jg = sg_ * NB + j
        gb = slice(jg * tb, (jg + 1) * tb)
        norm_fm(o, C, [xT[:, c, blk] for c in range(8)], [G['gmix'][:, l * 8 + c:l * 8 + c + 1] for c in range(8)],
                [hT[:, c, blk] for c in range(8)], 128, D, tb, XK(j), ('hT', j), ones[:], split=True)
        o.DMA('sp', cosb[:], I["c_cos"][:, gb], w=['cosb'])
        o.DMA('sp', sinb[:], I["c_sin"][:, gb], w=['sinb'])
        for c in range(3):
            b = C['rA'].next()
            for k in range(8):
                o.MM(ps[b][:, 0:tb], win[:, k, c * 128:(c + 1) * 128], hT[:, k, blk], k == 0, k == 7,
                     r=['win', ('hT', j)], w=[('ps', b)])
            o.CP('act', cq[:, c, :], ps[b][:, 0:tb], r=[('ps', b)], w=['cq'])
        for hp in range(2):
            b = C['rA'].next()
            for k in range(8):
                o.MM(ps[b][:, 0:tb], win[:, k, 384 + hp * 128:384 + (hp + 1) * 128], hT[:, k, blk], k == 0, k == 7,
                     r=['win', ('hT', j)], w=[('ps', b)])
            o.CP('act', qm[:, hp, :], ps[b][:, 0:tb], r=[('ps', b)], w=['qm'])
        norm_fm(o, C, [cq[:, c, :] for c in range(3)], [G['gql'][:, jl * 3 + c:jl * 3 + c + 1] for c in range(3)],
                [cqn[:, c, :] for c in range(3)], 128, 384, tb, ['cq'], 'cqn', ones[:])
        uq = wmisc[:, 0:3456].rearrange("p (k n) -> p k n", n=1152)
        nk = (jg + 1) * 4
        BQ, BN = 6, 7

        def prepA(h):
            ki = h % 2
            for k in range(3):
                o.MM(ps[BQ][0:96, 0:tb], uq[:, k, h * 96:(h + 1) * 96], cqn[:, k, :], k == 0, k == 2,
                     r=['wmisc', 'cqn'], w=[('ps', BQ)])
            o.ACT(qg[:, :], ps[BQ][0:96, 0:tb], AF.Square, r=[('ps', BQ)], w=['qg'])
            for kb in range(jg + 1):
                bk = C['rA'].next()
                for cc in range(2):
                    o.MM(ps[bk][0:64, 0:tb], C['wuk'][:, cc, h * 64:(h + 1) * 64], cT[:, cc, kb * tb:(kb + 1) * tb], cc == 0, cc == 1,
                         r=['wuk', 'cT'], w=[('ps', bk)])
                o.CP('dve', KT[ki][0:64, kb * tb:(kb + 1) * tb], ps[bk][0:64, 0:tb], r=[('ps', bk)], w=[('KT', ki)])
            off = 0 if h % 2 == 0 else 64
            for kg in range(nk // 8 + (1 if nk % 8 else 0)):
                bk = C['rA'].next()
                n8 = min(8, nk - kg * 8)
                for q8 in range(n8):
                    kc = kg * 8 + q8
                    for cc in range(2):
                        o.MM(ps[bk][:, q8 * 64:(q8 + 1) * 64], cT[:, cc, kc * 128:(kc + 1) * 128], C['wuv'][:, cc, h * 64:(h + 1) * 64],
                             cc == 0, cc == 1, r=['wuv', 'cT'], w=[('ps', bk)])
                o.CP('dve', Vh[ki][:, kg * 8:kg * 8 + n8, off:off + 64],
                     ps[bk][:, 0:n8 * 64].rearrange("p (k d) -> p k d", d=64), r=[('ps', bk)], w=[('Vh', ki)])

        def prepN1(h):
            o.MM(ps[BN][0:96, 0:tb], ones[0:96, 0:96], qg[:, :], True, True, r=['qg', 'ones'], w=[('ps', BN)])
            o.ACT(C['tmp'][0:96, 0:tb], ps[BN][0:96, 0:tb], AF.Ln, r=[('ps', BN)], w=['tmpn'], scale=1.0 / 96, bias=EPS)
            o.ACT(C['rstd'][0:96, 0:tb], C['tmp'][0:96, 0:tb], AF.Exp, r=['tmpn'], w=['rstd'], scale=-0.5)
            o.STT('dve', qg[:, :], ps[BQ][0:96, 0:tb], G['gq'][:, jl:jl + 1], C['rstd'][0:96, 0:tb], ALU.mult, ALU.mult,
                  r=[('ps', BQ), 'rstd', 'gains', 'qg'], w=['qg'])

        def prepN2(h):
            qi = h % 2
            o.MM(ps[BN][0:96, 0:tb], perm[:, :], qg[:, :], True, True, r=['perm', 'qg'], w=[('ps', BN)])
            o.STT('dve', t1[:, :], qg[:, :], G['gk'][:, 0:1], cosb[:, :], ALU.mult, ALU.mult, r=['qg', 'cosb', 'gains'], w=['t1'])
            o.STT('dve', t2[:, :], ps[BN][0:96, 0:tb], G['gk'][:, 0:1], sinb[:, :], ALU.mult, ALU.mult,
                  r=[('ps', BN), 'sinb', 'gains'], w=['t2'])
            o.TT('pool', QT[qi][:, :], t1[:, :], t2[:, :], ALU.add, r=['t1', 't2'], w=[('QT', qi)])

        def attend(h):
            qi = ki = h % 2
            off = 0 if h % 2 == 0 else 64
            bo = C['rO'].next()
            pend = {}

            def S(kc):
                r_ = kc - 4 * jg
                q0 = 128 * max(r_, 0)
                bs = C['rS'].next()
                o.MM(ps[bs][:, q0:tb], KT[ki][0:96, kc * 128:(kc + 1) * 128], QT[qi][:, q0:tb], True, True,
                     r=[('KT', ki), ('QT', qi)], w=[('ps', bs)])
                pi = rPT.next()
                o.ACT(PT[pi][:, q0:tb], ps[bs][:, q0:tb], AF.Exp, r=[('ps', bs), 'kinv'], w=[('PT', pi)],
                      scale=kinv[:, kc, h:h + 1])
                if r_ >= 0:
                    o.TT('pool', PT[pi][:, q0:q0 + 128], PT[pi][:, q0:q0 + 128], tri[:, :], ALU.mult, r=[('PT', pi), 'tri'], w=[('PT', pi)])
                pend[kc] = (pi, q0)

            def PV(kc):
                pi, q0 = pend.pop(kc)
                o.MM(ps[bo][:, q0:tb], Vh[ki][:, kc, :], PT[pi][:, q0:tb], kc == 0, kc == nk - 1,
                     r=[('Vh', ki), ('PT', pi)], w=[('ps', bo)])

            hook2 = (nk - 3) if nk >= 8 else (nk - 1)
            S(0)
            if h + 1 < 12:
                prepA(h + 1)
            for kc in range(nk):
                if kc + 1 < nk:
                    S(kc + 1)
                PV(kc)
                if h + 1 < 12:
                    if kc == 0:
                        prepN1(h + 1)
                    if kc == hook2:
                        prepN2(h + 1)
            vr = slice(off, off + 64)
            dr = slice(64 - off, 128 - off)
            o.ACT(rc[dr, :], ps[bo][dr, 0:tb], AF.Ln, r=[('ps', bo)], w=['rc'])
            o.ACT(rc[dr, :], rc[dr, :], AF.Exp, r=['rc'], w=['rc'], scale=-1.0)
            o.TT('dve', cat[vr, h // 2, :], ps[bo][vr, 0:tb], rc[dr, :], ALU.mult, r=[('ps', bo), 'rc'], w=['cat'])

        prepA(0)
        prepN1(0)
        prepN2(0)
        mem_qnorm(l)
        for h in range(12):
            attend(h)
        mem_attend(l, j, 6)
        out_proj(l, j)

    def ffn(l, pre, hook=None, first=False):
        barrier()
        for j in range(NB):
            blk = slice(j * tb, (j + 1) * tb)
            norm_fm(o, C, [xT[:, c, blk] for c in range(8)], [G['gffn'][:, l * 8 + c:l * 8 + c + 1] for c in range(8)],
                    [hT[:, c, blk] for c in range(8)], 128, D, tb, XK(j), ('hT', j), ones[:], split=True)
        rOut = Rot([0, 1, 6, 7])
        rStg = Rot([0, 1])
        pendG, pendO = list(pre[0]), list(pre[1])
        nx = {'g': len(pendG), 'o': len(pendO)}
        units = [(fg, j) for fg in range(NFG) for j in range(NB)]
        grpG, grpO = {}, {}

        def GU(u):
            fg, j = units[u]
            if fg not in grpG:
                grpG[fg] = pendG.pop(0)
            gv, uv, rk = grpG[fg]
            blk = slice(j * tb, (j + 1) * tb)
            ai = u % 2
            for cc in range(2):
                bg, bu = 2 + cc, 4 + cc
                for k in range(8):
                    o.MM(ps[bg][:, 0:tb], gv[:, k, cc * 128:(cc + 1) * 128], hT[:, k, blk], k == 0, k == 7,
                         r=[rk, ('hT', j)], w=[('ps', bg)])
                for k in range(8):
                    o.MM(ps[bu][:, 0:tb], uv[:, k, cc * 128:(cc + 1) * 128], hT[:, k, blk], k == 0, k == 7,
                         r=[rk, ('hT', j)], w=[('ps', bu)])
                o.ACT(sg[cc][:, :], ps[bg][:, 0:tb], AF.Silu, r=[('ps', bg)], w=[('sg', cc)])
                o.TT('dve', aT[ai][:, cc, :], sg[cc][:, :], ps[bu][:, 0:tb], ALU.mult, r=[('sg', cc), ('ps', bu)], w=[('aT', ai)])
            if j == NB - 1 and nx['g'] < NFG:
                pendG.append(load_G(l, nx['g'], first))
                nx['g'] += 1

        def OUT(u):
            fg, j = units[u]
            if fg not in grpO:
                grpO[fg] = pendO.pop(0)
            ov, rk = grpO[fg]
            blk = slice(j * tb, (j + 1) * tb)
            ai = u % 2
            for oc in range(8):
                b = rOut.next()
                for cc in range(2):
                    o.MM(ps[b][:, 0:tb], ov[:, cc, oc * 128:(oc + 1) * 128], aT[ai][:, cc, :], cc == 0, cc == 1,
                         r=[rk, ('aT', ai)], w=[('ps', b)])
                if oc % 2 == 0:
                    o.TT('dve', xT[:, oc, blk], xT[:, oc, blk], ps[b][:, 0:tb], ALU.add, r=[('ps', b), ('xT', j, oc)], w=[('xT', j, oc)])
                else:
                    si = rStg.next()
                    o.CP('act', stg[si][:, :], ps[b][:, 0:tb], r=[('ps', b)], w=[('stg', si)])
                    o.TT('pool', xT[:, oc, blk], xT[:, oc, blk], stg[si][:, :], ALU.add, r=[('stg', si), ('xT', j, oc)], w=[('xT', j, oc)])
            if j == NB - 1:
                if nx['o'] < NFG:
                    pendO.append(load_O(l, nx['o'], first))
                    nx['o'] += 1
                if fg == 1 and hook is not None:
                    hook()

        GU(0)
        for u in range(len(units)):
            if u + 1 < len(units):
                GU(u + 1)
            OUT(u)

    for sg_ in range(NSEG):
        barrier()
        for tt in range(SEG // 128):
            xb, xk = xinb[tt % 2], ('xin' if tt % 2 == 0 else 'xin1')
            o.DMA('sp', xb[:], I["xp"][sg_ * SEG + tt * 128: sg_ * SEG + (tt + 1) * 128, :], w=[xk])
            for c in range(8):
                b = C['rA'].next()
                o.TR(ps[b][:, 0:128], xb[:, c * 128:(c + 1) * 128], ident[:], r=[xk, 'ident'], w=[('ps', b)])
                o.CP('dve' if c % 2 else 'act', xT[:, c, tt * 128:(tt + 1) * 128], ps[b][:, 0:128], r=[('ps', b)], w=[('xT', tt // 4, c)])
        for l in range(nlayers):
            mem_kv(l, write_out=(sg_ == 0))
            if l == 0 and sg_ == 0:
                load_mix_weights(0, True)
            pre = []
            for j in range(NB):
                if l < 2:
                    mix_a(l, sg_, j)
                else:
                    mix_b(l, sg_, j)
                if j == 0:
                    pre = ([load_G(l, fg, sg_ == 0) for fg in range(NRG)], [load_O(l, fg, sg_ == 0) for fg in range(NRO)])
            ffn(l, pre, hook=(lambda l=l, sg_=sg_: load_mix_weights((l + 1) % nlayers, sg_ == 0 and l + 1 < nlayers))
                if not (l == nlayers - 1 and sg_ == NSEG - 1) else None, first=(sg_ == 0))
            if l == 1:
                for j in range(NB):
                    kv_stage(sg_, j)
        barrier()
        for tt in range(SEG // 128):
            xb, xk = xinb[tt % 2], ('xin' if tt % 2 == 0 else 'xin1')
            for c in range(8):
                b = C['rA'].next()
                o.TR(ps[b][:, 0:128], xT[:, c, tt * 128:(tt + 1) * 128], ident[:], r=[('xT', tt // 4, c), 'ident'], w=[('ps', b)])
                o.CP('dve' if c % 2 else 'act', xb[:, c * 128:(c + 1) * 128], ps[b][:, 0:128], r=[('ps', b)], w=[xk])
            o.DMA('sp', O["y_p"][sg_ * SEG + tt * 128: sg_ * SEG + (tt + 1) * 128, :], xb[:], r=[xk], out=True)
    P.finish()
    P.emit('p')
    print("prompt program: ops", len(P.ops), "waits", P.nwaits)


def sample_program(nc, es, I, O, nlayers, NPG=128, SC=None):
    P = Prog(nc, es, 'S_')
    o = Ops(P)
    C = alloc_common(P, o, I)
    ps = C['ps']
    G = C['G']
    ident, ones, blk2, perm = C['ident'], C['ones'], C['blk2'], C['perm']
    tb = 32
    NS = 4
    NPT = NS * NPG

    xT = P.sb("s_xT", [128, 8, tb], F32)
    hT = P.sb("s_hT", [128, 8, tb], BF16)
    win = P.sb("s_win", [128, 8, 1024], BF16)
    wout = P.sb("s_wout", [128, 10, 1024], BF16)
    wmisc = P.sb("s_wmisc", [128, 3456], BF16)
    NRING = 3
    ring = [P.sb("s_ring%d" % i, [128, 6144], BF16) for i in range(NRING)]
    cat = P.sb("s_cat", [128, 10, tb], BF16)
    qm = P.sb("s_qm", [128, 2, tb], F32)
    qn = P.sb("s_qn", [128, 2, tb], BF16)
    xin = P.sb("s_xin", [32, 1024], F32)
    memKT = P.sb("s_memKT", [128, NS, 2, 256], BF16)
    Vaug = P.sb("s_Vaug", [128, NS, 2, 4, 128], BF16)
    mst = P.sb("s_mst", [128, 2, 256], F32)
    Us = P.sb("s_U", [96, 32, 24], F32)
    tA = P.sb("s_tA", [96, 8, 24], F32)
    tB_ = P.sb("s_tB", [96, 8, 24], F32)
    dif = P.sb("s_dif", [96, 8, tb], BF16)
    spt = P.sb("s_spt", [16, 768], F32)
    pso = P.sb("s_pso", [16, 768], F32)
    PTm = P.sb("s_PTm", [128, 256], BF16)
    rc = P.sb("s_rc", [128, 128], F32)
    sg = [P.sb("s_sg%d" % i, [128, tb], F32) for i in range(2)]
    aT = [P.sb("s_aT%d" % i, [128, 2, tb], BF16) for i in range(2)]
    cosb = P.sb("s_cos", [96, tb], F32)
    sinb = P.sb("s_sin", [96, tb], F32)
    ckf = P.sb("s_ckf", [128, 2, tb], F32)
    cTs = P.sb("s_cT", [128, 2, tb], BF16)
    krf = P.sb("s_krf", [96, tb], F32)
    krb = P.sb("s_krb", [96, tb], BF16)
    t1 = P.sb("s_t1", [96, tb], F32)
    t2 = P.sb("s_t2", [96, tb], F32)
    kvo = P.sb("s_kvo", [32, 288], F32)
    KVbn = P.sb("s_KVbn", [32, 289], BF16)
    sqn = P.sb("s_sqn", [128, 768], BF16)
    sqr = P.sb("s_sqr", [128, 32], BF16)
    ssn = P.sb("s_ssn", [128, 4, 16], F32)
    kinvn = P.sb("s_kinvn", [32, 12], F32)
    cq = P.sb("s_cq", [128, 3, tb], F32)
    cqn = P.sb("s_cqn", [128, 3, tb], BF16)
    qg = P.sb("s_qg", [96, tb], BF16)
    QTs = P.sb("s_QT", [96, 12, tb], BF16)
    QL = P.sb("s_QL", [128, 2, 12, tb], BF16)
    wukf = P.sb("s_wukf", [128, 2, 768], F32)
    wukT = P.sb("s_wukT", [64, 12, 256], BF16)
    identb = P.sb("s_identb", [128, 128], BF16)
    smask = P.sb("s_smask", [32, 4, 96], BF16)
    ptb = P.sb("s_ptb", [128, NPT], I32)
    idxall = P.sb("s_idx", [128, NPT], I32)
    iot = P.sb("s_iota", [128, 1], I32)
    kinv = P.sb("s_kinv", [128, NPT, 12], F32)
    NKB = 16
    sqn2 = [P.sb("s_sqn2_%d" % i, [128, 800], BF16) for i in range(2)]
    KVb = [P.sb("s_KVb%d" % i, [128, 289], BF16) for i in range(NKB)]
    KVTc = [P.sb("s_KVTc%d" % i, [128, 4, 2, 128], BF16) for i in range(2)]
    KVTr = [P.sb("s_KVTr%d" % i, [96, 4, 128], BF16) for i in range(2)]
    tms = P.sb("s_tms", [128, 4, 96], F32)
    PTs = [P.sb("s_PTs%d" % i, [128, 4, 96], BF16) for i in range(2)]
    PTn = P.sb("s_PTn", [32, 96], BF16)
    olat = P.sb("s_olat", [96, 256], F32)
    rcs = P.sb("s_rcs", [96, 1], F32)
    olT = P.sb("s_olT", [128, 2, 96], BF16)

    o.MSET('pool', Vaug[:], 1.0, w=['Vaug'])
    for i in range(NKB):
        o.MSET('pool', KVb[i][:, 288:289], 1.0, w=[('KVb', i)])
    o.MSET('pool', KVbn[:, 288:289], 1.0, w=['KVbn'])
    o.CP('pool', identb[:], ident[:], r=['ident'], w=['identb'])
    o.DMA('pool', smask[:], I["c_smask"][:, :, :], w=['smask'])
    o.DMA('sp', wukf[:], I["w_uk"].rearrange("(k p) n -> p k n", p=128), w=['wukf'])
    for h in range(12):
        b = C['rA'].next()
        for cc in range(2):
            o.TR(ps[b][0:64, cc * 128:(cc + 1) * 128], wukf[:, cc, h * 64:(h + 1) * 64], ident[:, :], r=['wukf', 'ident'], w=[('ps', b)])
        o.CP('dve', wukT[:, h, :], ps[b][0:64, 0:256], r=[('ps', b)], w=['wukT'])
    o.DMA('sp', ptb[:], I["pt"].rearrange("s p -> (s p)").partition_broadcast(128), w=['ptb'])
    P.op('pool', lambda e: e.iota(iot[:], pattern=[[0, 1]], base=0, channel_multiplier=1), w=['iota'])
    ptf = P.sb("s_ptf", [128, NPT], F32)
    iotf = P.sb("s_iotf", [128, 1], F32)
    o.CP('dve', ptf[:], ptb[:], r=['ptb'], w=['ptf'])
    o.CP('dve', iotf[:], iot[:], r=['iota'], w=['iotf'])
    o.TS('dve', ptf[:], ptf[:], 128.0, iotf[:, 0:1], ALU.mult, ALU.add, r=['ptf', 'iotf'], w=['ptf'])
    o.CP('dve', idxall[:], ptf[:], r=['ptf'], w=['idx'])

    WS = SC if SC is not None else I
    WQ = 'sp' if SC is not None else 'pool'
    wv_ffn_in = [WS["w_ffn_in"][l].rearrange("(k p) n -> p k n", p=128) for l in range(4)]
    wv_ffn_out = [WS["w_ffn_out"][l].rearrange("(k p) n -> p k n", p=128) for l in range(4)]
    NFG = DFF // 256
    XK = [('xT', c) for c in range(8)]
    ring_state = {'n': 0}

    def load_ffn_group(l, fg):
        s = ring_state['n'] % NRING
        ring_state['n'] += 1
        rg = ring[s]
        k = ('ring', s)
        gv = rg[:, 0:2048].rearrange("p (k n) -> p k n", n=256)
        uv = rg[:, 2048:4096].rearrange("p (k n) -> p k n", n=256)
        ov = rg[:, 4096:6144].rearrange("p (k n) -> p k n", n=1024)
        o.DMA(WQ, gv, wv_ffn_in[l][:, :, fg * 256:(fg + 1) * 256], w=[k])
        o.DMA(WQ, uv, wv_ffn_in[l][:, :, DFF + fg * 256:DFF + (fg + 1) * 256], w=[k])
        o.DMA(WQ, ov, wv_ffn_out[l][:, fg * 2:fg * 2 + 2, :], w=[k])
        if False:
            si = SC["w_ffn_in"][l].rearrange("(k p) n -> p k n", p=128)
            so = SC["w_ffn_out"][l].rearrange("(k p) n -> p k n", p=128)
            o.DMA('sp', si[:, :, fg * 256:(fg + 1) * 256], gv, r=[k], out=True)
            o.DMA('sp', si[:, :, DFF + fg * 256:DFF + (fg + 1) * 256], uv, r=[k], out=True)
            o.DMA('sp', so[:, fg * 2:fg * 2 + 2, :], ov, r=[k], out=True)
        return gv, uv, ov, k

    def load_mix_weights(l):
        if l < 2:
            o.DMA(WQ, win[:], WS["w_in_a"][l].rearrange("(k p) n -> p k n", p=128), w=['win'])
            o.DMA(WQ, wout[0:96, 0:8, :], WS["w_out"][l][0:768, :].rearrange("(k p) n -> p k n", p=96), w=['wout'])
            o.DMA(WQ, wout[:, 8:10, :], WS["w_out"][l][768:1024, :].rearrange("(k p) n -> p k n", p=128), w=['wout'])
            gv = wmisc[0:96, 0:1536].rearrange("p (k n) -> p k n", n=192)
            o.DMA(WQ, gv, WS["w_pool_grp"][l].rearrange("g (c p) e -> p (g c) e", p=96), w=['wmisc'])
            if False:
                o.DMA('sp', SC["w_in_a"][l].rearrange("(k p) n -> p k n", p=128), win[:], r=['win'], out=True)
                o.DMA('sp', SC["w_out"][l][0:768, :].rearrange("(k p) n -> p k n", p=96), wout[0:96, 0:8, :], r=['wout'], out=True)
                o.DMA('sp', SC["w_out"][l][768:1024, :].rearrange("(k p) n -> p k n", p=128), wout[:, 8:10, :], r=['wout'], out=True)
                o.DMA('sp', SC["w_pool_grp"][l].rearrange("g (c p) e -> p (g c) e", p=96), gv, r=['wmisc'], out=True)
        else:
            j = l - 2
            o.DMA(WQ, win[:, :, 0:640], WS["w_in_b"][j].rearrange("(k p) n -> p k n", p=128), w=['win'])
            o.DMA(WQ, wout[:, 0:8, :], WS["w_out"][l].rearrange("(k p) n -> p k n", p=128), w=['wout'])
            uq = wmisc[:, 0:3456].rearrange("p (k n) -> p k n", n=1152)
            o.DMA(WQ, uq, WS["w_uq"][j].rearrange("(k p) n -> p k n", p=128), w=['wmisc'])
            if False:
                o.DMA('sp', SC["w_in_b"][j].rearrange("(k p) n -> p k n", p=128), win[:, :, 0:640], r=['win'], out=True)
                o.DMA('sp', SC["w_out"][l].rearrange("(k p) n -> p k n", p=128), wout[:, 0:8, :], r=['wout'], out=True)
                o.DMA('sp', SC["w_uq"][j].rearrange("(k p) n -> p k n", p=128), uq, r=['wmisc'], out=True)

    def xnorm(gkey, l):
        norm_fm(o, C, [xT[:, c, :] for c in range(8)], [G[gkey][:, l * 8 + c:l * 8 + c + 1] for c in range(8)],
                [hT[:, c, :] for c in range(8)], 128, D, tb, XK, 'hT', ones[:])

    def mem_load(l, seqs=None):
        for s_ in (range(NS) if seqs is None else seqs):
            o.DMA('pool', mst[:], I["cmk"][l, s_].rearrange("(m p) f -> p m f", p=128), w=['mst'])
            for mc in range(2):
                b = C['rA'].next()
                for hp in range(2):
                    o.TR(ps[b][:, hp * 128:(hp + 1) * 128], mst[:, mc, hp * 128:(hp + 1) * 128], ident[:], r=['mst', 'ident'], w=[('ps', b)])
                o.CP('dve', memKT[:, s_, :, mc * 128:(mc + 1) * 128], ps[b][:, 0:256].rearrange("p (h m) -> p h m", m=128),
                     r=[('ps', b)], w=['memKT'])
            o.DMA('pool', mst[:], I["cmv"][l, s_].rearrange("(m p) f -> p m f", p=128), r=['mst'], w=['mst'])
            for mc in range(2):
                for h in range(4):
                    off = 0 if h % 2 == 0 else 64
                    o.CP('pool', Vaug[:, s_, mc, h, off:off + 64], mst[:, mc, h * 64:(h + 1) * 64], r=['mst'], w=['Vaug'])

    def mem_attend(l, catbase):
        for hp in range(2):
            norm_fm(o, C, [qm[:, hp, :]], [G['gmq'][:, l:l + 1]], [qn[:, hp, :]], 128, 64, tb, ['qm'], 'qn', blk2[:])
        bsb = [2, 3]
        for mc in range(2):
            for s_ in range(NS):
                for h in range(4):
                    hp, par = h // 2, h % 2
                    base = par * 64
                    col = ((mc * NS + s_) * 2 + hp) * 8
                    o.MM(ps[bsb[par]][:, col:col + 8], memKT[base:base + 64, s_, hp, mc * 128:(mc + 1) * 128],
                         qn[base:base + 64, hp, s_ * 8:(s_ + 1) * 8], True, True, r=['memKT', 'qn'], w=[('ps', bsb[par])])
        for par in range(2):
            o.ACT(PTm[:, par * 128:(par + 1) * 128], ps[bsb[par]][:, 0:128], AF.Exp, r=[('ps', bsb[par])], w=['PTm'], scale=0.125)
        bo = C['rO'].next()
        for s_ in range(NS):
            for h in range(4):
                hp, par = h // 2, h % 2
                colo = (s_ * 4 + h) * 8
                for mc in range(2):
                    col = par * 128 + ((mc * NS + s_) * 2 + hp) * 8
                    o.MM(ps[bo][:, colo:colo + 8], Vaug[:, s_, mc, h, :], PTm[:, col:col + 8], mc == 0, mc == 1,
                         r=['Vaug', 'PTm'], w=[('ps', bo)])
        o.RCP(rc[:, 0:128], ps[bo][:, 0:128], r=[('ps', bo)], w=['rc'])
        for h in range(4):
            hp, base = h // 2, (h % 2) * 64
            vr = slice(base, base + 64)
            dr = slice(64 - base, 128 - base)
            pv = ps[bo][vr, 0:128].rearrange("p (s h i) -> p s h i", h=4, i=8)[:, :, h, :]
            rv = rc[dr, 0:128].rearrange("p (s h i) -> p s h i", h=4, i=8)[:, :, h, :]
            o.TT('dve', cat[vr, catbase + hp, :].rearrange("p (s i) -> p s i", i=8), pv, rv, ALU.mult, r=[('ps', bo), 'rc'], w=['cat'])

    def out_proj(l):
        if l < 2:
            chunks = [(96, c) for c in range(8)] + [(128, 8), (128, 9)]
        else:
            chunks = [(128, c) for c in range(8)]
        for oc in range(8):
            b = C['rA'].next()
            for i, (kk, c) in enumerate(chunks):
                o.MM(ps[b][:, 0:tb], wout[0:kk, c, oc * 128:(oc + 1) * 128], cat[0:kk, c, :], i == 0, i == len(chunks) - 1,
                     r=['wout', 'cat'], w=[('ps', b)])
            o.TT('dve', xT[:, oc, :], xT[:, oc, :], ps[b][:, 0:tb], ALU.add, r=[('ps', b), ('xT', oc)], w=[('xT', oc)])

    def halo_load(l, seqs=None):
        for s_ in (range(NS) if seqs is None else seqs):
            o.DMA('pool', spt[0:15, :], I["spool"][l, s_], w=['spt'])
            b = C['rA'].next()
            for c in range(8):
                o.TR(ps[b][0:96, c * 16:c * 16 + 15], spt[0:15, c * 96:(c + 1) * 96], ident[0:15, 0:15], r=['spt', 'ident'], w=[('ps', b)])
            o.CP('dve', Us[:, :, 1:16].rearrange("p (c s) t -> p c s t", s=NS)[:, :, s_, :],
                 ps[b][0:96, 0:128].rearrange("p (c t) -> p c t", t=16)[:, :, 0:15], r=[('ps', b)], w=['U'])

    def mix_a(l):
        xnorm('gmix', l)
        for c in range(8):
            b = C['rA'].next()
            for k in range(8):
                o.MM(ps[b][0:96, 0:tb], win[:, k, c * 96:(c + 1) * 96], hT[:, k, :], k == 0, k == 7, r=['win', 'hT'], w=[('ps', b)])
            o.CP('act', Us[:, c * NS:(c + 1) * NS, 16:24], ps[b][0:96, 0:tb].rearrange("p (s i) -> p s i", i=8), r=[('ps', b)], w=['U'])
        for hp in range(2):
            b = C['rA'].next()
            for k in range(8):
                o.MM(ps[b][:, 0:tb], win[:, k, 768 + hp * 128:768 + (hp + 1) * 128], hT[:, k, :], k == 0, k == 7,
                     r=['win', 'hT'], w=[('ps', b)])
            o.CP('act', qm[:, hp, :], ps[b][:, 0:tb], r=[('ps', b)], w=['qm'])
        gw = wmisc[0:96, 0:1536].rearrange("p (k n) -> p k n", n=192)
        E = 24
        for g in range(4):
            w_ = 2 ** (g + 1)
            src = Us[:, 8 * g:8 * g + 8, :]
            bufs = [tA, tB_]
            d, i, srck = 1, 0, 'U'
            while d < w_:
                dst = bufs[i % 2]
                dk = 'tA' if i % 2 == 0 else 'tB'
                lo = 2 * d
                o.TT('pool', dst[:, :, lo:E], src[:, :, lo:E], src[:, :, lo - d:E - d], ALU.add, r=[srck], w=[dk])
                src, srck = dst, dk
                d *= 2
                i += 1
            o.STT('dve', dif[:, 2 * g:2 * g + 2, :].rearrange("p c (s i) -> p (c s) i", i=8), src[:, :, 16:E], 1.0 / w_,
                  Us[:, 8 * g:8 * g + 8, 16:E], ALU.mult, ALU.subtract, r=[srck, 'U'], w=['dif'])
            for ec in range(2):
                b = C['rA'].next()
                for cc in range(2):
                    o.MM(ps[b][0:96, 0:tb], gw[:, 2 * g + cc, ec * 96:(ec + 1) * 96], dif[:, 2 * g + cc, :], cc == 0, cc == 1,
                         r=['wmisc', 'dif'], w=[('ps', b)])
                ch = 2 * g + ec
                o.TSM('dve', cat[0:96, ch, :], ps[b][0:96, 0:tb], G['pscale'][:, l * 8 + ch:l * 8 + ch + 1], r=[('ps', b), 'gains'], w=['cat'])
        for s_ in range(NS):
            b1, b2 = C['rA'].next(), C['rA'].next()
            for c in range(8):
                bb = b1 if c < 4 else b2
                o.TR(ps[bb][0:15, (c % 4) * 96:(c % 4 + 1) * 96], Us[:, c * NS + s_, 9:24], ident[0:96, 0:96], r=['U', 'ident'], w=[('ps', bb)])
            o.CP('act', pso[0:15, 0:384], ps[b1][0:15, 0:384], r=[('ps', b1)], w=['pso'])
            o.CP('act', pso[0:15, 384:768], ps[b2][0:15, 0:384], r=[('ps', b2)], w=['pso'])
            o.DMA('sp', O["pool_s"][l, s_], pso[0:15, :], r=['pso'], out=True)
        mem_attend(l, 8)
        out_proj(l)

    def ffn(l, pre, hook=None):
        xnorm('gffn', l)
        rOut = Rot([0, 1, 6, 7])
        pend = list(pre)
        nxt = len(pre)
        for fg in range(NFG):
            gv, uv, ov, rk = pend.pop(0)
            ai = fg % 2
            for cc in range(2):
                bg, bu = 2 + cc, 4 + cc
                for k in range(8):
                    o.MM(ps[bg][:, 0:tb], gv[:, k, cc * 128:(cc + 1) * 128], hT[:, k, :], k == 0, k == 7, r=[rk, 'hT'], w=[('ps', bg)])
                for k in range(8):
                    o.MM(ps[bu][:, 0:tb], uv[:, k, cc * 128:(cc + 1) * 128], hT[:, k, :], k == 0, k == 7, r=[rk, 'hT'], w=[('ps', bu)])
                o.ACT(sg[cc][:, :], ps[bg][:, 0:tb], AF.Silu, r=[('ps', bg)], w=[('sg', cc)])
                o.TT('dve', aT[ai][:, cc, :], sg[cc][:, :], ps[bu][:, 0:tb], ALU.mult, r=[('sg', cc), ('ps', bu)], w=[('aT', ai)])
            for oc in range(8):
                b = rOut.next()
                for cc in range(2):
                    o.MM(ps[b][:, 0:tb], ov[:, cc, oc * 128:(oc + 1) * 128], aT[ai][:, cc, :], cc == 0, cc == 1,
                         r=[rk, ('aT', ai)], w=[('ps', b)])
                o.TT('dve', xT[:, oc, :], xT[:, oc, :], ps[b][:, 0:tb], ALU.add, r=[('ps', b), ('xT', oc)], w=[('xT', oc)])
            if nxt < NFG:
                pend.append(load_ffn_group(l, nxt))
                nxt += 1
            if hook is not None:
                hook(fg)

    def kv_stage():
        wd = C['wdkv']
        norm_fm(o, C, [xT[:, c, :] for c in range(8)], [G['gkv'][:, c:c + 1] for c in range(8)],
                [hT[:, c, :] for c in range(8)], 128, D, tb, XK, 'hT', ones[:])
        o.DMA('sp', cosb[:], I["c_cos"][:, 2048:2080], w=['cosb'])
        o.DMA('sp', sinb[:], I["c_sin"][:, 2048:2080], w=['sinb'])
        for cc in range(2):
            b = C['rA'].next()
            for k in range(8):
                o.MM(ps[b][:, 0:tb], wd[:, k, cc * 128:(cc + 1) * 128], hT[:, k, :], k == 0, k == 7, r=['wdkv', 'hT'], w=[('ps', b)])
            o.CP('act', ckf[:, cc, :], ps[b][:, 0:tb], r=[('ps', b)], w=['ckf'])
        br = C['rG'].next()
        for k in range(8):
            o.MM(ps[br][0:96, 0:tb], wd[:, k, 256:352], hT[:, k, :], k == 0, k == 7, r=['wdkv', 'hT'], w=[('ps', br)])
        o.CP('act', krb[64:96, :], ps[br][64:96, 0:tb], r=[('ps', br)], w=['krb'])
        b2 = C['rG'].next()
        o.MM(ps[b2][0:96, 0:tb], perm[64:96, 0:96], krb[64:96, :], True, True, r=['perm', 'krb'], w=[('ps', b2)])
        o.TT('dve', t1[64:96, :], ps[br][64:96, 0:tb], cosb[64:96, :], ALU.mult, r=[('ps', br), 'cosb'], w=['t1'])
        o.TT('dve', t2[64:96, :], ps[b2][64:96, 0:tb], sinb[64:96, :], ALU.mult, r=[('ps', b2), 'sinb'], w=['t2'])
        o.TT('dve', krf[64:96, :], t1[64:96, :], t2[64:96, :], ALU.add, r=['t1', 't2'], w=['krf'])
        o.CP('pool', krb[64:96, :], krf[64:96, :], r=['krf'], w=['krb'])
        norm_fm(o, C, [ckf[:, cc, :] for cc in range(2)], [G['gkvl'][:, cc:cc + 1] for cc in range(2)],
                [cTs[:, cc, :] for cc in range(2)], 128, 256, tb, ['ckf'], 'cTs', ones[:])
        for cc in range(2):
            o.STT('dve', ckf[:, cc, :], ckf[:, cc, :], G['gkvl'][:, cc:cc + 1], C['rstd'][:, 0:tb], ALU.mult, ALU.mult,
                  r=['ckf', 'rstd', 'cTs'], w=['ckf'])
        b = C['rA'].next()
        o.TR(ps[b][0:32, 0:128], ckf[:, 0, :], ident[:], r=['ckf', 'ident'], w=[('ps', b)])
        o.TR(ps[b][0:32, 128:256], ckf[:, 1, :], ident[:], r=['ckf', 'ident'], w=[('ps', b)])
        o.TR(ps[b][0:32, 256:288], krf[64:96, :], ident[64:96, 64:96], r=['krf', 'ident'], w=[('ps', b)])
        o.CP('act', kvo[:, :], ps[b][0:32, 0:288], r=[('ps', b)], w=['kvo'])
        o.DMA('sp', O["kv_s"][:, :], kvo[:, :], r=['kvo'], out=True)
        o.CP('dve', KVbn[:, 0:288], kvo[:, :], r=['kvo'], w=['KVbn'])
        b3, b4 = C['rG'].next(), C['rG'].next()
        for cc in range(2):
            o.MM(ps[b3][0:32, 0:512], cTs[:, cc, :], C['wuk'][:, cc, 0:512], cc == 0, cc == 1, r=['cTs', 'wuk'], w=[('ps', b3)])
        for cc in range(2):
            o.MM(ps[b4][0:32, 0:256], cTs[:, cc, :], C['wuk'][:, cc, 512:768], cc == 0, cc == 1, r=['cTs', 'wuk'], w=[('ps', b4)])
        o.ACT(sqn[0:32, 0:512], ps[b3][0:32, 0:512], AF.Square, r=[('ps', b3)], w=['sqn'])
        o.ACT(sqn[0:32, 512:768], ps[b4][0:32, 0:256], AF.Square, r=[('ps', b4)], w=['sqn'])
        o.RED(ssn[0:32, 0, 0:12], sqn[0:32, :].rearrange("p (h d) -> p h d", d=64), r=['sqn'], w=['ssn'])
        o.ACT(sqr[0:32, 0:32], kvo[:, 256:288], AF.Square, r=['kvo'], w=['sqr'])
        o.RED(ssn[0:32, 0, 12:13], sqr[0:32, 0:32].rearrange("p (h d) -> p h d", d=32), r=['sqr'], w=['ssn2'])
        o.TSA('dve', ssn[0:32, 0, 0:12], ssn[0:32, 0, 0:12], ssn[0:32, 0, 12:13], r=['ssn', 'ssn2'], w=['ssn'])
        o.ACT(ssn[0:32, 0, 0:12], ssn[0:32, 0, 0:12], AF.Ln, r=['ssn'], w=['ssn'], scale=1.0, bias=96.0 * EPS)
        o.ACT(kinvn[:, :], ssn[0:32, 0, 0:12], AF.Exp, r=['ssn'], w=['kinvn'], scale=-0.5)

    def mix_b(l):
        jl = l - 2
        xnorm('gmix', l)
        for c in range(3):
            b = C['rA'].next()
            for k in range(8):
                o.MM(ps[b][:, 0:tb], win[:, k, c * 128:(c + 1) * 128], hT[:, k, :], k == 0, k == 7, r=['win', 'hT'], w=[('ps', b)])
            o.CP('act', cq[:, c, :], ps[b][:, 0:tb], r=[('ps', b)], w=['cq'])
        for hp in range(2):
            b = C['rA'].next()
            for k in range(8):
                o.MM(ps[b][:, 0:tb], win[:, k, 384 + hp * 128:384 + (hp + 1) * 128], hT[:, k, :], k == 0, k == 7,
                     r=['win', 'hT'], w=[('ps', b)])
            o.CP('act', qm[:, hp, :], ps[b][:, 0:tb], r=[('ps', b)], w=['qm'])
        norm_fm(o, C, [cq[:, c, :] for c in range(3)], [G['gql'][:, jl * 3 + c:jl * 3 + c + 1] for c in range(3)],
                [cqn[:, c, :] for c in range(3)], 128, 384, tb, ['cq'], 'cqn', ones[:])
        uq = wmisc[:, 0:3456].rearrange("p (k n) -> p k n", n=1152)
        for h in range(12):
            bq = C['rG'].next()
            for k in range(3):
                o.MM(ps[bq][0:96, 0:tb], uq[:, k, h * 96:(h + 1) * 96], cqn[:, k, :], k == 0, k == 2, r=['wmisc', 'cqn'], w=[('ps', bq)])
            norm_fm(o, C, [ps[bq][0:96, 0:tb]], [G['gq'][:, jl:jl + 1]], [qg[:, :]], 96, 96, tb, [('ps', bq)], 'qg', ones[0:96, 0:96])
            b2 = C['rG'].next()
            o.MM(ps[b2][0:96, 0:tb], perm[:, :], qg[:, :], True, True, r=['perm', 'qg'], w=[('ps', b2)])
            o.STT('dve', t1[:, :], qg[:, :], G['gk'][:, 0:1], cosb[:, :], ALU.mult, ALU.mult, r=['qg', 'cosb', 'gains'], w=['t1'])
            o.STT('dve', t2[:, :], ps[b2][0:96, 0:tb], G['gk'][:, 0:1], sinb[:, :], ALU.mult, ALU.mult,
                  r=[('ps', b2), 'sinb', 'gains'], w=['t2'])
            o.TT('pool', QTs[:, h, :], t1[:, :], t2[:, :], ALU.add, r=['t1', 't2'], w=['QTs'])
        for cc in range(2):
            b = C['rA'].next()
            for h in range(12):
                o.MM(ps[b][:, h * tb:(h + 1) * tb], wukT[0:64, h, cc * 128:(cc + 1) * 128], QTs[0:64, h, :], True, True,
                     r=['wukT', 'QTs'], w=[('ps', b)])
            o.CP('dve', QL[:, cc, :, :], ps[b][:, 0:12 * tb].rearrange("p (h t) -> p h t", t=tb), r=[('ps', b)], w=['QL'])
        BO, PSB, BSC = 4, 5, 3
        ngr = NPG // 4
        NG = NS * ngr
        psc = ps[PSB][:, :].bitcast(BF16)
        psr = ps[PSB + 1][:, :].bitcast(BF16)

        def slot(g, q):
            return (g % 4) * 4 + q

        def GATHER(g):
            for q in range(4):
                pg = g * 4 + q
                sl = slot(g, q)
                P.op('pool', lambda e, sl=sl, pg=pg: e.indirect_dma_start(
                    out=KVb[sl][:, 0:288], out_offset=None, in_=I["ckv"][:, :],
                    in_offset=bass.IndirectOffsetOnAxis(ap=idxall[:, pg:pg + 1], axis=0)),
                    r=['idx'], w=[('KVb', sl)], dma=True)

        def T(g):
            kt = g % 2
            for q in range(4):
                sl = slot(g, q)
                for cc in range(2):
                    o.TR(psc[:, (q * 2 + cc) * 128:(q * 2 + cc + 1) * 128], KVb[sl][:, cc * 128:(cc + 1) * 128], identb[:, :],
                         r=[('KVb', sl), 'identb'], w=[('ps', PSB)])
                o.TR(psr[0:96, q * 128:(q + 1) * 128], KVb[sl][:, 192:288], identb[:, :], r=[('KVb', sl), 'identb'], w=[('ps', PSB + 1)])
            o.CP('dve', KVTc[kt][:, :, :, :], psc[:, 0:1024].rearrange("p (q c k) -> p q c k", c=2, k=128), r=[('ps', PSB)], w=[('KVTc', kt)])
            o.CP('act', KVTr[kt][64:96, :, :], psr[64:96, 0:512].rearrange("p (q k) -> p q k", k=128), r=[('ps', PSB + 1)], w=[('KVTr', kt)])

        def KN(g):
            kt = g % 2
            for q in range(4):
                sl = slot(g, q)
                b3, b4 = (7, 0) if q % 2 == 0 else (1, 2)
                sq_ = sqn2[q % 2]
                sk = ('sqn', q % 2)
                for cc in range(2):
                    o.MM(ps[b3][:, 0:512], KVTc[kt][:, q, cc, :], C['wuk'][:, cc, 0:512], cc == 0, cc == 1,
                         r=[('KVTc', kt), 'wuk'], w=[('ps', b3)])
                for cc in range(2):
                    o.MM(ps[b4][:, 0:256], KVTc[kt][:, q, cc, :], C['wuk'][:, cc, 512:768], cc == 0, cc == 1,
                         r=[('KVTc', kt), 'wuk'], w=[('ps', b4)])
                o.ACT(sq_[:, 0:512], ps[b3][:, 0:512], AF.Square, r=[('ps', b3)], w=[sk])
                o.ACT(sq_[:, 512:768], ps[b4][:, 0:256], AF.Square, r=[('ps', b4)], w=[sk])
                o.ACT(sq_[:, 768:800], KVb[sl][:, 256:288], AF.Square, r=[('KVb', sl)], w=[sk])
                o.RED(ssn[:, q, 0:12], sq_[:, 0:768].rearrange("p (h d) -> p h d", d=64), r=[sk], w=['ssn'])
                o.RED(ssn[:, q, 12:13], sq_[:, 768:800].rearrange("p (h d) -> p h d", d=32), r=[sk], w=['ssn'])
            o.TT('dve', ssn[:, :, 0:12], ssn[:, :, 0:12], ssn[:, :, 12:13].to_broadcast([128, 4, 12]), ALU.add, r=['ssn'], w=['ssn'])
            o.ACT(ssn[:, :, 0:12], ssn[:, :, 0:12], AF.Ln, r=['ssn'], w=['ssn'], scale=1.0, bias=96.0 * EPS)
            o.ACT(kinv[:, g * 4:g * 4 + 4, :], ssn[:, :, 0:12], AF.Exp, r=['ssn'], w=['kinv'], scale=-0.5)

        def SE(g):
            kt = g % 2
            s_ = g // ngr
            qcols = slice(s_ * 8, (s_ + 1) * 8)
            for q in range(4):
                oc = ps[BSC][:, q * 96:(q + 1) * 96].rearrange("p (h i) -> p h i", i=8)
                o.MM(oc, KVTc[kt][:, q, 0, :], QL[:, 0, :, qcols], True, False, r=[('KVTc', kt), 'QL'], w=[('ps', BSC)])
                o.MM(oc, KVTc[kt][:, q, 1, :], QL[:, 1, :, qcols], False, False, r=[('KVTc', kt), 'QL'], w=[('ps', BSC)])
                o.MM(oc, KVTr[kt][64:96, q, :], QTs[64:96, :, qcols], False, True, r=[('KVTr', kt), 'QTs'], w=[('ps', BSC)])
            o.TT('dve', tms[:, :, :].rearrange("p q (h i) -> p q h i", i=8),
                 ps[BSC][:, 0:384].rearrange("p (q h i) -> p q h i", h=12, i=8),
                 kinv[:, g * 4:g * 4 + 4, :].unsqueeze(3).to_broadcast([128, 4, 12, 8]), ALU.mult, r=[('ps', BSC), 'kinv'], w=['tms'])
            o.ACT(PTs[kt][:, :, :], tms[:, :, :], AF.Exp, r=['tms'], w=[('PTs', kt)])

        def PV(g):
            kt = g % 2
            gi = g % ngr
            for q in range(4):
                sl = slot(g, q)
                o.MM(ps[BO][0:96, 0:289], PTs[kt][:, q, :], KVb[sl][:, 0:289], gi == 0 and q == 0, False,
                     r=[('PTs', kt), ('KVb', sl)], w=[('ps', BO)])

        def EPI(s_):
            qcols = slice(s_ * 8, (s_ + 1) * 8)
            oc = ps[BSC][0:32, 0:96].rearrange("p (h i) -> p h i", i=8)
            o.MM(oc, cTs[:, 0, :], QL[:, 0, :, qcols], True, False, r=['cTs', 'QL'], w=[('ps', BSC)])
            o.MM(oc, cTs[:, 1, :], QL[:, 1, :, qcols], False, False, r=['cTs', 'QL'], w=[('ps', BSC)])
            o.MM(oc, krb[64:96, :], QTs[64:96, :, qcols], False, True, r=['krb', 'QTs'], w=[('ps', BSC)])
            o.TT('dve', tms[0:32, 0, :].rearrange("p (h i) -> p h i", i=8), ps[BSC][0:32, 0:96].rearrange("p (h i) -> p h i", i=8),
                 kinvn[:, :].unsqueeze(2).to_broadcast([32, 12, 8]), ALU.mult, r=[('ps', BSC), 'kinvn'], w=['tms'])
            o.ACT(tms[0:32, 1, :], tms[0:32, 0, :], AF.Exp, r=['tms'], w=['tms'])
            o.TT('dve', PTn[:, :], tms[0:32, 1, :], smask[:, s_, :], ALU.mult, r=['tms', 'smask'], w=['PTn'])
            o.MM(ps[BO][0:96, 0:289], PTn[:, :], KVbn[:, 0:289], NPG == 0, True, r=['PTn', 'KVbn'], w=[('ps', BO)])
            o.RCP(rcs[:, :], ps[BO][0:96, 288:289], r=[('ps', BO)], w=['rcs'])
            o.TSM('dve', olat[:, :], ps[BO][0:96, 0:256], rcs[:, 0:1], r=[('ps', BO), 'rcs'], w=['olat'])
            b = PSB
            for cc in range(2):
                o.TR(ps[b][:, cc * 96:(cc + 1) * 96], olat[:, cc * 128:(cc + 1) * 128], ident[0:96, 0:96], r=['olat', 'ident'], w=[('ps', b)])
            o.CP('dve', olT[:, :, :], ps[b][:, 0:192].rearrange("p (c n) -> p c n", n=96), r=[('ps', b)], w=['olT'])
            b = PSB + 1
            for h in range(12):
                for cc in range(2):
                    o.MM(ps[b][:, h * 8:(h + 1) * 8], C['wuv'][:, cc, (h // 2) * 128:(h // 2 + 1) * 128], olT[:, cc, h * 8:(h + 1) * 8],
                         cc == 0, cc == 1, r=['wuv', 'olT'], w=[('ps', b)])
            for par in range(2):
                vr = slice(par * 64, par * 64 + 64)
                src = ps[b][vr, 0:96].rearrange("p (hp two i) -> p hp two i", two=2, i=8)[:, :, par, :]
                o.CP('dve', cat[vr, 0:6, qcols], src, r=[('ps', b)], w=['cat'])

        for g in range(min(3, NG)):
            GATHER(g)
        T(0)
        if l == 2:
            KN(0)
        if NG > 1:
            T(1)
        for g in range(NG):
            if g + 3 < NG:
                GATHER(g + 3)
            SE(g)
            if l == 2 and g + 1 < NG:
                KN(g + 1)
            if g + 2 < NG:
                T(g + 2)
            PV(g)
            if g % ngr == ngr - 1:
                EPI(g // ngr)
        mem_attend(l, 6)
        out_proj(l)

    o.DMA('sp', xin[:], I["xs"][:, :], w=['xin'])
    for c in range(8):
        b = C['rA'].next()
        o.TR(ps[b][:, 0:32], xin[:, c * 128:(c + 1) * 128], ident[0:32, 0:32], r=['xin', 'ident'], w=[('ps', b)])
        o.CP('dve', xT[:, c, :], ps[b][:, 0:32], r=[('ps', b)], w=[('xT', c)])
    load_mix_weights(0)
    mem_load(0)
    halo_load(0)

    def mk_hook(l):
        def hook(fg):
            if fg == 1:
                load_mix_weights(l + 1)
            elif fg in (2, 4, 6, 8):
                mem_load(l + 1, [(fg - 2) // 2])
            elif fg in (3, 5, 7, 9) and l + 1 < 2:
                halo_load(l + 1, [(fg - 3) // 2])
        return hook

    for l in range(nlayers):
        pre = [load_ffn_group(l, fg) for fg in range(min(NRING, NFG))]
        if l < 2:
            mix_a(l)
        else:
            mix_b(l)
        ffn(l, pre, hook=mk_hook(l) if l < nlayers - 1 else None)
        if l == 1:
            kv_stage()
    for c in range(8):
        b = C['rA'].next()
        o.TR(ps[b][0:32, 0:128], xT[:, c, :], ident[:, :], r=[('xT', c), 'ident'], w=[('ps', b)])
        o.CP('dve', xin[:, c * 128:(c + 1) * 128], ps[b][0:32, 0:128], r=[('ps', b)], w=['xin'])
    o.DMA('sp', O["y_s"][:, :], xin[:], r=['xin'], out=True)
    P.finish()
    P.emit('s')
    print("sample program: ops", len(P.ops), "waits", P.nwaits)


def make_in_maps(inputs, cores, NT=2048):
    consts = host_consts()
    maps = []
    ckv = np.ascontiguousarray(inputs["cache_kv"]).reshape(-1, 288)
    shared = {k: np.ascontiguousarray(inputs[k]) for k in
              ("norm_mix", "norm_ffn", "w_out", "w_ffn_in", "w_ffn_out", "norm_mem", "w_mem_kv", "g_mem_q", "g_mem_k",
               "w_in_a", "w_pool_grp", "pool_scale", "w_in_b", "g_q_lora", "w_uq", "g_q", "norm_kv", "w_dkv",
               "g_kv_lora", "w_uk", "w_uv", "g_k_nope", "g_k_rope")}
    for c in cores:
        m = dict(shared)
        m.update(consts)
        m["xp"] = np.ascontiguousarray(inputs["x_prompt"][c][:NT])
        m["xs"] = np.ascontiguousarray(inputs["x_sample"][4 * c:4 * c + 4]).reshape(32, D)
        m["ckv"] = ckv
        m["pt"] = np.ascontiguousarray(inputs["page_table"][4 * c:4 * c + 4]).astype(np.int32)
        m["spool"] = np.ascontiguousarray(inputs["state_pool"][:, 4 * c:4 * c + 4])
        m["cmk"] = np.ascontiguousarray(inputs["cache_mem_k"][:, 4 * c:4 * c + 4]).reshape(4, 4, 256, 256)
        m["cmv"] = np.ascontiguousarray(inputs["cache_mem_v"][:, 4 * c:4 * c + 4]).reshape(4, 4, 256, 256)
        m["memp"] = np.ascontiguousarray(inputs["mem_prompt"][c])
        maps.append(m)
    return maps


def kernel(**inputs):
    nc = build()
    maps = make_in_maps(inputs, list(range(NCORES)))
    res = run_bass_kernel_spmd(nc, maps, core_ids=list(range(NCORES)))
    R = res.results
    y_p = np.stack([R[c]["y_p"] for c in range(NCORES)])
    y_s = np.concatenate([R[c]["y_s"].reshape(4, 8, D) for c in range(NCORES)])
    kv_p = np.stack([R[c]["kv_p"] for c in range(NCORES)])
    kv_s = np.concatenate([R[c]["kv_s"].reshape(4, 8, 288) for c in range(NCORES)])
    pool_p = np.stack([R[c]["pool_p"] for c in range(NCORES)], axis=1)
    pool_s = np.concatenate([R[c]["pool_s"] for c in range(NCORES)], axis=1)
    mk = np.stack([R[c]["mk_p"] for c in range(NCORES)], axis=1).reshape(4, 8, 256, 4, 64)
    mv = np.stack([R[c]["mv_p"] for c in range(NCORES)], axis=1).reshape(4, 8, 256, 4, 64)
    return (y_p, y_s, kv_p, kv_s, pool_p, pool_s, mk, mv)
```

```python
import numpy as np
import concourse.bass as bass
import concourse.mybir as mybir
from concourse.bass_utils import run_bass_kernel_spmd
from contextlib import ExitStack

F32 = mybir.dt.float32
BF16 = mybir.dt.bfloat16
I32 = mybir.dt.int32
AF = mybir.ActivationFunctionType
ALU = mybir.AluOpType
AX = mybir.AxisListType

D = 1024
DFF = 2816
EPS = 1e-6
TB = 512
NCORES = 8


class Prog:
    ENGS = ('pe', 'act', 'dve', 'pool', 'sp')
    NDMA = 8

    def __init__(self, nc, es, pfx=''):
        self.nc, self.es = nc, es
        self.pfx = pfx
        self.ops = []
        self.lastw = {}
        self.readers = {}
        self.out_dmas = []

    def sb(self, name, shape, dt):
        return self.es.enter_context(self.nc.sbuf_tensor(self.pfx + name, shape, dt))

    def ps(self, name, shape, dt):
        return self.es.enter_context(self.nc.psum_tensor(self.pfx + name, shape, dt))

    def op(self, eng, fn, r=(), w=(), dma=False, out=False):
        i = len(self.ops)
        deps = set()
        pr = [k for k in r if isinstance(k, tuple) and k[0] == 'ps']
        if pr:
            w = list(w) + [k for k in pr if k not in w]
        for k in r:
            lw = self.lastw.get(k)
            if lw is not None:
                deps.add(lw)
        for k in w:
            lw = self.lastw.get(k)
            if lw is not None:
                deps.add(lw)
            for rd in self.readers.get(k, ()):
                deps.add(rd)
        keep = []
        for d in deps:
            de, ddma = self.ops[d][0], self.ops[d][3]
            if de == eng and not dma and not ddma and eng == 'pe':
                continue
            keep.append(d)
        for k in r:
            self.readers.setdefault(k, []).append(i)
        for k in w:
            self.lastw[k] = i
            self.readers[k] = []
        self.ops.append([eng, fn, keep, dma])
        if out:
            self.out_dmas.append(i)
        return i

    def finish(self):
        import os
        stop = int(os.environ.get("K_STOP", "0"))
        if stop:
            self.ops = self.ops[:stop]
            self.out_dmas = [i for i in range(len(self.ops)) if self.ops[i][3]]
        self.ops.append(['sp', None, list(self.out_dmas), False])

    def emit(self, tag=''):
        nc, es = self.nc, self.es
        ops = self.ops
        n = len(ops)
        needed = [False] * n
        for o in ops:
            for d in o[2]:
                needed[d] = True
        csem = {e: es.enter_context(nc.semaphore(tag + 'c_' + e)) for e in self.ENGS}
        ccnt = {e: 0 for e in self.ENGS}
        dsem = {e: [es.enter_context(nc.semaphore('%sd_%s%d' % (tag, e, j))) for j in range(self.NDMA)]
                for e in ('sp', 'pool', 'act')}
        dcnt = {e: [0] * self.NDMA for e in dsem}
        dlast = {e: [None] * self.NDMA for e in dsem}
        drr = {e: 0 for e in dsem}
        sig = [None] * n
        extra = [None] * n
        streams = {e: [] for e in self.ENGS}
        for i, o in enumerate(ops):
            eng, fn, deps, dma = o
            streams[eng].append(i)
            if dma:
                j = drr[eng]
                drr[eng] = (j + 1) % self.NDMA
                dcnt[eng][j] += 1
                sig[i] = (dsem[eng][j], 16 * dcnt[eng][j])
                extra[i] = dlast[eng][j]
                dlast[eng][j] = i
            elif needed[i]:
                ccnt[eng] += 1
                sig[i] = (csem[eng], ccnt[eng])
        self.nwaits = 0

        def body(e, eng):
            known = {}
            for i in streams[eng]:
                _, fn, deps, dma = ops[i]
                req = {}
                dl = list(deps)
                if extra[i] is not None:
                    dl.append(extra[i])
                for d in dl:
                    s, v = sig[d]
                    if req.get(id(s), (None, 0))[1] < v:
                        req[id(s)] = (s, v)
                for s, v in req.values():
                    if known.get(id(s), 0) < v:
                        e.wait_ge(s, v)
                        known[id(s)] = v
                        self.nwaits += 1
                if fn is None:
                    continue
                ins = fn(e)
                if sig[i] is not None:
                    ins.then_inc(sig[i][0], 16 if dma else 1)

        with nc.Block() as block:
            @block.tensor
            def _(e):
                body(e, 'pe')

            @block.scalar
            def _(e):
                body(e, 'act')

            @block.vector
            def _(e):
                body(e, 'dve')

            @block.gpsimd
            def _(e):
                body(e, 'pool')

            @block.sync
            def _(e):
                body(e, 'sp')


class Rot:
    def __init__(self, items):
        self.items, self.i = items, 0

    def next(self):
        x = self.items[self.i % len(self.items)]
        self.i += 1
        return x


class Ops:
    def __init__(self, P):
        self.P = P

    def MM(self, o, l, rh, st, sp, r, w):
        self.P.op('pe', lambda e: e.matmul(o, lhsT=l, rhs=rh, start=st, stop=sp), r=r, w=w)

    def TR(self, o, i, idn, r, w):
        self.P.op('pe', lambda e: e.transpose(out=o, in_=i, identity=idn), r=r, w=w)

    def ACT(self, o, i, f, r, w, **kw):
        self.P.op('act', lambda e: e.activation(out=o, in_=i, func=f, **kw), r=r, w=w)

    def TT(self, eng, o, a, b, op, r, w):
        self.P.op(eng, lambda e: e.tensor_tensor(out=o, in0=a, in1=b, op=op), r=r, w=w)

    def STT(self, eng, o, a, s, b, op0, op1, r, w):
        self.P.op(eng, lambda e: e.scalar_tensor_tensor(out=o, in0=a, scalar=s, in1=b, op0=op0, op1=op1), r=r, w=w)

    def TS(self, eng, o, a, s1, s2, op0, op1, r, w):
        self.P.op(eng, lambda e: e.tensor_scalar(out=o, in0=a, scalar1=s1, scalar2=s2, op0=op0, op1=op1), r=r, w=w)

    def CP(self, eng, o, i, r, w):
        if eng == 'act':
            self.P.op('act', lambda e: e.activation(out=o, in_=i, func=AF.Copy), r=r, w=w)
        else:
            self.P.op(eng, lambda e: e.tensor_copy(out=o, in_=i), r=r, w=w)

    def TSA(self, eng, o, a, s1, r, w):
        self.P.op(eng, lambda e: e.tensor_scalar_add(out=o, in0=a, scalar1=s1), r=r, w=w)

    def TSM(self, eng, o, a, s1, r, w):
        self.P.op(eng, lambda e: e.tensor_scalar_mul(out=o, in0=a, scalar1=s1), r=r, w=w)

    def RCP(self, o, i, r, w):
        self.P.op('dve', lambda e: e.reciprocal(out=o, in_=i), r=r, w=w)

    def MSET(self, eng, o, v, w):
        self.P.op(eng, lambda e: e.memset(o, v), w=w)

    def RED(self, o, i, r, w):
        self.P.op('dve', lambda e: e.tensor_reduce(out=o, in_=i, axis=AX.X, op=ALU.add), r=r, w=w)

    def DMA(self, eng, o, i, r=(), w=(), out=False, slow=False):
        if slow:
            self.P.op(eng, lambda e: e.dma_start(out=o, in_=i, allow_slow_non_contiguous=True), r=r, w=w, dma=True, out=out)
        else:
            self.P.op(eng, lambda e: e.dma_start(out=o, in_=i), r=r, w=w, dma=True, out=out)


CKV_ROWS = [5120 * 128]
PT_COLS = [128]


def in_specs(NT):
    return [
        ("xp", [NT, D], F32), ("xs", [32, D], F32), ("ckv", [CKV_ROWS[0], 288], F32), ("pt", [4, PT_COLS[0]], I32),
        ("spool", [2, 4, 15, 768], F32), ("cmk", [4, 4, 256, 256], F32), ("cmv", [4, 4, 256, 256], F32),
        ("memp", [256, D], F32),
        ("norm_mix", [4, D], F32), ("norm_ffn", [4, D], F32), ("w_out", [4, D, D], F32),
        ("w_ffn_in", [4, D, 2 * DFF], F32), ("w_ffn_out", [4, DFF, D], F32), ("norm_mem", [4, D], F32),
        ("w_mem_kv", [4, D, 512], F32), ("g_mem_q", [4, 64], F32), ("g_mem_k", [4, 64], F32),
        ("w_in_a", [2, D, D], F32), ("w_pool_grp", [2, 4, 192, 192], F32), ("pool_scale", [2, 768], F32),
        ("w_in_b", [2, D, 640], F32), ("g_q_lora", [2, 384], F32), ("w_uq", [2, 384, 1152], F32), ("g_q", [2, 96], F32),
        ("norm_kv", [D], F32), ("w_dkv", [D, 288], F32), ("g_kv_lora", [256], F32), ("w_uk", [256, 768], F32),
        ("w_uv", [256, 768], F32), ("g_k_nope", [64], F32), ("g_k_rope", [16], F32),
        ("c_ident", [128, 128], F32), ("c_perm", [96, 96], F32), ("c_cos", [96, 2048 + 32], F32),
        ("c_sin", [96, 2048 + 32], F32), ("c_tri", [128, 128], F32), ("c_rcnt", [96, 8, 16], F32),
        ("c_smask", [32, 4, 96], F32),
    ]


def out_specs(NT):
    return [
        ("y_p", [NT, D]), ("y_s", [32, D]), ("kv_p", [NT, 288]), ("kv_s", [32, 288]),
        ("pool_p", [2, 15, 768]), ("pool_s", [2, 4, 15, 768]), ("mk_p", [4, 256, 256]), ("mv_p", [4, 256, 256]),
    ]


def host_consts():
    c = {}
    c["c_ident"] = np.eye(128, dtype=np.float32)
    perm = np.zeros((96, 96), np.float32)
    for i in range(16):
        perm[80 + i, 64 + i] = -1.0
        perm[64 + i, 80 + i] = 1.0
    c["c_perm"] = perm
    half = 16
    inv_freq = (np.float32(10000.0) ** (-np.arange(half, dtype=np.float32) / np.float32(half))).astype(np.float32)
    pos = np.concatenate([np.arange(2048), np.tile(16384 + np.arange(8), 4)]).astype(np.float32)
    ang = (pos[None, :] * inv_freq[:, None]).astype(np.float32)
    cos = np.ones((96, 2080), np.float32)
    sin = np.zeros((96, 2080), np.float32)
    cos[64:80] = np.cos(ang)
    cos[80:96] = np.cos(ang)
    sin[64:80] = np.sin(ang)
    sin[80:96] = np.sin(ang)
    c["c_cos"], c["c_sin"] = cos, sin
    kk = np.arange(128)
    c["c_tri"] = (kk[None, :] >= kk[:, None]).astype(np.float32)
    rc = np.zeros((96, 8, 16), np.float32)
    for ch in range(8):
        w = 2 ** (ch // 2 + 1)
        rc[:, ch, :] = (1.0 / np.minimum(w, np.arange(16) + 1)).astype(np.float32)[None, :]
    c["c_rcnt"] = rc
    sm = np.zeros((32, 4, 12, 8), np.float32)
    for k in range(32):
        s = k // 8
        for q in range(8):
            if k % 8 <= q:
                sm[k, s, :, q] = 1.0
    c["c_smask"] = sm.reshape(32, 4, 96)
    return c


def build(NT=2048, SEG=512, do_prompt=True, do_sample=True, nlayers=4, NPG=128):
    nc = bass.Bass("TRN2", target_bir_lowering=False)
    I = {n: nc.dram_tensor(n, s, dt, kind="ExternalInput").ap() for n, s, dt in in_specs(NT)}
    O = {n: nc.dram_tensor(n, s, F32, kind="ExternalOutput").ap() for n, s in out_specs(NT)}
    SC = None
    if do_prompt and do_sample:
        SC = {n: nc.dram_tensor("sc_" + n, shp, BF16).ap() for n, shp in (
            ("w_ffn_in", [4, D, 2 * DFF]), ("w_ffn_out", [4, DFF, D]), ("w_in_a", [2, D, D]), ("w_in_b", [2, D, 640]),
            ("w_out", [4, D, D]), ("w_pool_grp", [2, 4, 192, 192]), ("w_uq", [2, 384, 1152]))}
    if do_prompt:
        with ExitStack() as es:
            prompt_program(nc, es, I, O, NT, SEG, nlayers, SC)
    if do_sample:
        with ExitStack() as es:
            sample_program(nc, es, I, O, nlayers, NPG, SC)
    return nc


def load_small(P, o, I, C):
    g1 = P.sb("gst1", [128, 128], F32)
    g2 = P.sb("gst2", [32, 128], F32)
    gT1 = P.sb("gT1", [128, 128], F32)
    gT2 = P.sb("gT2", [128, 32], F32)
    o.MSET('dve', g1[:], 0.0, w=['gst1'])
    o.MSET('dve', g2[:], 0.0, w=['gst2'])
    o.DMA('sp', g1[0:32, :], I["norm_mix"].rearrange("l (k p) -> (l k) p", p=128), w=['gst1'])
    o.DMA('sp', g1[32:64, :], I["norm_ffn"].rearrange("l (k p) -> (l k) p", p=128), w=['gst1'])
    o.DMA('sp', g1[64:96, :], I["norm_mem"].rearrange("l (k p) -> (l k) p", p=128), w=['gst1'])
    o.DMA('sp', g1[96:104, :], I["norm_kv"].rearrange("(k p) -> k p", p=128), w=['gst1'])
    o.DMA('sp', g1[104:106, :], I["g_kv_lora"].rearrange("(k p) -> k p", p=128), w=['gst1'])
    o.DMA('sp', g1[106:112, :], I["g_q_lora"].rearrange("l (k p) -> (l k) p", p=128), w=['gst1'])
    o.DMA('sp', g2[0:16, 0:96], I["pool_scale"].rearrange("l (k p) -> (l k) p", p=96), w=['gst2'])
    o.DMA('sp', g2[16:20, 0:64], I["g_mem_q"][:, :], w=['gst2'])
    o.DMA('sp', g2[16:20, 64:128], I["g_mem_q"][:, :], w=['gst2'])
    o.DMA('sp', g2[20:24, 0:64], I["g_mem_k"][:, :], w=['gst2'])
    o.DMA('sp', g2[20:24, 64:128], I["g_mem_k"][:, :], w=['gst2'])
    o.DMA('sp', g2[24:26, 0:96], I["g_q"][:, :], w=['gst2'])
    o.DMA('sp', g2[26:27, 0:64], I["g_k_nope"].rearrange("(o p) -> o p", o=1), w=['gst2'])
    o.DMA('sp', g2[26:27, 64:80], I["g_k_rope"].rearrange("(o p) -> o p", o=1), w=['gst2'])
    o.DMA('sp', g2[26:27, 80:96], I["g_k_rope"].rearrange("(o p) -> o p", o=1), w=['gst2'])
    ps = C['ps']
    o.TR(ps[0][:, 0:128], g1[:, :], C['ident'][:, :], r=['gst1', 'ident'], w=[('ps', 0)])
    o.CP('dve', gT1[:, :], ps[0][:, 0:128], r=[('ps', 0)], w=['gains'])
    o.TR(ps[1][:, 0:32], g2[:, :], C['ident'][0:32, 0:32], r=['gst2', 'ident'], w=[('ps', 1)])
    o.CP('dve', gT2[:, :], ps[1][:, 0:32], r=[('ps', 1)], w=['gains'])
    G = {'gmix': gT1[:, 0:32], 'gffn': gT1[:, 32:64], 'gmem': gT1[:, 64:96], 'gkv': gT1[:, 96:104], 'gkvl': gT1[:, 104:106],
         'gql': gT1[:, 106:112], 'pscale': gT2[0:96, 0:16], 'gmq': gT2[:, 16:20], 'gmk': gT2[:, 20:24], 'gq': gT2[0:96, 24:26],
         'gk': gT2[0:96, 26:27]}
    return G


def alloc_common(P, o, I):
    C = {}
    C['ident'] = P.sb("ident", [128, 128], F32)
    C['ones'] = P.sb("ones", [128, 128], BF16)
    C['blk2'] = P.sb("blk2", [128, 128], BF16)
    C['perm'] = P.sb("perm", [96, 96], BF16)
    C['tri'] = P.sb("tri", [128, 128], BF16)
    C['rcnt'] = P.sb("rcnt", [96, 8, 16], F32)
    o.DMA('sp', C['ident'][:], I["c_ident"][:, :], w=['ident'])
    o.DMA('pool', C['perm'][:], I["c_perm"][:, :], w=['perm'])
    o.DMA('pool', C['tri'][:], I["c_tri"][:, :], w=['tri'])
    o.DMA('sp', C['rcnt'][:], I["c_rcnt"][:, :, :], w=['rcnt'])
    o.MSET('dve', C['ones'][:], 1.0, w=['ones'])
    o.MSET('dve', C['blk2'][:], 0.0, w=['blk2'])
    o.MSET('dve', C['blk2'][0:64, 0:64], 1.0, w=['blk2'])
    o.MSET('dve', C['blk2'][64:128, 64:128], 1.0, w=['blk2'])
    C['wuk'] = P.sb("wuk", [128, 2, 768], BF16)
    C['wuv'] = P.sb("wuv", [128, 2, 768], BF16)
    C['wdkv'] = P.sb("wdkv", [128, 8, 352], BF16)
    o.DMA('pool', C['wuk'][:], I["w_uk"].rearrange("(k p) n -> p k n", p=128), w=['wuk'])
    o.DMA('pool', C['wuv'][:], I["w_uv"].rearrange("(k p) n -> p k n", p=128), w=['wuv'])
    o.MSET('pool', C['wdkv'][:, :, 256:320], 0.0, w=['wdkv'])
    o.DMA('pool', C['wdkv'][:, :, 0:256], I["w_dkv"].rearrange("(k p) n -> p k n", p=128)[:, :, 0:256], w=['wdkv'])
    o.DMA('pool', C['wdkv'][:, :, 320:352], I["w_dkv"].rearrange("(k p) n -> p k n", p=128)[:, :, 256:288], w=['wdkv'])
    ps = [P.ps("ps%d" % i, [128, 512], F32) for i in range(8)]
    C['ps'] = ps
    C['G'] = load_small(P, o, I, C)
    C['rA'] = Rot([0, 1])
    C['rS'] = Rot([2, 3])
    C['rO'] = Rot([4, 5])
    C['rG'] = Rot([6, 7])
    C['tmp'] = P.sb("tmpn", [128, 512], F32)
    C['rstd'] = P.sb("rstd", [128, 512], F32)
    C['rtmp'] = Rot([0])
    return C


def norm_fm(o, C, xaps, gcols, haps, np_, Dn, tb, rkeys, hkey, ones, eps=EPS, sq=None, split=False):
    ps = C['ps']
    b = C['rA'].next()
    pk = ('ps', b)
    nch = len(xaps)
    sqaps, sqkey = (haps, hkey) if sq is None else sq
    if split and nch == 8:
        order = [0, 2, 4, 6, 1, 3, 5, 7]
        for c in order:
            rk_c = [rkeys[c]] if len(rkeys) == 8 else rkeys
            if c % 2 == 0:
                o.ACT(sqaps[c], xaps[c], AF.Square, r=rk_c, w=[(sqkey, 'sq', c)])
            else:
                o.TT('dve', sqaps[c], xaps[c], xaps[c], ALU.mult, r=rk_c, w=[(sqkey, 'sq', c)])
        for i, c in enumerate(order):
            o.MM(ps[b][0:np_, 0:tb], ones, sqaps[c], i == 0, i == nch - 1, r=[(sqkey, 'sq', c), 'ones', 'blk2'], w=[pk])
        sqdeps = [(sqkey, 'sq', c) for c in range(8)]
        if C.get('keep') is not None:
            kb, kap, kkey, nkeep = C['keep']
            for _ in range(nkeep):
                o.MM(ps[kb][:, 0:tb], ones, kap, True, True, r=['ones', kkey], w=[('ps', kb)])
    else:
        for c in range(nch):
            o.ACT(sqaps[c], xaps[c], AF.Square, r=rkeys, w=[sqkey])
        for c in range(nch):
            o.MM(ps[b][0:np_, 0:tb], ones, sqaps[c], c == 0, c == nch - 1, r=[sqkey, 'ones', 'blk2'], w=[pk])
        sqdeps = [sqkey]
    o.ACT(C['tmp'][0:np_, 0:tb], ps[b][0:np_, 0:tb], AF.Ln, r=[pk], w=['tmpn'], scale=1.0 / Dn, bias=eps)
    o.ACT(C['rstd'][0:np_, 0:tb], C['tmp'][0:np_, 0:tb], AF.Exp, r=['tmpn'], w=['rstd'], scale=-0.5)
    for c in range(nch):
        o.STT('dve', haps[c], xaps[c], gcols[c], C['rstd'][0:np_, 0:tb], ALU.mult, ALU.mult,
              r=list(rkeys) + ['rstd', 'gains'] + sqdeps, w=[hkey])


def prompt_program(nc, es, I, O, NT, SEG, nlayers, SC=None):
    P = Prog(nc, es)
    o = Ops(P)
    C = alloc_common(P, o, I)
    ps = C['ps']
    G = C['G']
    ident, ones, blk2, perm, tri = C['ident'], C['ones'], C['blk2'], C['perm'], C['tri']
    SCM = {"k": nc.dram_tensor("sc_memKT", [4, 128, 512], BF16).ap(), "v": nc.dram_tensor("sc_Vaug", [4, 128, 1024], BF16).ap()}
    NB = SEG // TB
    NSEG = NT // SEG
    NKC = NT // 128
    tb = TB

    xT = P.sb("xT", [128, 8, SEG], F32)
    hT = P.sb("hT", [128, 8, SEG], BF16)
    win = P.sb("win", [128, 8, 1024], BF16)
    wout = P.sb("wout", [128, 10, 1024], BF16)
    wmisc = P.sb("wmisc", [128, 3456], BF16)
    NRG, NRO = 2, 3
    ringG = [P.sb("ringG%d" % i, [128, 4096], BF16) for i in range(NRG)]
    ringO = [P.sb("ringO%d" % i, [128, 2048], BF16) for i in range(NRO)]
    cT = P.sb("cT", [128, 2, NT], BF16)
    KT = [P.sb("KT%d" % i, [96, NT], BF16) for i in range(2)]
    Vh = [P.sb("Vh%d" % i, [128, NKC, 128], BF16) for i in range(2)]
    kinv = P.sb("kinv", [128, NKC, 12], F32)
    cat = P.sb("cat", [128, 10, tb], BF16)
    qm = P.sb("qm", [128, 2, tb], F32)
    qn = P.sb("qn", [128, 2, tb], BF16)
    PT = [P.sb("PT%d" % i, [128, tb], BF16) for i in range(4)]
    rPT = Rot([0, 1, 2, 3])
    rc = P.sb("rc", [128, tb], F32)
    cosb = P.sb("cosb", [96, tb], F32)
    sinb = P.sb("sinb", [96, tb], F32)
    memKT = P.sb("memKT", [128, 2, 256], BF16)
    Vaug = P.sb("Vaug", [128, 2, 4, 128], BF16)
    halo = P.sb("halo", [96, 2, 8, 16], F32)
    AFN, ABN = 7616, 4096
    arf = P.sb("arena_f", [128, AFN], F32)
    arb = P.sb("arena_b", [128, ABN], BF16)

    def cv(ar, off, parts, n, pat=None, **kw):
        v = ar[0:parts, off:off + n]
        return v.rearrange(pat, **kw) if pat else v
    xin = cv(arf, 0, 128, 1024)
    xinb = [xin, cv(arf, 4608, 128, 1024)]
    memT = cv(arf, 1024, 128, 2048, "p (c t) -> p c t", t=256)
    kf = cv(arf, 3072, 128, 512, "p (c t) -> p c t", t=256)
    kout = cv(arf, 3584, 128, 512, "p (c t) -> p c t", t=256)
    vout = cv(arf, 4096, 128, 512, "p (c t) -> p c t", t=256)
    hmT = cv(arb, 0, 128, 2048, "p (c t) -> p c t", t=256)
    E_ = 16 + tb
    U = cv(arf, 0, 96, 8 * E_, "p (c t) -> p c t", t=E_)
    tA = cv(arf, 8 * E_, 96, 2 * E_, "p (c t) -> p c t", t=E_)
    tB_ = cv(arf, 10 * E_, 96, 2 * E_, "p (c t) -> p c t", t=E_)
    t1a = cv(arf, 12 * E_, 96, 512)
    pso = cv(arf, 12 * E_ + 512, 16, 768)
    dif = cv(arb, 0, 96, 4096, "p (c t) -> p c t", t=tb)
    cq = cv(arf, 0, 128, 1536, "p (c t) -> p c t", t=tb)
    ckf = cv(arf, 0, 128, 1024, "p (c t) -> p c t", t=tb)
    krf = cv(arf, 1024, 96, 512)
    t1 = cv(arf, 1536, 96, 512)
    t2 = cv(arf, 2048, 96, 512)
    kvo = cv(arf, 2560, 128, 288)
    ssn = cv(arf, 2848, 128, 16)
    cqn = cv(arb, 0, 128, 1536, "p (c t) -> p c t", t=tb)
    QT = [cv(arb, 1536, 96, 512), cv(arb, 2048, 96, 512)]
    qg = cv(arb, 2560, 96, 512)
    krb = cv(arb, 1536, 96, 512)
    sqn = cv(arb, 2048, 128, 768)
    sqr = cv(arb, 2816, 128, 32)
    sg = [cv(arf, 0, 128, 512), cv(arf, 512, 128, 512)]
    stg = [cv(arf, 1024, 128, 512), cv(arf, 1536, 128, 512)]
    aT = [cv(arb, 0, 128, 1024, "p (c t) -> p c t", t=tb), cv(arb, 1024, 128, 1024, "p (c t) -> p c t", t=tb)]
    wmem = ringG[0]
    ARK = ['xin', 'xin1', 'memT', 'kf', 'kout', 'vout', 'hmT', 'U', 'tA', 'tB', 't1', 't2', 'pso', 'dif', 'cq', 'ckf', 'krf', 'kvo',
           'ssn', 'ssn2', 'cqn', ('QT', 0), ('QT', 1), 'qg', 'krb', 'sqn', 'sqr', ('sg', 0), ('sg', 1), ('aT', 0), ('aT', 1), ('stg', 0), ('stg', 1)]
    dummy = P.sb("dummyb", [128, 8], F32)

    def barrier():
        P.op('pool', lambda e: e.memset(dummy[:], 0.0), r=ARK, w=ARK)

    o.MSET('pool', cat[:], 0.0, w=['cat'])
    C['keep'] = (3, cat[:, 0, :], 'cat', 18)
    o.MSET('pool', Vaug[:], 1.0, w=['Vaug'])
    for i in range(2):
        o.MSET('pool', Vh[i][:], 1.0, w=[('Vh', i)])

    def wsrc(first):
        return (I, 'pool') if (SC is None or first) else (SC, 'sp')

    def wb(first, dst, src_sb, rkey, wkey):
        if SC is not None and first:
            o.DMA('sp', dst, src_sb, r=[rkey], w=[wkey], out=True)
    NFG = DFF // 256

    def XK(j):
        return [('xT', j, c) for c in range(8)]

    ring_state = {'g': 0, 'o': 0}

    def load_G(l, fg, first):
        src, q = wsrc(first)
        s_ = ring_state['g'] % NRG
        ring_state['g'] += 1
        rg = ringG[s_]
        k = ('rg', s_)
        gv = rg[:, 0:2048].rearrange("p (k n) -> p k n", n=256)
        uv = rg[:, 2048:4096].rearrange("p (k n) -> p k n", n=256)
        wi = src["w_ffn_in"][l].rearrange("(k p) n -> p k n", p=128)
        rk = [] if src is I else [('scw', 'g', l, fg)]
        o.DMA(q, gv, wi[:, :, fg * 256:(fg + 1) * 256], r=rk, w=[k])
        o.DMA(q, uv, wi[:, :, DFF + fg * 256:DFF + (fg + 1) * 256], r=rk, w=[k])
        if SC is not None and first:
            so = SC["w_ffn_in"][l].rearrange("(k p) n -> p k n", p=128)
            wb(first, so[:, :, fg * 256:(fg + 1) * 256], gv, k, ('scw', 'g', l, fg))
            wb(first, so[:, :, DFF + fg * 256:DFF + (fg + 1) * 256], uv, k, ('scw', 'g', l, fg))
        return gv, uv, k

    def load_O(l, fg, first):
        src, q = wsrc(first)
        s_ = ring_state['o'] % NRO
        ring_state['o'] += 1
        k = ('ro', s_)
        ov = ringO[s_][:, 0:2048].rearrange("p (k n) -> p k n", n=1024)
        wo = src["w_ffn_out"][l].rearrange("(k p) n -> p k n", p=128)
        rk = [] if src is I else [('scw', 'o', l, fg)]
        o.DMA(q, ov, wo[:, fg * 2:fg * 2 + 2, :], r=rk, w=[k])
        if SC is not None and first:
            so = SC["w_ffn_out"][l].rearrange("(k p) n -> p k n", p=128)
            wb(first, so[:, fg * 2:fg * 2 + 2, :], ov, k, ('scw', 'o', l, fg))
        return ov, k

    def load_mix_weights(l, first):
        W, WQ = wsrc(first)
        rk = [] if W is I else [('scw', 'mix', l)]
        wk = ('scw', 'mix', l)
        if l < 2:
            o.DMA(WQ, win[:], W["w_in_a"][l].rearrange("(k p) n -> p k n", p=128), r=rk, w=['win'])
            o.DMA(WQ, wout[0:96, 0:8, :], W["w_out"][l][0:768, :].rearrange("(k p) n -> p k n", p=96), r=rk, w=['wout'])
            o.DMA(WQ, wout[:, 8:10, :], W["w_out"][l][768:1024, :].rearrange("(k p) n -> p k n", p=128), r=rk, w=['wout'])
            gv = wmisc[0:96, 0:1536].rearrange("p (k n) -> p k n", n=192)
            o.DMA(WQ, gv, W["w_pool_grp"][l].rearrange("g (c p) e -> p (g c) e", p=96), r=rk, w=['wmisc'])
            if SC is not None and first:
                wb(first, SC["w_in_a"][l].rearrange("(k p) n -> p k n", p=128), win[:], 'win', wk)
                wb(first, SC["w_out"][l][0:768, :].rearrange("(k p) n -> p k n", p=96), wout[0:96, 0:8, :], 'wout', wk)
                wb(first, SC["w_out"][l][768:1024, :].rearrange("(k p) n -> p k n", p=128), wout[:, 8:10, :], 'wout', wk)
                wb(first, SC["w_pool_grp"][l].rearrange("g (c p) e -> p (g c) e", p=96), gv, 'wmisc', wk)
        else:
            j = l - 2
            o.DMA(WQ, win[:, :, 0:640], W["w_in_b"][j].rearrange("(k p) n -> p k n", p=128), r=rk, w=['win'])
            o.DMA(WQ, wout[:, 0:8, :], W["w_out"][l].rearrange("(k p) n -> p k n", p=128), r=rk, w=['wout'])
            uq = wmisc[:, 0:3456].rearrange("p (k n) -> p k n", n=1152)
            o.DMA(WQ, uq, W["w_uq"][j].rearrange("(k p) n -> p k n", p=128), r=rk, w=['wmisc'])
            if SC is not None and first:
                wb(first, SC["w_in_b"][j].rearrange("(k p) n -> p k n", p=128), win[:, :, 0:640], 'win', wk)
                wb(first, SC["w_out"][l].rearrange("(k p) n -> p k n", p=128), wout[:, 0:8, :], 'wout', wk)
                wb(first, SC["w_uq"][j].rearrange("(k p) n -> p k n", p=128), uq, 'wmisc', wk)

    def mem_kv(l, write_out):
        if not write_out:
            o.DMA('sp', memKT[:], SCM["k"][l].rearrange("p (c m) -> p c m", m=256), r=[('scm', l)], w=['memKT'])
            o.DMA('sp', Vaug[:], SCM["v"][l].rearrange("p (c h d) -> p c h d", h=4, d=128), r=[('scm', l)], w=['Vaug'])
            return
        barrier()
        o.DMA('pool', wmem[:, 0:4096].rearrange("p (k n) -> p k n", n=512),
              I["w_mem_kv"][l].rearrange("(k p) n -> p k n", p=128), w=[('rg', 0)])
        wm = wmem[:, 0:4096].rearrange("p (k n) -> p k n", n=512)
        for mt in range(2):
            o.DMA('sp', xin[:], I["memp"][mt * 128:(mt + 1) * 128, :], w=['xin'])
            for c in range(8):
                b = C['rA'].next()
                o.TR(ps[b][:, 0:128], xin[:, c * 128:(c + 1) * 128], ident[:], r=['xin', 'ident'], w=[('ps', b)])
                o.CP('dve' if c % 2 else 'act', memT[:, c, mt * 128:(mt + 1) * 128], ps[b][:, 0:128], r=[('ps', b)], w=['memT'])
        norm_fm(o, C, [memT[:, c, :] for c in range(8)], [G['gmem'][:, l * 8 + c:l * 8 + c + 1] for c in range(8)],
                [hmT[:, c, :] for c in range(8)], 128, D, 256, ['memT'], 'hmT', ones[:])
        for hp in range(2):
            b = C['rA'].next()
            for k in range(8):
                o.MM(ps[b][:, 0:256], wm[:, k, hp * 128:(hp + 1) * 128], hmT[:, k, :], k == 0, k == 7,
                     r=['hmT', ('rg', 0)], w=[('ps', b)])
            norm_fm(o, C, [ps[b][:, 0:256]], [G['gmk'][:, l:l + 1]], [kf[:, hp, :]], 128, 64, 256, [('ps', b)], 'kf', blk2[:],
                    sq=([qn[:, 0, 0:256]], 'qn'))
        o.CP('dve', memKT[:], kf[:], r=['kf'], w=['memKT'])
        for mc in range(2):
            b = C['rA'].next()
            for k in range(8):
                o.MM(ps[b][:, 0:256], hmT[:, k, mc * 128:(mc + 1) * 128], wm[:, k, 256:512], k == 0, k == 7,
                     r=['hmT', ('rg', 0)], w=[('ps', b)])
            if write_out:
                o.CP('act', vout[:, mc, :], ps[b][:, 0:256], r=[('ps', b)], w=['vout'])
            for h in range(4):
                off = 0 if h % 2 == 0 else 64
                o.CP('dve', Vaug[:, mc, h, off:off + 64], ps[b][:, h * 64:(h + 1) * 64], r=[('ps', b)], w=['Vaug'])
        o.DMA('sp', SCM["k"][l].rearrange("p (c m) -> p c m", m=256), memKT[:], r=['memKT'], w=[('scm', l)])
        o.DMA('sp', SCM["v"][l].rearrange("p (c h d) -> p c h d", h=4, d=128), Vaug[:], r=['Vaug'], w=[('scm', l)])
        if write_out:
            for hp in range(2):
                for mc in range(2):
                    b = C['rA'].next()
                    o.TR(ps[b][:, 0:128], kf[:, hp, mc * 128:(mc + 1) * 128], ident[:], r=['kf', 'ident'], w=[('ps', b)])
                    o.CP('act', kout[:, mc, hp * 128:(hp + 1) * 128], ps[b][:, 0:128], r=[('ps', b)], w=['kout'])
            o.DMA('sp', O["mk_p"][l].rearrange("(m p) f -> p m f", p=128), kout[:], r=['kout'], out=True)
            o.DMA('sp', O["mv_p"][l].rearrange("(m p) f -> p m f", p=128), vout[:], r=['vout'], out=True)

    def mem_qnorm(l):
        for hp in range(2):
            norm_fm(o, C, [qm[:, hp, :]], [G['gmq'][:, l:l + 1]], [qn[:, hp, :]], 128, 64, tb, ['qm'], 'qn', blk2[:])

    def mem_attend(l, j, catbase):
        st = {}

        def SS(h):
            hp, base = h // 2, (h % 2) * 64
            pts = []
            for mc in range(2):
                bsc = C['rS'].next()
                o.MM(ps[bsc][:, 0:tb], memKT[base:base + 64, hp, mc * 128:(mc + 1) * 128], qn[base:base + 64, hp, :], True, True,
                     r=['memKT', 'qn'], w=[('ps', bsc)])
                pi = rPT.next()
                o.ACT(PT[pi][:, :], ps[bsc][:, 0:tb], AF.Exp, r=[('ps', bsc)], w=[('PT', pi)], scale=0.125)
                pts.append(pi)
            st[h] = pts

        def PVV(h):
            hp, base = h // 2, (h % 2) * 64
            pts = st[h]
            bo = C['rO'].next()
            for mc in range(2):
                o.MM(ps[bo][:, 0:tb], Vaug[:, mc, h, :], PT[pts[mc]][:, :], mc == 0, mc == 1,
                     r=['Vaug', ('PT', pts[mc])], w=[('ps', bo)])
            vr = slice(base, base + 64)
            dr = slice(64 - base, 128 - base)
            o.ACT(rc[dr, :], ps[bo][dr, 0:tb], AF.Ln, r=[('ps', bo)], w=['rc'])
            o.ACT(rc[dr, :], rc[dr, :], AF.Exp, r=['rc'], w=['rc'], scale=-1.0)
            o.TT('dve', cat[vr, catbase + hp, :], ps[bo][vr, 0:tb], rc[dr, :], ALU.mult, r=[('ps', bo), 'rc'], w=['cat'])

        SS(0)
        for h in range(4):
            if h + 1 < 4:
                SS(h + 1)
            PVV(h)

    def out_proj(l, j):
        blk = slice(j * tb, (j + 1) * tb)
        if l < 2:
            chunks = [(96, c) for c in range(8)] + [(128, 8), (128, 9)]
        else:
            chunks = [(128, c) for c in range(8)]
        for oc in range(8):
            b = C['rA'].next()
            for i, (kk, c) in enumerate(chunks):
                o.MM(ps[b][:, 0:tb], wout[0:kk, c, oc * 128:(oc + 1) * 128], cat[0:kk, c, :], i == 0, i == len(chunks) - 1,
                     r=['wout', 'cat'], w=[('ps', b)])
            o.TT('dve', xT[:, oc, blk], xT[:, oc, blk], ps[b][:, 0:tb], ALU.add, r=[('ps', b), ('xT', j, oc)], w=[('xT', j, oc)])

    def mix_a(l, sg_, j):
        barrier()
        blk = slice(j * tb, (j + 1) * tb)
        jg = sg_ * NB + j
        norm_fm(o, C, [xT[:, c, blk] for c in range(8)], [G['gmix'][:, l * 8 + c:l * 8 + c + 1] for c in range(8)],
                [hT[:, c, blk] for c in range(8)], 128, D, tb, XK(j), ('hT', j), ones[:], split=True)
        if jg == 0:
            o.MSET('pool', U[:, :, 0:16], 0.0, w=['U'])
        elif j == 0:
            o.CP('pool', U[:, :, 0:16], halo[:, l, :, :], r=[('halo', l)], w=['U'])
        for c in range(8):
            b = C['rA'].next()
            for k in range(8):
                o.MM(ps[b][0:96, 0:tb], win[:, k, c * 96:(c + 1) * 96], hT[:, k, blk], k == 0, k == 7,
                     r=['win', ('hT', j)], w=[('ps', b)])
            o.CP('act', U[:, c, 16:16 + tb], ps[b][0:96, 0:tb], r=[('ps', b)], w=['U'])
        for hp in range(2):
            b = C['rA'].next()
            for k in range(8):
                o.MM(ps[b][:, 0:tb], win[:, k, 768 + hp * 128:768 + (hp + 1) * 128], hT[:, k, blk], k == 0, k == 7,
                     r=['win', ('hT', j)], w=[('ps', b)])
            o.CP('act', qm[:, hp, :], ps[b][:, 0:tb], r=[('ps', b)], w=['qm'])
        mem_qnorm(l)
        gw = wmisc[0:96, 0:1536].rearrange("p (k n) -> p k n", n=192)
        E = 16 + tb
        for g in range(4):
            w_ = 2 ** (g + 1)
            src = U[:, 2 * g:2 * g + 2, :]
            bufs = [tA, tB_]
            d = 1
            i = 0
            srck = 'U'
            while d < w_:
                dst = bufs[i % 2]
                dk = 'tA' if i % 2 == 0 else 'tB'
                lo = 2 * d
                o.TT('pool', dst[:, :, lo:E], src[:, :, lo:E], src[:, :, lo - d:E - d], ALU.add, r=[srck], w=[dk])
                src, srck = dst, dk
                d *= 2
                i += 1
            o.STT('dve', dif[:, 2 * g:2 * g + 2, :], src[:, :, 16:E], 1.0 / w_, U[:, 2 * g:2 * g + 2, 16:E], ALU.mult, ALU.subtract,
                  r=[srck, 'U'], w=['dif'])
            if jg == 0:
                o.TT('dve', t1a[:, 0:32].rearrange("p (c t) -> p c t", t=16), src[:, :, 16:32], C['rcnt'][:, 2 * g:2 * g + 2, :],
                     ALU.mult, r=[srck, 'rcnt'], w=['t1'])
                o.TT('dve', dif[:, 2 * g:2 * g + 2, 0:16], t1a[:, 0:32].rearrange("p (c t) -> p c t", t=16),
                     U[:, 2 * g:2 * g + 2, 16:32], ALU.subtract, r=['t1', 'U', 'dif'], w=['dif'])
            for ec in range(2):
                b = C['rA'].next()
                for cc in range(2):
                    o.MM(ps[b][0:96, 0:tb], gw[:, 2 * g + cc, ec * 96:(ec + 1) * 96], dif[:, 2 * g + cc, :], cc == 0, cc == 1,
                         r=['wmisc', 'dif'], w=[('ps', b)])
                ch = 2 * g + ec
                o.TSM('dve', cat[0:96, ch, :], ps[b][0:96, 0:tb], G['pscale'][:, l * 8 + ch:l * 8 + ch + 1],
                      r=[('ps', b), 'gains'], w=['cat'])
        if jg == NSEG * NB - 1:
            for c in range(8):
                b = C['rA'].next()
                o.TR(ps[b][0:15, 0:96], U[:, c, E - 15:E], ident[0:96, 0:96], r=['U', 'ident'], w=[('ps', b)])
                o.CP('act', pso[0:15, c * 96:(c + 1) * 96], ps[b][0:15, 0:96], r=[('ps', b)], w=['pso'])
            o.DMA('sp', O["pool_p"][l], pso[0:15, :], r=['pso'], out=True)
        elif j == NB - 1:
            o.CP('pool', halo[:, l, :, :], U[:, :, tb:tb + 16], r=['U'], w=[('halo', l)])
        else:
            o.CP('pool', tA[:, 0, 0:128].rearrange("p (c t) -> p c t", t=16), U[:, :, tb:tb + 16], r=['U'], w=['tA'])
            o.CP('pool', U[:, :, 0:16], tA[:, 0, 0:128].rearrange("p (c t) -> p c t", t=16), r=['tA'], w=['U'])
        mem_attend(l, j, 8)
        out_proj(l, j)

    def kv_stage(sg_, j):
        barrier()
        blk = slice(j * tb, (j + 1) * tb)
        jg = sg_ * NB + j
        gb = slice(jg * tb, (jg + 1) * tb)
        wd = C['wdkv']
        norm_fm(o, C, [xT[:, c, blk] for c in range(8)], [G['gkv'][:, c:c + 1] for c in range(8)],
                [hT[:, c, blk] for c in range(8)], 128, D, tb, XK(j), ('hT', j), ones[:], split=True)
        o.DMA('sp', cosb[:], I["c_cos"][:, gb], w=['cosb'])
        o.DMA('sp', sinb[:], I["c_sin"][:, gb], w=['sinb'])
        for cc in range(2):
            b = C['rA'].next()
            for k in range(8):
                o.MM(ps[b][:, 0:tb], wd[:, k, cc * 128:(cc + 1) * 128], hT[:, k, blk], k == 0, k == 7,
                     r=['wdkv', ('hT', j)], w=[('ps', b)])
            o.CP('act', ckf[:, cc, :], ps[b][:, 0:tb], r=[('ps', b)], w=['ckf'])
        br = C['rG'].next()
        for k in range(8):
            o.MM(ps[br][0:96, 0:tb], wd[:, k, 256:352], hT[:, k, blk], k == 0, k == 7, r=['wdkv', ('hT', j)], w=[('ps', br)])
        o.MSET('pool', krb[0:64, :], 0.0, w=['krb'])
        o.CP('act', krb[64:96, :], ps[br][64:96, 0:tb], r=[('ps', br)], w=['krb'])
        b2 = C['rG'].next()
        o.MM(ps[b2][0:96, 0:tb], perm[0:96, 0:96], krb[0:96, :], True, True, r=['perm', 'krb'], w=[('ps', b2)])
        o.TT('dve', t1[64:96, :], ps[br][64:96, 0:tb], cosb[64:96, :], ALU.mult, r=[('ps', br), 'cosb'], w=['t1'])
        o.TT('dve', t2[64:96, :], ps[b2][64:96, 0:tb], sinb[64:96, :], ALU.mult, r=[('ps', b2), 'sinb'], w=['t2'])
        o.TT('dve', krf[64:96, :], t1[64:96, :], t2[64:96, :], ALU.add, r=['t1', 't2'], w=['krf'])
        for i in range(2):
            o.CP('pool', KT[i][64:96, gb], krf[64:96, :], r=['krf'], w=[('KT', i)])
        norm_fm(o, C, [ckf[:, cc, :] for cc in range(2)], [G['gkvl'][:, cc:cc + 1] for cc in range(2)],
                [cqn[:, cc, :] for cc in range(2)], 128, 256, tb, ['ckf'], 'cqn', ones[:])
        o.CP('pool', cT[:, :, gb], cqn[:, 0:2, :], r=['cqn'], w=['cT'])
        for cc in range(2):
            o.STT('dve', ckf[:, cc, :], ckf[:, cc, :], G['gkvl'][:, cc:cc + 1], C['rstd'][:, 0:tb], ALU.mult, ALU.mult,
                  r=['ckf', 'rstd', 'cqn'], w=['ckf'])
        for tt in range(tb // 128):
            ts_ = slice(tt * 128, (tt + 1) * 128)
            kc = jg * 4 + tt
            b = C['rA'].next()
            o.TR(ps[b][:, 0:128], ckf[:, 0, ts_], ident[:], r=['ckf', 'ident'], w=[('ps', b)])
            o.TR(ps[b][:, 128:256], ckf[:, 1, ts_], ident[:], r=['ckf', 'ident'], w=[('ps', b)])
            o.TR(ps[b][:, 256:288], krf[64:96, ts_], ident[64:96, 64:96], r=['krf', 'ident'], w=[('ps', b)])
            o.CP('act', kvo[:, :], ps[b][:, 0:288], r=[('ps', b)], w=['kvo'])
            o.DMA('sp', O["kv_p"][jg * tb + tt * 128: jg * tb + (tt + 1) * 128, :], kvo[:, :], r=['kvo'], out=True)
            b3, b4 = C['rG'].next(), C['rG'].next()
            for cc in range(2):
                o.MM(ps[b3][:, 0:512], cT[:, cc, jg * tb + tt * 128: jg * tb + (tt + 1) * 128], C['wuk'][:, cc, 0:512], cc == 0, cc == 1,
                     r=['cT', 'wuk'], w=[('ps', b3)])
            for cc in range(2):
                o.MM(ps[b4][:, 0:256], cT[:, cc, jg * tb + tt * 128: jg * tb + (tt + 1) * 128], C['wuk'][:, cc, 512:768], cc == 0, cc == 1,
                     r=['cT', 'wuk'], w=[('ps', b4)])
            o.ACT(sqn[:, 0:512], ps[b3][:, 0:512], AF.Square, r=[('ps', b3)], w=['sqn'])
            o.ACT(sqn[:, 512:768], ps[b4][:, 0:256], AF.Square, r=[('ps', b4)], w=['sqn'])
            o.RED(ssn[:, 0:12], sqn[:, :].rearrange("p (h d) -> p h d", d=64), r=['sqn'], w=['ssn'])
            o.ACT(sqr[:, 0:32], kvo[:, 256:288], AF.Square, r=['kvo'], w=['sqr'])
            o.RED(ssn[:, 12:13], sqr[:, 0:32].rearrange("p (h d) -> p h d", d=32), r=['sqr'], w=['ssn2'])
            o.TSA('dve', ssn[:, 0:12], ssn[:, 0:12], ssn[:, 12:13], r=['ssn', 'ssn2'], w=['ssn'])
            o.ACT(ssn[:, 0:12], ssn[:, 0:12], AF.Ln, r=['ssn'], w=['ssn'], scale=1.0, bias=96.0 * EPS)
            o.ACT(kinv[:, kc, :], ssn[:, 0:12], AF.Exp, r=['ssn'], w=['kinv'], scale=-0.5)

    def mix_b(l, sg_, j):
        barrier()
        jl = l - 2
        blk = slice(j * tb, (j + 1) * tb)
        jg = sg_ * NB + j
        gb = slice(jg * tb, (jg + 1) * tb)
        norm_fm(o, C, [xT[:, c, blk] for c in range(8)], [G['gmix'][:, l * 8 + c:l * 8 + c + 1] for c in range(8)],
                [hT[:, c, blk] for c in range(8)], 128, D, tb, XK(j), ('hT', j), ones[:], split=True)
        o.DMA('sp', cosb[:], I["c_cos"][:, gb], w=['cosb'])
        o.DMA('sp', sinb[:], I["c_sin"][:, gb], w=['sinb'])
        for c in range(3):
            b = C['rA'].next()
            for k in range(8):
                o.MM(ps[b][:, 0:tb], win[:, k, c * 128:(c + 1) * 128], hT[:, k, blk], k == 0, k == 7,
                     r=['win', ('hT', j)], w=[('ps', b)])
            o.CP('act', cq[:, c, :], ps[b][:, 0:tb], r=[('ps', b)], w=['cq'])
        for hp in range(2):
            b = C['rA'].next()
            for k in range(8):
                o.MM(ps[b][:, 0:tb], win[:, k, 384 + hp * 128:384 + (hp + 1) * 128], hT[:, k, blk], k == 0, k == 7,
                     r=['win', ('hT', j)], w=[('ps', b)])
            o.CP('act', qm[:, hp, :], ps[b][:, 0:tb], r=[('ps', b)], w=['qm'])
        norm_fm(o, C, [cq[:, c, :] for c in range(3)], [G['gql'][:, jl * 3 + c:jl * 3 + c + 1] for c in range(3)],
                [cqn[:, c, :] for c in range(3)], 128, 384, tb, ['cq'], 'cqn', ones[:])
        uq = wmisc[:, 0:3456].rearrange("p (k n) -> p k n", n=1152)
        nk = (jg + 1) * 4
        BQ, BN = 6, 7

        def prepA(h):
            ki = h % 2
            for k in range(3):
                o.MM(ps[BQ][0:96, 0:tb], uq[:, k, h * 96:(h + 1) * 96], cqn[:, k, :], k == 0, k == 2,
                     r=['wmisc', 'cqn'], w=[('ps', BQ)])
            o.ACT(qg[:, :], ps[BQ][0:96, 0:tb], AF.Square, r=[('ps', BQ)], w=['qg'])
            for kb in range(jg + 1):
                bk = C['rA'].next()
                for cc in range(2):
                    o.MM(ps[bk][0:64, 0:tb], C['wuk'][:, cc, h * 64:(h + 1) * 64], cT[:, cc, kb * tb:(kb + 1) * tb], cc == 0, cc == 1,
                         r=['wuk', 'cT'], w=[('ps', bk)])
                o.CP('dve', KT[ki][0:64, kb * tb:(kb + 1) * tb], ps[bk][0:64, 0:tb], r=[('ps', bk)], w=[('KT', ki)])
            off = 0 if h % 2 == 0 else 64
            for kg in range(nk // 8 + (1 if nk % 8 else 0)):
                bk = C['rA'].next()
                n8 = min(8, nk - kg * 8)
                for q8 in range(n8):
                    kc = kg * 8 + q8
                    for cc in range(2):
                        o.MM(ps[bk][:, q8 * 64:(q8 + 1) * 64], cT[:, cc, kc * 128:(kc + 1) * 128], C['wuv'][:, cc, h * 64:(h + 1) * 64],
                             cc == 0, cc == 1, r=['wuv', 'cT'], w=[('ps', bk)])
                o.CP('dve', Vh[ki][:, kg * 8:kg * 8 + n8, off:off + 64],
                     ps[bk][:, 0:n8 * 64].rearrange("p (k d) -> p k d", d=64), r=[('ps', bk)], w=[('Vh', ki)])

        def prepN1(h):
            o.MM(ps[BN][0:96, 0:tb], ones[0:96, 0:96], qg[:, :], True, True, r=['qg', 'ones'], w=[('ps', BN)])
            o.ACT(C['tmp'][0:96, 0:tb], ps[BN][0:96, 0:tb], AF.Ln, r=[('ps', BN)], w=['tmpn'], scale=1.0 / 96, bias=EPS)
            o.ACT(C['rstd'][0:96, 0:tb], C['tmp'][0:96, 0:tb], AF.Exp, r=['tmpn'], w=['rstd'], scale=-0.5)
            o.STT('dve', qg[:, :], ps[BQ][0:96, 0:tb], G['gq'][:, jl:jl + 1], C['rstd'][0:96, 0:tb], ALU.mult, ALU.mult,
                  r=[('ps', BQ), 'rstd', 'gains', 'qg'], w=['qg'])

        def prepN2(h):
            qi = h % 2
            o.MM(ps[BN][0:96, 0:tb], perm[:, :], qg[:, :], True, True, r=['perm', 'qg'], w=[('ps', BN)])
            o.STT('dve', t1[:, :], qg[:, :], G['gk'][:, 0:1], cosb[:, :], ALU.mult, ALU.mult, r=['qg', 'cosb', 'gains'], w=['t1'])
            o.STT('dve', t2[:, :], ps[BN][0:96, 0:tb], G['gk'][:, 0:1], sinb[:, :], ALU.mult, ALU.mult,
                  r=[('ps', BN), 'sinb', 'gains'], w=['t2'])
            o.TT('pool', QT[qi][:, :], t1[:, :], t2[:, :], ALU.add, r=['t1', 't2'], w=[('QT', qi)])

        def attend(h):
            qi = ki = h % 2
            off = 0 if h % 2 == 0 else 64
            bo = C['rO'].next()
            pend = {}

            def S(kc):
                r_ = kc - 4 * jg
                q0 = 128 * max(r_, 0)
                bs = C['rS'].next()
                o.MM(ps[bs][:, q0:tb], KT[ki][0:96, kc * 128:(kc + 1) * 128], QT[qi][:, q0:tb], True, True,
                     r=[('KT', ki), ('QT', qi)], w=[('ps', bs)])
                pi = rPT.next()
                o.ACT(PT[pi][:, q0:tb], ps[bs][:, q0:tb], AF.Exp, r=[('ps', bs), 'kinv'], w=[('PT', pi)],
                      scale=kinv[:, kc, h:h + 1])
                if r_ >= 0:
                    o.TT('pool', PT[pi][:, q0:q0 + 128], PT[pi][:, q0:q0 + 128], tri[:, :], ALU.mult, r=[('PT', pi), 'tri'], w=[('PT', pi)])
                pend[kc] = (pi, q0)

            def PV(kc):
                pi, q0 = pend.pop(kc)
                o.MM(ps[bo][:, q0:tb], Vh[ki][:, kc, :], PT[pi][:, q0:tb], kc == 0, kc == nk - 1,
                     r=[('Vh', ki), ('PT', pi)], w=[('ps', bo)])

            hook2 = (nk - 3) if nk >= 8 else (nk - 1)
            S(0)
            if h + 1 < 12:
                prepA(h + 1)
            for kc in range(nk):
                if kc + 1 < nk:
                    S(kc + 1)
                PV(kc)
                if h + 1 < 12:
                    if kc == 0:
                        prepN1(h + 1)
                    if kc == hook2:
                        prepN2(h + 1)
            vr = slice(off, off + 64)
            dr = slice(64 - off, 128 - off)
            o.ACT(rc[dr, :], ps[bo][dr, 0:tb], AF.Ln, r=[('ps', bo)], w=['rc'])
            o.ACT(rc[dr, :], rc[dr, :], AF.Exp, r=['rc'], w=['rc'], scale=-1.0)
            o.TT('dve', cat[vr, h // 2, :], ps[bo][vr, 0:tb], rc[dr, :], ALU.mult, r=[('ps', bo), 'rc'], w=['cat'])

        prepA(0)
        prepN1(0)
        prepN2(0)
        mem_qnorm(l)
        for h in range(12):
            attend(h)
        mem_attend(l, j, 6)
        out_proj(l, j)

    def ffn(l, pre, hook=None, first=False):
        barrier()
        for j in range(NB):
            blk = slice(j * tb, (j + 1) * tb)
            norm_fm(o, C, [xT[:, c, blk] for c in range(8)], [G['gffn'][:, l * 8 + c:l * 8 + c + 1] for c in range(8)],
                    [hT[:, c, blk] for c in range(8)], 128, D, tb, XK(j), ('hT', j), ones[:], split=True)
        rOut = Rot([0, 1, 6, 7])
        rStg = Rot([0, 1])
        pendG, pendO = list(pre[0]), list(pre[1])
        nx = {'g': len(pendG), 'o': len(pendO)}
        units = [(fg, j) for fg in range(NFG) for j in range(NB)]
        grpG, grpO = {}, {}

        def GU(u):
            fg, j = units[u]
            if fg not in grpG:
                grpG[fg] = pendG.pop(0)
            gv, uv, rk = grpG[fg]
            blk = slice(j * tb, (j + 1) * tb)
            ai = u % 2
            for cc in range(2):
                bg, bu = 2 + cc, 4 + cc
                for k in range(8):
                    o.MM(ps[bg][:, 0:tb], gv[:, k, cc * 128:(cc + 1) * 128], hT[:, k, blk], k == 0, k == 7,
                         r=[rk, ('hT', j)], w=[('ps', bg)])
                for k in range(8):
                    o.MM(ps[bu][:, 0:tb], uv[:, k, cc * 128:(cc + 1) * 128], hT[:, k, blk], k == 0, k == 7,
                         r=[rk, ('hT', j)], w=[('ps', bu)])
                o.ACT(sg[cc][:, :], ps[bg][:, 0:tb], AF.Silu, r=[('ps', bg)], w=[('sg', cc)])
                o.TT('dve', aT[ai][:, cc, :], sg[cc][:, :], ps[bu][:, 0:tb], ALU.mult, r=[('sg', cc), ('ps', bu)], w=[('aT', ai)])
            if j == NB - 1 and nx['g'] < NFG:
                pendG.append(load_G(l, nx['g'], first))
                nx['g'] += 1

        def OUT(u):
            fg, j = units[u]
            if fg not in grpO:
                grpO[fg] = pendO.pop(0)
            ov, rk = grpO[fg]
            blk = slice(j * tb, (j + 1) * tb)
            ai = u % 2
            for oc in range(8):
                b = rOut.next()
                for cc in range(2):
                    o.MM(ps[b][:, 0:tb], ov[:, cc, oc * 128:(oc + 1) * 128], aT[ai][:, cc, :], cc == 0, cc == 1,
                         r=[rk, ('aT', ai)], w=[('ps', b)])
                if oc % 2 == 0:
                    o.TT('dve', xT[:, oc, blk], xT[:, oc, blk], ps[b][:, 0:tb], ALU.add, r=[('ps', b), ('xT', j, oc)], w=[('xT', j, oc)])
                else:
                    si = rStg.next()
                    o.CP('act', stg[si][:, :], ps[b][:, 0:tb], r=[('ps', b)], w=[('stg', si)])
                    o.TT('pool', xT[:, oc, blk], xT[:, oc, blk], stg[si][:, :], ALU.add, r=[('stg', si), ('xT', j, oc)], w=[('xT', j, oc)])
            if j == NB - 1:
                if nx['o'] < NFG:
                    pendO.append(load_O(l, nx['o'], first))
                    nx['o'] += 1
                if fg == 1 and hook is not None:
                    hook()

        GU(0)
        for u in range(len(units)):
            if u + 1 < len(units):
                GU(u + 1)
            OUT(u)

    for sg_ in range(NSEG):
        barrier()
        for tt in range(SEG // 128):
            xb, xk = xinb[tt % 2], ('xin' if tt % 2 == 0 else 'xin1')
            o.DMA('sp', xb[:], I["xp"][sg_ * SEG + tt * 128: sg_ * SEG + (tt + 1) * 128, :], w=[xk])
            for c in range(8):
                b = C['rA'].next()
                o.TR(ps[b][:, 0:128], xb[:, c * 128:(c + 1) * 128], ident[:], r=[xk, 'ident'], w=[('ps', b)])
                o.CP('dve' if c % 2 else 'act', xT[:, c, tt * 128:(tt + 1) * 128], ps[b][:, 0:128], r=[('ps', b)], w=[('xT', tt // 4, c)])
        for l in range(nlayers):
            mem_kv(l, write_out=(sg_ == 0))
            if l == 0 and sg_ == 0:
                load_mix_weights(0, True)
            pre = []
            for j in range(NB):
                if l < 2:
                    mix_a(l, sg_, j)
                else:
                    mix_b(l, sg_, j)
                if j == 0:
                    pre = ([load_G(l, fg, sg_ == 0) for fg in range(NRG)], [load_O(l, fg, sg_ == 0) for fg in range(NRO)])
            ffn(l, pre, hook=(lambda l=l, sg_=sg_: load_mix_weights((l + 1) % nlayers, sg_ == 0 and l + 1 < nlayers))
                if not (l == nlayers - 1 and sg_ == NSEG - 1) else None, first=(sg_ == 0))
            if l == 1:
                for j in range(NB):
                    kv_stage(sg_, j)
        barrier()
        for tt in range(SEG // 128):
            xb, xk = xinb[tt % 2], ('xin' if tt % 2 == 0 else 'xin1')
            for c in range(8):
                b = C['rA'].next()
                o.TR(ps[b][:, 0:128], xT[:, c, tt * 128:(tt + 1) * 128], ident[:], r=[('xT', tt // 4, c), 'ident'], w=[('ps', b)])
                o.CP('dve' if c % 2 else 'act', xb[:, c * 128:(c + 1) * 128], ps[b][:, 0:128], r=[('ps', b)], w=[xk])
            o.DMA('sp', O["y_p"][sg_ * SEG + tt * 128: sg_ * SEG + (tt + 1) * 128, :], xb[:], r=[xk], out=True)
    P.finish()
    P.emit('p')
    print("prompt program: ops", len(P.ops), "waits", P.nwaits)


def sample_program(nc, es, I, O, nlayers, NPG=128, SC=None):
    P = Prog(nc, es, 'S_')
    o = Ops(P)
    C = alloc_common(P, o, I)
    ps = C['ps']
    G = C['G']
    ident, ones, blk2, perm = C['ident'], C['ones'], C['blk2'], C['perm']
    tb = 32
    NS = 4
    NPT = NS * NPG

    xT = P.sb("s_xT", [128, 8, tb], F32)
    hT = P.sb("s_hT", [128, 8, tb], BF16)
    win = P.sb("s_win", [128, 8, 1024], BF16)
    wout = P.sb("s_wout", [128, 10, 1024], BF16)
    wmisc = P.sb("s_wmisc", [128, 3456], BF16)
    NRING = 3
    ring = [P.sb("s_ring%d" % i, [128, 6144], BF16) for i in range(NRING)]
    cat = P.sb("s_cat", [128, 10, tb], BF16)
    qm = P.sb("s_qm", [128, 2, tb], F32)
    qn = P.sb("s_qn", [128, 2, tb], BF16)
    xin = P.sb("s_xin", [32, 1024], F32)
    memKT = P.sb("s_memKT", [128, NS, 2, 256], BF16)
    Vaug = P.sb("s_Vaug", [128, NS, 2, 4, 128], BF16)
    mst = P.sb("s_mst", [128, 2, 256], F32)
    Us = P.sb("s_U", [96, 32, 24], F32)
    tA = P.sb("s_tA", [96, 8, 24], F32)
    tB_ = P.sb("s_tB", [96, 8, 24], F32)
    dif = P.sb("s_dif", [96, 8, tb], BF16)
    spt = P.sb("s_spt", [16, 768], F32)
    pso = P.sb("s_pso", [16, 768], F32)
    PTm = P.sb("s_PTm", [128, 256], BF16)
    rc = P.sb("s_rc", [128, 128], F32)
    sg = [P.sb("s_sg%d" % i, [128, tb], F32) for i in range(2)]
    aT = [P.sb("s_aT%d" % i, [128, 2, tb], BF16) for i in range(2)]
    cosb = P.sb("s_cos", [96, tb], F32)
    sinb = P.sb("s_sin", [96, tb], F32)
    ckf = P.sb("s_ckf", [128, 2, tb], F32)
    cTs = P.sb("s_cT", [128, 2, tb], BF16)
    krf = P.sb("s_krf", [96, tb], F32)
    krb = P.sb("s_krb", [96, tb], BF16)
    t1 = P.sb("s_t1", [96, tb], F32)
    t2 = P.sb("s_t2", [96, tb], F32)
    kvo = P.sb("s_kvo", [32, 288], F32)
    KVbn = P.sb("s_KVbn", [32, 289], BF16)
    sqn = P.sb("s_sqn", [128, 768], BF16)
    sqr = P.sb("s_sqr", [128, 32], BF16)
    ssn = P.sb("s_ssn", [128, 4, 16], F32)
    kinvn = P.sb("s_kinvn", [32, 12], F32)
    cq = P.sb("s_cq", [128, 3, tb], F32)
    cqn = P.sb("s_cqn", [128, 3, tb], BF16)
    qg = P.sb("s_qg", [96, tb], BF16)
    QTs = P.sb("s_QT", [96, 12, tb], BF16)
    QL = P.sb("s_QL", [128, 2, 12, tb], BF16)
    wukf = P.sb("s_wukf", [128, 2, 768], F32)
    wukT = P.sb("s_wukT", [64, 12, 256], BF16)
    identb = P.sb("s_identb", [128, 128], BF16)
    smask = P.sb("s_smask", [32, 4, 96], BF16)
    ptb = P.sb("s_ptb", [128, NPT], I32)
    idxall = P.sb("s_idx", [128, NPT], I32)
    iot = P.sb("s_iota", [128, 1], I32)
    kinv = P.sb("s_kinv", [128, NPT, 12], F32)
    NKB = 16
    sqn2 = [P.sb("s_sqn2_%d" % i, [128, 800], BF16) for i in range(2)]
    KVb = [P.sb("s_KVb%d" % i, [128, 289], BF16) for i in range(NKB)]
    KVTc = [P.sb("s_KVTc%d" % i, [128, 4, 2, 128], BF16) for i in range(2)]
    KVTr = [P.sb("s_KVTr%d" % i, [96, 4, 128], BF16) for i in range(2)]
    tms = P.sb("s_tms", [128, 4, 96], F32)
    PTs = [P.sb("s_PTs%d" % i, [128, 4, 96], BF16) for i in range(2)]
    PTn = P.sb("s_PTn", [32, 96], BF16)
    olat = P.sb("s_olat", [96, 256], F32)
    rcs = P.sb("s_rcs", [96, 1], F32)
    olT = P.sb("s_olT", [128, 2, 96], BF16)

    o.MSET('pool', Vaug[:], 1.0, w=['Vaug'])
    for i in range(NKB):
        o.MSET('pool', KVb[i][:, 288:289], 1.0, w=[('KVb', i)])
    o.MSET('pool', KVbn[:, 288:289], 1.0, w=['KVbn'])
    o.CP('pool', identb[:], ident[:], r=['ident'], w=['identb'])
    o.DMA('pool', smask[:], I["c_smask"][:, :, :], w=['smask'])
    o.DMA('sp', wukf[:], I["w_uk"].rearrange("(k p) n -> p k n", p=128), w=['wukf'])
    for h in range(12):
        b = C['rA'].next()
        for cc in range(2):
            o.TR(ps[b][0:64, cc * 128:(cc + 1) * 128], wukf[:, cc, h * 64:(h + 1) * 64], ident[:, :], r=['wukf', 'ident'], w=[('ps', b)])
        o.CP('dve', wukT[:, h, :], ps[b][0:64, 0:256], r=[('ps', b)], w=['wukT'])
    o.DMA('sp', ptb[:], I["pt"].rearrange("s p -> (s p)").partition_broadcast(128), w=['ptb'])
    P.op('pool', lambda e: e.iota(iot[:], pattern=[[0, 1]], base=0, channel_multiplier=1), w=['iota'])
    ptf = P.sb("s_ptf", [128, NPT], F32)
    iotf = P.sb("s_iotf", [128, 1], F32)
    o.CP('dve', ptf[:], ptb[:], r=['ptb'], w=['ptf'])
    o.CP('dve', iotf[:], iot[:], r=['iota'], w=['iotf'])
    o.TS('dve', ptf[:], ptf[:], 128.0, iotf[:, 0:1], ALU.mult, ALU.add, r=['ptf', 'iotf'], w=['ptf'])
    o.CP('dve', idxall[:], ptf[:], r=['ptf'], w=['idx'])

    WS = SC if SC is not None else I
    WQ = 'sp' if SC is not None else 'pool'
    wv_ffn_in = [WS["w_ffn_in"][l].rearrange("(k p) n -> p k n", p=128) for l in range(4)]
    wv_ffn_out = [WS["w_ffn_out"][l].rearrange("(k p) n -> p k n", p=128) for l in range(4)]
    NFG = DFF // 256
    XK = [('xT', c) for c in range(8)]
    ring_state = {'n': 0}

    def load_ffn_group(l, fg):
        s = ring_state['n'] % NRING
        ring_state['n'] += 1
        rg = ring[s]
        k = ('ring', s)
        gv = rg[:, 0:2048].rearrange("p (k n) -> p k n", n=256)
        uv = rg[:, 2048:4096].rearrange("p (k n) -> p k n", n=256)
        ov = rg[:, 4096:6144].rearrange("p (k n) -> p k n", n=1024)
        o.DMA(WQ, gv, wv_ffn_in[l][:, :, fg * 256:(fg + 1) * 256], w=[k])
        o.DMA(WQ, uv, wv_ffn_in[l][:, :, DFF + fg * 256:DFF + (fg + 1) * 256], w=[k])
        o.DMA(WQ, ov, wv_ffn_out[l][:, fg * 2:fg * 2 + 2, :], w=[k])
        if False:
            si = SC["w_ffn_in"][l].rearrange("(k p) n -> p k n", p=128)
            so = SC["w_ffn_out"][l].rearrange("(k p) n -> p k n", p=128)
            o.DMA('sp', si[:, :, fg * 256:(fg + 1) * 256], gv, r=[k], out=True)
            o.DMA('sp', si[:, :, DFF + fg * 256:DFF + (fg + 1) * 256], uv, r=[k], out=True)
            o.DMA('sp', so[:, fg * 2:fg * 2 + 2, :], ov, r=[k], out=True)
        return gv, uv, ov, k

    def load_mix_weights(l):
        if l < 2:
            o.DMA(WQ, win[:], WS["w_in_a"][l].rearrange("(k p) n -> p k n", p=128), w=['win'])
            o.DMA(WQ, wout[0:96, 0:8, :], WS["w_out"][l][0:768, :].rearrange("(k p) n -> p k n", p=96), w=['wout'])
            o.DMA(WQ, wout[:, 8:10, :], WS["w_out"][l][768:1024, :].rearrange("(k p) n -> p k n", p=128), w=['wout'])
            gv = wmisc[0:96, 0:1536].rearrange("p (k n) -> p k n", n=192)
            o.DMA(WQ, gv, WS["w_pool_grp"][l].rearrange("g (c p) e -> p (g c) e", p=96), w=['wmisc'])
            if False:
                o.DMA('sp', SC["w_in_a"][l].rearrange("(k p) n -> p k n", p=128), win[:], r=['win'], out=True)
                o.DMA('sp', SC["w_out"][l][0:768, :].rearrange("(k p) n -> p k n", p=96), wout[0:96, 0:8, :], r=['wout'], out=True)
                o.DMA('sp', SC["w_out"][l][768:1024, :].rearrange("(k p) n -> p k n", p=128), wout[:, 8:10, :], r=['wout'], out=True)
                o.DMA('sp', SC["w_pool_grp"][l].rearrange("g (c p) e -> p (g c) e", p=96), gv, r=['wmisc'], out=True)
        else:
            j = l - 2
            o.DMA(WQ, win[:, :, 0:640], WS["w_in_b"][j].rearrange("(k p) n -> p k n", p=128), w=['win'])
            o.DMA(WQ, wout[:, 0:8, :], WS["w_out"][l].rearrange("(k p) n -> p k n", p=128), w=['wout'])
            uq = wmisc[:, 0:3456].rearrange("p (k n) -> p k n", n=1152)
            o.DMA(WQ, uq, WS["w_uq"][j].rearrange("(k p) n -> p k n", p=128), w=['wmisc'])
            if False:
                o.DMA('sp', SC["w_in_b"][j].rearrange("(k p) n -> p k n", p=128), win[:, :, 0:640], r=['win'], out=True)
                o.DMA('sp', SC["w_out"][l].rearrange("(k p) n -> p k n", p=128), wout[:, 0:8, :], r=['wout'], out=True)
                o.DMA('sp', SC["w_uq"][j].rearrange("(k p) n -> p k n", p=128), uq, r=['wmisc'], out=True)

    def xnorm(gkey, l):
        norm_fm(o, C, [xT[:, c, :] for c in range(8)], [G[gkey][:, l * 8 + c:l * 8 + c + 1] for c in range(8)],
                [hT[:, c, :] for c in range(8)], 128, D, tb, XK, 'hT', ones[:])

    def mem_load(l, seqs=None):
        for s_ in (range(NS) if seqs is None else seqs):
            o.DMA('pool', mst[:], I["cmk"][l, s_].rearrange("(m p) f -> p m f", p=128), w=['mst'])
            for mc in range(2):
                b = C['rA'].next()
                for hp in range(2):
                    o.TR(ps[b][:, hp * 128:(hp + 1) * 128], mst[:, mc, hp * 128:(hp + 1) * 128], ident[:], r=['mst', 'ident'], w=[('ps', b)])
                o.CP('dve', memKT[:, s_, :, mc * 128:(mc + 1) * 128], ps[b][:, 0:256].rearrange("p (h m) -> p h m", m=128),
                     r=[('ps', b)], w=['memKT'])
            o.DMA('pool', mst[:], I["cmv"][l, s_].rearrange("(m p) f -> p m f", p=128), r=['mst'], w=['mst'])
            for mc in range(2):
                for h in range(4):
                    off = 0 if h % 2 == 0 else 64
                    o.CP('pool', Vaug[:, s_, mc, h, off:off + 64], mst[:, mc, h * 64:(h + 1) * 64], r=['mst'], w=['Vaug'])

    def mem_attend(l, catbase):
        for hp in range(2):
            norm_fm(o, C, [qm[:, hp, :]], [G['gmq'][:, l:l + 1]], [qn[:, hp, :]], 128, 64, tb, ['qm'], 'qn', blk2[:])
        bsb = [2, 3]
        for mc in range(2):
            for s_ in range(NS):
                for h in range(4):
                    hp, par = h // 2, h % 2
                    base = par * 64
                    col = ((mc * NS + s_) * 2 + hp) * 8
                    o.MM(ps[bsb[par]][:, col:col + 8], memKT[base:base + 64, s_, hp, mc * 128:(mc + 1) * 128],
                         qn[base:base + 64, hp, s_ * 8:(s_ + 1) * 8], True, True, r=['memKT', 'qn'], w=[('ps', bsb[par])])
        for par in range(2):
            o.ACT(PTm[:, par * 128:(par + 1) * 128], ps[bsb[par]][:, 0:128], AF.Exp, r=[('ps', bsb[par])], w=['PTm'], scale=0.125)
        bo = C['rO'].next()
        for s_ in range(NS):
            for h in range(4):
                hp, par = h // 2, h % 2
                colo = (s_ * 4 + h) * 8
                for mc in range(2):
                    col = par * 128 + ((mc * NS + s_) * 2 + hp) * 8
                    o.MM(ps[bo][:, colo:colo + 8], Vaug[:, s_, mc, h, :], PTm[:, col:col + 8], mc == 0, mc == 1,
                         r=['Vaug', 'PTm'], w=[('ps', bo)])
        o.RCP(rc[:, 0:128], ps[bo][:, 0:128], r=[('ps', bo)], w=['rc'])
        for h in range(4):
            hp, base = h // 2, (h % 2) * 64
            vr = slice(base, base + 64)
            dr = slice(64 - base, 128 - base)
            pv = ps[bo][vr, 0:128].rearrange("p (s h i) -> p s h i", h=4, i=8)[:, :, h, :]
            rv = rc[dr, 0:128].rearrange("p (s h i) -> p s h i", h=4, i=8)[:, :, h, :]
            o.TT('dve', cat[vr, catbase + hp, :].rearrange("p (s i) -> p s i", i=8), pv, rv, ALU.mult, r=[('ps', bo), 'rc'], w=['cat'])

    def out_proj(l):
        if l < 2:
            chunks = [(96, c) for c in range(8)] + [(128, 8), (128, 9)]
        else:
            chunks = [(128, c) for c in range(8)]
        for oc in range(8):
            b = C['rA'].next()
            for i, (kk, c) in enumerate(chunks):
                o.MM(ps[b][:, 0:tb], wout[0:kk, c, oc * 128:(oc + 1) * 128], cat[0:kk, c, :], i == 0, i == len(chunks) - 1,
                     r=['wout', 'cat'], w=[('ps', b)])
            o.TT('dve', xT[:, oc, :], xT[:, oc, :], ps[b][:, 0:tb], ALU.add, r=[('ps', b), ('xT', oc)], w=[('xT', oc)])

    def halo_load(l, seqs=None):
        for s_ in (range(NS) if seqs is None else seqs):
            o.DMA('pool', spt[0:15, :], I["spool"][l, s_], w=['spt'])
            b = C['rA'].next()
            for c in range(8):
                o.TR(ps[b][0:96, c * 16:c * 16 + 15], spt[0:15, c * 96:(c + 1) * 96], ident[0:15, 0:15], r=['spt', 'ident'], w=[('ps', b)])
            o.CP('dve', Us[:, :, 1:16].rearrange("p (c s) t -> p c s t", s=NS)[:, :, s_, :],
                 ps[b][0:96, 0:128].rearrange("p (c t) -> p c t", t=16)[:, :, 0:15], r=[('ps', b)], w=['U'])

    def mix_a(l):
        xnorm('gmix', l)
        for c in range(8):
            b = C['rA'].next()
            for k in range(8):
                o.MM(ps[b][0:96, 0:tb], win[:, k, c * 96:(c + 1) * 96], hT[:, k, :], k == 0, k == 7, r=['win', 'hT'], w=[('ps', b)])
            o.CP('act', Us[:, c * NS:(c + 1) * NS, 16:24], ps[b][0:96, 0:tb].rearrange("p (s i) -> p s i", i=8), r=[('ps', b)], w=['U'])
        for hp in range(2):
            b = C['rA'].next()
            for k in range(8):
                o.MM(ps[b][:, 0:tb], win[:, k, 768 + hp * 128:768 + (hp + 1) * 128], hT[:, k, :], k == 0, k == 7,
                     r=['win', 'hT'], w=[('ps', b)])
            o.CP('act', qm[:, hp, :], ps[b][:, 0:tb], r=[('ps', b)], w=['qm'])
        gw = wmisc[0:96, 0:1536].rearrange("p (k n) -> p k n", n=192)
        E = 24
        for g in range(4):
            w_ = 2 ** (g + 1)
            src = Us[:, 8 * g:8 * g + 8, :]
            bufs = [tA, tB_]
            d, i, srck = 1, 0, 'U'
            while d < w_:
                dst = bufs[i % 2]
                dk = 'tA' if i % 2 == 0 else 'tB'
                lo = 2 * d
                o.TT('pool', dst[:, :, lo:E], src[:, :, lo:E], src[:, :, lo - d:E - d], ALU.add, r=[srck], w=[dk])
                src, srck = dst, dk
                d *= 2
                i += 1
            o.STT('dve', dif[:, 2 * g:2 * g + 2, :].rearrange("p c (s i) -> p (c s) i", i=8), src[:, :, 16:E], 1.0 / w_,
                  Us[:, 8 * g:8 * g + 8, 16:E], ALU.mult, ALU.subtract, r=[srck, 'U'], w=['dif'])
            for ec in range(2):
                b = C['rA'].next()
                for cc in range(2):
                    o.MM(ps[b][0:96, 0:tb], gw[:, 2 * g + cc, ec * 96:(ec + 1) * 96], dif[:, 2 * g + cc, :], cc == 0, cc == 1,
                         r=['wmisc', 'dif'], w=[('ps', b)])
                ch = 2 * g + ec
                o.TSM('dve', cat[0:96, ch, :], ps[b][0:96, 0:tb], G['pscale'][:, l * 8 + ch:l * 8 + ch + 1], r=[('ps', b), 'gains'], w=['cat'])
        for s_ in range(NS):
            b1, b2 = C['rA'].next(), C['rA'].next()
            for c in range(8):
                bb = b1 if c < 4 else b2
                o.TR(ps[bb][0:15, (c % 4) * 96:(c % 4 + 1) * 96], Us[:, c * NS + s_, 9:24], ident[0:96, 0:96], r=['U', 'ident'], w=[('ps', bb)])
            o.CP('act', pso[0:15, 0:384], ps[b1][0:15, 0:384], r=[('ps', b1)], w=['pso'])
            o.CP('act', pso[0:15, 384:768], ps[b2][0:15, 0:384], r=[('ps', b2)], w=['pso'])
            o.DMA('sp', O["pool_s"][l, s_], pso[0:15, :], r=['pso'], out=True)
        mem_attend(l, 8)
        out_proj(l)

    def ffn(l, pre, hook=None):
        xnorm('gffn', l)
        rOut = Rot([0, 1, 6, 7])
        pend = list(pre)
        nxt = len(pre)
        for fg in range(NFG):
            gv, uv, ov, rk = pend.pop(0)
            ai = fg % 2
            for cc in range(2):
                bg, bu = 2 + cc, 4 + cc
                for k in range(8):
                    o.MM(ps[bg][:, 0:tb], gv[:, k, cc * 128:(cc + 1) * 128], hT[:, k, :], k == 0, k == 7, r=[rk, 'hT'], w=[('ps', bg)])
                for k in range(8):
                    o.MM(ps[bu][:, 0:tb], uv[:, k, cc * 128:(cc + 1) * 128], hT[:, k, :], k == 0, k == 7, r=[rk, 'hT'], w=[('ps', bu)])
                o.ACT(sg[cc][:, :], ps[bg][:, 0:tb], AF.Silu, r=[('ps', bg)], w=[('sg', cc)])
                o.TT('dve', aT[ai][:, cc, :], sg[cc][:, :], ps[bu][:, 0:tb], ALU.mult, r=[('sg', cc), ('ps', bu)], w=[('aT', ai)])
            for oc in range(8):
                b = rOut.next()
                for cc in range(2):
                    o.MM(ps[b][:, 0:tb], ov[:, cc, oc * 128:(oc + 1) * 128], aT[ai][:, cc, :], cc == 0, cc == 1,
                         r=[rk, ('aT', ai)], w=[('ps', b)])
                o.TT('dve', xT[:, oc, :], xT[:, oc, :], ps[b][:, 0:tb], ALU.add, r=[('ps', b), ('xT', oc)], w=[('xT', oc)])
            if nxt < NFG:
                pend.append(load_ffn_group(l, nxt))
                nxt += 1
            if hook is not None:
                hook(fg)

    def kv_stage():
        wd = C['wdkv']
        norm_fm(o, C, [xT[:, c, :] for c in range(8)], [G['gkv'][:, c:c + 1] for c in range(8)],
                [hT[:, c, :] for c in range(8)], 128, D, tb, XK, 'hT', ones[:])
        o.DMA('sp', cosb[:], I["c_cos"][:, 2048:2080], w=['cosb'])
        o.DMA('sp', sinb[:], I["c_sin"][:, 2048:2080], w=['sinb'])
        for cc in range(2):
            b = C['rA'].next()
            for k in range(8):
                o.MM(ps[b][:, 0:tb], wd[:, k, cc * 128:(cc + 1) * 128], hT[:, k, :], k == 0, k == 7, r=['wdkv', 'hT'], w=[('ps', b)])
            o.CP('act', ckf[:, cc, :], ps[b][:, 0:tb], r=[('ps', b)], w=['ckf'])
        br = C['rG'].next()
        for k in range(8):
            o.MM(ps[br][0:96, 0:tb], wd[:, k, 256:352], hT[:, k, :], k == 0, k == 7, r=['wdkv', 'hT'], w=[('ps', br)])
        o.CP('act', krb[64:96, :], ps[br][64:96, 0:tb], r=[('ps', br)], w=['krb'])
        b2 = C['rG'].next()
        o.MM(ps[b2][0:96, 0:tb], perm[64:96, 0:96], krb[64:96, :], True, True, r=['perm', 'krb'], w=[('ps', b2)])
        o.TT('dve', t1[64:96, :], ps[br][64:96, 0:tb], cosb[64:96, :], ALU.mult, r=[('ps', br), 'cosb'], w=['t1'])
        o.TT('dve', t2[64:96, :], ps[b2][64:96, 0:tb], sinb[64:96, :], ALU.mult, r=[('ps', b2), 'sinb'], w=['t2'])
        o.TT('dve', krf[64:96, :], t1[64:96, :], t2[64:96, :], ALU.add, r=['t1', 't2'], w=['krf'])
        o.CP('pool', krb[64:96, :], krf[64:96, :], r=['krf'], w=['krb'])
        norm_fm(o, C, [ckf[:, cc, :] for cc in range(2)], [G['gkvl'][:, cc:cc + 1] for cc in range(2)],
                [cTs[:, cc, :] for cc in range(2)], 128, 256, tb, ['ckf'], 'cTs', ones[:])
        for cc in range(2):
            o.STT('dve', ckf[:, cc, :], ckf[:, cc, :], G['gkvl'][:, cc:cc + 1], C['rstd'][:, 0:tb], ALU.mult, ALU.mult,
                  r=['ckf', 'rstd', 'cTs'], w=['ckf'])
        b = C['rA'].next()
        o.TR(ps[b][0:32, 0:128], ckf[:, 0, :], ident[:], r=['ckf', 'ident'], w=[('ps', b)])
        o.TR(ps[b][0:32, 128:256], ckf[:, 1, :], ident[:], r=['ckf', 'ident'], w=[('ps', b)])
        o.TR(ps[b][0:32, 256:288], krf[64:96, :], ident[64:96, 64:96], r=['krf', 'ident'], w=[('ps', b)])
        o.CP('act', kvo[:, :], ps[b][0:32, 0:288], r=[('ps', b)], w=['kvo'])
        o.DMA('sp', O["kv_s"][:, :], kvo[:, :], r=['kvo'], out=True)
        o.CP('dve', KVbn[:, 0:288], kvo[:, :], r=['kvo'], w=['KVbn'])
        b3, b4 = C['rG'].next(), C['rG'].next()
        for cc in range(2):
            o.MM(ps[b3][0:32, 0:512], cTs[:, cc, :], C['wuk'][:, cc, 0:512], cc == 0, cc == 1, r=['cTs', 'wuk'], w=[('ps', b3)])
        for cc in range(2):
            o.MM(ps[b4][0:32, 0:256], cTs[:, cc, :], C['wuk'][:, cc, 512:768], cc == 0, cc == 1, r=['cTs', 'wuk'], w=[('ps', b4)])
        o.ACT(sqn[0:32, 0:512], ps[b3][0:32, 0:512], AF.Square, r=[('ps', b3)], w=['sqn'])
        o.ACT(sqn[0:32, 512:768], ps[b4][0:32, 0:256], AF.Square, r=[('ps', b4)], w=['sqn'])
        o.RED(ssn[0:32, 0, 0:12], sqn[0:32, :].rearrange("p (h d) -> p h d", d=64), r=['sqn'], w=['ssn'])
        o.ACT(sqr[0:32, 0:32], kvo[:, 256:288], AF.Square, r=['kvo'], w=['sqr'])
        o.RED(ssn[0:32, 0, 12:13], sqr[0:32, 0:32].rearrange("p (h d) -> p h d", d=32), r=['sqr'], w=['ssn2'])
        o.TSA('dve', ssn[0:32, 0, 0:12], ssn[0:32, 0, 0:12], ssn[0:32, 0, 12:13], r=['ssn', 'ssn2'], w=['ssn'])
        o.ACT(ssn[0:32, 0, 0:12], ssn[0:32, 0, 0:12], AF.Ln, r=['ssn'], w=['ssn'], scale=1.0, bias=96.0 * EPS)
        o.ACT(kinvn[:, :], ssn[0:32, 0, 0:12], AF.Exp, r=['ssn'], w=['kinvn'], scale=-0.5)

    def mix_b(l):
        jl = l - 2
        xnorm('gmix', l)
        for c in range(3):
            b = C['rA'].next()
            for k in range(8):
                o.MM(ps[b][:, 0:tb], win[:, k, c * 128:(c + 1) * 128], hT[:, k, :], k == 0, k == 7, r=['win', 'hT'], w=[('ps', b)])
            o.CP('act', cq[:, c, :], ps[b][:, 0:tb], r=[('ps', b)], w=['cq'])
        for hp in range(2):
            b = C['rA'].next()
            for k in range(8):
                o.MM(ps[b][:, 0:tb], win[:, k, 384 + hp * 128:384 + (hp + 1) * 128], hT[:, k, :], k == 0, k == 7,
                     r=['win', 'hT'], w=[('ps', b)])
            o.CP('act', qm[:, hp, :], ps[b][:, 0:tb], r=[('ps', b)], w=['qm'])
        norm_fm(o, C, [cq[:, c, :] for c in range(3)], [G['gql'][:, jl * 3 + c:jl * 3 + c + 1] for c in range(3)],
                [cqn[:, c, :] for c in range(3)], 128, 384, tb, ['cq'], 'cqn', ones[:])
        uq = wmisc[:, 0:3456].rearrange("p (k n) -> p k n", n=1152)
        for h in range(12):
            bq = C['rG'].next()
            for k in range(3):
                o.MM(ps[bq][0:96, 0:tb], uq[:, k, h * 96:(h + 1) * 96], cqn[:, k, :], k == 0, k == 2, r=['wmisc', 'cqn'], w=[('ps', bq)])
            norm_fm(o, C, [ps[bq][0:96, 0:tb]], [G['gq'][:, jl:jl + 1]], [qg[:, :]], 96, 96, tb, [('ps', bq)], 'qg', ones[0:96, 0:96])
            b2 = C['rG'].next()
            o.MM(ps[b2][0:96, 0:tb], perm[:, :], qg[:, :], True, True, r=['perm', 'qg'], w=[('ps', b2)])
            o.STT('dve', t1[:, :], qg[:, :], G['gk'][:, 0:1], cosb[:, :], ALU.mult, ALU.mult, r=['qg', 'cosb', 'gains'], w=['t1'])
            o.STT('dve', t2[:, :], ps[b2][0:96, 0:tb], G['gk'][:, 0:1], sinb[:, :], ALU.mult, ALU.mult,
                  r=[('ps', b2), 'sinb', 'gains'], w=['t2'])
            o.TT('pool', QTs[:, h, :], t1[:, :], t2[:, :], ALU.add, r=['t1', 't2'], w=['QTs'])
        for cc in range(2):
            b = C['rA'].next()
            for h in range(12):
                o.MM(ps[b][:, h * tb:(h + 1) * tb], wukT[0:64, h, cc * 128:(cc + 1) * 128], QTs[0:64, h, :], True, True,
                     r=['wukT', 'QTs'], w=[('ps', b)])
            o.CP('dve', QL[:, cc, :, :], ps[b][:, 0:12 * tb].rearrange("p (h t) -> p h t", t=tb), r=[('ps', b)], w=['QL'])
        BO, PSB, BSC = 4, 5, 3
        ngr = NPG // 4
        NG = NS * ngr
        psc = ps[PSB][:, :].bitcast(BF16)
        psr = ps[PSB + 1][:, :].bitcast(BF16)

        def slot(g, q):
            return (g % 4) * 4 + q

        def GATHER(g):
            for q in range(4):
                pg = g * 4 + q
                sl = slot(g, q)
                P.op('pool', lambda e, sl=sl, pg=pg: e.indirect_dma_start(
                    out=KVb[sl][:, 0:288], out_offset=None, in_=I["ckv"][:, :],
                    in_offset=bass.IndirectOffsetOnAxis(ap=idxall[:, pg:pg + 1], axis=0)),
                    r=['idx'], w=[('KVb', sl)], dma=True)

        def T(g):
            kt = g % 2
            for q in range(4):
                sl = slot(g, q)
                for cc in range(2):
                    o.TR(psc[:, (q * 2 + cc) * 128:(q * 2 + cc + 1) * 128], KVb[sl][:, cc * 128:(cc + 1) * 128], identb[:, :],
                         r=[('KVb', sl), 'identb'], w=[('ps', PSB)])
                o.TR(psr[0:96, q * 128:(q + 1) * 128], KVb[sl][:, 192:288], identb[:, :], r=[('KVb', sl), 'identb'], w=[('ps', PSB + 1)])
            o.CP('dve', KVTc[kt][:, :, :, :], psc[:, 0:1024].rearrange("p (q c k) -> p q c k", c=2, k=128), r=[('ps', PSB)], w=[('KVTc', kt)])
            o.CP('act', KVTr[kt][64:96, :, :], psr[64:96, 0:512].rearrange("p (q k) -> p q k", k=128), r=[('ps', PSB + 1)], w=[('KVTr', kt)])

        def KN(g):
            kt = g % 2
            for q in range(4):
                sl = slot(g, q)
                b3, b4 = (7, 0) if q % 2 == 0 else (1, 2)
                sq_ = sqn2[q % 2]
                sk = ('sqn', q % 2)
                for cc in range(2):
                    o.MM(ps[b3][:, 0:512], KVTc[kt][:, q, cc, :], C['wuk'][:, cc, 0:512], cc == 0, cc == 1,
                         r=[('KVTc', kt), 'wuk'], w=[('ps', b3)])
                for cc in range(2):
                    o.MM(ps[b4][:, 0:256], KVTc[kt][:, q, cc, :], C['wuk'][:, cc, 512:768], cc == 0, cc == 1,
                         r=[('KVTc', kt), 'wuk'], w=[('ps', b4)])
                o.ACT(sq_[:, 0:512], ps[b3][:, 0:512], AF.Square, r=[('ps', b3)], w=[sk])
                o.ACT(sq_[:, 512:768], ps[b4][:, 0:256], AF.Square, r=[('ps', b4)], w=[sk])
                o.ACT(sq_[:, 768:800], KVb[sl][:, 256:288], AF.Square, r=[('KVb', sl)], w=[sk])
                o.RED(ssn[:, q, 0:12], sq_[:, 0:768].rearrange("p (h d) -> p h d", d=64), r=[sk], w=['ssn'])
                o.RED(ssn[:, q, 12:13], sq_[:, 768:800].rearrange("p (h d) -> p h d", d=32), r=[sk], w=['ssn'])
            o.TT('dve', ssn[:, :, 0:12], ssn[:, :, 0:12], ssn[:, :, 12:13].to_broadcast([128, 4, 12]), ALU.add, r=['ssn'], w=['ssn'])
            o.ACT(ssn[:, :, 0:12], ssn[:, :, 0:12], AF.Ln, r=['ssn'], w=['ssn'], scale=1.0, bias=96.0 * EPS)
            o.ACT(kinv[:, g * 4:g * 4 + 4, :], ssn[:, :, 0:12], AF.Exp, r=['ssn'], w=['kinv'], scale=-0.5)

        def SE(g):
            kt = g % 2
            s_ = g // ngr
            qcols = slice(s_ * 8, (s_ + 1) * 8)
            for q in range(4):
                oc = ps[BSC][:, q * 96:(q + 1) * 96].rearrange("p (h i) -> p h i", i=8)
                o.MM(oc, KVTc[kt][:, q, 0, :], QL[:, 0, :, qcols], True, False, r=[('KVTc', kt), 'QL'], w=[('ps', BSC)])
                o.MM(oc, KVTc[kt][:, q, 1, :], QL[:, 1, :, qcols], False, False, r=[('KVTc', kt), 'QL'], w=[('ps', BSC)])
                o.MM(oc, KVTr[kt][64:96, q, :], QTs[64:96, :, qcols], False, True, r=[('KVTr', kt), 'QTs'], w=[('ps', BSC)])
            o.TT('dve', tms[:, :, :].rearrange("p q (h i) -> p q h i", i=8),
                 ps[BSC][:, 0:384].rearrange("p (q h i) -> p q h i", h=12, i=8),
                 kinv[:, g * 4:g * 4 + 4, :].unsqueeze(3).to_broadcast([128, 4, 12, 8]), ALU.mult, r=[('ps', BSC), 'kinv'], w=['tms'])
            o.ACT(PTs[kt][:, :, :], tms[:, :, :], AF.Exp, r=['tms'], w=[('PTs', kt)])

        def PV(g):
            kt = g % 2
            gi = g % ngr
            for q in range(4):
                sl = slot(g, q)
                o.MM(ps[BO][0:96, 0:289], PTs[kt][:, q, :], KVb[sl][:, 0:289], gi == 0 and q == 0, False,
                     r=[('PTs', kt), ('KVb', sl)], w=[('ps', BO)])

        def EPI(s_):
            qcols = slice(s_ * 8, (s_ + 1) * 8)
            oc = ps[BSC][0:32, 0:96].rearrange("p (h i) -> p h i", i=8)
            o.MM(oc, cTs[:, 0, :], QL[:, 0, :, qcols], True, False, r=['cTs', 'QL'], w=[('ps', BSC)])
            o.MM(oc, cTs[:, 1, :], QL[:, 1, :, qcols], False, False, r=['cTs', 'QL'], w=[('ps', BSC)])
            o.MM(oc, krb[64:96, :], QTs[64:96, :, qcols], False, True, r=['krb', 'QTs'], w=[('ps', BSC)])
            o.TT('dve', tms[0:32, 0, :].rearrange("p (h i) -> p h i", i=8), ps[BSC][0:32, 0:96].rearrange("p (h i) -> p h i", i=8),
                 kinvn[:, :].unsqueeze(2).to_broadcast([32, 12, 8]), ALU.mult, r=[('ps', BSC), 'kinvn'], w=['tms'])
            o.ACT(tms[0:32, 1, :], tms[0:32, 0, :], AF.Exp, r=['tms'], w=['tms'])
            o.TT('dve', PTn[:, :], tms[0:32, 1, :], smask[:, s_, :], ALU.mult, r=['tms', 'smask'], w=['PTn'])
            o.MM(ps[BO][0:96, 0:289], PTn[:, :], KVbn[:, 0:289], NPG == 0, True, r=['PTn', 'KVbn'], w=[('ps', BO)])
            o.RCP(rcs[:, :], ps[BO][0:96, 288:289], r=[('ps', BO)], w=['rcs'])
            o.TSM('dve', olat[:, :], ps[BO][0:96, 0:256], rcs[:, 0:1], r=[('ps', BO), 'rcs'], w=['olat'])
            b = PSB
            for cc in range(2):
                o.TR(ps[b][:, cc * 96:(cc + 1) * 96], olat[:, cc * 128:(cc + 1) * 128], ident[0:96, 0:96], r=['olat', 'ident'], w=[('ps', b)])
            o.CP('dve', olT[:, :, :], ps[b][:, 0:192].rearrange("p (c n) -> p c n", n=96), r=[('ps', b)], w=['olT'])
            b = PSB + 1
            for h in range(12):
                for cc in range(2):
                    o.MM(ps[b][:, h * 8:(h + 1) * 8], C['wuv'][:, cc, (h // 2) * 128:(h // 2 + 1) * 128], olT[:, cc, h * 8:(h + 1) * 8],
                         cc == 0, cc == 1, r=['wuv', 'olT'], w=[('ps', b)])
            for par in range(2):
                vr = slice(par * 64, par * 64 + 64)
                src = ps[b][vr, 0:96].rearrange("p (hp two i) -> p hp two i", two=2, i=8)[:, :, par, :]
                o.CP('dve', cat[vr, 0:6, qcols], src, r=[('ps', b)], w=['cat'])

        for g in range(min(3, NG)):
            GATHER(g)
        T(0)
        if l == 2:
            KN(0)
        if NG > 1:
            T(1)
        for g in range(NG):
            if g + 3 < NG:
                GATHER(g + 3)
            SE(g)
            if l == 2 and g + 1 < NG:
                KN(g + 1)
            if g + 2 < NG:
                T(g + 2)
            PV(g)
            if g % ngr == ngr - 1:
                EPI(g // ngr)
        mem_attend(l, 6)
        out_proj(l)

    o.DMA('sp', xin[:], I["xs"][:, :], w=['xin'])
    for c in range(8):
        b = C['rA'].next()
        o.TR(ps[b][:, 0:32], xin[:, c * 128:(c + 1) * 128], ident[0:32, 0:32], r=['xin', 'ident'], w=[('ps', b)])
        o.CP('dve', xT[:, c, :], ps[b][:, 0:32], r=[('ps', b)], w=[('xT', c)])
    load_mix_weights(0)
    mem_load(0)
    halo_load(0)

    def mk_hook(l):
        def hook(fg):
            if fg == 1:
                load_mix_weights(l + 1)
            elif fg in (2, 4, 6, 8):
                mem_load(l + 1, [(fg - 2) // 2])
            elif fg in (3, 5, 7, 9) and l + 1 < 2:
                halo_load(l + 1, [(fg - 3) // 2])
        return hook

    for l in range(nlayers):
        pre = [load_ffn_group(l, fg) for fg in range(min(NRING, NFG))]
        if l < 2:
            mix_a(l)
        else:
            mix_b(l)
        ffn(l, pre, hook=mk_hook(l) if l < nlayers - 1 else None)
        if l == 1:
            kv_stage()
    for c in range(8):
        b = C['rA'].next()
        o.TR(ps[b][0:32, 0:128], xT[:, c, :], ident[:, :], r=[('xT', c), 'ident'], w=[('ps', b)])
        o.CP('dve', xin[:, c * 128:(c + 1) * 128], ps[b][0:32, 0:128], r=[('ps', b)], w=['xin'])
    o.DMA('sp', O["y_s"][:, :], xin[:], r=['xin'], out=True)
    P.finish()
    P.emit('s')
    print("sample program: ops", len(P.ops), "waits", P.nwaits)


def make_in_maps(inputs, cores, NT=2048):
    consts = host_consts()
    maps = []
    ckv = np.ascontiguousarray(inputs["cache_kv"]).reshape(-1, 288)
    shared = {k: np.ascontiguousarray(inputs[k]) for k in
              ("norm_mix", "norm_ffn", "w_out", "w_ffn_in", "w_ffn_out", "norm_mem", "w_mem_kv", "g_mem_q", "g_mem_k",
               "w_in_a", "w_pool_grp", "pool_scale", "w_in_b", "g_q_lora", "w_uq", "g_q", "norm_kv", "w_dkv",
               "g_kv_lora", "w_uk", "w_uv", "g_k_nope", "g_k_rope")}
    for c in cores:
        m = dict(shared)
        m.update(consts)
        m["xp"] = np.ascontiguousarray(inputs["x_prompt"][c][:NT])
        m["xs"] = np.ascontiguousarray(inputs["x_sample"][4 * c:4 * c + 4]).reshape(32, D)
        m["ckv"] = ckv
        m["pt"] = np.ascontiguousarray(inputs["page_table"][4 * c:4 * c + 4]).astype(np.int32)
        m["spool"] = np.ascontiguousarray(inputs["state_pool"][:, 4 * c:4 * c + 4])
        m["cmk"] = np.ascontiguousarray(inputs["cache_mem_k"][:, 4 * c:4 * c + 4]).reshape(4, 4, 256, 256)
        m["cmv"] = np.ascontiguousarray(inputs["cache_mem_v"][:, 4 * c:4 * c + 4]).reshape(4, 4, 256, 256)
        m["memp"] = np.ascontiguousarray(inputs["mem_prompt"][c])
        maps.append(m)
    return maps


def kernel(**inputs):
    nc = build()
    maps = make_in_maps(inputs, list(range(NCORES)))
    res = run_bass_kernel_spmd(nc, maps, core_ids=list(range(NCORES)))
    R = res.results
    y_p = np.stack([R[c]["y_p"] for c in range(NCORES)])
    y_s = np.concatenate([R[c]["y_s"].reshape(4, 8, D) for c in range(NCORES)])
    kv_p = np.stack([R[c]["kv_p"] for c in range(NCORES)])
    kv_s = np.concatenate([R[c]["kv_s"].reshape(4, 8, 288) for c in range(NCORES)])
    pool_p = np.stack([R[c]["pool_p"] for c in range(NCORES)], axis=1)
    pool_s = np.concatenate([R[c]["pool_s"] for c in range(NCORES)], axis=1)
    mk = np.stack([R[c]["mk_p"] for c in range(NCORES)], axis=1).reshape(4, 8, 256, 4, 64)
    mv = np.stack([R[c]["mv_p"] for c in range(NCORES)], axis=1).reshape(4, 8, 256, 4, 64)
    return (y_p, y_s, kv_p, kv_s, pool_p, pool_s, mk, mv)
```
